# Optimizing a Trainium2 kernel written in Bass

```python
import math
import jax, jax.numpy as jnp
from jax import lax
import numpy as np

D_MODEL = 1024
BATCH = 2
SEQ = 8192
DEPTH = 1

GRID_W = 64
CTX_LEN = 256
N_HEADS_M = 4
D_MLSTM = 1024
HEAD_DIM_M = D_MLSTM // N_HEADS_M
MLSTM_CHUNK = 128
CONV_W = 3
N_GROUPS_S = 4
D_SGU = 1024
GROUP_DIM_S = D_SGU // N_GROUPS_S
SGU_CHUNK = 128
D_FF = 2816
N_MOD = 9
POS_BASE = 10000.0
F_BIAS_LO = 3.0
F_BIAS_HI = 6.0
EPS = 1e-6
PROJ_SIZES = (D_MLSTM, D_MLSTM, D_MLSTM, 4 * N_HEADS_M, D_MLSTM, D_SGU, D_SGU, D_MODEL, D_MODEL)
N_STATE_PIECES = 4
D_PROJ = sum(PROJ_SIZES)

kernel_name = 'hybrid_mlstm_sgu_macaron_block'


def rmsnorm(x, g):
    x32 = x.astype(jnp.float32)
    y = x32 * lax.rsqrt(jnp.mean(x32 * x32, axis=-1, keepdims=True) + EPS)
    return (y * g.astype(jnp.float32)).astype(x.dtype)


def modulate(x, shift, scale):
    return x * (1.0 + scale[:, None, :]) + shift[:, None, :]


def ffn_sublayer(h, shift, scale, gate, g, w_in, w_out):
    hn = modulate(rmsnorm(h, g), shift, scale)
    a, b = jnp.split(hn @ w_in, 2, axis=-1)
    y = (jax.nn.silu(a) * b) @ w_out
    return h + 0.5 * gate[:, None, :] * y


def grid_pos_emb(rows):
    t = jnp.arange(rows * GRID_W)
    r = (t // GRID_W).astype(jnp.float32)
    col = (t % GRID_W).astype(jnp.float32)
    quarter = D_MODEL // 4
    freqs = jnp.exp(-math.log(POS_BASE) * jnp.arange(quarter, dtype=jnp.float32) / quarter)
    ar = r[:, None] * freqs
    ac = col[:, None] * freqs
    return jnp.concatenate([jnp.sin(ar), jnp.cos(ar), jnp.sin(ac), jnp.cos(ac)], axis=-1)


def split_cols(p, sizes):
    out = []
    off = 0
    for s in sizes:
        out.append(p[..., off:off + s])
        off += s
    return out


def project(h, shift, scale, g, w_in, sizes):
    hn = modulate(rmsnorm(h, g), shift, scale)
    return split_cols(hn @ w_in[:, :sum(sizes)], sizes)


def short_conv(x, w, b):
    pad = CONV_W // 2
    s = x.shape[1]
    xp = jnp.pad(x, ((0, 0), (pad, pad), (0, 0)))
    return sum(xp[:, j:j + s] * w[j] for j in range(CONV_W)) + b


def to_heads(t):
    b, s, _ = t.shape
    return t.reshape(b, s, N_HEADS_M, HEAD_DIM_M).transpose(0, 2, 1, 3).astype(jnp.float32)


def mlstm_prep(q, k, v, gates, conv_w, conv_b, b_gates):
    qk = jax.nn.silu(short_conv(jnp.concatenate([q, k], axis=-1), conv_w, conv_b))
    q, k = jnp.split(qk, 2, axis=-1)
    q = to_heads(q) * HEAD_DIM_M ** -0.5
    k = to_heads(k)
    v = to_heads(v)
    gt = jnp.moveaxis((gates + b_gates).astype(jnp.float32), -1, 1)
    nh = N_HEADS_M
    fwd = (gt[:, :nh], jax.nn.log_sigmoid(gt[:, nh:2 * nh]))
    bwd = (gt[:, 2 * nh:3 * nh], jax.nn.log_sigmoid(gt[:, 3 * nh:]))
    return q, k, v, fwd, bwd


def zero_state(bsz):
    return (jnp.zeros((bsz, N_HEADS_M, HEAD_DIM_M, HEAD_DIM_M), jnp.float32),
            jnp.zeros((bsz, N_HEADS_M, HEAD_DIM_M), jnp.float32),
            jnp.zeros((bsz, N_HEADS_M), jnp.float32))


def mlstm_scan(q, k, v, li, lf, state, reverse, emit):
    if reverse:
        q, k, v = jnp.flip(q, 2), jnp.flip(k, 2), jnp.flip(v, 2)
        li, lf = jnp.flip(li, 2), jnp.flip(lf, 2)
    bsz, nh, s, dh = q.shape
    nc = s // MLSTM_CHUNK

    def to_chunks(t):
        return jnp.moveaxis(t.reshape(t.shape[:2] + (nc, MLSTM_CHUNK) + t.shape[3:]), 2, 0)

    tri = jnp.tril(jnp.ones((MLSTM_CHUNK, MLSTM_CHUNK), dtype=bool))

    def body(carry, xs):
        c_st, n_st, m_st = carry
        qc, kc, vc, lic, lfc = xs
        b = jnp.cumsum(lfc, axis=-1)
        g = b[..., -1]
        w = g[..., None] - b + lic
        m_new = jnp.maximum(g + m_st, jnp.max(w, axis=-1))
        ws = jnp.exp(w - m_new[..., None])
        decay = jnp.exp(g + m_st - m_new)
        c_new = decay[..., None, None] * c_st + jnp.einsum('bhsv,bhsk->bhvk', vc * ws[..., None], kc)
        n_new = decay[..., None] * n_st + jnp.einsum('bhs,bhsk->bhk', ws, kc)
        if not emit:
            return (c_new, n_new, m_new), None
        dm = b[..., :, None] - b[..., None, :] + lic[..., None, :]
        dm = jnp.where(tri, dm, -jnp.inf)
        a = b + m_st[..., None]
        m_t = jnp.maximum(a, jnp.max(dm, axis=-1))
        s_ts = jnp.einsum('bhtk,bhsk->bhts', qc, kc) * jnp.exp(dm - m_t[..., None])
        inter = jnp.exp(a - m_t)
        num = jnp.einsum('bhts,bhsv->bhtv', s_ts, vc) + inter[..., None] * jnp.einsum('bhvk,bhtk->bhtv', c_st, qc)
        den = jnp.sum(s_ts, axis=-1) + inter * jnp.einsum('bhk,bhtk->bht', n_st, qc)
        h = num / jnp.maximum(jnp.abs(den), jnp.exp(-m_t))[..., None]
        return (c_new, n_new, m_new), h

    state, hs = lax.scan(body, state, (to_chunks(q), to_chunks(k), to_chunks(v), to_chunks(li), to_chunks(lf)))
    if not emit:
        return None, state
    h = jnp.moveaxis(hs, 0, 2).reshape(bsz, nh, s, dh)
    if reverse:
        h = jnp.flip(h, 2)
    return h, state


def head_layernorm(h, g):
    mu = jnp.mean(h, axis=-1, keepdims=True)
    hc = h - mu
    y = hc * lax.rsqrt(jnp.mean(hc * hc, axis=-1, keepdims=True) + EPS)
    bsz, nh, s, dh = h.shape
    return y.transpose(0, 2, 1, 3).reshape(bsz, s, nh * dh) * g.astype(jnp.float32)


def spatial_gating(u, v, g, w_s, b_s):
    bsz, s, _ = v.shape
    nc = s // SGU_CHUNK
    vn = rmsnorm(v, g).reshape(bsz, nc, SGU_CHUNK, N_GROUPS_S, GROUP_DIM_S)
    mixed = jnp.einsum('gts,bnsgc->bntgc', w_s, vn) + jnp.swapaxes(b_s, 0, 1)[:, :, None]
    return u * mixed.reshape(bsz, s, D_SGU)


def merge_branches(pieces, h_m, g_head, g_sgu, w_s, b_s, w_a, w_b, w_o):
    o, u, vs, ga, gb = pieces
    y_a = jax.nn.sigmoid(o) * head_layernorm(h_m, g_head).astype(o.dtype)
    y_b = spatial_gating(jax.nn.gelu(u), jax.nn.gelu(vs), g_sgu, w_s, b_s)
    mixed = jax.nn.sigmoid(ga) * (y_a @ w_a) + jax.nn.sigmoid(gb) * (y_b @ w_b)
    return mixed @ w_o


def setup_inputs(seed: int = 0) -> dict:
    key = jax.random.key(seed)
    ks = jax.random.split(key, 28)
    f32 = jnp.float32
    L = DEPTH

    def nrm(k, shape, s):
        return jax.random.normal(k, shape, f32) * s

    def gain(k, shape):
        return 1.0 + 0.05 * jax.random.normal(k, shape, f32)

    i_bias = nrm(ks[10], (L, 2, N_HEADS_M), 0.1)
    f_bias = jnp.linspace(F_BIAS_LO, F_BIAS_HI, N_HEADS_M, dtype=f32) + nrm(ks[11], (L, 2, N_HEADS_M), 0.1)
    return {
        'x': nrm(ks[0], (BATCH, SEQ, D_MODEL), 1.0),
        'c': nrm(ks[1], (BATCH, D_MODEL), 1.0),
        'ctx': nrm(ks[2], (BATCH, CTX_LEN, D_MODEL), 1.0),
        'c_ctx': nrm(ks[3], (D_MODEL,), 1.0),
        'w_ada': nrm(ks[4], (L, D_MODEL, N_MOD * D_MODEL), 0.5 * D_MODEL ** -0.5),
        'b_ada': nrm(ks[5], (L, N_MOD * D_MODEL), 0.01),
        'g_ffn1': gain(ks[6], (L, D_MODEL)),
        'w_ffn1_in': nrm(ks[7], (L, D_MODEL, 2 * D_FF), D_MODEL ** -0.5),
        'w_ffn1_out': nrm(ks[8], (L, D_FF, D_MODEL), D_FF ** -0.5),
        'g_mix': gain(ks[9], (L, D_MODEL)),
        'w_in': nrm(ks[12], (L, D_MODEL, D_PROJ), D_MODEL ** -0.5),
        'b_gates': jnp.stack([i_bias, f_bias], axis=2).reshape(L, 4 * N_HEADS_M),
        'conv_qk_w': nrm(ks[13], (L, CONV_W, 2 * D_MLSTM), CONV_W ** -0.5),
        'conv_qk_b': nrm(ks[14], (L, 2 * D_MLSTM), 0.01),
        'g_head': gain(ks[15], (L, D_MLSTM)),
        'g_sgu': gain(ks[16], (L, D_SGU)),
        'w_s': nrm(ks[17], (L, N_GROUPS_S, SGU_CHUNK, SGU_CHUNK), SGU_CHUNK ** -0.5),
        'b_s': 1.0 + nrm(ks[18], (L, N_GROUPS_S, SGU_CHUNK), 0.1),
        'w_branch_a': nrm(ks[19], (L, D_MLSTM, D_MODEL), D_MLSTM ** -0.5),
        'w_branch_b': nrm(ks[20], (L, D_SGU, D_MODEL), D_SGU ** -0.5),
        'w_out': nrm(ks[21], (L, D_MODEL, D_MODEL), D_MODEL ** -0.5),
        'g_ffn2': gain(ks[22], (L, D_MODEL)),
        'w_ffn2_in': nrm(ks[23], (L, D_MODEL, 2 * D_FF), D_MODEL ** -0.5),
        'w_ffn2_out': nrm(ks[24], (L, D_FF, D_MODEL), D_FF ** -0.5),
        'g_final': gain(ks[25], (D_MODEL,)),
    }


def reference(x, c, ctx, c_ctx, w_ada, b_ada, g_ffn1, w_ffn1_in, w_ffn1_out, g_mix, w_in, b_gates,
              conv_qk_w, conv_qk_b, g_head, g_sgu, w_s, b_s, w_branch_a, w_branch_b, w_out,
              g_ffn2, w_ffn2_in, w_ffn2_out, g_final):
    rows = x.shape[1] // GRID_W
    bsz = x.shape[0]
    h = x + grid_pos_emb(rows).astype(x.dtype)[None]
    hc = ctx
    for layer in range(DEPTH):
        is_last = layer == DEPTH - 1
        mods = jnp.split(jax.nn.silu(c) @ w_ada[layer] + b_ada[layer], N_MOD, axis=-1)
        mods_c = jnp.split(jax.nn.silu(c_ctx)[None] @ w_ada[layer] + b_ada[layer], N_MOD, axis=-1)

        h = ffn_sublayer(h, mods[0], mods[1], mods[2], g_ffn1[layer], w_ffn1_in[layer], w_ffn1_out[layer])
        hc = ffn_sublayer(hc, mods_c[0], mods_c[1], mods_c[2], g_ffn1[layer], w_ffn1_in[layer], w_ffn1_out[layer])

        p_lat = project(h, mods[3], mods[4], g_mix[layer], w_in[layer], PROJ_SIZES)
        ctx_sizes = PROJ_SIZES[:N_STATE_PIECES] if is_last else PROJ_SIZES
        p_ctx = project(hc, mods_c[3], mods_c[4], g_mix[layer], w_in[layer], ctx_sizes)
        q_l, k_l, v_l, gf_l, gb_l = mlstm_prep(*p_lat[:N_STATE_PIECES], conv_qk_w[layer], conv_qk_b[layer], b_gates[layer])
        q_c, k_c, v_c, gf_c, gb_c = mlstm_prep(*p_ctx[:N_STATE_PIECES], conv_qk_w[layer], conv_qk_b[layer], b_gates[layer])

        state0 = zero_state(bsz)
        hcf, st_f = mlstm_scan(q_c, k_c, v_c, gf_c[0], gf_c[1], state0, False, not is_last)
        hcb, st_b = mlstm_scan(q_c, k_c, v_c, gb_c[0], gb_c[1], state0, True, not is_last)
        hlf, _ = mlstm_scan(q_l, k_l, v_l, gf_l[0], gf_l[1], st_f, False, True)
        hlb, _ = mlstm_scan(q_l, k_l, v_l, gb_l[0], gb_l[1], st_b, True, True)

        y_lat = merge_branches(p_lat[N_STATE_PIECES:], hlf + hlb, g_head[layer], g_sgu[layer], w_s[layer], b_s[layer],
                               w_branch_a[layer], w_branch_b[layer], w_out[layer])
        h = h + mods[5][:, None, :] * y_lat
        if not is_last:
            y_ctx = merge_branches(p_ctx[N_STATE_PIECES:], hcf + hcb, g_head[layer], g_sgu[layer], w_s[layer], b_s[layer],
                                   w_branch_a[layer], w_branch_b[layer], w_out[layer])
            hc = hc + mods_c[5][:, None, :] * y_ctx
            hc = ffn_sublayer(hc, mods_c[6], mods_c[7], mods_c[8], g_ffn2[layer], w_ffn2_in[layer], w_ffn2_out[layer])

        h = ffn_sublayer(h, mods[6], mods[7], mods[8], g_ffn2[layer], w_ffn2_in[layer], w_ffn2_out[layer])
    return rmsnorm(h, g_final)
```

```python
import math
from contextlib import ExitStack

import numpy as np
import concourse.bass as bass
import concourse.mybir as mybir
from concourse.bass_utils import run_bass_kernel_spmd

F32 = mybir.dt.float32
BF16 = mybir.dt.bfloat16
I32 = mybir.dt.int32
AF = mybir.ActivationFunctionType
ALU = mybir.AluOpType

D = 1024
NT = 2048
NX = NT + 2
NCTX = 256
NCH = 16
DFF = 2816
NFF = 22
DPROJ = 8208
EPS = 1e-6
TWO_PI = 2.0 * math.pi
PI_SAFE = 3.1415925
XW = 8 * 514 + 8

DEBUG = {}
import os
KSTOP = os.environ.get("KSTOP", "all")
KITEMS = int(os.environ.get("KITEMS", "1"))


class Sch:
    NDS = 40

    def __init__(self, nc, es):
        self.nc = nc
        self.E = {'pe': nc.tensor, 'act': nc.scalar, 'dve': nc.vector, 'pool': nc.gpsimd, 'sp': nc.sync}
        self.semobj = {}
        for e in ['pe', 'act', 'dve', 'pool']:
            self.semobj[e] = es.enter_context(nc.semaphore(f"sem_{e}"))
        self.cnt = {e: 0 for e in ['pe', 'act', 'dve', 'pool']}
        self.seen = {e: {} for e in self.E}
        self.lw = {}
        self.rd = {}
        self.pend = {e: [] for e in self.cnt}
        self.dcnt = [0] * self.NDS
        for i in range(self.NDS):
            self.semobj[('d', i)] = es.enter_context(nc.semaphore(f"sem_d{i}"))
        self.dnext = 0
        self.nops = 0
        self.semobj['cc'] = es.enter_context(nc.semaphore("sem_cc"))
        self.cccnt = 0
        self.smallp = set()

    def _deps(self, eng, r, w):
        deps = set()
        for t in r:
            d = self.lw.get(t)
            if d is not None:
                deps.add((d, True))
        for t in w:
            d = self.lw.get(t)
            if d is not None:
                deps.add((d, True))
            for d in self.rd.get(t, ()):
                deps.add((d, False))
        return deps

    def _wait(self, eng, deps, strict=False):
        for (d, is_w) in deps:
            if d[0] == 'PEND':
                assert d[1] == eng, f"dependency on unsignaled op of {d[1]} from {eng}"
                continue
            key, val, src = d
            if src == eng:
                if eng == 'pe' or not is_w:
                    continue
                if not (strict or (key, val) in self.smallp):
                    continue
            if self.seen[eng].get(key, 0) >= val:
                continue
            self.E[eng].wait_ge(self.semobj[key], val)
            self.seen[eng][key] = val

    def op(self, eng, fn, r=(), w=(), sig=True, strict=False, small=False):
        strict = strict or small
        self._wait(eng, self._deps(eng, r, w), strict)
        ins = fn()
        self.nops += 1
        if sig:
            self.cnt[eng] += 1
            ins.then_inc(self.semobj[eng], 1)
            me = (eng, self.cnt[eng], eng)
            if small:
                self.smallp.add((eng, self.cnt[eng]))
            for (pr, pw) in self.pend[eng] + [(r, w)]:
                for t in pw:
                    self.lw[t] = me
                    self.rd[t] = []
            pm = ('PEND', eng)
            for (pr, pw) in self.pend[eng] + [(r, w)]:
                for t in pr:
                    lst = self.rd.setdefault(t, [])
                    if pm in lst:
                        lst[:] = [d for d in lst if d != pm]
                    if me not in lst:
                        lst.append(me)
            self.pend[eng] = []
        else:
            self.pend[eng].append((tuple(r), tuple(w)))
            for t in w:
                self.lw[t] = ('PEND', eng)
                self.rd[t] = []
            for t in r:
                self.rd.setdefault(t, []).append(('PEND', eng))
        return ins

    def dma(self, q, out, in_, r=(), w=(), **kw):
        deps = self._deps(q, r, w)
        idx = self.dnext
        self.dnext = (self.dnext + 1) % self.NDS
        if self.dcnt[idx] > 0:
            deps.add(((('d', idx), self.dcnt[idx], 'dma'), True))
        self._wait(q, deps)
        self.dcnt[idx] += 16
        self.E[q].dma_start(out=out, in_=in_, **kw).then_inc(self.semobj[('d', idx)], 16)
        me = (('d', idx), self.dcnt[idx], 'dma')
        for t in w:
            self.lw[t] = me
            self.rd[t] = []
        for t in r:
            self.rd.setdefault(t, []).append(me)

    def custom(self, eng, fn, inc, r=(), w=()):
        deps = self._deps(eng, r, w)
        self._wait(eng, deps)
        self.cccnt += inc
        fn(self.semobj['cc'])
        me = ('cc', self.cccnt, 'dma')
        for t in w:
            self.lw[t] = me
            self.rd[t] = []
        for t in r:
            self.rd.setdefault(t, []).append(me)

    def barrier(self):
        for e in self.cnt:
            assert not self.pend[e], f"pending unsignaled ops on {e} at barrier"
        deps = set()
        for e in self.cnt:
            if self.cnt[e] > 0:
                deps.add(((e, self.cnt[e], e), False))
        for i in range(self.NDS):
            if self.dcnt[i] > 0:
                deps.add(((('d', i), self.dcnt[i], 'dma'), True))
        if self.cccnt > 0:
            deps.add((('cc', self.cccnt, 'dma'), True))
        for e in self.E:
            self._wait(e, deps)
        self.lw = {}
        self.rd = {}


def build():
    nc = bass.Bass("TRN2", target_bir_lowering=False)

    def din(name, shape, dt=F32):
        return nc.dram_tensor(name, list(shape), dt, kind="ExternalInput").ap()

    xs = din("xs", [NX, D])
    ctxb = din("ctxb", [NCTX, D])
    vecs = din("vecs", [192, 128])
    meta = din("meta", [128, 8])
    freq = din("freq", [128, 2])
    consts = din("consts", [128, 3, 128])
    g_sgu = din("g_sgu", [D])
    b_gates = din("b_gates", [16])
    b_s = din("b_s", [512])
    w_s = din("w_s", [4, 128, 128])
    w_ada = din("w_ada", [D, 9 * D // 4])
    w_f1i = din("w_ffn1_in", [D, 2 * DFF])
    w_f1o = din("w_ffn1_out", [DFF, D])
    w_in = din("w_in", [D, DPROJ])
    w_ba = din("w_branch_a", [D, D])
    w_bb = din("w_branch_b", [D, D])
    w_o = din("w_out", [D, D])
    w_f2i = din("w_ffn2_in", [D, 2 * DFF])
    w_f2o = din("w_ffn2_out", [DFF, D])
    out = nc.dram_tensor("out", [NT, D], F32, kind="ExternalOutput").ap()

    def dscr(name, shape, dt):
        return nc.dram_tensor(name, list(shape), dt).ap()

    qT_d = dscr("qT_d", [8, 128, NT], BF16)
    kT_d = dscr("kT_d", [8, 128, NT + NCTX], BF16)
    ktok_d = dscr("ktok_d", [18, 128, D], BF16)
    v_d = dscr("v_d", [18, 128, D], BF16)
    vs_d = dscr("vs_d", [16, 128, D], F32)
    og_d = dscr("og_d", [8, 128, NT], BF16)
    gu_d = dscr("gu_d", [8, 128, NT], BF16)
    ga_d = dscr("ga_d", [8, 128, NT], BF16)
    gb_d = dscr("gb_d", [8, 128, NT], BF16)
    hb_d = dscr("hb_d", [16, 128, D], F32)
    mp_in = dscr("mp_in", [128, 36], F32)
    mp_out = dscr("mp_out", [4 * 128, 36], F32)
    xin_l = [dscr(f"xin_d{i}", [128, 1030], F32) for i in range(4)]
    xout_l = [dscr(f"xout_d{i}", [4 * 128, 1030], F32) for i in range(4)]

    dbg = {}
    for name, (shape, dt) in DEBUG.items():
        dbg[name] = nc.dram_tensor("dbg_" + name, list(shape), dt, kind="ExternalOutput").ap()

    with ExitStack() as es:
        S = Sch(nc, es)
        ACT, DVE, PE, POOL = nc.scalar, nc.vector, nc.tensor, nc.gpsimd

        def sb(stack, name, shape, dt):
            return stack.enter_context(nc.sbuf_tensor(name, list(shape), dt))

        def pstep(t):
            return t[:].ap[0][0]

        def cap(t, off, dims):
            return bass.AP(t, off, [[pstep(t), 128]] + [list(d) for d in dims])

        psall = es.enter_context(nc.psum_tensor("psall", [128, 8, 512], F32))
        ps = [psall[:, i, :] for i in range(8)]

        def P(i):
            return ("ps", i)

        ident_f = sb(es, "ident_f", [128, 128], F32)
        triL = sb(es, "triL", [128, 128], F32)
        triU = sb(es, "triU", [128, 128], F32)
        ident_b = sb(es, "ident_b", [128, 128], BF16)
        mL16 = sb(es, "mL16", [128, 128], F32)
        mU16 = sb(es, "mU16", [128, 128], F32)
        ones_f = sb(es, "ones_f", [128, 128], F32)
        ones_b = sb(es, "ones_b", [128, 128], BF16)
        vecT = sb(es, "vecT", [128, 192], F32)
        metat = sb(es, "metat", [128, 8], F32)
        modB = sb(es, "modB", [128, 72], F32)
        modC = sb(es, "modC", [128, 72], F32)
        prm = sb(es, "prm", [128, 16, 8], F32)
        hT = sb(es, "hT", [128, 8, NX], F32)
        hcT = sb(es, "hcT", [128, 8, NCTX], F32)
        eps_t = sb(es, "eps_t", [128, 1], F32)

        R_BADA, R_GF1, R_GMIX, R_CW, R_CB, R_GH, R_GF2, R_GFIN, R_C, R_CC = 0, 72, 80, 88, 136, 152, 160, 168, 176, 184
        (P_GS1, P_SH1, P_GT1, P_GS1C, P_SH1C, P_GT1C, P_GSM, P_SHM, P_GSMC, P_SHMC, P_G5, P_GS2, P_SH2, P_GT2) = range(14)

        S.dma('sp', ident_f[:], consts[:, 0, :], w=["ident_f"])
        S.dma('sp', triL[:], consts[:, 1, :], w=["triL"])
        S.dma('sp', triU[:], consts[:, 2, :], w=["triU"])
        S.dma('sp', metat[:], meta, w=["meta"])
        S.op('dve', lambda: DVE.tensor_copy(out=ident_b[:], in_=ident_f[:]), r=["ident_f"], w=["ident_b"])
        S.op('dve', lambda: DVE.tensor_scalar(out=mL16[:], in0=triL[:], scalar1=0.0625, scalar2=None, op0=ALU.mult), r=["triL"], w=["mL16"])
        S.op('dve', lambda: DVE.tensor_scalar(out=mU16[:], in0=triU[:], scalar1=0.0625, scalar2=None, op0=ALU.mult), r=["triU"], w=["mU16"])
        S.op('dve', lambda: DVE.memset(ones_f[:], 1.0), w=["ones_f"])
        S.op('dve', lambda: DVE.memset(ones_b[:], 1.0), w=["ones_b"])

        def wload(dst_tile, dst_tok, src_ap):
            S.dma('pool', dst_tile, src_ap, w=[dst_tok])

        with ExitStack() as p1:
            vst = sb(p1, "vst", [96, 2, 128], F32)
            S.dma('sp', vst[:, 0, :], vecs[0:96, :], w=["vst0"])
            S.dma('sp', vst[:, 1, :], vecs[96:192, :], w=["vst1"])
            for i in range(2):
                S.op('pe', lambda i=i: PE.transpose(out=ps[0][:, i * 96:(i + 1) * 96], in_=vst[:, i, :], identity=ident_f[0:96, 0:96]),
                     r=[f"vst{i}", "ident_f"], w=[P(0)], sig=(i == 1))
            S.op('dve', lambda: DVE.tensor_copy(out=vecT[:], in_=ps[0][:, 0:192]), r=[P(0)], w=["vecT"])

            scT = sb(p1, "scT", [128, 8, 2], F32)
            S.op('act', lambda: ACT.activation(out=scT[:, :, 0], in_=vecT[:, R_C:R_C + 8], func=AF.Silu), r=["vecT"], w=["scT0"])
            S.op('act', lambda: ACT.activation(out=scT[:, :, 1], in_=vecT[:, R_CC:R_CC + 8], func=AF.Silu), r=["vecT"], w=["scT1"])

            wad = [sb(p1, f"wad{i}", [128, 8, 256], F32) for i in range(3)]
            w_ada_v = w_ada.rearrange("(kc p) n -> p kc n", p=128)
            for blk in range(9):
                slot = blk % 3
                S.dma('sp', wad[slot][:], w_ada_v[:, :, blk * 256:(blk + 1) * 256], w=[f"wad{slot}"])
                for j in range(2):
                    b128 = blk * 2 + j
                    for kc in range(8):
                        S.op('pe', lambda slot=slot, j=j, kc=kc, b128=b128: PE.matmul(
                            ps[1][:, b128 * 2:b128 * 2 + 2], lhsT=wad[slot][:, kc, j * 128:(j + 1) * 128], rhs=scT[:, kc, :],
                            start=(kc == 0), stop=(kc == 7)),
                            r=[f"wad{slot}", "scT0", "scT1"], w=[P(1)], sig=(kc == 7 and j == 1))
            mpart = sb(p1, "mpart", [128, 36], F32)
            mg = sb(p1, "mg", [128, 4, 36], F32)
            S.op('dve', lambda: DVE.tensor_copy(out=mpart[:], in_=ps[1][:, 0:36]), r=[P(1)], w=["mpart"])
            S.dma('sp', mp_in, mpart[:], r=["mpart"], w=["mp_in"])
            S.custom('pool', lambda sem: POOL.collective_compute("AllGather", ALU.bypass, replica_groups=[[0, 1, 2, 3], [4, 5, 6, 7]],
                                                                 ins=[mp_in.opt()], outs=[mp_out.opt()]).then_inc(sem, 1),
                     1, r=["mp_in"], w=["mp_out"])
            S.dma('sp', mg[:], mp_out.rearrange("(r p) w -> p r w", p=128), r=["mp_out"], w=["mg"])
            psm = mg[:, :, :].rearrange("p r (b t) -> p (r b) t", t=2)
            S.op('dve', lambda: DVE.tensor_tensor(out=modB[:], in0=psm[:, :, 0], in1=vecT[:, R_BADA:R_BADA + 72], op=ALU.add), r=["mg", "vecT"], w=["modB"], small=True)
            S.op('dve', lambda: DVE.tensor_tensor(out=modC[:], in0=psm[:, :, 1], in1=vecT[:, R_BADA:R_BADA + 72], op=ALU.add), r=["mg", "vecT"], w=["modC"], small=True)

            def mk_gs(slot, gain_row, mod, scale_idx):
                S.op('dve', lambda: DVE.scalar_tensor_tensor(out=prm[:, slot, :], in0=mod[:, scale_idx * 8:scale_idx * 8 + 8], scalar=1.0,
                                                             in1=vecT[:, gain_row:gain_row + 8], op0=ALU.add, op1=ALU.mult),
                     r=["modB", "modC", "vecT"], w=[("prm", slot)], small=True)

            def mk_cp(slot, mod, idx, mul=1.0):
                S.op('dve', lambda: DVE.tensor_scalar(out=prm[:, slot, :], in0=mod[:, idx * 8:idx * 8 + 8], scalar1=mul, scalar2=None, op0=ALU.mult),
                     r=["modB", "modC"], w=[("prm", slot)], small=True)

            mk_gs(P_GS1, R_GF1, modB, 1); mk_cp(P_SH1, modB, 0); mk_cp(P_GT1, modB, 2, 0.5)
            mk_gs(P_GS1C, R_GF1, modC, 1); mk_cp(P_SH1C, modC, 0); mk_cp(P_GT1C, modC, 2, 0.5)
            mk_gs(P_GSM, R_GMIX, modB, 4); mk_cp(P_SHM, modB, 3)
            mk_gs(P_GSMC, R_GMIX, modC, 4); mk_cp(P_SHMC, modC, 3)
            mk_cp(P_G5, modB, 5)
            mk_gs(P_GS2, R_GF2, modB, 7); mk_cp(P_SH2, modB, 6); mk_cp(P_GT2, modB, 8, 0.5)

            fq = sb(p1, "fq", [128, 2], F32)
            S.dma('sp', fq[:], freq, w=["fq"])
            tab_r = sb(p1, "tab_r", [128, 4, 34], F32)
            tab_c = sb(p1, "tab_c", [128, 4, 64], F32)
            io_i = sb(p1, "io_i", [128, 64], I32)
            io_f = sb(p1, "io_f", [128, 64], F32)
            rv = sb(p1, "rv", [128, 34], F32)
            S.op('pool', lambda: POOL.iota(io_i[:], pattern=[[1, 64]], base=0, channel_multiplier=0), w=["io_i"])
            S.op('dve', lambda: DVE.tensor_copy(out=io_f[:], in_=io_i[:]), r=["io_i"], w=["io_f"], small=True)
            S.op('dve', lambda: DVE.tensor_scalar(out=rv[:], in0=io_f[:, 0:34], scalar1=metat[:, 0:1], scalar2=None, op0=ALU.add), r=["io_f", "meta"], w=["rv"], small=True)

            def mk_tab(tab, vals, n):
                arg = sb(p1, f"arg_{n}", [128, 4, n], F32)
                ki = sb(p1, f"ki_{n}", [128, 4, n], I32)
                kf = sb(p1, f"kf_{n}", [128, 4, n], F32)
                for cj in range(2):
                    for sc in range(2):
                        idx = sc * 2 + cj
                        S.op('dve', lambda idx=idx, cj=cj, sc=sc: DVE.tensor_scalar(
                            out=arg[:, idx, :], in0=vals, scalar1=fq[:, cj:cj + 1], scalar2=(0.5 * math.pi if sc else 0.0),
                            op0=ALU.mult, op1=ALU.add), r=["rv", "io_f", "fq"], w=[f"arg{n}"], small=True)
                S.op('dve', lambda: DVE.tensor_scalar(out=kf[:], in0=arg[:], scalar1=1.0 / TWO_PI, scalar2=None, op0=ALU.mult), r=[f"arg{n}"], w=[f"kf{n}"], small=True)
                S.op('dve', lambda: DVE.tensor_copy(out=ki[:], in_=kf[:]), r=[f"kf{n}"], w=[f"ki{n}"], small=True)
                S.op('dve', lambda: DVE.tensor_copy(out=kf[:], in_=ki[:]), r=[f"ki{n}"], w=[f"kf{n}"], small=True)
                S.op('dve', lambda: DVE.scalar_tensor_tensor(out=arg[:], in0=kf[:], scalar=-TWO_PI, in1=arg[:], op0=ALU.mult, op1=ALU.add),
                     r=[f"kf{n}"], w=[f"arg{n}"], small=True)
                S.op('dve', lambda: DVE.tensor_scalar(out=arg[:], in0=arg[:], scalar1=-PI_SAFE, scalar2=PI_SAFE, op0=ALU.max, op1=ALU.min), w=[f"arg{n}"], small=True)
                S.op('act', lambda: ACT.activation(out=tab[:], in_=arg[:], func=AF.Sin), r=[f"arg{n}"], w=[f"tab{n}"], small=True)

            mk_tab(tab_r, rv[:], 34)
            mk_tab(tab_c, io_f[:], 64)

            xt = [sb(p1, f"xt{i}", [128, D], F32) for i in range(2)]
            xh = sb(p1, "xh", [2, D], F32)
            for c in range(NCH):
                slot = c % 2
                S.dma('sp', xt[slot][:], xs[1 + 128 * c:1 + 128 * (c + 1), :], w=[f"xt{slot}"])
                for half in range(2):
                    pb = 2 + half
                    for k4 in range(4):
                        dc = half * 4 + k4
                        S.op('pe', lambda slot=slot, dc=dc, pb=pb, k4=k4: PE.transpose(
                            out=ps[pb][:, k4 * 128:(k4 + 1) * 128], in_=xt[slot][:, dc * 128:(dc + 1) * 128], identity=ident_f[:]),
                            r=[f"xt{slot}", "ident_f"], w=[P(pb)], sig=(k4 == 3))
                    o_ap = cap(hT, half * 4 * NX + 1 + 128 * c, [[NX, 4], [64, 2], [1, 64]])
                    i_ap = ps[pb][:, :].rearrange("p (a b c) -> p a b c", a=4, b=2, c=64)
                    if half == 0:
                        t_ap = cap(tab_r, 1 + 2 * c, [[34, 4], [1, 2], [0, 64]])
                        S.op('dve', lambda o_ap=o_ap, i_ap=i_ap, t_ap=t_ap: DVE.tensor_tensor(out=o_ap, in0=i_ap, in1=t_ap, op=ALU.add),
                             r=[P(pb), "tab34"], w=[("hT", c)])
                    else:
                        t_ap = cap(tab_c, 0, [[64, 4], [0, 2], [1, 64]])
                        S.op('dve', lambda o_ap=o_ap, i_ap=i_ap, t_ap=t_ap: DVE.tensor_tensor(out=o_ap, in0=i_ap, in1=t_ap, op=ALU.add),
                             r=[P(pb), "tab64"], w=[("hT", c)])
            S.dma('sp', xh[0:1, :], xs[0:1, :], w=["xh0"])
            S.dma('sp', xh[1:2, :], xs[NX - 1:NX, :], w=["xh1"])
            for dc in range(8):
                S.op('pe', lambda dc=dc: PE.transpose(out=ps[2][:, dc * 2:dc * 2 + 2], in_=xh[:, dc * 128:(dc + 1) * 128], identity=ident_f[0:2, 0:2]),
                     r=["xh0", "xh1", "ident_f"], w=[P(2)], sig=(dc == 7))
            pv = ps[2][:, 0:16].rearrange("p (d t) -> p d t", t=2)
            S.op('dve', lambda: DVE.tensor_tensor(out=cap(hT, 0, [[NX, 4]]), in0=pv[:, 0:4, 0], in1=tab_r[:, :, 0], op=ALU.add), r=[P(2), "tab34"], w=[("hT", "h0a")])
            S.op('dve', lambda: DVE.tensor_tensor(out=cap(hT, 4 * NX, [[NX, 4]]), in0=pv[:, 4:8, 0], in1=tab_c[:, :, 63], op=ALU.add), r=[P(2), "tab64"], w=[("hT", "h0b")])
            S.op('dve', lambda: DVE.tensor_tensor(out=cap(hT, NX - 1, [[NX, 4]]), in0=pv[:, 0:4, 1], in1=tab_r[:, :, 33], op=ALU.add), r=[P(2), "tab34"], w=[("hT", "h1a")])
            S.op('dve', lambda: DVE.tensor_tensor(out=cap(hT, 4 * NX + NX - 1, [[NX, 4]]), in0=pv[:, 4:8, 1], in1=tab_c[:, :, 0], op=ALU.add), r=[P(2), "tab64"], w=[("hT", "h1b")])
            for c in range(2):
                slot = c % 2
                S.dma('sp', xt[slot][:], ctxb[128 * c:128 * (c + 1), :], w=[f"xt{slot}"])
                for half in range(2):
                    pb = 2 + half
                    for k4 in range(4):
                        dc = half * 4 + k4
                        S.op('pe', lambda slot=slot, dc=dc, pb=pb, k4=k4: PE.transpose(
                            out=ps[pb][:, k4 * 128:(k4 + 1) * 128], in_=xt[slot][:, dc * 128:(dc + 1) * 128], identity=ident_f[:]),
                            r=[f"xt{slot}", "ident_f"], w=[P(pb)], sig=(k4 == 3))
                    S.op('dve', lambda pb=pb, half=half, c=c: DVE.tensor_copy(
                        out=hcT[:, half * 4:half * 4 + 4, c * 128:(c + 1) * 128], in_=ps[pb][:, :].rearrange("p (a b) -> p a b", a=4)),
                        r=[P(pb)], w=[("hcT", c)])
            S.barrier()

        def norm_mod(stk, src, src_off, n, gs_slot, sh_slot, dst, dst_off, uid, stride=1):
            sq, rstd, tmp = stk["sq"], stk["rstd"], stk["tmp"]
            ssrc = src.shape[2]
            sdst = dst.shape[2]

            def s_ap(dc):
                return cap(src, dc * ssrc + src_off, [[stride, n]])

            for dc in range(8):
                S.op('act', lambda dc=dc: ACT.activation(out=sq[:, dc, 0:n], in_=s_ap(dc), func=AF.Square), r=[("src", uid)], w=["sq"])
            for dc in range(8):
                S.op('pe', lambda dc=dc: PE.matmul(ps[7][:, 0:n], lhsT=ones_b[:], rhs=sq[:, dc, 0:n], start=(dc == 0), stop=(dc == 7)),
                     r=["sq", "ones_b"], w=[P(7)], sig=(dc == 7))
            S.op('act', lambda: ACT.activation(out=rstd[:, 0:n], in_=ps[7][:, 0:n], func=AF.Ln, scale=1.0 / D, bias=eps_t[:, 0:1]), r=[P(7)], w=["rstd"], small=(n < 256))
            S.op('act', lambda: ACT.activation(out=rstd[:, 0:n], in_=rstd[:, 0:n], func=AF.Exp, scale=-0.5), w=["rstd"], small=(n < 256))
            for dc in range(8):
                tt = tmp[dc % 2]
                S.op('dve', lambda dc=dc, tt=tt: DVE.scalar_tensor_tensor(out=tt[:, 0:n], in0=s_ap(dc), scalar=prm[:, gs_slot, dc:dc + 1], in1=rstd[:, 0:n],
                                                                         op0=ALU.mult, op1=ALU.mult),
                     r=[("src", uid), "rstd", ("prm", gs_slot)], w=[f"nm_tmp{dc % 2}"])
                S.op('act', lambda dc=dc, tt=tt: ACT.activation(out=dst[:, dc, dst_off:dst_off + n], in_=tt[:, 0:n], func=AF.Identity,
                                                                bias=prm[:, sh_slot, dc:dc + 1], scale=1.0),
                     r=[f"nm_tmp{dc % 2}", ("prm", sh_slot)], w=[("hn", uid)])

        S.op('dve', lambda: DVE.memset(eps_t[:], EPS), w=["eps_t"])
        S.barrier()

        def ffn(w_i, w_o2, tiles, prm_main, prm_ctx, tag):
            with ExitStack() as st:
                hnT = sb(st, "hnT" + tag, [128, 8, NX], BF16)
                hncT = sb(st, "hncT" + tag, [128, 8, NCTX], BF16)
                stk = {"sq": sb(st, "sq" + tag, [128, 8, 512], BF16), "rstd": sb(st, "rstd" + tag, [128, 512], F32),
                       "tmp": [sb(st, f"nmt{i}" + tag, [128, 512], F32) for i in range(2)]}
                ntile = len(tiles)
                for ti, (kind, off, n) in enumerate(tiles):
                    if kind == 'm':
                        norm_mod(stk, hT, off, n, prm_main[0], prm_main[1], hnT, off, (tag, ti))
                    else:
                        norm_mod(stk, hcT, off, n, prm_ctx[0], prm_ctx[1], hncT, off, (tag, ti))
                GRP = 6
                groups = [(0, 6), (6, 6), (12, 6), (18, 4)]
                zT = sb(st, "zT" + tag, [128, GRP, NX + NCTX], BF16)
                wa = [sb(st, f"wa{i}" + tag, [128, 8, 256], BF16) for i in range(2)]
                wb = [sb(st, f"wb{i}" + tag, [128, 8, 256], BF16) for i in range(2)]
                wo = [sb(st, f"wo{i}" + tag, [128, GRP, D], BF16) for i in range(2)]
                sl = [sb(st, f"sl{i}" + tag, [128, 512], F32) for i in range(2)]
                w_i_v = w_i.rearrange("(kc p) n -> p kc n", p=128)
                w_o_v = w_o2.rearrange("(fc p) n -> p fc n", p=128)
                blk_ctr = 0
                for gi, (g0, gn) in enumerate(groups):
                    gslot = gi % 2
                    wload(wo[gslot][:, 0:gn, :], f"wo{gslot}" + tag, w_o_v[:, g0:g0 + gn, :])
                    for b2 in range(gn // 2):
                        f0 = g0 + 2 * b2
                        slot = blk_ctr % 2
                        blk_ctr += 1
                        wload(wa[slot][:], f"wa{slot}" + tag, w_i_v[:, :, f0 * 128:(f0 + 2) * 128])
                        wload(wb[slot][:], f"wb{slot}" + tag, w_i_v[:, :, DFF + f0 * 128:DFF + (f0 + 2) * 128])
                        for ti, (kind, off, n) in enumerate(tiles):
                            src = hnT if kind == 'm' else hncT
                            zoff = off if kind == 'm' else NX + off
                            for j in range(2):
                                fz = 2 * b2 + j
                                pa, pb = (0, 1) if (j == 0) else (2, 3)
                                for kc in range(8):
                                    S.op('pe', lambda kc=kc, j=j, pa=pa, src=src, off=off, n=n, slot=slot: PE.matmul(
                                        ps[pa][:, 0:n], lhsT=wa[slot][:, kc, j * 128:(j + 1) * 128], rhs=src[:, kc, off:off + n],
                                        start=(kc == 0), stop=(kc == 7)),
                                        r=[f"wa{slot}" + tag, ("hn", (tag, ti))], w=[P(pa)], sig=(kc == 7))
                                for kc in range(8):
                                    S.op('pe', lambda kc=kc, j=j, pb=pb, src=src, off=off, n=n, slot=slot: PE.matmul(
                                        ps[pb][:, 0:n], lhsT=wb[slot][:, kc, j * 128:(j + 1) * 128], rhs=src[:, kc, off:off + n],
                                        start=(kc == 0), stop=(kc == 7)),
                                        r=[f"wb{slot}" + tag, ("hn", (tag, ti))], w=[P(pb)], sig=(kc == 7))
                                S.op('act', lambda pa=pa, j=j, n=n: ACT.activation(out=sl[j][:, 0:n], in_=ps[pa][:, 0:n], func=AF.Silu),
                                     r=[P(pa)], w=[f"sl{j}" + tag])
                                S.op('dve', lambda pb=pb, j=j, n=n, fz=fz, zoff=zoff: DVE.tensor_tensor(
                                    out=zT[:, fz, zoff:zoff + n], in0=ps[pb][:, 0:n], in1=sl[j][:, 0:n], op=ALU.mult),
                                    r=[P(pb), f"sl{j}" + tag], w=[("z", tag, ti, fz)])
                    for ti, (kind, off, n) in enumerate(tiles):
                        dstT = hT if kind == 'm' else hcT
                        zoff = off if kind == 'm' else NX + off
                        gt = (prm_main if kind == 'm' else prm_ctx)[2]
                        for dc in range(8):
                            pb = 4 + (dc % 3)
                            for fz in range(gn):
                                S.op('pe', lambda dc=dc, pb=pb, fz=fz, zoff=zoff, n=n, gslot=gslot: PE.matmul(
                                    ps[pb][:, 0:n], lhsT=wo[gslot][:, fz, dc * 128:(dc + 1) * 128], rhs=zT[:, fz, zoff:zoff + n],
                                    start=(fz == 0), stop=(fz == gn - 1)),
                                    r=[f"wo{gslot}" + tag, ("z", tag, ti, fz)], w=[P(pb)], sig=(fz == gn - 1))
                            S.op('dve', lambda dc=dc, pb=pb, dstT=dstT, off=off, n=n, gt=gt: DVE.scalar_tensor_tensor(
                                out=dstT[:, dc, off:off + n], in0=ps[pb][:, 0:n], scalar=prm[:, gt, dc:dc + 1], in1=dstT[:, dc, off:off + n],
                                op0=ALU.mult, op1=ALU.add),
                                r=[P(pb), ("prm", gt)], w=[("res", tag, ti, dc)])
            S.barrier()

        main_tiles = [('m', 1 + 512 * i, 512) for i in range(4)]
        halo_tiles = [('m', 0, 1), ('m', NX - 1, 1)]
        ctx_tile = [('c', 0, NCTX)]

        ffn(w_f1i, w_f1o, main_tiles + halo_tiles + ctx_tile, (P_GS1, P_SH1, P_GT1), (P_GS1C, P_SH1C, P_GT1C), "f1")

        mx = ExitStack()
        gates_all = sb(mx, "gates_all", [128, 18, 16], F32)
        sc_all = sb(mx, "sc_all", [128, 18, 32], F32)
        cs_b = sb(mx, "cs_b", [128, 2, 18, 4], F32)
        cs_g = sb(mx, "cs_g", [128, 18, 8], F32)
        Gseg = sb(mx, "Gseg", [128, 8], F32)
        LL = sb(mx, "LL", [128, 18, 8], F32)
        psc = sb(mx, "psc", [128, 18, 8], F32)
        wsT = sb(mx, "wsT", [128, 4, 128], BF16)
        bs_row = sb(mx, "bs_row", [1, 512], BF16)
        bg_bc = sb(mx, "bg_bc", [128, 16], F32)
        one_t = sb(mx, "one_t", [128, 1], F32)

        def dbg_dump(name, src_ap, rtoks):
            if name in dbg:
                S.dma('sp', dbg[name], src_ap, r=rtoks)

        with ExitStack() as st:
            wst = sb(st, "wst", [128, 4, 128], F32)
            bsf = sb(st, "bsf", [1, 512], F32)
            S.dma('sp', wst[:], w_s.rearrange("g t s -> t g s"), w=["wst"])
            S.dma('sp', bsf[:], b_s.rearrange("(o n) -> o n", o=1), w=["bsf"])
            S.dma('sp', bg_bc[:], bass.AP(b_gates.tensor, 0, [[0, 128], [1, 16]]), w=["bg_bc"])
            S.op('dve', lambda: DVE.memset(one_t[:], 1.0), w=["one_t"])
            for g in range(4):
                S.op('pe', lambda g=g: PE.transpose(out=ps[0][:, g * 128:(g + 1) * 128], in_=wst[:, g, :], identity=ident_f[:]),
                     r=["wst"], w=[P(0)], sig=(g == 3))
            S.op('dve', lambda: DVE.tensor_copy(out=wsT[:], in_=ps[0][:, :].rearrange("p (g t) -> p g t", g=4)), r=[P(0)], w=["wsT"])
            S.op('dve', lambda: DVE.tensor_copy(out=bs_row[:], in_=bsf[:]), r=["bsf"], w=["bs_row"])
            S.barrier()

        GC = 1.5957691216057308

        class GeluPipe:
            def __init__(self):
                self.prev = None

            def push(self, x_ap, t1, out_ap, xtoks, t1tok, outtok, after=None):
                S.op('act', lambda: ACT.activation(out=t1, in_=x_ap, func=AF.Square), r=xtoks, w=[t1tok])
                S.op('dve', lambda: DVE.tensor_scalar(out=t1, in0=t1, scalar1=0.044715, scalar2=1.0, op0=ALU.mult, op1=ALU.add), w=[t1tok])
                S.op('dve', lambda: DVE.tensor_tensor(out=t1, in0=x_ap, in1=t1, op=ALU.mult), r=xtoks, w=[t1tok])
                self.flush()
                self.prev = (x_ap, t1, out_ap, xtoks, t1tok, outtok, after)

            def flush(self):
                if self.prev is None:
                    return
                x_ap, t1, out_ap, xtoks, t1tok, outtok, after = self.prev
                self.prev = None
                S.op('act', lambda: ACT.activation(out=t1, in_=t1, func=AF.Sigmoid, scale=GC), r=[t1tok], w=[t1tok])
                S.op('dve', lambda: DVE.tensor_tensor(out=out_ap, in0=x_ap, in1=t1, op=ALU.mult), r=xtoks + [t1tok], w=[outtok])
                if after:
                    after()

        gpipe = GeluPipe()

        with ExitStack() as st:
            hnT = sb(st, "hnT_m", [128, 8, NX], BF16)
            hncT = sb(st, "hncT_m", [128, 8, NCTX], BF16)
            with ExitStack() as nst:
                stk = {"sq": sb(nst, "sq_m", [128, 8, 512], BF16), "rstd": sb(nst, "rstd_m", [128, 512], F32),
                       "tmp": [sb(nst, f"nmt{i}_m", [128, 512], F32) for i in range(2)]}
                tl = main_tiles + halo_tiles
                for ti, (kind, off, n) in enumerate(tl):
                    norm_mod(stk, hT, off, n, P_GSM, P_SHM, hnT, off, ("mx", ti))
                norm_mod(stk, hcT, 0, NCTX, P_GSMC, P_SHMC, hncT, 0, ("mx", 6))
                S.barrier()
            HN_MAIN = [("hn", ("mx", i)) for i in range(4)]
            HN_HALO = [("hn", ("mx", 4)), ("hn", ("mx", 5))]
            HN_CTX = [("hn", ("mx", 6))]

            wblk = [sb(st, f"wblk{i}", [128, 8, 512], BF16) for i in range(2)]
            w_in_v = w_in.rearrange("(kc p) n -> p kc n", p=128)
            bctr = [0]
            BLKS = ([(i * 512, 512) for i in range(4)] + [(2048, 512), (2560, 512), (3072, 16), (5136, 512), (5648, 512)]
                    + [(c0 + b * 512, 512) for c0 in (3088, 4112, 6160, 7184) for b in range(2)])
            issued = [0]

            def _issue(i):
                c0, ncols = BLKS[i]
                wload(wblk[i % 2][:, :, 0:ncols], f"wblk{i % 2}", w_in_v[:, :, c0:c0 + ncols])

            def load_blk(c0, ncols):
                i = bctr[0]
                assert BLKS[i] == (c0, ncols), (i, BLKS[i], c0, ncols)
                bctr[0] += 1
                while issued[0] <= min(i + 1, len(BLKS) - 1):
                    _issue(issued[0])
                    issued[0] += 1
                return i % 2

            pre = [sb(st, f"pre{i}", [128, NX], F32) for i in range(2)]
            prec = sb(st, "prec", [128, NCTX + 2], F32)
            accs = [sb(st, f"acc{i}", [128, NT], F32) for i in range(2)]
            qks = [sb(st, f"qks{i}", [128, NT + NCTX], BF16) for i in range(2)]
            ktk = sb(st, "ktk", [128, 18, 128], BF16)
            stg = [sb(st, f"stg{i}", [128, NT], BF16) for i in range(2)]
            ut = [sb(st, f"ut{i}", [128, 512], F32) for i in range(3)]
            vstg = [sb(st, f"vstg{i}", [128, 512], BF16) for i in range(3)]
            vsstg = [sb(st, f"vsstg{i}", [128, 512], F32) for i in range(3)]
            S.op('dve', lambda: DVE.memset(prec[:], 0.0), w=["prec"])

            for blk in range(4):
                slot = load_blk(blk * 512, 512)
                for j in range(4):
                    fc = blk * 4 + j
                    is_k = fc >= 8
                    pr = pre[fc % 2]
                    ptok = f"pre{fc % 2}"
                    acc = accs[fc % 2]
                    atok = f"acc{fc % 2}"
                    for ti in range(4):
                        pb = (0, 1, 6, 7)[ti]
                        for kc in range(8):
                            S.op('pe', lambda kc=kc, j=j, pb=pb, ti=ti, slot=slot: PE.matmul(
                                ps[pb][:, :], lhsT=wblk[slot][:, kc, j * 128:(j + 1) * 128], rhs=hnT[:, kc, 1 + 512 * ti:1 + 512 * (ti + 1)],
                                start=(kc == 0), stop=(kc == 7)), r=[f"wblk{slot}", HN_MAIN[ti]], w=[P(pb)], sig=(kc == 7))
                        S.op('act', lambda pb=pb, ti=ti, pr=pr: ACT.copy(out=pr[:, 1 + 512 * ti:1 + 512 * (ti + 1)], in_=ps[pb][:, :]),
                             r=[P(pb)], w=[(ptok, ti)])
                    for kc in range(8):
                        S.op('pe', lambda kc=kc, j=j, slot=slot: PE.matmul(
                            ps[2][:, 0:2], lhsT=wblk[slot][:, kc, j * 128:(j + 1) * 128], rhs=cap(hnT, kc * NX, [[NX - 1, 2]]),
                            start=(kc == 0), stop=(kc == 7)), r=[f"wblk{slot}"] + HN_HALO, w=[P(2)], sig=(kc == 7))
                    S.op('dve', lambda pr=pr: DVE.tensor_tensor(out=cap(pr, 0, [[NX - 1, 2]]), in0=ps[2][:, 0:2], in1=metat[:, 5:7], op=ALU.mult),
                         r=[P(2), "meta"], w=[(ptok, 4)])
                    if is_k:
                        for kc in range(8):
                            S.op('pe', lambda kc=kc, j=j, slot=slot: PE.matmul(
                                ps[3][:, 0:NCTX], lhsT=wblk[slot][:, kc, j * 128:(j + 1) * 128], rhs=hncT[:, kc, :],
                                start=(kc == 0), stop=(kc == 7)), r=[f"wblk{slot}"] + HN_CTX, w=[P(3)], sig=(kc == 7))
                        S.op('act', lambda: ACT.copy(out=prec[:, 1:1 + NCTX], in_=ps[3][:, 0:NCTX]), r=[P(3)], w=["prec"])
                    w0 = vecT[:, R_CW + 0 * 16 + fc:R_CW + 0 * 16 + fc + 1]
                    w1 = vecT[:, R_CW + 1 * 16 + fc:R_CW + 1 * 16 + fc + 1]
                    w2 = vecT[:, R_CW + 2 * 16 + fc:R_CW + 2 * 16 + fc + 1]
                    cb = vecT[:, R_CB + fc:R_CB + fc + 1]
                    qs = qks[fc % 2]
                    qtok = f"qks{fc % 2}"
                    allpre = [(ptok, i) for i in range(5)]
                    S.op('pool', lambda pr=pr, w0=w0: POOL.tensor_scalar(out=acc[:], in0=pr[:, 0:NT], scalar1=w0, scalar2=0.0, op0=ALU.mult, op1=ALU.add),
                         r=allpre, w=[atok])
                    S.op('dve', lambda pr=pr, w1=w1: DVE.scalar_tensor_tensor(out=acc[:], in0=pr[:, 1:NT + 1], scalar=w1, in1=acc[:], op0=ALU.mult, op1=ALU.add),
                         r=allpre + [atok], w=[atok])
                    S.op('dve', lambda pr=pr, w2=w2: DVE.scalar_tensor_tensor(out=acc[:], in0=pr[:, 2:NT + 2], scalar=w2, in1=acc[:], op0=ALU.mult, op1=ALU.add),
                         r=allpre, w=[atok])
                    S.op('act', lambda qs=qs, cb=cb: ACT.activation(out=qs[:, 0:NT], in_=acc[:], func=AF.Silu, bias=cb, scale=1.0), r=[atok], w=[qtok])
                    if not is_k:
                        S.dma('sp', qT_d[fc], qs[:, 0:NT], r=[qtok], w=[("qT_d", fc)])
                    else:
                        S.op('dve', lambda w0=w0: DVE.tensor_scalar(out=acc[:, 0:NCTX], in0=prec[:, 0:NCTX], scalar1=w0, scalar2=None, op0=ALU.mult),
                             r=["prec", atok], w=[atok])
                        S.op('dve', lambda w1=w1: DVE.scalar_tensor_tensor(out=acc[:, 0:NCTX], in0=prec[:, 1:NCTX + 1], scalar=w1, in1=acc[:, 0:NCTX], op0=ALU.mult, op1=ALU.add),
                             r=["prec"], w=[atok])
                        S.op('dve', lambda w2=w2: DVE.scalar_tensor_tensor(out=acc[:, 0:NCTX], in0=prec[:, 2:NCTX + 2], scalar=w2, in1=acc[:, 0:NCTX], op0=ALU.mult, op1=ALU.add),
                             r=["prec"], w=[atok])
                        S.op('act', lambda qs=qs, cb=cb: ACT.activation(out=qs[:, NT:NT + NCTX], in_=acc[:, 0:NCTX], func=AF.Silu, bias=cb, scale=1.0),
                             r=[atok], w=[qtok])
                        S.dma('sp', kT_d[fc - 8], qs[:, :], r=[qtok], w=[("kT_d", fc - 8)])
                        for grp in range(3):
                            c0 = grp * 8
                            ncg = min(8, 18 - c0)
                            pbank = 4 + (grp % 2)
                            psb = ps[pbank][:, :].bitcast(BF16)
                            for ci in range(ncg):
                                S.op('pe', lambda ci=ci, c0=c0, psb=psb, qs=qs: PE.transpose(
                                    out=psb[:, ci * 128:(ci + 1) * 128], in_=qs[:, (c0 + ci) * 128:(c0 + ci + 1) * 128], identity=ident_b[:]),
                                    r=[qtok, "ident_b"], w=[P(pbank)], sig=(ci == ncg - 1))
                            S.op('dve', lambda c0=c0, ncg=ncg, psb=psb: DVE.tensor_copy(
                                out=ktk[:, c0:c0 + ncg, :], in_=psb[:, 0:ncg * 128].rearrange("p (c f) -> p c f", f=128)),
                                r=[P(pbank)], w=["ktk"])
                        S.dma('sp', ktok_d.rearrange("c p f -> p c f")[:, :, (fc - 8) * 128:(fc - 7) * 128], ktk[:], r=["ktk"], w=[("ktok_d", fc - 8)])

            def hn_chunk(c, kc):
                if c < 16:
                    return hnT[:, kc, 1 + 128 * c:1 + 128 * (c + 1)]
                return hncT[:, kc, (c - 16) * 128:(c - 15) * 128]

            def hn_tok(c):
                return HN_MAIN[c // 4] if c < 16 else HN_CTX[0]

            def bform(c0, ncols, nch, epi):
                slot = load_blk(c0, ncols)
                for c in range(nch):
                    pb = c % 6
                    for kc in range(8):
                        S.op('pe', lambda kc=kc, c=c, pb=pb, slot=slot: PE.matmul(
                            ps[pb][:, 0:ncols], lhsT=hn_chunk(c, kc), rhs=wblk[slot][:, kc, 0:ncols], start=(kc == 0), stop=(kc == 7)),
                            r=[f"wblk{slot}", hn_tok(c)], w=[P(pb)], sig=(kc == 7))
                    epi(c, pb)

            vctr = [0]
            for half in range(2):
                def epi_v(c, pb, half=half):
                    s3 = vctr[0] % 3
                    vctr[0] += 1
                    S.op('act', lambda: ACT.copy(out=vstg[s3][:], in_=ps[pb][:, :]), r=[P(pb)], w=[f"vstg{s3}"])
                    S.dma('sp', v_d[c][:, half * 512:(half + 1) * 512], vstg[s3][:], r=[f"vstg{s3}"], w=[("v_d", c, half)])
                bform(2048 + half * 512, 512, 18, epi_v)

            def epi_g(c, pb):
                S.op('dve', lambda: DVE.tensor_tensor(out=gates_all[:, c, :], in0=ps[pb][:, 0:16], in1=bg_bc[:], op=ALU.add),
                     r=[P(pb), "bg_bc"], w=[("gates", c)])
            bform(3072, 16, 18, epi_g)

            vsctr = [0]
            for half in range(2):
                def epi_vs(c, pb, half=half):
                    s2 = vsctr[0] % 3
                    vsctr[0] += 1
                    gpipe.push(ps[pb][:, :], ut[s2][:], vsstg[s2][:], [P(pb)], f"ut{s2}", f"vsstg{s2}",
                               after=lambda c=c, half=half, s2=s2: S.dma('sp', vs_d[c][:, half * 512:(half + 1) * 512], vsstg[s2][:], r=[f"vsstg{s2}"], w=[("vs_d", c, half)]))
                bform(5136 + half * 512, 512, 16, epi_vs)
            gpipe.flush()

            def aform(col0, kind, dst_d):
                for blk in range(2):
                    slot = load_blk(col0 + blk * 512, 512)
                    for j in range(4):
                        fc = blk * 4 + j
                        sg = stg[fc % 2]
                        stok = f"stg{fc % 2}"
                        for ti in range(4):
                            pb = (fc * 4 + ti) % 8
                            for kc in range(8):
                                S.op('pe', lambda kc=kc, j=j, pb=pb, ti=ti, slot=slot: PE.matmul(
                                    ps[pb][:, :], lhsT=wblk[slot][:, kc, j * 128:(j + 1) * 128], rhs=hnT[:, kc, 1 + 512 * ti:1 + 512 * (ti + 1)],
                                    start=(kc == 0), stop=(kc == 7)), r=[f"wblk{slot}", HN_MAIN[ti]], w=[P(pb)], sig=(kc == 7))
                            dst = sg[:, 512 * ti:512 * (ti + 1)]
                            if kind == 'sig':
                                S.op('act', lambda pb=pb, dst=dst: ACT.activation(out=dst, in_=ps[pb][:, :], func=AF.Sigmoid), r=[P(pb)], w=[(stok, ti)])
                            elif kind == 'sigg':
                                t1 = ut[ti % 3]
                                S.op('act', lambda pb=pb, t1=t1: ACT.activation(out=t1[:], in_=ps[pb][:, :], func=AF.Sigmoid), r=[P(pb)], w=[f"ut{ti % 3}"])
                                S.op('pool', lambda t1=t1, dst=dst, fc=fc: POOL.tensor_scalar(out=dst, in0=t1[:], scalar1=vecT[:, R_GH + fc:R_GH + fc + 1], scalar2=0.0,
                                                                                               op0=ALU.mult, op1=ALU.add), r=[f"ut{ti % 3}"], w=[(stok, ti)])
                            else:
                                gpipe.push(ps[pb][:, :], ut[ti % 3][:], dst, [P(pb)], f"ut{ti % 3}", (stok, ti))
                        if kind == 'gelu':
                            gpipe.flush()
                        S.dma('sp', dst_d[fc], sg[:], r=[(stok, i) for i in range(4)], w=[(dst_d.tensor.name, fc)])

            aform(3088, 'sigg', og_d)
            aform(4112, 'gelu', gu_d)
            aform(6160, 'sig', ga_d)
            aform(7184, 'sig', gb_d)
            S.barrier()

        Sin = sb(mx, "Sin", [128, 8, 514], F32)
        with ExitStack() as st:
            lfw = sb(st, "lfw", [128, 18, 8], F32)
            dif = sb(st, "dif", [128, 18, 8], F32)
            S.op('act', lambda: ACT.activation(out=lfw[:, :, 0:4], in_=gates_all[:, :, 4:8], func=AF.Exp, scale=-1.0), w=["lfw"], small=True)
            S.op('act', lambda: ACT.activation(out=lfw[:, :, 4:8], in_=gates_all[:, :, 12:16], func=AF.Exp, scale=-1.0), w=["lfw"], small=True)
            S.op('act', lambda: ACT.activation(out=lfw[:], in_=lfw[:], func=AF.Ln, bias=one_t[:, 0:1], scale=1.0), w=["lfw"], small=True)
            S.op('dve', lambda: DVE.tensor_scalar(out=lfw[:], in0=lfw[:], scalar1=-1.0, scalar2=None, op0=ALU.mult), r=["lfw"], w=["lfw"], small=True)
            S.op('pe', lambda: PE.matmul(ps[0][:, 0:72], lhsT=triL[:], rhs=lfw[:, :, 0:4], start=True, stop=True), r=["lfw"], w=[P(0)], sig=False)
            S.op('pe', lambda: PE.matmul(ps[0][:, 72:144], lhsT=triU[:], rhs=lfw[:, :, 4:8], start=True, stop=True), r=["lfw"], w=[P(0)], sig=False)
            S.op('pe', lambda: PE.matmul(ps[1][:, 0:144], lhsT=ones_f[:], rhs=lfw[:], start=True, stop=True), r=["lfw"], w=[P(1)], sig=True)
            S.op('dve', lambda: DVE.tensor_copy(out=cs_b[:], in_=ps[0][:, 0:144].rearrange("p (d c h) -> p d c h", d=2, c=18)), r=[P(0)], w=["cs_b"], small=True)
            S.op('dve', lambda: DVE.tensor_copy(out=cs_g[:], in_=ps[1][:, 0:144].rearrange("p (c h) -> p c h", c=18)), r=[P(1)], w=["cs_g"], small=True)
            S.op('act', lambda: ACT.activation(out=sc_all[:, :, 0:4], in_=cs_b[:, 0, :, :], func=AF.Exp), r=["cs_b"], w=["sc_all"], small=True)
            S.op('act', lambda: ACT.activation(out=sc_all[:, :, 4:8], in_=cs_b[:, 1, :, :], func=AF.Exp), r=["cs_b"], w=["sc_all"], small=True)
            S.op('dve', lambda: DVE.tensor_tensor(out=dif[:, :, 0:4], in0=gates_all[:, :, 0:4], in1=cs_b[:, 0, :, :], op=ALU.subtract), r=["cs_b"], w=["dif"], small=True)
            S.op('dve', lambda: DVE.tensor_tensor(out=dif[:, :, 4:8], in0=gates_all[:, :, 8:12], in1=cs_b[:, 1, :, :], op=ALU.subtract), r=["cs_b"], w=["dif"], small=True)
            S.op('act', lambda: ACT.activation(out=sc_all[:, :, 8:16], in_=dif[:], func=AF.Exp), r=["dif"], w=["sc_all"], small=True)
            S.op('act', lambda: ACT.activation(out=sc_all[:, :, 16:24], in_=cs_g[:], func=AF.Exp), r=["cs_g"], w=["sc_all"], small=True)
            S.op('dve', lambda: DVE.tensor_tensor(out=sc_all[:, :, 24:32], in0=sc_all[:, :, 8:16], in1=sc_all[:, :, 16:24], op=ALU.mult), r=["sc_all"], w=["sc_all"], small=True)
            S.op('dve', lambda: DVE.tensor_reduce(out=Gseg[:], in_=cap(cs_g, 0, [[1, 8], [8, 16]]), axis=mybir.AxisListType.X, op=ALU.add), r=["cs_g"], w=["Gseg"], small=True)
            S.op('dve', lambda: DVE.memset(LL[:], 0.0), w=["LL"], small=True)
            for c in range(14, -1, -1):
                S.op('dve', lambda c=c: DVE.tensor_tensor(out=LL[:, c, 0:4], in0=LL[:, c + 1, 0:4], in1=cs_g[:, c + 1, 0:4], op=ALU.add), r=["cs_g"], w=["LL"], small=True)
            for c in range(1, 16):
                S.op('dve', lambda c=c: DVE.tensor_tensor(out=LL[:, c, 4:8], in0=LL[:, c - 1, 4:8], in1=cs_g[:, c - 1, 4:8], op=ALU.add), r=["cs_g"], w=["LL"], small=True)
            S.op('dve', lambda: DVE.tensor_copy(out=LL[:, 16, 0:4], in_=cs_g[:, 17, 0:4]), w=["LL"], small=True)
            S.op('dve', lambda: DVE.tensor_copy(out=LL[:, 17, 4:8], in_=cs_g[:, 16, 4:8]), w=["LL"], small=True)
            S.op('act', lambda: ACT.activation(out=LL[:], in_=LL[:], func=AF.Exp), r=["LL"], w=["LL"], small=True)
            S.op('dve', lambda: DVE.tensor_tensor(out=psc[:], in0=LL[:], in1=sc_all[:, :, 24:32], op=ALU.mult), r=["LL", "sc_all"], w=["psc"], small=True)
            S.barrier()
        dbg_dump("gates", gates_all[:], [])
        dbg_dump("sc_all", sc_all[:], [])

        _slc = [0]

        def sweep_loads(pool, names):
            _slc[0] += 1
            return {n: [sb(pool, f"ld{_slc[0]}_{n}{i}", shp, dt) for i in range(2)] for n, (shp, dt) in names.items()}

        with ExitStack() as st:
            St = sb(st, "St", [128, 8, 514], F32)
            Sctx = sb(st, "Sctx", [128, 8, 514], F32)
            lds = sweep_loads(st, {"ktok": ([128, D], BF16), "v": ([128, D], BF16)})
            vtl = [sb(st, f"vtl{i}", [128, 4, 257], BF16) for i in range(2)]
            lctr = [0]

            def p1_pass(chunks, d, dst, dtok):
                n = len(chunks)
                for i, c in enumerate(chunks):
                    slot = lctr[0] % 2
                    lctr[0] += 1
                    kt, vv, vt = lds["ktok"][slot], lds["v"][slot], vtl[slot]
                    S.dma('sp', kt[:], ktok_d[c], w=[f"ld_ktok{slot}"])
                    S.dma('sp', vv[:], v_d[c], w=[f"ld_v{slot}"])
                    S.op('dve', lambda: DVE.tensor_tensor(out=vt[:, :, 0:256], in0=vv[:, :].rearrange("p (h v) -> p h v", h=4),
                                                          in1=cap(psc, c * 8 + 4 * d, [[1, 4], [0, 256]]), op=ALU.mult),
                         r=[f"ld_v{slot}", "psc"], w=[f"vtl{slot}"])
                    S.op('dve', lambda: DVE.tensor_copy(out=vt[:, :, 256], in_=psc[:, c, 4 * d:4 * d + 4]), w=[f"vtl{slot}"], small=True)
                    for h in range(4):
                        for kc in range(2):
                            pb = h * 2 + kc
                            S.op('pe', lambda h=h, kc=kc, pb=pb: PE.matmul(ps[pb][:, 0:257], lhsT=kt[:, h * 256 + kc * 128:h * 256 + (kc + 1) * 128], rhs=vt[:, h, :],
                                                                          start=(i == 0), stop=(i == n - 1)), r=[f"ld_ktok{slot}", f"vtl{slot}"], w=[P(pb)],
                                 sig=(i == n - 1 or (h == 3 and kc == 1)))
                for h in range(4):
                    for kc in range(2):
                        pb = h * 2 + kc
                        eng = 'act' if (pb % 2 == 0) else 'dve'
                        if eng == 'act':
                            S.op('act', lambda h=h, kc=kc, pb=pb: ACT.copy(out=dst[:, d * 4 + h, kc * 257:(kc + 1) * 257], in_=ps[pb][:, 0:257]), r=[P(pb)], w=[(dtok, d * 4 + h, kc)])
                        else:
                            S.op('dve', lambda h=h, kc=kc, pb=pb: DVE.tensor_copy(out=dst[:, d * 4 + h, kc * 257:(kc + 1) * 257], in_=ps[pb][:, 0:257]), r=[P(pb)], w=[(dtok, d * 4 + h, kc)])

            p1_pass([16, 17], 0, Sctx, "Sctx")
            p1_pass([17, 16], 1, Sctx, "Sctx")
            p1_pass(list(range(16)), 0, St, "St")
            p1_pass(list(range(16)), 1, St, "St")
            S.barrier()
            dbg_dump("Sctx", Sctx[:], ["Sctx"])
            dbg_dump("Sloc", St[:], [("St", i) for i in range(8)])
            xout_v = [xo.rearrange("(r p) w -> p r w", p=128) for xo in xout_l]
            for i in range(4):
                S.dma('sp', xin_l[i][:, 0:1028], St[:, 2 * i:2 * i + 2, :].rearrange("p a b -> p (a b)"), r=[("St", 2 * i), ("St", 2 * i + 1)], w=[("xin", i)])
                S.dma('sp', xin_l[i][:, 1028:1030], Gseg[:, 2 * i:2 * i + 2], r=[], w=[("xin2", i)])
            for i in range(4):
                if KSTOP == 'p1':
                    S.dma('sp', xout_l[i][0:128, :], xin_l[i], r=[("xin", i), ("xin2", i)], w=[("xout", i)])
                else:
                    S.custom('pool', lambda sem, i=i: POOL.collective_compute("AllGather", ALU.bypass, replica_groups=[[0, 1, 2, 3], [4, 5, 6, 7]],
                                                                              ins=[xin_l[i].opt()], outs=[xout_l[i].opt()]).then_inc(sem, 1),
                             1, r=[("xin", i), ("xin2", i)], w=[("xout", i)])
            with ExitStack() as sg:
                gvt = sb(sg, "sgu_gv", [128, 4, D], F32)
                vnt = sb(sg, "sgu_vn", [128, 4, D], BF16)
                gut = sb(sg, "sgu_gu", [128, 8, 512], BF16)
                ybt = sb(sg, "sgu_yb", [128, 8, 512], BF16)
                gsgu_bc = sb(sg, "gsgu_bc", [128, D], F32)
                st6s = sb(sg, "sgu_st6", [128, 4, 2, 6], F32)
                mvs = sb(sg, "sgu_mv", [128, 4, 2], F32)
                msq = sb(sg, "sgu_msq", [128, 2, 4], F32)
                S.dma('sp', gsgu_bc[:], bass.AP(g_sgu.tensor, 0, [[0, 128], [1, D]]), w=["gsgu_bc"])
                for g4 in range(4):
                    S.dma('sp', gvt[:], vs_d.rearrange("c p f -> p c f")[:, 4 * g4:4 * g4 + 4, :], w=["sgu_gv"])
                    S.dma('sp', gut[:], gu_d.rearrange("f p t -> p f t")[:, :, 512 * g4:512 * (g4 + 1)], w=["sgu_gu"])
                    for cq in range(4):
                        for i2 in range(2):
                            S.op('dve', lambda cq=cq, i2=i2: DVE.bn_stats(out=st6s[:, cq, i2, :], in_=gvt[:, cq, i2 * 512:(i2 + 1) * 512]),
                                 r=["sgu_gv"], w=[("sgu_st6", cq)], small=True)
                    for cq in range(4):
                        S.op('dve', lambda cq=cq: DVE.bn_aggr(out=mvs[:, cq, :], in_=st6s[:, cq, :, :].rearrange("p a b -> p (a b)")),
                             r=[("sgu_st6", cq)], w=["sgu_mv"], small=True)
                    S.op('dve', lambda: DVE.tensor_tensor(out=msq[:, 0, :], in0=mvs[:, :, 0], in1=mvs[:, :, 0], op=ALU.mult), r=["sgu_mv"], w=["sgu_msq"], small=True)
                    S.op('dve', lambda: DVE.tensor_tensor(out=msq[:, 0, :], in0=msq[:, 0, :], in1=mvs[:, :, 1], op=ALU.add), w=["sgu_msq"], small=True)
                    S.op('act', lambda: ACT.activation(out=msq[:, 1, :], in_=msq[:, 0, :], func=AF.Ln, bias=eps_t[:, 0:1], scale=1.0), r=["sgu_msq"], w=["sgu_rs"], small=True)
                    S.op('act', lambda: ACT.activation(out=msq[:, 1, :], in_=msq[:, 1, :], func=AF.Exp, scale=-0.5), w=["sgu_rs"], small=True)
                    for cq in range(4):
                        S.op('dve', lambda cq=cq: DVE.scalar_tensor_tensor(out=vnt[:, cq, :], in0=gvt[:, cq, :], scalar=msq[:, 1, cq:cq + 1], in1=gsgu_bc[:],
                                                                           op0=ALU.mult, op1=ALU.mult), r=["sgu_rs", "sgu_gv", "gsgu_bc"], w=[("sgu_vn", cq)], small=True)
                    for ccf in range(8):
                        gq = ccf // 2
                        for cq in range(4):
                            S.op('pe', lambda ccf=ccf, cq=cq, gq=gq: PE.matmul(ps[ccf][:, cq * 128:(cq + 1) * 128], lhsT=vnt[:, cq, ccf * 128:(ccf + 1) * 128], rhs=wsT[:, gq, :],
                                                                               start=True, stop=False), r=[("sgu_vn", cq), "wsT"], w=[P(ccf)], sig=False)
                            S.op('pe', lambda ccf=ccf, cq=cq, gq=gq: PE.matmul(ps[ccf][:, cq * 128:(cq + 1) * 128], lhsT=ones_b[0:1, :], rhs=bs_row[0:1, gq * 128:(gq + 1) * 128],
                                                                               start=False, stop=True), w=[P(ccf)], sig=(cq == 3))
                        S.op('dve', lambda ccf=ccf: DVE.tensor_tensor(out=ybt[:, ccf, :], in0=ps[ccf][:, :], in1=gut[:, ccf, :], op=ALU.mult),
                             r=[P(ccf), "sgu_gu"], w=[("sgu_yb", ccf)])
                    S.dma('sp', gu_d.rearrange("f p t -> p f t")[:, :, 512 * g4:512 * (g4 + 1)], ybt[:], r=[("sgu_yb", i) for i in range(8)] + ["sgu_gu"], w=[("gu_d", g4)])
                S.barrier()
            Gall = sb(st, "Gall", [128, 4, 8], F32)
            Ug = [sb(st, f"Ug{i}", [128, 4, 514], F32) for i in range(2)]
            Tt = sb(st, "Tt", [128, 514], F32)
            for i in range(4):
                S.dma('sp', Gall[:, :, 2 * i:2 * i + 2], xout_v[i][:, :, 1028:1030], r=[("xout", i)], w=[("Gall", i)])
            S.op('act', lambda: ACT.activation(out=Gall[:], in_=Gall[:], func=AF.Exp), r=[("Gall", i) for i in range(4)], w=["Gall"], small=True)
            for hd in range(8):
                d = hd // 4
                U = Ug[hd % 2]
                utok = f"Ug{hd % 2}"
                S.dma('sp', U[:], xout_v[hd // 2][:, :, (hd % 2) * 514:(hd % 2 + 1) * 514], r=[("xout", hd // 2)], w=[utok])
                S.op('dve', lambda hd=hd: DVE.tensor_copy(out=Tt[:], in_=Sctx[:, hd, :]), r=["Sctx"], w=["Tt"])
                first = 0 if d == 0 else 3
                S.op('dve', lambda hd=hd, first=first: DVE.tensor_scalar(out=Sin[:, hd, :], in0=Tt[:], scalar1=metat[:, 1 + first:2 + first], scalar2=None, op0=ALU.mult),
                     r=["meta"], w=[("Sin", hd)])
                order = [0, 1, 2] if d == 0 else [3, 2, 1]
                for i in order:
                    tgt = i + 1 if d == 0 else i - 1
                    S.op('dve', lambda i=i, hd=hd, U=U: DVE.scalar_tensor_tensor(out=Tt[:], in0=Tt[:], scalar=Gall[:, i, hd:hd + 1], in1=U[:, i, :],
                                                                                  op0=ALU.mult, op1=ALU.add), r=[utok, "Gall"], w=["Tt"])
                    S.op('dve', lambda tgt=tgt, hd=hd: DVE.scalar_tensor_tensor(out=Sin[:, hd, :], in0=Tt[:], scalar=metat[:, 1 + tgt:2 + tgt], in1=Sin[:, hd, :],
                                                                                op0=ALU.mult, op1=ALU.add), w=[("Sin", hd)])
            dbg_dump("Sin", Sin[:], [("Sin", i) for i in range(8)])
            S.barrier()

        def mlstm_chunk(c, d, bufs, S16, emit_all, hook=None):
            q_t, kT_t, kt_t, v_t = bufs["q"], bufs["kT"], bufs["ktok"], bufs["v"]
            vt, vte, PM4, sm = bufs["vt"], bufs["vte"], bufs["PM4"], bufs["sm"]
            ltoks = bufs["ltoks"]
            mask = mL16 if d == 0 else mU16
            rs0, vs0, eg0, vse0 = 0 + 4 * d, 8 + 4 * d, 16 + 4 * d, 24 + 4 * d
            S.op('dve', lambda: DVE.tensor_tensor(out=vt[:, :, 0:256], in0=v_t[:, :].rearrange("p (h v) -> p h v", h=4),
                                                  in1=cap(sc_all, c * 32 + vs0, [[1, 4], [0, 256]]), op=ALU.mult), r=[ltoks["v"]], w=["vt"])
            S.op('dve', lambda: DVE.tensor_copy(out=vt[:, :, 256], in_=sc_all[:, c, vs0:vs0 + 4]), w=["vt"], small=True)
            S.op('pool', lambda: POOL.tensor_tensor(out=vte[:, :, 0:256], in0=v_t[:, :].rearrange("p (h v) -> p h v", h=4),
                                                    in1=cap(sc_all, c * 32 + vse0, [[1, 4], [0, 256]]), op=ALU.mult), r=[ltoks["v"]], w=["vte"])
            S.op('pool', lambda: POOL.tensor_copy(out=vte[:, :, 256], in_=sc_all[:, c, vse0:vse0 + 4]), w=["vte"], small=True)
            for h in range(4):
                for kc in range(2):
                    S.op('pe', lambda kc=kc, h=h: PE.matmul(ps[0][:, h * 128:(h + 1) * 128], lhsT=kT_t[:, h * 2 + kc, :], rhs=q_t[:, h * 2 + kc, :],
                                                           start=(kc == 0), stop=(kc == 1)), r=[ltoks["kT"], ltoks["q"]], w=[P(0)], sig=(h == 3 and kc == 1))
            S.op('dve', lambda: DVE.tensor_tensor(out=PM4[:], in0=ps[0][:, :].rearrange("p (h t) -> p h t", h=4),
                                                  in1=cap(mask, 0, [[0, 4], [1, 128]]), op=ALU.mult), r=[P(0)], w=["PM4"])
            if hook:
                hook('s')
            for h in range(4):
                S.op('pe', lambda h=h: PE.matmul(ps[1 + h][:, 0:257], lhsT=PM4[:, h, :], rhs=vt[:, h, :], start=True, stop=False),
                     r=["PM4", "vt"], w=[P(1 + h)], sig=False)
                for kc in range(2):
                    S.op('pe', lambda kc=kc, h=h: PE.matmul(ps[1 + h][:, 0:257], lhsT=q_t[:, h * 2 + kc, :], rhs=S16[:, h, kc, :], start=False, stop=(kc == 1)),
                         r=[ltoks["q"], ("S16", h)], w=[P(1 + h)], sig=(kc == 1))
            if hook:
                hook('o')
            den4 = psall[:, 1:5, 256]
            rs4 = sc_all[:, c, rs0:rs0 + 4]
            PO = [P(1 + h) for h in range(4)]
            S.op('dve', lambda: DVE.tensor_tensor(out=sm[:, 0, :], in0=den4, in1=rs4, op=ALU.mult), r=PO, w=["sm"], small=True)
            S.op('dve', lambda: DVE.tensor_scalar(out=sm[:, 1, :], in0=sm[:, 0, :], scalar1=-1.0, scalar2=1.0, op0=ALU.mult, op1=ALU.max), w=["sm"], small=True)
            S.op('dve', lambda: DVE.tensor_scalar(out=sm[:, 2, :], in0=sm[:, 0, :], scalar1=1.0, scalar2=None, op0=ALU.max), w=["sm"], small=True)
            S.op('dve', lambda: DVE.tensor_tensor(out=sm[:, 2, :], in0=sm[:, 2, :], in1=sm[:, 1, :], op=ALU.max), w=["sm"], small=True)
            S.op('dve', lambda: DVE.reciprocal(out=sm[:, 3, :], in_=sm[:, 2, :]), w=["sm"], small=True)
            S.op('dve', lambda: DVE.tensor_tensor(out=sm[:, 4, :], in0=sm[:, 3, :], in1=rs4, op=ALU.mult), w=["sm"], small=True)
            emit_all(sm, 4)
            for h in range(4):
                hd = d * 4 + h
                for kc in range(2):
                    pU = 5 + kc
                    S.op('pe', lambda kc=kc, h=h, pU=pU: PE.matmul(ps[pU][:, 0:257], lhsT=kt_t[:, h * 256 + kc * 128:h * 256 + (kc + 1) * 128], rhs=vte[:, h, :],
                                                                  start=True, stop=True), r=[ltoks["ktok"], "vte"], w=[P(pU)])
                    S.op('dve', lambda kc=kc, hd=hd, pU=pU, h=h: DVE.scalar_tensor_tensor(
                        out=Sin[:, hd, kc * 257:(kc + 1) * 257], in0=Sin[:, hd, kc * 257:(kc + 1) * 257], scalar=sc_all[:, c, eg0 + h:eg0 + h + 1],
                        in1=ps[pU][:, 0:257], op0=ALU.mult, op1=ALU.add), r=[P(pU)], w=[("Sin", hd)])
            if hook:
                hook('u')
            S.op('act', lambda: ACT.activation(out=S16[:, :, :, :].rearrange("p h k v -> p h (k v)"), in_=Sin[:, d * 4:d * 4 + 4, :], func=AF.Copy, scale=0.0625),
                 r=[("Sin", d * 4 + h) for h in range(4)], w=[("S16", h) for h in range(4)])

        SINGLE = {"hb", "vs"}

        def ltok(name, slot):
            return f"ld_{name}" if name in SINGLE else f"ld_{name}{slot}"

        def issue_loads(lds, slot, c, extra=()):
            S.dma('sp', lds["q"][slot][:], qT_d.rearrange("f p t -> p f t")[:, :, c * 128:(c + 1) * 128], w=[f"ld_q{slot}"])
            S.dma('sp', lds["kT"][slot][:], kT_d.rearrange("f p t -> p f t")[:, :, c * 128:(c + 1) * 128], w=[f"ld_kT{slot}"])
            S.dma('sp', lds["ktok"][slot][:], ktok_d[c], w=[f"ld_ktok{slot}"])
            S.dma('sp', lds["v"][slot][:], v_d[c], w=[f"ld_v{slot}"])
            for (name, src) in extra:
                S.dma('sp', lds[name][slot][:], src, w=[ltok(name, slot)])

        def mk_bufs(lds, slot, common):
            b = dict(common)
            for n in lds:
                b[n] = lds[n][slot]
            b["ltoks"] = {n: ltok(n, slot) for n in lds}
            return b

        with ExitStack() as st:
          if KSTOP not in ('xchg', 'p1'):
                lds = sweep_loads(st, {"q": ([128, 8, 128], BF16), "kT": ([128, 8, 128], BF16), "ktok": ([128, D], BF16), "v": ([128, D], BF16)})
                common = {"vt": sb(st, "vt", [128, 4, 257], BF16), "vte": sb(st, "vte", [128, 4, 257], BF16),
                          "PM4": sb(st, "PM4", [128, 4, 128], BF16), "sm": sb(st, "sm", [128, 5, 4], F32)}
                S16 = sb(st, "S16", [128, 4, 2, 257], BF16)
                hbt = [sb(st, f"hbt{i}", [128, D], F32) for i in range(2)]
                for h in range(4):
                    S.op('act', lambda h=h: ACT.activation(out=S16[:, h, :, :], in_=Sin[:, 4 + h, :].rearrange("p (k v) -> p k v", k=2), func=AF.Copy, scale=0.0625),
                         w=[("S16", h)])
                issue_loads(lds, 0, 15)
                for it, c in enumerate(range(15, -1, -1)):
                    slot = it % 2
                    if c > 0:
                        issue_loads(lds, 1 - slot, c - 1)
                    hb = hbt[slot]

                    def emit_all(sm, row, hb=hb, slot=slot):
                        S.op('dve', lambda: DVE.tensor_tensor(out=hb[:, :].rearrange("p (h v) -> p h v", h=4), in0=psall[:, 1:5, 0:256],
                                                              in1=cap(sm, row * 4, [[1, 4], [0, 256]]), op=ALU.mult),
                             r=[P(1 + h) for h in range(4)] + ["sm"], w=[f"hbt{slot}"], small=True)
                    mlstm_chunk(c, 1, mk_bufs(lds, slot, common), S16, emit_all)
                    S.dma('sp', hb_d[c], hb[:], r=[f"hbt{slot}"], w=[("hb_d", c)])
                dbg_dump("Sfin", Sin[:], [("Sin", i) for i in range(8)])
                S.barrier()

        with ExitStack() as st:
          if KSTOP not in ('xchg', 'bwd', 'p1'):
                lds = sweep_loads(st, {"q": ([128, 8, 128], BF16), "kT": ([128, 8, 128], BF16), "ktok": ([128, D], BF16), "v": ([128, D], BF16),
                                       "og": ([128, 8, 128], BF16)})
                hb1 = sb(st, "hb1", [128, D], F32)
                lds["hb"] = [hb1, hb1]
                common = {"vt": sb(st, "vtc", [128, 4, 257], BF16), "vte": sb(st, "vtec", [128, 4, 257], BF16),
                          "PM4": sb(st, "PM4c", [128, 4, 128], BF16), "sm": sb(st, "smc", [128, 5, 4], F32)}
                S16 = sb(st, "S16c", [128, 4, 2, 257], BF16)
                hm_t = sb(st, "hm_t", [128, D], F32)
                hh = sb(st, "hh", [128, D], BF16)
                st6 = sb(st, "st6", [128, 4, 6], F32)
                mv = sb(st, "mv", [128, 4, 2], F32)
                lnr = sb(st, "lnr", [128, 4], F32)
                t1 = cap(hcT, 0, [[1, D]])
                gv = cap(hcT, D, [[1, D]])
                ssq = sb(st, "ssq", [128, 2], F32)
                st6b = sb(st, "st6b", [128, 2, 6], F32)
                mvb = sb(st, "mvb", [128, 4], F32)
                vn = sb(st, "vn", [128, D], BF16)
                yaT = sb(st, "yaT", [128, 8, 512], BF16)
                ybT = sb(st, "ybT", [128, 8, 512], BF16)
                mixT = sb(st, "mixT", [128, 8, 512], BF16)
                tA = [sb(st, "tA0", [128, 512], F32)] * 2
                tB = [sb(st, "tB0", [128, 512], F32)] * 2
                sga = [sb(st, f"sga{i}", [128, 512], BF16) for i in range(2)]
                sgb = [sb(st, f"sgb{i}", [128, 512], BF16) for i in range(2)]
                wbr = [sb(st, f"wbr{i}", [128, 8, 512], BF16) for i in range(2)]

                def extra_for(c):
                    return [("og", og_d.rearrange("f p t -> p f t")[:, :, c * 128:(c + 1) * 128])]

                for h in range(4):
                    S.op('act', lambda h=h: ACT.activation(out=S16[:, h, :, :], in_=Sin[:, h, :].rearrange("p (k v) -> p k v", k=2), func=AF.Copy, scale=0.0625),
                         w=[("S16", h)])
                pending = []

                def flush_items():
                    while pending:
                        pending.pop(0)[1]()

                def c_hook(stage):
                    n_ab = sum(1 for k, _ in pending if k == 'ab')
                    if n_ab > 0:
                        n = n_ab if stage == 'u' else min(3, n_ab)
                    else:
                        n = min(1, len(pending))
                    for _ in range(n):
                        pending.pop(0)[1]()

                issue_loads(lds, 0, 0, extra_for(0))
                S.dma('sp', hb1[:], hb_d[0], w=["ld_hb"])
                for c in range(16):
                    slot = c % 2
                    cc = c % 4
                    tile = c // 4
                    if c < 15:
                        issue_loads(lds, 1 - slot, c + 1, extra_for(c + 1))
                    b = mk_bufs(lds, slot, common)
                    hbl, ogl = b["hb"], b["og"]

                    def emit_all(sm, row, hbl=hbl):
                        for h in range(4):
                            S.op('dve', lambda h=h: DVE.scalar_tensor_tensor(out=hm_t[:, h * 256:(h + 1) * 256], in0=ps[1 + h][:, 0:256], scalar=sm[:, row, h:h + 1],
                                                                             in1=hbl[:, h * 256:(h + 1) * 256], op0=ALU.mult, op1=ALU.add),
                                 r=[P(1 + h), "sm", "ld_hb"], w=[("hm", h)], small=True)
                        for h in range(4):
                            S.op('dve', lambda h=h: DVE.bn_stats(out=st6[:, h, :], in_=hm_t[:, h * 256:(h + 1) * 256]), r=[("hm", h)], w=[("st6", h)], small=True)
                        for h in range(4):
                            S.op('dve', lambda h=h: DVE.bn_aggr(out=mv[:, h, :], in_=st6[:, h, :]), r=[("st6", h)], w=[("mv", h)], small=True)
                    mlstm_chunk(c, 0, b, S16, emit_all, hook=c_hook)
                    if c < 15:
                        S.dma('sp', hb1[:], hb_d[c + 1], w=["ld_hb"])
                    S.op('act', lambda: ACT.activation(out=lnr[:], in_=mv[:, :, 1], func=AF.Ln, bias=eps_t[:, 0:1], scale=1.0), r=[("mv", h) for h in range(4)], w=["lnr"], small=True)
                    S.op('act', lambda: ACT.activation(out=lnr[:], in_=lnr[:], func=AF.Exp, scale=-0.5), w=["lnr"], small=True)
                    for h in range(4):
                        S.op('dve', lambda h=h: DVE.tensor_scalar(out=hh[:, h * 256:(h + 1) * 256], in0=hm_t[:, h * 256:(h + 1) * 256], scalar1=mv[:, h, 0:1],
                                                                  scalar2=lnr[:, h:h + 1], op0=ALU.subtract, op1=ALU.mult), r=["lnr", ("mv", h)], w=["hh"], strict=True, small=True)
                    psb = ps[0][:, :].bitcast(BF16)
                    for fc in range(8):
                        S.op('pe', lambda fc=fc: PE.transpose(out=psb[:, fc * 128:(fc + 1) * 128], in_=hh[:, fc * 128:(fc + 1) * 128], identity=ident_b[:]),
                             r=["hh"], w=[P(0)], sig=(fc == 7))
                    S.op('dve', lambda cc=cc, ogl=ogl: DVE.tensor_tensor(out=yaT[:, :, cc * 128:(cc + 1) * 128], in0=psb[:, :].rearrange("p (f t) -> p f t", f=8),
                                                                          in1=ogl[:], op=ALU.mult), r=[P(0), f"ld_og{slot}"], w=[("yaT", cc)])
                    if cc == 0:
                        S.dma('sp', ybT[:], gu_d.rearrange("f p t -> p f t")[:, :, 512 * tile:512 * (tile + 1)], w=["ybT"])
                    if c in (0, 4):
                        dbg_dump(f"hm{c}", hm_t[:], [("hm", h) for h in range(4)])
                        dbg_dump(f"hh{c}", hh[:], ["hh"])
                    if cc < 3:
                        continue
                    if tile in (0, 1):
                        dbg_dump(f"ya{tile}", yaT[:], [("yaT", i) for i in range(4)])
                        dbg_dump(f"yb{tile}", ybT[:], ["ybT"])
                    flush_items()
                    t0 = 512 * tile
                    YA = [("yaT", i) for i in range(4)]
                    YB = ["ybT"]
                    MX = [("mixT", i) for i in range(8)]

                    def ab_item(dc, t0=t0, YA=YA, YB=YB):
                        blk, j = dc // 4, dc % 4
                        if j == 0:
                            wload(wbr[0][:], "wbr0", w_ba.rearrange("(kc p) n -> p kc n", p=128)[:, :, blk * 512:(blk + 1) * 512])
                            wload(wbr[1][:], "wbr1", w_bb.rearrange("(kc p) n -> p kc n", p=128)[:, :, blk * 512:(blk + 1) * 512])
                        s2 = dc % 2
                        S.dma('sp', sga[s2][:], ga_d[dc][:, t0:t0 + 512], w=[f"sga{s2}"])
                        S.dma('sp', sgb[s2][:], gb_d[dc][:, t0:t0 + 512], w=[f"sgb{s2}"])
                        pa, pb2 = 7, 0
                        for kc in range(8):
                            S.op('pe', lambda kc=kc: PE.matmul(ps[pa][:, :], lhsT=wbr[0][:, kc, j * 128:(j + 1) * 128], rhs=yaT[:, kc, :],
                                                               start=(kc == 0), stop=(kc == 7)), r=["wbr0"] + YA, w=[P(pa)], sig=(kc == 7))
                        for kc in range(8):
                            S.op('pe', lambda kc=kc: PE.matmul(ps[pb2][:, :], lhsT=wbr[1][:, kc, j * 128:(j + 1) * 128], rhs=ybT[:, kc, :],
                                                               start=(kc == 0), stop=(kc == 7)), r=["wbr1"] + YB, w=[P(pb2)], sig=(kc == 7))
                        S.op('dve', lambda: DVE.tensor_tensor(out=tA[s2][:], in0=ps[pa][:, :], in1=sga[s2][:], op=ALU.mult),
                             r=[P(pa), f"sga{s2}"], w=["tA0"])
                        S.op('dve', lambda: DVE.tensor_tensor(out=tB[s2][:], in0=ps[pb2][:, :], in1=sgb[s2][:], op=ALU.mult),
                             r=[P(pb2), f"sgb{s2}"], w=["tB0"])
                        S.op('pool', lambda: POOL.tensor_tensor(out=mixT[:, dc, :], in0=tA[s2][:], in1=tB[s2][:], op=ALU.add),
                             r=["tA0", "tB0"], w=[("mixT", dc)])

                    def out_item(dc, t0=t0, tile=tile, MX=MX):
                        blk, j = dc // 4, dc % 4
                        if j == 0:
                            wload(wbr[blk][:], f"wbr{blk}", w_o.rearrange("(kc p) n -> p kc n", p=128)[:, :, blk * 512:(blk + 1) * 512])
                        pb = 7 if dc % 2 == 0 else 0
                        for kc in range(8):
                            S.op('pe', lambda kc=kc: PE.matmul(ps[pb][:, :], lhsT=wbr[blk][:, kc, j * 128:(j + 1) * 128], rhs=mixT[:, kc, :],
                                                               start=(kc == 0), stop=(kc == 7)), r=[f"wbr{blk}"] + MX, w=[P(pb)], sig=(kc == 7))
                        S.op('dve', lambda: DVE.scalar_tensor_tensor(
                            out=hT[:, dc, 1 + t0:1 + t0 + 512], in0=ps[pb][:, :], scalar=prm[:, P_G5, dc:dc + 1], in1=hT[:, dc, 1 + t0:1 + t0 + 512],
                            op0=ALU.mult, op1=ALU.add), r=[P(pb)], w=[("hT2", tile, dc)])

                    for dc in range(8):
                        pending.append(('ab', lambda dc=dc, f=ab_item: f(dc)))
                    for dc in range(8):
                        pending.append(('out', lambda dc=dc, f=out_item: f(dc)))
                    if KITEMS == 0:
                        flush_items()
                flush_items()
                S.barrier()
        for nm, src in [("qT", qT_d), ("kT", kT_d), ("ktok", ktok_d), ("v", v_d), ("hb", hb_d), ("og", og_d), ("gu", gu_d), ("ga", ga_d), ("vs", vs_d)]:
            if nm in dbg:
                S.dma('sp', dbg[nm], src)
        if "h2" in dbg:
            S.dma('sp', dbg["h2"].rearrange("dc p t -> p dc t"), hT[:, :, :])
        S.barrier()
        mx.close()
        S.barrier()

        if KSTOP == "all":
            ffn(w_f2i, w_f2o, main_tiles, (P_GS2, P_SH2, P_GT2), (P_GS1C, P_SH1C, P_GT1C), "f2")

        if "h1" in dbg:
            S.dma('sp', dbg["h1"].rearrange("dc p t -> p dc t"), hT[:, :, :], r=[])
            S.dma('sp', dbg["hc1"].rearrange("dc p t -> p dc t"), hcT[:, :, :], r=[])
            S.barrier()

        def final_out():
            with ExitStack() as st:
                sq = sb(st, "fsq", [128, 8, 512], BF16)
                rstd = sb(st, "frstd", [128, 512], F32)
                yT = [sb(st, f"fy{i}", [128, 8, 512], F32) for i in range(2)]
                ot = [sb(st, f"fot{i}", [128, D], F32) for i in range(2)]
                for t in range(4):
                    o0 = 1 + 512 * t
                    y = yT[t % 2]
                    ytok = f"fy{t % 2}"
                    for dc in range(8):
                        S.op('act', lambda dc=dc: ACT.activation(out=sq[:, dc, :], in_=hT[:, dc, o0:o0 + 512], func=AF.Square), w=["fsq"])
                    for dc in range(8):
                        S.op('pe', lambda dc=dc: PE.matmul(ps[7][:, :], lhsT=ones_b[:], rhs=sq[:, dc, :], start=(dc == 0), stop=(dc == 7)),
                             r=["fsq"], w=[P(7)], sig=(dc == 7))
                    S.op('act', lambda: ACT.activation(out=rstd[:], in_=ps[7][:, :], func=AF.Ln, scale=1.0 / D, bias=eps_t[:, 0:1]), r=[P(7)], w=["frstd"])
                    S.op('act', lambda: ACT.activation(out=rstd[:], in_=rstd[:], func=AF.Exp, scale=-0.5), w=["frstd"])
                    for dc in range(8):
                        S.op('dve', lambda dc=dc, y=y: DVE.scalar_tensor_tensor(out=y[:, dc, :], in0=hT[:, dc, o0:o0 + 512], scalar=vecT[:, R_GFIN + dc:R_GFIN + dc + 1],
                                                                              in1=rstd[:], op0=ALU.mult, op1=ALU.mult),
                             r=["frstd"], w=[(ytok, dc)])
                    for cc in range(4):
                        c = 4 * t + cc
                        o = ot[c % 2]
                        for half in range(2):
                            pb = 2 + half + 2 * (c % 2)
                            for k4 in range(4):
                                dc = half * 4 + k4
                                S.op('pe', lambda dc=dc, k4=k4, pb=pb, y=y, cc=cc: PE.transpose(out=ps[pb][:, k4 * 128:(k4 + 1) * 128], in_=y[:, dc, cc * 128:(cc + 1) * 128],
                                                                                          identity=ident_f[:]),
                                     r=[(ytok, dc)], w=[P(pb)], sig=(k4 == 3))
                            if half == 0:
                                S.op('act', lambda pb=pb, o=o: ACT.copy(out=o[:, 0:512], in_=ps[pb][:, :]), r=[P(pb)], w=[f"fot{c % 2}a"])
                            else:
                                S.op('dve', lambda pb=pb, o=o: DVE.tensor_copy(out=o[:, 512:1024], in_=ps[pb][:, :]), r=[P(pb)], w=[f"fot{c % 2}b"])
                        S.dma('sp', out[128 * c:128 * (c + 1), :], o[:], r=[f"fot{c % 2}a", f"fot{c % 2}b"], w=[])
            S.barrier()

        final_out()
    return nc


def _host_inputs(inp):
    x = np.ascontiguousarray(inp["x"], dtype=np.float32)
    f32 = np.float32
    vec_common = [
        inp["b_ada"][0].reshape(72, 128), inp["g_ffn1"][0].reshape(8, 128), inp["g_mix"][0].reshape(8, 128),
        inp["conv_qk_w"][0].reshape(48, 128), inp["conv_qk_b"][0].reshape(16, 128), inp["g_head"][0].reshape(8, 128),
        inp["g_ffn2"][0].reshape(8, 128), inp["g_final"].reshape(8, 128)]
    quarter = D // 4
    fr = np.exp(-math.log(10000.0) * np.arange(quarter, dtype=f32) / quarter).astype(f32)
    freq = np.ascontiguousarray(fr.reshape(2, 128).T)
    consts = np.zeros((128, 3, 128), f32)
    consts[:, 0, :] = np.eye(128, dtype=f32)
    ii = np.arange(128)
    consts[:, 1, :] = (ii[:, None] <= ii[None, :]).astype(f32)
    consts[:, 2, :] = (ii[:, None] >= ii[None, :]).astype(f32)
    shared = dict(
        consts=consts, freq=freq, g_sgu=np.ascontiguousarray(inp["g_sgu"][0]), b_gates=np.ascontiguousarray(inp["b_gates"][0]),
        b_s=np.ascontiguousarray(inp["b_s"][0].reshape(512)), w_s=np.ascontiguousarray(inp["w_s"][0]),
        w_ffn1_in=np.ascontiguousarray(inp["w_ffn1_in"][0]),
        w_ffn1_out=np.ascontiguousarray(inp["w_ffn1_out"][0]), w_in=np.ascontiguousarray(inp["w_in"][0]),
        w_branch_a=np.ascontiguousarray(inp["w_branch_a"][0]), w_branch_b=np.ascontiguousarray(inp["w_branch_b"][0]),
        w_out=np.ascontiguousarray(inp["w_out"][0]), w_ffn2_in=np.ascontiguousarray(inp["w_ffn2_in"][0]),
        w_ffn2_out=np.ascontiguousarray(inp["w_ffn2_out"][0]))
    maps = []
    for core in range(8):
        b, j = core // 4, core % 4
        a = j * NT
        xs = np.zeros((NX, D), f32)
        xs[1:NT + 1] = x[b, a:a + NT]
        if j > 0:
            xs[0] = x[b, a - 1]
        if j < 3:
            xs[NX - 1] = x[b, a + NT]
        vecs = np.concatenate(vec_common + [inp["c"][b].reshape(8, 128), inp["c_ctx"].reshape(8, 128)], axis=0).astype(f32)
        meta = np.zeros((128, 8), f32)
        meta[:, 0] = j * 32 - 1
        meta[:, 1 + j] = 1.0
        meta[:, 5] = 1.0 if j > 0 else 0.0
        meta[:, 6] = 1.0 if j < 3 else 0.0
        m = dict(shared)
        m.update(w_ada=np.ascontiguousarray(inp["w_ada"][0][:, j * 2304:(j + 1) * 2304]), xs=xs, ctxb=np.ascontiguousarray(inp["ctx"][b], dtype=f32), vecs=np.ascontiguousarray(vecs), meta=meta)
        maps.append(m)
    return maps


_NC_CACHE = {}


def kernel(**inputs):
    inp = {k: np.asarray(v) for k, v in inputs.items()}
    maps = _host_inputs(inp)
    if "nc" not in _NC_CACHE:
        _NC_CACHE["nc"] = build()
    res = run_bass_kernel_spmd(_NC_CACHE["nc"], maps, core_ids=list(range(8)))
    outp = np.zeros((2, 4 * NT, D), np.float32)
    for core in range(8):
        b, j = core // 4, core % 4
        outp[b, j * NT:(j + 1) * NT] = res.results[core]["out"]
    kernel.last_results = res.results
    return outp
```

```python
import math
from contextlib import ExitStack

import numpy as np
import concourse.bass as bass
import concourse.mybir as mybir
from concourse.bass_utils import run_bass_kernel_spmd

F32 = mybir.dt.float32
BF16 = mybir.dt.bfloat16
I32 = mybir.dt.int32
AF = mybir.ActivationFunctionType
ALU = mybir.AluOpType

D = 1024
NT = 2048
NX = NT + 2
NCTX = 256
NCH = 16
DFF = 2816
NFF = 22
DPROJ = 8208
EPS = 1e-6
TWO_PI = 2.0 * math.pi
PI_SAFE = 3.1415925
XW = 8 * 514 + 8

DEBUG = {}
import os
KSTOP = os.environ.get("KSTOP", "all")
KITEMS = int(os.environ.get("KITEMS", "1"))


class Sch:
    NDS = 40

    def __init__(self, nc, es):
        self.nc = nc
        self.E = {'pe': nc.tensor, 'act': nc.scalar, 'dve': nc.vector, 'pool': nc.gpsimd, 'sp': nc.sync}
        self.semobj = {}
        for e in ['pe', 'act', 'dve', 'pool']:
            self.semobj[e] = es.enter_context(nc.semaphore(f"sem_{e}"))
        self.cnt = {e: 0 for e in ['pe', 'act', 'dve', 'pool']}
        self.seen = {e: {} for e in self.E}
        self.lw = {}
        self.rd = {}
        self.pend = {e: [] for e in self.cnt}
        self.dcnt = [0] * self.NDS
        for i in range(self.NDS):
            self.semobj[('d', i)] = es.enter_context(nc.semaphore(f"sem_d{i}"))
        self.dnext = 0
        self.nops = 0
        self.semobj['cc'] = es.enter_context(nc.semaphore("sem_cc"))
        self.cccnt = 0
        self.smallp = set()

    def _deps(self, eng, r, w):
        deps = set()
        for t in r:
            d = self.lw.get(t)
            if d is not None:
                deps.add((d, True))
        for t in w:
            d = self.lw.get(t)
            if d is not None:
                deps.add((d, True))
            for d in self.rd.get(t, ()):
                deps.add((d, False))
        return deps

    def _wait(self, eng, deps, strict=False):
        for (d, is_w) in deps:
            if d[0] == 'PEND':
                assert d[1] == eng, f"dependency on unsignaled op of {d[1]} from {eng}"
                continue
            key, val, src = d
            if src == eng:
                if eng == 'pe' or not is_w:
                    continue
                if not (strict or (key, val) in self.smallp):
                    continue
            if self.seen[eng].get(key, 0) >= val:
                continue
            self.E[eng].wait_ge(self.semobj[key], val)
            self.seen[eng][key] = val

    def op(self, eng, fn, r=(), w=(), sig=True, strict=False, small=False):
        strict = strict or small
        self._wait(eng, self._deps(eng, r, w), strict)
        ins = fn()
        self.nops += 1
        if sig:
            self.cnt[eng] += 1
            ins.then_inc(self.semobj[eng], 1)
            me = (eng, self.cnt[eng], eng)
            if small:
                self.smallp.add((eng, self.cnt[eng]))
            for (pr, pw) in self.pend[eng] + [(r, w)]:
                for t in pw:
                    self.lw[t] = me
                    self.rd[t] = []
            pm = ('PEND', eng)
            for (pr, pw) in self.pend[eng] + [(r, w)]:
                for t in pr:
                    lst = self.rd.setdefault(t, [])
                    if pm in lst:
                        lst[:] = [d for d in lst if d != pm]
                    if me not in lst:
                        lst.append(me)
            self.pend[eng] = []
        else:
            self.pend[eng].append((tuple(r), tuple(w)))
            for t in w:
                self.lw[t] = ('PEND', eng)
                self.rd[t] = []
            for t in r:
                self.rd.setdefault(t, []).append(('PEND', eng))
        return ins

    def dma(self, q, out, in_, r=(), w=(), **kw):
        deps = self._deps(q, r, w)
        idx = self.dnext
        self.dnext = (self.dnext + 1) % self.NDS
        if self.dcnt[idx] > 0:
            deps.add(((('d', idx), self.dcnt[idx], 'dma'), True))
        self._wait(q, deps)
        self.dcnt[idx] += 16
        self.E[q].dma_start(out=out, in_=in_, **kw).then_inc(self.semobj[('d', idx)], 16)
        me = (('d', idx), self.dcnt[idx], 'dma')
        for t in w:
            self.lw[t] = me
            self.rd[t] = []
        for t in r:
            self.rd.setdefault(t, []).append(me)

    def custom(self, eng, fn, inc, r=(), w=()):
        deps = self._deps(eng, r, w)
        self._wait(eng, deps)
        self.cccnt += inc
        fn(self.semobj['cc'])
        me = ('cc', self.cccnt, 'dma')
        for t in w:
            self.lw[t] = me
            self.rd[t] = []
        for t in r:
            self.rd.setdefault(t, []).append(me)

    def barrier(self):
        for e in self.cnt:
            assert not self.pend[e], f"pending unsignaled ops on {e} at barrier"
        deps = set()
        for e in self.cnt:
            if self.cnt[e] > 0:
                deps.add(((e, self.cnt[e], e), False))
        for i in range(self.NDS):
            if self.dcnt[i] > 0:
                deps.add(((('d', i), self.dcnt[i], 'dma'), True))
        if self.cccnt > 0:
            deps.add((('cc', self.cccnt, 'dma'), True))
        for e in self.E:
            self._wait(e, deps)
        self.lw = {}
        self.rd = {}


def build():
    nc = bass.Bass("TRN2", target_bir_lowering=False)

    def din(name, shape, dt=F32):
        return nc.dram_tensor(name, list(shape), dt, kind="ExternalInput").ap()

    xs = din("xs", [NX, D])
    ctxb = din("ctxb", [NCTX, D])
    vecs = din("vecs", [192, 128])
    meta = din("meta", [128, 8])
    freq = din("freq", [128, 2])
    consts = din("consts", [128, 3, 128])
    g_sgu = din("g_sgu", [D])
    b_gates = din("b_gates", [16])
    b_s = din("b_s", [512])
    w_s = din("w_s", [4, 128, 128])
    w_ada = din("w_ada", [D, 9 * D // 4])
    w_f1i = din("w_ffn1_in", [D, 2 * DFF])
    w_f1o = din("w_ffn1_out", [DFF, D])
    w_in = din("w_in", [D, DPROJ])
    w_ba = din("w_branch_a", [D, D])
    w_bb = din("w_branch_b", [D, D])
    w_o = din("w_out", [D, D])
    w_f2i = din("w_ffn2_in", [D, 2 * DFF])
    w_f2o = din("w_ffn2_out", [DFF, D])
    out = nc.dram_tensor("out", [NT, D], F32, kind="ExternalOutput").ap()

    def dscr(name, shape, dt):
        return nc.dram_tensor(name, list(shape), dt).ap()

    qT_d = dscr("qT_d", [8, 128, NT], BF16)
    kT_d = dscr("kT_d", [8, 128, NT + NCTX], BF16)
    ktok_d = dscr("ktok_d", [18, 128, D], BF16)
    v_d = dscr("v_d", [18, 128, D], BF16)
    vs_d = dscr("vs_d", [16, 128, D], F32)
    og_d = dscr("og_d", [8, 128, NT], BF16)
    gu_d = dscr("gu_d", [8, 128, NT], BF16)
    ga_d = dscr("ga_d", [8, 128, NT], BF16)
    gb_d = dscr("gb_d", [8, 128, NT], BF16)
    hb_d = dscr("hb_d", [16, 128, D], F32)
    mp_in = dscr("mp_in", [128, 36], F32)
    mp_out = dscr("mp_out", [4 * 128, 36], F32)
    xin_l = [dscr(f"xin_d{i}", [128, 1030], F32) for i in range(4)]
    xout_l = [dscr(f"xout_d{i}", [4 * 128, 1030], F32) for i in range(4)]

    dbg = {}
    for name, (shape, dt) in DEBUG.items():
        dbg[name] = nc.dram_tensor("dbg_" + name, list(shape), dt, kind="ExternalOutput").ap()

    with ExitStack() as es:
        S = Sch(nc, es)
        ACT, DVE, PE, POOL = nc.scalar, nc.vector, nc.tensor, nc.gpsimd

        def sb(stack, name, shape, dt):
            return stack.enter_context(nc.sbuf_tensor(name, list(shape), dt))

        def pstep(t):
            return t[:].ap[0][0]

        def cap(t, off, dims):
            return bass.AP(t, off, [[pstep(t), 128]] + [list(d) for d in dims])

        psall = es.enter_context(nc.psum_tensor("psall", [128, 8, 512], F32))
        ps = [psall[:, i, :] for i in range(8)]

        def P(i):
            return ("ps", i)

        ident_f = sb(es, "ident_f", [128, 128], F32)
        triL = sb(es, "triL", [128, 128], F32)
        triU = sb(es, "triU", [128, 128], F32)
        ident_b = sb(es, "ident_b", [128, 128], BF16)
        mL16 = sb(es, "mL16", [128, 128], F32)
        mU16 = sb(es, "mU16", [128, 128], F32)
        ones_f = sb(es, "ones_f", [128, 128], F32)
        ones_b = sb(es, "ones_b", [128, 128], BF16)
        vecT = sb(es, "vecT", [128, 192], F32)
        metat = sb(es, "metat", [128, 8], F32)
        modB = sb(es, "modB", [128, 72], F32)
        modC = sb(es, "modC", [128, 72], F32)
        prm = sb(es, "prm", [128, 16, 8], F32)
        hT = sb(es, "hT", [128, 8, NX], F32)
        hcT = sb(es, "hcT", [128, 8, NCTX], F32)
        eps_t = sb(es, "eps_t", [128, 1], F32)

        R_BADA, R_GF1, R_GMIX, R_CW, R_CB, R_GH, R_GF2, R_GFIN, R_C, R_CC = 0, 72, 80, 88, 136, 152, 160, 168, 176, 184
        (P_GS1, P_SH1, P_GT1, P_GS1C, P_SH1C, P_GT1C, P_GSM, P_SHM, P_GSMC, P_SHMC, P_G5, P_GS2, P_SH2, P_GT2) = range(14)

        S.dma('sp', ident_f[:], consts[:, 0, :], w=["ident_f"])
        S.dma('sp', triL[:], consts[:, 1, :], w=["triL"])
        S.dma('sp', triU[:], consts[:, 2, :], w=["triU"])
        S.dma('sp', metat[:], meta, w=["meta"])
        S.op('dve', lambda: DVE.tensor_copy(out=ident_b[:], in_=ident_f[:]), r=["ident_f"], w=["ident_b"])
        S.op('dve', lambda: DVE.tensor_scalar(out=mL16[:], in0=triL[:], scalar1=0.0625, scalar2=None, op0=ALU.mult), r=["triL"], w=["mL16"])
        S.op('dve', lambda: DVE.tensor_scalar(out=mU16[:], in0=triU[:], scalar1=0.0625, scalar2=None, op0=ALU.mult), r=["triU"], w=["mU16"])
        S.op('dve', lambda: DVE.memset(ones_f[:], 1.0), w=["ones_f"])
        S.op('dve', lambda: DVE.memset(ones_b[:], 1.0), w=["ones_b"])

        def wload(dst_tile, dst_tok, src_ap):
            S.dma('pool', dst_tile, src_ap, w=[dst_tok])

        with ExitStack() as p1:
            vst = sb(p1, "vst", [96, 2, 128], F32)
            S.dma('sp', vst[:, 0, :], vecs[0:96, :], w=["vst0"])
            S.dma('sp', vst[:, 1, :], vecs[96:192, :], w=["vst1"])
            for i in range(2):
                S.op('pe', lambda i=i: PE.transpose(out=ps[0][:, i * 96:(i + 1) * 96], in_=vst[:, i, :], identity=ident_f[0:96, 0:96]),
                     r=[f"vst{i}", "ident_f"], w=[P(0)], sig=(i == 1))
            S.op('dve', lambda: DVE.tensor_copy(out=vecT[:], in_=ps[0][:, 0:192]), r=[P(0)], w=["vecT"])

            scT = sb(p1, "scT", [128, 8, 2], F32)
            S.op('act', lambda: ACT.activation(out=scT[:, :, 0], in_=vecT[:, R_C:R_C + 8], func=AF.Silu), r=["vecT"], w=["scT0"])
            S.op('act', lambda: ACT.activation(out=scT[:, :, 1], in_=vecT[:, R_CC:R_CC + 8], func=AF.Silu), r=["vecT"], w=["scT1"])

            wad = [sb(p1, f"wad{i}", [128, 8, 256], F32) for i in range(3)]
            w_ada_v = w_ada.rearrange("(kc p) n -> p kc n", p=128)
            for blk in range(9):
                slot = blk % 3
                S.dma('sp', wad[slot][:], w_ada_v[:, :, blk * 256:(blk + 1) * 256], w=[f"wad{slot}"])
                for j in range(2):
                    b128 = blk * 2 + j
                    for kc in range(8):
                        S.op('pe', lambda slot=slot, j=j, kc=kc, b128=b128: PE.matmul(
                            ps[1][:, b128 * 2:b128 * 2 + 2], lhsT=wad[slot][:, kc, j * 128:(j + 1) * 128], rhs=scT[:, kc, :],
                            start=(kc == 0), stop=(kc == 7)),
                            r=[f"wad{slot}", "scT0", "scT1"], w=[P(1)], sig=(kc == 7 and j == 1))
            mpart = sb(p1, "mpart", [128, 36], F32)
            mg = sb(p1, "mg", [128, 4, 36], F32)
            S.op('dve', lambda: DVE.tensor_copy(out=mpart[:], in_=ps[1][:, 0:36]), r=[P(1)], w=["mpart"])
            S.dma('sp', mp_in, mpart[:], r=["mpart"], w=["mp_in"])
            S.custom('pool', lambda sem: POOL.collective_compute("AllGather", ALU.bypass, replica_groups=[[0, 1, 2, 3], [4, 5, 6, 7]],
                                                                 ins=[mp_in.opt()], outs=[mp_out.opt()]).then_inc(sem, 1),
                     1, r=["mp_in"], w=["mp_out"])
            S.dma('sp', mg[:], mp_out.rearrange("(r p) w -> p r w", p=128), r=["mp_out"], w=["mg"])
            psm = mg[:, :, :].rearrange("p r (b t) -> p (r b) t", t=2)
            S.op('dve', lambda: DVE.tensor_tensor(out=modB[:], in0=psm[:, :, 0], in1=vecT[:, R_BADA:R_BADA + 72], op=ALU.add), r=["mg", "vecT"], w=["modB"], small=True)
            S.op('dve', lambda: DVE.tensor_tensor(out=modC[:], in0=psm[:, :, 1], in1=vecT[:, R_BADA:R_BADA + 72], op=ALU.add), r=["mg", "vecT"], w=["modC"], small=True)

            def mk_gs(slot, gain_row, mod, scale_idx):
                S.op('dve', lambda: DVE.scalar_tensor_tensor(out=prm[:, slot, :], in0=mod[:, scale_idx * 8:scale_idx * 8 + 8], scalar=1.0,
                                                             in1=vecT[:, gain_row:gain_row + 8], op0=ALU.add, op1=ALU.mult),
                     r=["modB", "modC", "vecT"], w=[("prm", slot)], small=True)

            def mk_cp(slot, mod, idx, mul=1.0):
                S.op('dve', lambda: DVE.tensor_scalar(out=prm[:, slot, :], in0=mod[:, idx * 8:idx * 8 + 8], scalar1=mul, scalar2=None, op0=ALU.mult),
                     r=["modB", "modC"], w=[("prm", slot)], small=True)

            mk_gs(P_GS1, R_GF1, modB, 1); mk_cp(P_SH1, modB, 0); mk_cp(P_GT1, modB, 2, 0.5)
            mk_gs(P_GS1C, R_GF1, modC, 1); mk_cp(P_SH1C, modC, 0); mk_cp(P_GT1C, modC, 2, 0.5)
            mk_gs(P_GSM, R_GMIX, modB, 4); mk_cp(P_SHM, modB, 3)
            mk_gs(P_GSMC, R_GMIX, modC, 4); mk_cp(P_SHMC, modC, 3)
            mk_cp(P_G5, modB, 5)
            mk_gs(P_GS2, R_GF2, modB, 7); mk_cp(P_SH2, modB, 6); mk_cp(P_GT2, modB, 8, 0.5)

            fq = sb(p1, "fq", [128, 2], F32)
            S.dma('sp', fq[:], freq, w=["fq"])
            tab_r = sb(p1, "tab_r", [128, 4, 34], F32)
            tab_c = sb(p1, "tab_c", [128, 4, 64], F32)
            io_i = sb(p1, "io_i", [128, 64], I32)
            io_f = sb(p1, "io_f", [128, 64], F32)
            rv = sb(p1, "rv", [128, 34], F32)
            S.op('pool', lambda: POOL.iota(io_i[:], pattern=[[1, 64]], base=0, channel_multiplier=0), w=["io_i"])
            S.op('dve', lambda: DVE.tensor_copy(out=io_f[:], in_=io_i[:]), r=["io_i"], w=["io_f"], small=True)
            S.op('dve', lambda: DVE.tensor_scalar(out=rv[:], in0=io_f[:, 0:34], scalar1=metat[:, 0:1], scalar2=None, op0=ALU.add), r=["io_f", "meta"], w=["rv"], small=True)

            def mk_tab(tab, vals, n):
                arg = sb(p1, f"arg_{n}", [128, 4, n], F32)
                ki = sb(p1, f"ki_{n}", [128, 4, n], I32)
                kf = sb(p1, f"kf_{n}", [128, 4, n], F32)
                for cj in range(2):
                    for sc in range(2):
                        idx = sc * 2 + cj
                        S.op('dve', lambda idx=idx, cj=cj, sc=sc: DVE.tensor_scalar(
                            out=arg[:, idx, :], in0=vals, scalar1=fq[:, cj:cj + 1], scalar2=(0.5 * math.pi if sc else 0.0),
                            op0=ALU.mult, op1=ALU.add), r=["rv", "io_f", "fq"], w=[f"arg{n}"], small=True)
                S.op('dve', lambda: DVE.tensor_scalar(out=kf[:], in0=arg[:], scalar1=1.0 / TWO_PI, scalar2=None, op0=ALU.mult), r=[f"arg{n}"], w=[f"kf{n}"], small=True)
                S.op('dve', lambda: DVE.tensor_copy(out=ki[:], in_=kf[:]), r=[f"kf{n}"], w=[f"ki{n}"], small=True)
                S.op('dve', lambda: DVE.tensor_copy(out=kf[:], in_=ki[:]), r=[f"ki{n}"], w=[f"kf{n}"], small=True)
                S.op('dve', lambda: DVE.scalar_tensor_tensor(out=arg[:], in0=kf[:], scalar=-TWO_PI, in1=arg[:], op0=ALU.mult, op1=ALU.add),
                     r=[f"kf{n}"], w=[f"arg{n}"], small=True)
                S.op('dve', lambda: DVE.tensor_scalar(out=arg[:], in0=arg[:], scalar1=-PI_SAFE, scalar2=PI_SAFE, op0=ALU.max, op1=ALU.min), w=[f"arg{n}"], small=True)
                S.op('act', lambda: ACT.activation(out=tab[:], in_=arg[:], func=AF.Sin), r=[f"arg{n}"], w=[f"tab{n}"], small=True)

            mk_tab(tab_r, rv[:], 34)
            mk_tab(tab_c, io_f[:], 64)

            xt = [sb(p1, f"xt{i}", [128, D], F32) for i in range(2)]
            xh = sb(p1, "xh", [2, D], F32)
            for c in range(NCH):
                slot = c % 2
                S.dma('sp', xt[slot][:], xs[1 + 128 * c:1 + 128 * (c + 1), :], w=[f"xt{slot}"])
                for half in range(2):
                    pb = 2 + half
                    for k4 in range(4):
                        dc = half * 4 + k4
                        S.op('pe', lambda slot=slot, dc=dc, pb=pb, k4=k4: PE.transpose(
                            out=ps[pb][:, k4 * 128:(k4 + 1) * 128], in_=xt[slot][:, dc * 128:(dc + 1) * 128], identity=ident_f[:]),
                            r=[f"xt{slot}", "ident_f"], w=[P(pb)], sig=(k4 == 3))
                    o_ap = cap(hT, half * 4 * NX + 1 + 128 * c, [[NX, 4], [64, 2], [1, 64]])
                    i_ap = ps[pb][:, :].rearrange("p (a b c) -> p a b c", a=4, b=2, c=64)
                    if half == 0:
                        t_ap = cap(tab_r, 1 + 2 * c, [[34, 4], [1, 2], [0, 64]])
                        S.op('dve', lambda o_ap=o_ap, i_ap=i_ap, t_ap=t_ap: DVE.tensor_tensor(out=o_ap, in0=i_ap, in1=t_ap, op=ALU.add),
                             r=[P(pb), "tab34"], w=[("hT", c)])
                    else:
                        t_ap = cap(tab_c, 0, [[64, 4], [0, 2], [1, 64]])
                        S.op('dve', lambda o_ap=o_ap, i_ap=i_ap, t_ap=t_ap: DVE.tensor_tensor(out=o_ap, in0=i_ap, in1=t_ap, op=ALU.add),
                             r=[P(pb), "tab64"], w=[("hT", c)])
            S.dma('sp', xh[0:1, :], xs[0:1, :], w=["xh0"])
            S.dma('sp', xh[1:2, :], xs[NX - 1:NX, :], w=["xh1"])
            for dc in range(8):
                S.op('pe', lambda dc=dc: PE.transpose(out=ps[2][:, dc * 2:dc * 2 + 2], in_=xh[:, dc * 128:(dc + 1) * 128], identity=ident_f[0:2, 0:2]),
                     r=["xh0", "xh1", "ident_f"], w=[P(2)], sig=(dc == 7))
            pv = ps[2][:, 0:16].rearrange("p (d t) -> p d t", t=2)
            S.op('dve', lambda: DVE.tensor_tensor(out=cap(hT, 0, [[NX, 4]]), in0=pv[:, 0:4, 0], in1=tab_r[:, :, 0], op=ALU.add), r=[P(2), "tab34"], w=[("hT", "h0a")])
            S.op('dve', lambda: DVE.tensor_tensor(out=cap(hT, 4 * NX, [[NX, 4]]), in0=pv[:, 4:8, 0], in1=tab_c[:, :, 63], op=ALU.add), r=[P(2), "tab64"], w=[("hT", "h0b")])
            S.op('dve', lambda: DVE.tensor_tensor(out=cap(hT, NX - 1, [[NX, 4]]), in0=pv[:, 0:4, 1], in1=tab_r[:, :, 33], op=ALU.add), r=[P(2), "tab34"], w=[("hT", "h1a")])
            S.op('dve', lambda: DVE.tensor_tensor(out=cap(hT, 4 * NX + NX - 1, [[NX, 4]]), in0=pv[:, 4:8, 1], in1=tab_c[:, :, 0], op=ALU.add), r=[P(2), "tab64"], w=[("hT", "h1b")])
            for c in range(2):
                slot = c % 2
                S.dma('sp', xt[slot][:], ctxb[128 * c:128 * (c + 1), :], w=[f"xt{slot}"])
                for half in range(2):
                    pb = 2 + half
                    for k4 in range(4):
                        dc = half * 4 + k4
                        S.op('pe', lambda slot=slot, dc=dc, pb=pb, k4=k4: PE.transpose(
                            out=ps[pb][:, k4 * 128:(k4 + 1) * 128], in_=xt[slot][:, dc * 128:(dc + 1) * 128], identity=ident_f[:]),
                            r=[f"xt{slot}", "ident_f"], w=[P(pb)], sig=(k4 == 3))
                    S.op('dve', lambda pb=pb, half=half, c=c: DVE.tensor_copy(
                        out=hcT[:, half * 4:half * 4 + 4, c * 128:(c + 1) * 128], in_=ps[pb][:, :].rearrange("p (a b) -> p a b", a=4)),
                        r=[P(pb)], w=[("hcT", c)])
            S.barrier()

        def norm_mod(stk, src, src_off, n, gs_slot, sh_slot, dst, dst_off, uid, stride=1):
            sq, rstd, tmp = stk["sq"], stk["rstd"], stk["tmp"]
            ssrc = src.shape[2]
            sdst = dst.shape[2]

            def s_ap(dc):
                return cap(src, dc * ssrc + src_off, [[stride, n]])

            for dc in range(8):
                S.op('act', lambda dc=dc: ACT.activation(out=sq[:, dc, 0:n], in_=s_ap(dc), func=AF.Square), r=[("src", uid)], w=["sq"])
            for dc in range(8):
                S.op('pe', lambda dc=dc: PE.matmul(ps[7][:, 0:n], lhsT=ones_b[:], rhs=sq[:, dc, 0:n], start=(dc == 0), stop=(dc == 7)),
                     r=["sq", "ones_b"], w=[P(7)], sig=(dc == 7))
            S.op('act', lambda: ACT.activation(out=rstd[:, 0:n], in_=ps[7][:, 0:n], func=AF.Ln, scale=1.0 / D, bias=eps_t[:, 0:1]), r=[P(7)], w=["rstd"], small=(n < 256))
            S.op('act', lambda: ACT.activation(out=rstd[:, 0:n], in_=rstd[:, 0:n], func=AF.Exp, scale=-0.5), w=["rstd"], small=(n < 256))
            for dc in range(8):
                tt = tmp[dc % 2]
                S.op('dve', lambda dc=dc, tt=tt: DVE.scalar_tensor_tensor(out=tt[:, 0:n], in0=s_ap(dc), scalar=prm[:, gs_slot, dc:dc + 1], in1=rstd[:, 0:n],
                                                                         op0=ALU.mult, op1=ALU.mult),
                     r=[("src", uid), "rstd", ("prm", gs_slot)], w=[f"nm_tmp{dc % 2}"])
                S.op('act', lambda dc=dc, tt=tt: ACT.activation(out=dst[:, dc, dst_off:dst_off + n], in_=tt[:, 0:n], func=AF.Identity,
                                                                bias=prm[:, sh_slot, dc:dc + 1], scale=1.0),
                     r=[f"nm_tmp{dc % 2}", ("prm", sh_slot)], w=[("hn", uid)])

        S.op('dve', lambda: DVE.memset(eps_t[:], EPS), w=["eps_t"])
        S.barrier()

        def ffn(w_i, w_o2, tiles, prm_main, prm_ctx, tag):
            with ExitStack() as st:
                hnT = sb(st, "hnT" + tag, [128, 8, NX], BF16)
                hncT = sb(st, "hncT" + tag, [128, 8, NCTX], BF16)
                stk = {"sq": sb(st, "sq" + tag, [128, 8, 512], BF16), "rstd": sb(st, "rstd" + tag, [128, 512], F32),
                       "tmp": [sb(st, f"nmt{i}" + tag, [128, 512], F32) for i in range(2)]}
                ntile = len(tiles)
                for ti, (kind, off, n) in enumerate(tiles):
                    if kind == 'm':
                        norm_mod(stk, hT, off, n, prm_main[0], prm_main[1], hnT, off, (tag, ti))
                    else:
                        norm_mod(stk, hcT, off, n, prm_ctx[0], prm_ctx[1], hncT, off, (tag, ti))
                GRP = 6
                groups = [(0, 6), (6, 6), (12, 6), (18, 4)]
                zT = sb(st, "zT" + tag, [128, GRP, NX + NCTX], BF16)
                wa = [sb(st, f"wa{i}" + tag, [128, 8, 256], BF16) for i in range(2)]
                wb = [sb(st, f"wb{i}" + tag, [128, 8, 256], BF16) for i in range(2)]
                wo = [sb(st, f"wo{i}" + tag, [128, GRP, D], BF16) for i in range(2)]
                sl = [sb(st, f"sl{i}" + tag, [128, 512], F32) for i in range(2)]
                w_i_v = w_i.rearrange("(kc p) n -> p kc n", p=128)
                w_o_v = w_o2.rearrange("(fc p) n -> p fc n", p=128)
                blk_ctr = 0
                for gi, (g0, gn) in enumerate(groups):
                    gslot = gi % 2
                    wload(wo[gslot][:, 0:gn, :], f"wo{gslot}" + tag, w_o_v[:, g0:g0 + gn, :])
                    for b2 in range(gn // 2):
                        f0 = g0 + 2 * b2
                        slot = blk_ctr % 2
                        blk_ctr += 1
                        wload(wa[slot][:], f"wa{slot}" + tag, w_i_v[:, :, f0 * 128:(f0 + 2) * 128])
                        wload(wb[slot][:], f"wb{slot}" + tag, w_i_v[:, :, DFF + f0 * 128:DFF + (f0 + 2) * 128])
                        for ti, (kind, off, n) in enumerate(tiles):
                            src = hnT if kind == 'm' else hncT
                            zoff = off if kind == 'm' else NX + off
                            for j in range(2):
                                fz = 2 * b2 + j
                                pa, pb = (0, 1) if (j == 0) else (2, 3)
                                for kc in range(8):
                                    S.op('pe', lambda kc=kc, j=j, pa=pa, src=src, off=off, n=n, slot=slot: PE.matmul(
                                        ps[pa][:, 0:n], lhsT=wa[slot][:, kc, j * 128:(j + 1) * 128], rhs=src[:, kc, off:off + n],
                                        start=(kc == 0), stop=(kc == 7)),
                                        r=[f"wa{slot}" + tag, ("hn", (tag, ti))], w=[P(pa)], sig=(kc == 7))
                                for kc in range(8):
                                    S.op('pe', lambda kc=kc, j=j, pb=pb, src=src, off=off, n=n, slot=slot: PE.matmul(
                                        ps[pb][:, 0:n], lhsT=wb[slot][:, kc, j * 128:(j + 1) * 128], rhs=src[:, kc, off:off + n],
                                        start=(kc == 0), stop=(kc == 7)),
                                        r=[f"wb{slot}" + tag, ("hn", (tag, ti))], w=[P(pb)], sig=(kc == 7))
                                S.op('act', lambda pa=pa, j=j, n=n: ACT.activation(out=sl[j][:, 0:n], in_=ps[pa][:, 0:n], func=AF.Silu),
                                     r=[P(pa)], w=[f"sl{j}" + tag])
                                S.op('dve', lambda pb=pb, j=j, n=n, fz=fz, zoff=zoff: DVE.tensor_tensor(
                                    out=zT[:, fz, zoff:zoff + n], in0=ps[pb][:, 0:n], in1=sl[j][:, 0:n], op=ALU.mult),
                                    r=[P(pb), f"sl{j}" + tag], w=[("z", tag, ti, fz)])
                    for ti, (kind, off, n) in enumerate(tiles):
                        dstT = hT if kind == 'm' else hcT
                        zoff = off if kind == 'm' else NX + off
                        gt = (prm_main if kind == 'm' else prm_ctx)[2]
                        for dc in range(8):
                            pb = 4 + (dc % 3)
                            for fz in range(gn):
                                S.op('pe', lambda dc=dc, pb=pb, fz=fz, zoff=zoff, n=n, gslot=gslot: PE.matmul(
                                    ps[pb][:, 0:n], lhsT=wo[gslot][:, fz, dc * 128:(dc + 1) * 128], rhs=zT[:, fz, zoff:zoff + n],
                                    start=(fz == 0), stop=(fz == gn - 1)),
                                    r=[f"wo{gslot}" + tag, ("z", tag, ti, fz)], w=[P(pb)], sig=(fz == gn - 1))
                            S.op('dve', lambda dc=dc, pb=pb, dstT=dstT, off=off, n=n, gt=gt: DVE.scalar_tensor_tensor(
                                out=dstT[:, dc, off:off + n], in0=ps[pb][:, 0:n], scalar=prm[:, gt, dc:dc + 1], in1=dstT[:, dc, off:off + n],
                                op0=ALU.mult, op1=ALU.add),
                                r=[P(pb), ("prm", gt)], w=[("res", tag, ti, dc)])
            S.barrier()

        main_tiles = [('m', 1 + 512 * i, 512) for i in range(4)]
        halo_tiles = [('m', 0, 1), ('m', NX - 1, 1)]
        ctx_tile = [('c', 0, NCTX)]

        ffn(w_f1i, w_f1o, main_tiles + halo_tiles + ctx_tile, (P_GS1, P_SH1, P_GT1), (P_GS1C, P_SH1C, P_GT1C), "f1")

        mx = ExitStack()
        gates_all = sb(mx, "gates_all", [128, 18, 16], F32)
        sc_all = sb(mx, "sc_all", [128, 18, 32], F32)
        cs_b = sb(mx, "cs_b", [128, 2, 18, 4], F32)
        cs_g = sb(mx, "cs_g", [128, 18, 8], F32)
        Gseg = sb(mx, "Gseg", [128, 8], F32)
        LL = sb(mx, "LL", [128, 18, 8], F32)
        psc = sb(mx, "psc", [128, 18, 8], F32)
        wsT = sb(mx, "wsT", [128, 4, 128], BF16)
        bs_row = sb(mx, "bs_row", [1, 512], BF16)
        bg_bc = sb(mx, "bg_bc", [128, 16], F32)
        one_t = sb(mx, "one_t", [128, 1], F32)

        def dbg_dump(name, src_ap, rtoks):
            if name in dbg:
                S.dma('sp', dbg[name], src_ap, r=rtoks)

        with ExitStack() as st:
            wst = sb(st, "wst", [128, 4, 128], F32)
            bsf = sb(st, "bsf", [1, 512], F32)
            S.dma('sp', wst[:], w_s.rearrange("g t s -> t g s"), w=["wst"])
            S.dma('sp', bsf[:], b_s.rearrange("(o n) -> o n", o=1), w=["bsf"])
            S.dma('sp', bg_bc[:], bass.AP(b_gates.tensor, 0, [[0, 128], [1, 16]]), w=["bg_bc"])
            S.op('dve', lambda: DVE.memset(one_t[:], 1.0), w=["one_t"])
            for g in range(4):
                S.op('pe', lambda g=g: PE.transpose(out=ps[0][:, g * 128:(g + 1) * 128], in_=wst[:, g, :], identity=ident_f[:]),
                     r=["wst"], w=[P(0)], sig=(g == 3))
            S.op('dve', lambda: DVE.tensor_copy(out=wsT[:], in_=ps[0][:, :].rearrange("p (g t) -> p g t", g=4)), r=[P(0)], w=["wsT"])
            S.op('dve', lambda: DVE.tensor_copy(out=bs_row[:], in_=bsf[:]), r=["bsf"], w=["bs_row"])
            S.barrier()

        GC = 1.5957691216057308

        class GeluPipe:
            def __init__(self):
                self.prev = None

            def push(self, x_ap, t1, out_ap, xtoks, t1tok, outtok, after=None):
                S.op('act', lambda: ACT.activation(out=t1, in_=x_ap, func=AF.Square), r=xtoks, w=[t1tok])
                S.op('dve', lambda: DVE.tensor_scalar(out=t1, in0=t1, scalar1=0.044715, scalar2=1.0, op0=ALU.mult, op1=ALU.add), w=[t1tok])
                S.op('dve', lambda: DVE.tensor_tensor(out=t1, in0=x_ap, in1=t1, op=ALU.mult), r=xtoks, w=[t1tok])
                self.flush()
                self.prev = (x_ap, t1, out_ap, xtoks, t1tok, outtok, after)

            def flush(self):
                if self.prev is None:
                    return
                x_ap, t1, out_ap, xtoks, t1tok, outtok, after = self.prev
                self.prev = None
                S.op('act', lambda: ACT.activation(out=t1, in_=t1, func=AF.Sigmoid, scale=GC), r=[t1tok], w=[t1tok])
                S.op('dve', lambda: DVE.tensor_tensor(out=out_ap, in0=x_ap, in1=t1, op=ALU.mult), r=xtoks + [t1tok], w=[outtok])
                if after:
                    after()

        gpipe = GeluPipe()

        with ExitStack() as st:
            hnT = sb(st, "hnT_m", [128, 8, NX], BF16)
            hncT = sb(st, "hncT_m", [128, 8, NCTX], BF16)
            with ExitStack() as nst:
                stk = {"sq": sb(nst, "sq_m", [128, 8, 512], BF16), "rstd": sb(nst, "rstd_m", [128, 512], F32),
                       "tmp": [sb(nst, f"nmt{i}_m", [128, 512], F32) for i in range(2)]}
                tl = main_tiles + halo_tiles
                for ti, (kind, off, n) in enumerate(tl):
                    norm_mod(stk, hT, off, n, P_GSM, P_SHM, hnT, off, ("mx", ti))
                norm_mod(stk, hcT, 0, NCTX, P_GSMC, P_SHMC, hncT, 0, ("mx", 6))
                S.barrier()
            HN_MAIN = [("hn", ("mx", i)) for i in range(4)]
            HN_HALO = [("hn", ("mx", 4)), ("hn", ("mx", 5))]
            HN_CTX = [("hn", ("mx", 6))]

            wblk = [sb(st, f"wblk{i}", [128, 8, 512], BF16) for i in range(2)]
            w_in_v = w_in.rearrange("(kc p) n -> p kc n", p=128)
            bctr = [0]
            BLKS = ([(i * 512, 512) for i in range(4)] + [(2048, 512), (2560, 512), (3072, 16), (5136, 512), (5648, 512)]
                    + [(c0 + b * 512, 512) for c0 in (3088, 4112, 6160, 7184) for b in range(2)])
            issued = [0]

            def _issue(i):
                c0, ncols = BLKS[i]
                wload(wblk[i % 2][:, :, 0:ncols], f"wblk{i % 2}", w_in_v[:, :, c0:c0 + ncols])

            def load_blk(c0, ncols):
                i = bctr[0]
                assert BLKS[i] == (c0, ncols), (i, BLKS[i], c0, ncols)
                bctr[0] += 1
                while issued[0] <= min(i + 1, len(BLKS) - 1):
                    _issue(issued[0])
                    issued[0] += 1
                return i % 2

            pre = [sb(st, f"pre{i}", [128, NX], F32) for i in range(2)]
            prec = sb(st, "prec", [128, NCTX + 2], F32)
            accs = [sb(st, f"acc{i}", [128, NT], F32) for i in range(2)]
            qks = [sb(st, f"qks{i}", [128, NT + NCTX], BF16) for i in range(2)]
            ktk = sb(st, "ktk", [128, 18, 128], BF16)
            stg = [sb(st, f"stg{i}", [128, NT], BF16) for i in range(2)]
            ut = [sb(st, f"ut{i}", [128, 512], F32) for i in range(3)]
            vstg = [sb(st, f"vstg{i}", [128, 512], BF16) for i in range(3)]
            vsstg = [sb(st, f"vsstg{i}", [128, 512], F32) for i in range(3)]
            S.op('dve', lambda: DVE.memset(prec[:], 0.0), w=["prec"])

            for blk in range(4):
                slot = load_blk(blk * 512, 512)
                for j in range(4):
                    fc = blk * 4 + j
                    is_k = fc >= 8
                    pr = pre[fc % 2]
                    ptok = f"pre{fc % 2}"
                    acc = accs[fc % 2]
                    atok = f"acc{fc % 2}"
                    for ti in range(4):
                        pb = (0, 1, 6, 7)[ti]
                        for kc in range(8):
                            S.op('pe', lambda kc=kc, j=j, pb=pb, ti=ti, slot=slot: PE.matmul(
                                ps[pb][:, :], lhsT=wblk[slot][:, kc, j * 128:(j + 1) * 128], rhs=hnT[:, kc, 1 + 512 * ti:1 + 512 * (ti + 1)],
                                start=(kc == 0), stop=(kc == 7)), r=[f"wblk{slot}", HN_MAIN[ti]], w=[P(pb)], sig=(kc == 7))
                        S.op('act', lambda pb=pb, ti=ti, pr=pr: ACT.copy(out=pr[:, 1 + 512 * ti:1 + 512 * (ti + 1)], in_=ps[pb][:, :]),
                             r=[P(pb)], w=[(ptok, ti)])
                    for kc in range(8):
                        S.op('pe', lambda kc=kc, j=j, slot=slot: PE.matmul(
                            ps[2][:, 0:2], lhsT=wblk[slot][:, kc, j * 128:(j + 1) * 128], rhs=cap(hnT, kc * NX, [[NX - 1, 2]]),
                            start=(kc == 0), stop=(kc == 7)), r=[f"wblk{slot}"] + HN_HALO, w=[P(2)], sig=(kc == 7))
                    S.op('dve', lambda pr=pr: DVE.tensor_tensor(out=cap(pr, 0, [[NX - 1, 2]]), in0=ps[2][:, 0:2], in1=metat[:, 5:7], op=ALU.mult),
                         r=[P(2), "meta"], w=[(ptok, 4)])
                    if is_k:
                        for kc in range(8):
                            S.op('pe', lambda kc=kc, j=j, slot=slot: PE.matmul(
                                ps[3][:, 0:NCTX], lhsT=wblk[slot][:, kc, j * 128:(j + 1) * 128], rhs=hncT[:, kc, :],
                                start=(kc == 0), stop=(kc == 7)), r=[f"wblk{slot}"] + HN_CTX, w=[P(3)], sig=(kc == 7))
                        S.op('act', lambda: ACT.copy(out=prec[:, 1:1 + NCTX], in_=ps[3][:, 0:NCTX]), r=[P(3)], w=["prec"])
                    w0 = vecT[:, R_CW + 0 * 16 + fc:R_CW + 0 * 16 + fc + 1]
                    w1 = vecT[:, R_CW + 1 * 16 + fc:R_CW + 1 * 16 + fc + 1]
                    w2 = vecT[:, R_CW + 2 * 16 + fc:R_CW + 2 * 16 + fc + 1]
                    cb = vecT[:, R_CB + fc:R_CB + fc + 1]
                    qs = qks[fc % 2]
                    qtok = f"qks{fc % 2}"
                    allpre = [(ptok, i) for i in range(5)]
                    S.op('pool', lambda pr=pr, w0=w0: POOL.tensor_scalar(out=acc[:], in0=pr[:, 0:NT], scalar1=w0, scalar2=0.0, op0=ALU.mult, op1=ALU.add),
                         r=allpre, w=[atok])
                    S.op('dve', lambda pr=pr, w1=w1: DVE.scalar_tensor_tensor(out=acc[:], in0=pr[:, 1:NT + 1], scalar=w1, in1=acc[:], op0=ALU.mult, op1=ALU.add),
                         r=allpre + [atok], w=[atok])
                    S.op('dve', lambda pr=pr, w2=w2: DVE.scalar_tensor_tensor(out=acc[:], in0=pr[:, 2:NT + 2], scalar=w2, in1=acc[:], op0=ALU.mult, op1=ALU.add),
                         r=allpre, w=[atok])
                    S.op('act', lambda qs=qs, cb=cb: ACT.activation(out=qs[:, 0:NT], in_=acc[:], func=AF.Silu, bias=cb, scale=1.0), r=[atok], w=[qtok])
                    if not is_k:
                        S.dma('sp', qT_d[fc], qs[:, 0:NT], r=[qtok], w=[("qT_d", fc)])
                    else:
                        S.op('dve', lambda w0=w0: DVE.tensor_scalar(out=acc[:, 0:NCTX], in0=prec[:, 0:NCTX], scalar1=w0, scalar2=None, op0=ALU.mult),
                             r=["prec", atok], w=[atok])
                        S.op('dve', lambda w1=w1: DVE.scalar_tensor_tensor(out=acc[:, 0:NCTX], in0=prec[:, 1:NCTX + 1], scalar=w1, in1=acc[:, 0:NCTX], op0=ALU.mult, op1=ALU.add),
                             r=["prec"], w=[atok])
                        S.op('dve', lambda w2=w2: DVE.scalar_tensor_tensor(out=acc[:, 0:NCTX], in0=prec[:, 2:NCTX + 2], scalar=w2, in1=acc[:, 0:NCTX], op0=ALU.mult, op1=ALU.add),
                             r=["prec"], w=[atok])
                        S.op('act', lambda qs=qs, cb=cb: ACT.activation(out=qs[:, NT:NT + NCTX], in_=acc[:, 0:NCTX], func=AF.Silu, bias=cb, scale=1.0),
                             r=[atok], w=[qtok])
                        S.dma('sp', kT_d[fc - 8], qs[:, :], r=[qtok], w=[("kT_d", fc - 8)])
                        for grp in range(3):
                            c0 = grp * 8
                            ncg = min(8, 18 - c0)
                            pbank = 4 + (grp % 2)
                            psb = ps[pbank][:, :].bitcast(BF16)
                            for ci in range(ncg):
                                S.op('pe', lambda ci=ci, c0=c0, psb=psb, qs=qs: PE.transpose(
                                    out=psb[:, ci * 128:(ci + 1) * 128], in_=qs[:, (c0 + ci) * 128:(c0 + ci + 1) * 128], identity=ident_b[:]),
                                    r=[qtok, "ident_b"], w=[P(pbank)], sig=(ci == ncg - 1))
                            S.op('dve', lambda c0=c0, ncg=ncg, psb=psb: DVE.tensor_copy(
                                out=ktk[:, c0:c0 + ncg, :], in_=psb[:, 0:ncg * 128].rearrange("p (c f) -> p c f", f=128)),
                                r=[P(pbank)], w=["ktk"])
                        S.dma('sp', ktok_d.rearrange("c p f -> p c f")[:, :, (fc - 8) * 128:(fc - 7) * 128], ktk[:], r=["ktk"], w=[("ktok_d", fc - 8)])

            def hn_chunk(c, kc):
                if c < 16:
                    return hnT[:, kc, 1 + 128 * c:1 + 128 * (c + 1)]
                return hncT[:, kc, (c - 16) * 128:(c - 15) * 128]

            def hn_tok(c):
                return HN_MAIN[c // 4] if c < 16 else HN_CTX[0]

            def bform(c0, ncols, nch, epi):
                slot = load_blk(c0, ncols)
                for c in range(nch):
                    pb = c % 6
                    for kc in range(8):
                        S.op('pe', lambda kc=kc, c=c, pb=pb, slot=slot: PE.matmul(
                            ps[pb][:, 0:ncols], lhsT=hn_chunk(c, kc), rhs=wblk[slot][:, kc, 0:ncols], start=(kc == 0), stop=(kc == 7)),
                            r=[f"wblk{slot}", hn_tok(c)], w=[P(pb)], sig=(kc == 7))
                    epi(c, pb)

            vctr = [0]
            for half in range(2):
                def epi_v(c, pb, half=half):
                    s3 = vctr[0] % 3
                    vctr[0] += 1
                    S.op('act', lambda: ACT.copy(out=vstg[s3][:], in_=ps[pb][:, :]), r=[P(pb)], w=[f"vstg{s3}"])
                    S.dma('sp', v_d[c][:, half * 512:(half + 1) * 512], vstg[s3][:], r=[f"vstg{s3}"], w=[("v_d", c, half)])
                bform(2048 + half * 512, 512, 18, epi_v)

            def epi_g(c, pb):
                S.op('dve', lambda: DVE.tensor_tensor(out=gates_all[:, c, :], in0=ps[pb][:, 0:16], in1=bg_bc[:], op=ALU.add),
                     r=[P(pb), "bg_bc"], w=[("gates", c)])
            bform(3072, 16, 18, epi_g)

            vsctr = [0]
            for half in range(2):
                def epi_vs(c, pb, half=half):
                    s2 = vsctr[0] % 3
                    vsctr[0] += 1
                    gpipe.push(ps[pb][:, :], ut[s2][:], vsstg[s2][:], [P(pb)], f"ut{s2}", f"vsstg{s2}",
                               after=lambda c=c, half=half, s2=s2: S.dma('sp', vs_d[c][:, half * 512:(half + 1) * 512], vsstg[s2][:], r=[f"vsstg{s2}"], w=[("vs_d", c, half)]))
                bform(5136 + half * 512, 512, 16, epi_vs)
            gpipe.flush()

            def aform(col0, kind, dst_d):
                for blk in range(2):
                    slot = load_blk(col0 + blk * 512, 512)
                    for j in range(4):
                        fc = blk * 4 + j
                        sg = stg[fc % 2]
                        stok = f"stg{fc % 2}"
                        for ti in range(4):
                            pb = (fc * 4 + ti) % 8
                            for kc in range(8):
                                S.op('pe', lambda kc=kc, j=j, pb=pb, ti=ti, slot=slot: PE.matmul(
                                    ps[pb][:, :], lhsT=wblk[slot][:, kc, j * 128:(j + 1) * 128], rhs=hnT[:, kc, 1 + 512 * ti:1 + 512 * (ti + 1)],
                                    start=(kc == 0), stop=(kc == 7)), r=[f"wblk{slot}", HN_MAIN[ti]], w=[P(pb)], sig=(kc == 7))
                            dst = sg[:, 512 * ti:512 * (ti + 1)]
                            if kind == 'sig':
                                S.op('act', lambda pb=pb, dst=dst: ACT.activation(out=dst, in_=ps[pb][:, :], func=AF.Sigmoid), r=[P(pb)], w=[(stok, ti)])
                            elif kind == 'sigg':
                                t1 = ut[ti % 3]
                                S.op('act', lambda pb=pb, t1=t1: ACT.activation(out=t1[:], in_=ps[pb][:, :], func=AF.Sigmoid), r=[P(pb)], w=[f"ut{ti % 3}"])
                                S.op('pool', lambda t1=t1, dst=dst, fc=fc: POOL.tensor_scalar(out=dst, in0=t1[:], scalar1=vecT[:, R_GH + fc:R_GH + fc + 1], scalar2=0.0,
                                                                                               op0=ALU.mult, op1=ALU.add), r=[f"ut{ti % 3}"], w=[(stok, ti)])
                            else:
                                gpipe.push(ps[pb][:, :], ut[ti % 3][:], dst, [P(pb)], f"ut{ti % 3}", (stok, ti))
                        if kind == 'gelu':
                            gpipe.flush()
                        S.dma('sp', dst_d[fc], sg[:], r=[(stok, i) for i in range(4)], w=[(dst_d.tensor.name, fc)])

            aform(3088, 'sigg', og_d)
            aform(4112, 'gelu', gu_d)
            aform(6160, 'sig', ga_d)
            aform(7184, 'sig', gb_d)
            S.barrier()

        Sin = sb(mx, "Sin", [128, 8, 514], F32)
        with ExitStack() as st:
            lfw = sb(st, "lfw", [128, 18, 8], F32)
            dif = sb(st, "dif", [128, 18, 8], F32)
            S.op('act', lambda: ACT.activation(out=lfw[:, :, 0:4], in_=gates_all[:, :, 4:8], func=AF.Exp, scale=-1.0), w=["lfw"], small=True)
            S.op('act', lambda: ACT.activation(out=lfw[:, :, 4:8], in_=gates_all[:, :, 12:16], func=AF.Exp, scale=-1.0), w=["lfw"], small=True)
            S.op('act', lambda: ACT.activation(out=lfw[:], in_=lfw[:], func=AF.Ln, bias=one_t[:, 0:1], scale=1.0), w=["lfw"], small=True)
            S.op('dve', lambda: DVE.tensor_scalar(out=lfw[:], in0=lfw[:], scalar1=-1.0, scalar2=None, op0=ALU.mult), r=["lfw"], w=["lfw"], small=True)
            S.op('pe', lambda: PE.matmul(ps[0][:, 0:72], lhsT=triL[:], rhs=lfw[:, :, 0:4], start=True, stop=True), r=["lfw"], w=[P(0)], sig=False)
            S.op('pe', lambda: PE.matmul(ps[0][:, 72:144], lhsT=triU[:], rhs=lfw[:, :, 4:8], start=True, stop=True), r=["lfw"], w=[P(0)], sig=False)
            S.op('pe', lambda: PE.matmul(ps[1][:, 0:144], lhsT=ones_f[:], rhs=lfw[:], start=True, stop=True), r=["lfw"], w=[P(1)], sig=True)
            S.op('dve', lambda: DVE.tensor_copy(out=cs_b[:], in_=ps[0][:, 0:144].rearrange("p (d c h) -> p d c h", d=2, c=18)), r=[P(0)], w=["cs_b"], small=True)
            S.op('dve', lambda: DVE.tensor_copy(out=cs_g[:], in_=ps[1][:, 0:144].rearrange("p (c h) -> p c h", c=18)), r=[P(1)], w=["cs_g"], small=True)
            S.op('act', lambda: ACT.activation(out=sc_all[:, :, 0:4], in_=cs_b[:, 0, :, :], func=AF.Exp), r=["cs_b"], w=["sc_all"], small=True)
            S.op('act', lambda: ACT.activation(out=sc_all[:, :, 4:8], in_=cs_b[:, 1, :, :], func=AF.Exp), r=["cs_b"], w=["sc_all"], small=True)
            S.op('dve', lambda: DVE.tensor_tensor(out=dif[:, :, 0:4], in0=gates_all[:, :, 0:4], in1=cs_b[:, 0, :, :], op=ALU.subtract), r=["cs_b"], w=["dif"], small=True)
            S.op('dve', lambda: DVE.tensor_tensor(out=dif[:, :, 4:8], in0=gates_all[:, :, 8:12], in1=cs_b[:, 1, :, :], op=ALU.subtract), r=["cs_b"], w=["dif"], small=True)
            S.op('act', lambda: ACT.activation(out=sc_all[:, :, 8:16], in_=dif[:], func=AF.Exp), r=["dif"], w=["sc_all"], small=True)
            S.op('act', lambda: ACT.activation(out=sc_all[:, :, 16:24], in_=cs_g[:], func=AF.Exp), r=["cs_g"], w=["sc_all"], small=True)
            S.op('dve', lambda: DVE.tensor_tensor(out=sc_all[:, :, 24:32], in0=sc_all[:, :, 8:16], in1=sc_all[:, :, 16:24], op=ALU.mult), r=["sc_all"], w=["sc_all"], small=True)
            S.op('dve', lambda: DVE.tensor_reduce(out=Gseg[:], in_=cap(cs_g, 0, [[1, 8], [8, 16]]), axis=mybir.AxisListType.X, op=ALU.add), r=["cs_g"], w=["Gseg"], small=True)
            S.op('dve', lambda: DVE.memset(LL[:], 0.0), w=["LL"], small=True)
            for c in range(14, -1, -1):
                S.op('dve', lambda c=c: DVE.tensor_tensor(out=LL[:, c, 0:4], in0=LL[:, c + 1, 0:4], in1=cs_g[:, c + 1, 0:4], op=ALU.add), r=["cs_g"], w=["LL"], small=True)
            for c in range(1, 16):
                S.op('dve', lambda c=c: DVE.tensor_tensor(out=LL[:, c, 4:8], in0=LL[:, c - 1, 4:8], in1=cs_g[:, c - 1, 4:8], op=ALU.add), r=["cs_g"], w=["LL"], small=True)
            S.op('dve', lambda: DVE.tensor_copy(out=LL[:, 16, 0:4], in_=cs_g[:, 17, 0:4]), w=["LL"], small=True)
            S.op('dve', lambda: DVE.tensor_copy(out=LL[:, 17, 4:8], in_=cs_g[:, 16, 4:8]), w=["LL"], small=True)
            S.op('act', lambda: ACT.activation(out=LL[:], in_=LL[:], func=AF.Exp), r=["LL"], w=["LL"], small=True)
            S.op('dve', lambda: DVE.tensor_tensor(out=psc[:], in0=LL[:], in1=sc_all[:, :, 24:32], op=ALU.mult), r=["LL", "sc_all"], w=["psc"], small=True)
            S.barrier()
        dbg_dump("gates", gates_all[:], [])
        dbg_dump("sc_all", sc_all[:], [])

        _slc = [0]

        def sweep_loads(pool, names):
            _slc[0] += 1
            return {n: [sb(pool, f"ld{_slc[0]}_{n}{i}", shp, dt) for i in range(2)] for n, (shp, dt) in names.items()}

        with ExitStack() as st:
            St = sb(st, "St", [128, 8, 514], F32)
            Sctx = sb(st, "Sctx", [128, 8, 514], F32)
            lds = sweep_loads(st, {"ktok": ([128, D], BF16), "v": ([128, D], BF16)})
            vtl = [sb(st, f"vtl{i}", [128, 4, 257], BF16) for i in range(2)]
            lctr = [0]

            def p1_pass(chunks, d, dst, dtok):
                n = len(chunks)
                for i, c in enumerate(chunks):
                    slot = lctr[0] % 2
                    lctr[0] += 1
                    kt, vv, vt = lds["ktok"][slot], lds["v"][slot], vtl[slot]
                    S.dma('sp', kt[:], ktok_d[c], w=[f"ld_ktok{slot}"])
                    S.dma('sp', vv[:], v_d[c], w=[f"ld_v{slot}"])
                    S.op('dve', lambda: DVE.tensor_tensor(out=vt[:, :, 0:256], in0=vv[:, :].rearrange("p (h v) -> p h v", h=4),
                                                          in1=cap(psc, c * 8 + 4 * d, [[1, 4], [0, 256]]), op=ALU.mult),
                         r=[f"ld_v{slot}", "psc"], w=[f"vtl{slot}"])
                    S.op('dve', lambda: DVE.tensor_copy(out=vt[:, :, 256], in_=psc[:, c, 4 * d:4 * d + 4]), w=[f"vtl{slot}"], small=True)
                    for h in range(4):
                        for kc in range(2):
                            pb = h * 2 + kc
                            S.op('pe', lambda h=h, kc=kc, pb=pb: PE.matmul(ps[pb][:, 0:257], lhsT=kt[:, h * 256 + kc * 128:h * 256 + (kc + 1) * 128], rhs=vt[:, h, :],
                                                                          start=(i == 0), stop=(i == n - 1)), r=[f"ld_ktok{slot}", f"vtl{slot}"], w=[P(pb)],
                                 sig=(i == n - 1 or (h == 3 and kc == 1)))
                for h in range(4):
                    for kc in range(2):
                        pb = h * 2 + kc
                        eng = 'act' if (pb % 2 == 0) else 'dve'
                        if eng == 'act':
                            S.op('act', lambda h=h, kc=kc, pb=pb: ACT.copy(out=dst[:, d * 4 + h, kc * 257:(kc + 1) * 257], in_=ps[pb][:, 0:257]), r=[P(pb)], w=[(dtok, d * 4 + h, kc)])
                        else:
                            S.op('dve', lambda h=h, kc=kc, pb=pb: DVE.tensor_copy(out=dst[:, d * 4 + h, kc * 257:(kc + 1) * 257], in_=ps[pb][:, 0:257]), r=[P(pb)], w=[(dtok, d * 4 + h, kc)])

            p1_pass([16, 17], 0, Sctx, "Sctx")
            p1_pass([17, 16], 1, Sctx, "Sctx")
            p1_pass(list(range(16)), 0, St, "St")
            p1_pass(list(range(16)), 1, St, "St")
            S.barrier()
            dbg_dump("Sctx", Sctx[:], ["Sctx"])
            dbg_dump("Sloc", St[:], [("St", i) for i in range(8)])
            xout_v = [xo.rearrange("(r p) w -> p r w", p=128) for xo in xout_l]
            for i in range(4):
                S.dma('sp', xin_l[i][:, 0:1028], St[:, 2 * i:2 * i + 2, :].rearrange("p a b -> p (a b)"), r=[("St", 2 * i), ("St", 2 * i + 1)], w=[("xin", i)])
                S.dma('sp', xin_l[i][:, 1028:1030], Gseg[:, 2 * i:2 * i + 2], r=[], w=[("xin2", i)])
            for i in range(4):
                if KSTOP == 'p1':
                    S.dma('sp', xout_l[i][0:128, :], xin_l[i], r=[("xin", i), ("xin2", i)], w=[("xout", i)])
                else:
                    S.custom('pool', lambda sem, i=i: POOL.collective_compute("AllGather", ALU.bypass, replica_groups=[[0, 1, 2, 3], [4, 5, 6, 7]],
                                                                              ins=[xin_l[i].opt()], outs=[xout_l[i].opt()]).then_inc(sem, 1),
                             1, r=[("xin", i), ("xin2", i)], w=[("xout", i)])
            with ExitStack() as sg:
                gvt = sb(sg, "sgu_gv", [128, 4, D], F32)
                vnt = sb(sg, "sgu_vn", [128, 4, D], BF16)
                gut = sb(sg, "sgu_gu", [128, 8, 512], BF16)
                ybt = sb(sg, "sgu_yb", [128, 8, 512], BF16)
                gsgu_bc = sb(sg, "gsgu_bc", [128, D], F32)
                st6s = sb(sg, "sgu_st6", [128, 4, 2, 6], F32)
                mvs = sb(sg, "sgu_mv", [128, 4, 2], F32)
                msq = sb(sg, "sgu_msq", [128, 2, 4], F32)
                S.dma('sp', gsgu_bc[:], bass.AP(g_sgu.tensor, 0, [[0, 128], [1, D]]), w=["gsgu_bc"])
                for g4 in range(4):
                    S.dma('sp', gvt[:], vs_d.rearrange("c p f -> p c f")[:, 4 * g4:4 * g4 + 4, :], w=["sgu_gv"])
                    S.dma('sp', gut[:], gu_d.rearrange("f p t -> p f t")[:, :, 512 * g4:512 * (g4 + 1)], w=["sgu_gu"])
                    for cq in range(4):
                        for i2 in range(2):
                            S.op('dve', lambda cq=cq, i2=i2: DVE.bn_stats(out=st6s[:, cq, i2, :], in_=gvt[:, cq, i2 * 512:(i2 + 1) * 512]),
                                 r=["sgu_gv"], w=[("sgu_st6", cq)], small=True)
                    for cq in range(4):
                        S.op('dve', lambda cq=cq: DVE.bn_aggr(out=mvs[:, cq, :], in_=st6s[:, cq, :, :].rearrange("p a b -> p (a b)")),
                             r=[("sgu_st6", cq)], w=["sgu_mv"], small=True)
                    S.op('dve', lambda: DVE.tensor_tensor(out=msq[:, 0, :], in0=mvs[:, :, 0], in1=mvs[:, :, 0], op=ALU.mult), r=["sgu_mv"], w=["sgu_msq"], small=True)
                    S.op('dve', lambda: DVE.tensor_tensor(out=msq[:, 0, :], in0=msq[:, 0, :], in1=mvs[:, :, 1], op=ALU.add), w=["sgu_msq"], small=True)
                    S.op('act', lambda: ACT.activation(out=msq[:, 1, :], in_=msq[:, 0, :], func=AF.Ln, bias=eps_t[:, 0:1], scale=1.0), r=["sgu_msq"], w=["sgu_rs"], small=True)
                    S.op('act', lambda: ACT.activation(out=msq[:, 1, :], in_=msq[:, 1, :], func=AF.Exp, scale=-0.5), w=["sgu_rs"], small=True)
                    for cq in range(4):
                        S.op('dve', lambda cq=cq: DVE.scalar_tensor_tensor(out=vnt[:, cq, :], in0=gvt[:, cq, :], scalar=msq[:, 1, cq:cq + 1], in1=gsgu_bc[:],
                                                                           op0=ALU.mult, op1=ALU.mult), r=["sgu_rs", "sgu_gv", "gsgu_bc"], w=[("sgu_vn", cq)], small=True)
                    for ccf in range(8):
                        gq = ccf // 2
                        for cq in range(4):
                            S.op('pe', lambda ccf=ccf, cq=cq, gq=gq: PE.matmul(ps[ccf][:, cq * 128:(cq + 1) * 128], lhsT=vnt[:, cq, ccf * 128:(ccf + 1) * 128], rhs=wsT[:, gq, :],
                                                                               start=True, stop=False), r=[("sgu_vn", cq), "wsT"], w=[P(ccf)], sig=False)
                            S.op('pe', lambda ccf=ccf, cq=cq, gq=gq: PE.matmul(ps[ccf][:, cq * 128:(cq + 1) * 128], lhsT=ones_b[0:1, :], rhs=bs_row[0:1, gq * 128:(gq + 1) * 128],
                                                                               start=False, stop=True), w=[P(ccf)], sig=(cq == 3))
                        S.op('dve', lambda ccf=ccf: DVE.tensor_tensor(out=ybt[:, ccf, :], in0=ps[ccf][:, :], in1=gut[:, ccf, :], op=ALU.mult),
                             r=[P(ccf), "sgu_gu"], w=[("sgu_yb", ccf)])
                    S.dma('sp', gu_d.rearrange("f p t -> p f t")[:, :, 512 * g4:512 * (g4 + 1)], ybt[:], r=[("sgu_yb", i) for i in range(8)] + ["sgu_gu"], w=[("gu_d", g4)])
                S.barrier()
            Gall = sb(st, "Gall", [128, 4, 8], F32)
            Ug = [sb(st, f"Ug{i}", [128, 4, 514], F32) for i in range(2)]
            Tt = sb(st, "Tt", [128, 514], F32)
            for i in range(4):
                S.dma('sp', Gall[:, :, 2 * i:2 * i + 2], xout_v[i][:, :, 1028:1030], r=[("xout", i)], w=[("Gall", i)])
            S.op('act', lambda: ACT.activation(out=Gall[:], in_=Gall[:], func=AF.Exp), r=[("Gall", i) for i in range(4)], w=["Gall"], small=True)
            for hd in range(8):
                d = hd // 4
                U = Ug[hd % 2]
                utok = f"Ug{hd % 2}"
                S.dma('sp', U[:], xout_v[hd // 2][:, :, (hd % 2) * 514:(hd % 2 + 1) * 514], r=[("xout", hd // 2)], w=[utok])
                S.op('dve', lambda hd=hd: DVE.tensor_copy(out=Tt[:], in_=Sctx[:, hd, :]), r=["Sctx"], w=["Tt"])
                first = 0 if d == 0 else 3
                S.op('dve', lambda hd=hd, first=first: DVE.tensor_scalar(out=Sin[:, hd, :], in0=Tt[:], scalar1=metat[:, 1 + first:2 + first], scalar2=None, op0=ALU.mult),
                     r=["meta"], w=[("Sin", hd)])
                order = [0, 1, 2] if d == 0 else [3, 2, 1]
                for i in order:
                    tgt = i + 1 if d == 0 else i - 1
                    S.op('dve', lambda i=i, hd=hd, U=U: DVE.scalar_tensor_tensor(out=Tt[:], in0=Tt[:], scalar=Gall[:, i, hd:hd + 1], in1=U[:, i, :],
                                                                                  op0=ALU.mult, op1=ALU.add), r=[utok, "Gall"], w=["Tt"])
                    S.op('dve', lambda tgt=tgt, hd=hd: DVE.scalar_tensor_tensor(out=Sin[:, hd, :], in0=Tt[:], scalar=metat[:, 1 + tgt:2 + tgt], in1=Sin[:, hd, :],
                                                                                op0=ALU.mult, op1=ALU.add), w=[("Sin", hd)])
            dbg_dump("Sin", Sin[:], [("Sin", i) for i in range(8)])
            S.barrier()

        def mlstm_chunk(c, d, bufs, S16, emit_all, hook=None):
            q_t, kT_t, kt_t, v_t = bufs["q"], bufs["kT"], bufs["ktok"], bufs["v"]
            vt, vte, PM4, sm = bufs["vt"], bufs["vte"], bufs["PM4"], bufs["sm"]
            ltoks = bufs["ltoks"]
            mask = mL16 if d == 0 else mU16
            rs0, vs0, eg0, vse0 = 0 + 4 * d, 8 + 4 * d, 16 + 4 * d, 24 + 4 * d
            S.op('dve', lambda: DVE.tensor_tensor(out=vt[:, :, 0:256], in0=v_t[:, :].rearrange("p (h v) -> p h v", h=4),
                                                  in1=cap(sc_all, c * 32 + vs0, [[1, 4], [0, 256]]), op=ALU.mult), r=[ltoks["v"]], w=["vt"])
            S.op('dve', lambda: DVE.tensor_copy(out=vt[:, :, 256], in_=sc_all[:, c, vs0:vs0 + 4]), w=["vt"], small=True)
            S.op('pool', lambda: POOL.tensor_tensor(out=vte[:, :, 0:256], in0=v_t[:, :].rearrange("p (h v) -> p h v", h=4),
                                                    in1=cap(sc_all, c * 32 + vse0, [[1, 4], [0, 256]]), op=ALU.mult), r=[ltoks["v"]], w=["vte"])
            S.op('pool', lambda: POOL.tensor_copy(out=vte[:, :, 256], in_=sc_all[:, c, vse0:vse0 + 4]), w=["vte"], small=True)
            for h in range(4):
                for kc in range(2):
                    S.op('pe', lambda kc=kc, h=h: PE.matmul(ps[0][:, h * 128:(h + 1) * 128], lhsT=kT_t[:, h * 2 + kc, :], rhs=q_t[:, h * 2 + kc, :],
                                                           start=(kc == 0), stop=(kc == 1)), r=[ltoks["kT"], ltoks["q"]], w=[P(0)], sig=(h == 3 and kc == 1))
            S.op('dve', lambda: DVE.tensor_tensor(out=PM4[:], in0=ps[0][:, :].rearrange("p (h t) -> p h t", h=4),
                                                  in1=cap(mask, 0, [[0, 4], [1, 128]]), op=ALU.mult), r=[P(0)], w=["PM4"])
            if hook:
                hook('s')
            for h in range(4):
                S.op('pe', lambda h=h: PE.matmul(ps[1 + h][:, 0:257], lhsT=PM4[:, h, :], rhs=vt[:, h, :], start=True, stop=False),
                     r=["PM4", "vt"], w=[P(1 + h)], sig=False)
                for kc in range(2):
                    S.op('pe', lambda kc=kc, h=h: PE.matmul(ps[1 + h][:, 0:257], lhsT=q_t[:, h * 2 + kc, :], rhs=S16[:, h, kc, :], start=False, stop=(kc == 1)),
                         r=[ltoks["q"], ("S16", h)], w=[P(1 + h)], sig=(kc == 1))
            if hook:
                hook('o')
            den4 = psall[:, 1:5, 256]
            rs4 = sc_all[:, c, rs0:rs0 + 4]
            PO = [P(1 + h) for h in range(4)]
            S.op('dve', lambda: DVE.tensor_tensor(out=sm[:, 0, :], in0=den4, in1=rs4, op=ALU.mult), r=PO, w=["sm"], small=True)
            S.op('dve', lambda: DVE.tensor_scalar(out=sm[:, 1, :], in0=sm[:, 0, :], scalar1=-1.0, scalar2=1.0, op0=ALU.mult, op1=ALU.max), w=["sm"], small=True)
            S.op('dve', lambda: DVE.tensor_scalar(out=sm[:, 2, :], in0=sm[:, 0, :], scalar1=1.0, scalar2=None, op0=ALU.max), w=["sm"], small=True)
            S.op('dve', lambda: DVE.tensor_tensor(out=sm[:, 2, :], in0=sm[:, 2, :], in1=sm[:, 1, :], op=ALU.max), w=["sm"], small=True)
            S.op('dve', lambda: DVE.reciprocal(out=sm[:, 3, :], in_=sm[:, 2, :]), w=["sm"], small=True)
            S.op('dve', lambda: DVE.tensor_tensor(out=sm[:, 4, :], in0=sm[:, 3, :], in1=rs4, op=ALU.mult), w=["sm"], small=True)
            emit_all(sm, 4)
            for h in range(4):
                hd = d * 4 + h
                for kc in range(2):
                    pU = 5 + kc
                    S.op('pe', lambda kc=kc, h=h, pU=pU: PE.matmul(ps[pU][:, 0:257], lhsT=kt_t[:, h * 256 + kc * 128:h * 256 + (kc + 1) * 128], rhs=vte[:, h, :],
                                                                  start=True, stop=True), r=[ltoks["ktok"], "vte"], w=[P(pU)])
                    S.op('dve', lambda kc=kc, hd=hd, pU=pU, h=h: DVE.scalar_tensor_tensor(
                        out=Sin[:, hd, kc * 257:(kc + 1) * 257], in0=Sin[:, hd, kc * 257:(kc + 1) * 257], scalar=sc_all[:, c, eg0 + h:eg0 + h + 1],
                        in1=ps[pU][:, 0:257], op0=ALU.mult, op1=ALU.add), r=[P(pU)], w=[("Sin", hd)])
                S.op('act', lambda h=h, hd=hd: ACT.activation(out=S16[:, h, :, :], in_=Sin[:, hd, :].rearrange("p (k v) -> p k v", k=2), func=AF.Copy, scale=0.0625),
                     r=[("Sin", hd)], w=[("S16", h)])
            if hook:
                hook('u')

        SINGLE = {"hb", "vs"}

        def ltok(name, slot):
            return f"ld_{name}" if name in SINGLE else f"ld_{name}{slot}"

        def issue_loads(lds, slot, c, extra=()):
            S.dma('sp', lds["q"][slot][:], qT_d.rearrange("f p t -> p f t")[:, :, c * 128:(c + 1) * 128], w=[f"ld_q{slot}"])
            S.dma('sp', lds["kT"][slot][:], kT_d.rearrange("f p t -> p f t")[:, :, c * 128:(c + 1) * 128], w=[f"ld_kT{slot}"])
            S.dma('sp', lds["ktok"][slot][:], ktok_d[c], w=[f"ld_ktok{slot}"])
            S.dma('sp', lds["v"][slot][:], v_d[c], w=[f"ld_v{slot}"])
            for (name, src) in extra:
                S.dma('sp', lds[name][slot][:], src, w=[ltok(name, slot)])

        def mk_bufs(lds, slot, common):
            b = dict(common)
            for n in lds:
                b[n] = lds[n][slot]
            b["ltoks"] = {n: ltok(n, slot) for n in lds}
            return b

        with ExitStack() as st:
          if KSTOP not in ('xchg', 'p1'):
                lds = sweep_loads(st, {"q": ([128, 8, 128], BF16), "kT": ([128, 8, 128], BF16), "ktok": ([128, D], BF16), "v": ([128, D], BF16)})
                common = {"vt": sb(st, "vt", [128, 4, 257], BF16), "vte": sb(st, "vte", [128, 4, 257], BF16),
                          "PM4": sb(st, "PM4", [128, 4, 128], BF16), "sm": sb(st, "sm", [128, 5, 4], F32)}
                S16 = sb(st, "S16", [128, 4, 2, 257], BF16)
                hbt = [sb(st, f"hbt{i}", [128, D], F32) for i in range(2)]
                for h in range(4):
                    S.op('act', lambda h=h: ACT.activation(out=S16[:, h, :, :], in_=Sin[:, 4 + h, :].rearrange("p (k v) -> p k v", k=2), func=AF.Copy, scale=0.0625),
                         w=[("S16", h)])
                issue_loads(lds, 0, 15)
                for it, c in enumerate(range(15, -1, -1)):
                    slot = it % 2
                    if c > 0:
                        issue_loads(lds, 1 - slot, c - 1)
                    hb = hbt[slot]

                    def emit_all(sm, row, hb=hb, slot=slot):
                        for h in range(4):
                            S.op('act', lambda h=h: ACT.activation(out=hb[:, h * 256:(h + 1) * 256], in_=ps[1 + h][:, 0:256], func=AF.Identity, scale=sm[:, row, h:h + 1]),
                                 r=[P(1 + h), "sm"], w=[(f"hbt{slot}", h)])
                    mlstm_chunk(c, 1, mk_bufs(lds, slot, common), S16, emit_all)
                    S.dma('sp', hb_d[c], hb[:], r=[(f"hbt{slot}", h) for h in range(4)], w=[("hb_d", c)])
                dbg_dump("Sfin", Sin[:], [("Sin", i) for i in range(8)])
                S.barrier()

        with ExitStack() as st:
          if KSTOP not in ('xchg', 'bwd', 'p1'):
                lds = sweep_loads(st, {"q": ([128, 8, 128], BF16), "kT": ([128, 8, 128], BF16), "ktok": ([128, D], BF16), "v": ([128, D], BF16),
                                       "og": ([128, 8, 128], BF16)})
                hb1 = sb(st, "hb1", [128, D], F32)
                lds["hb"] = [hb1, hb1]
                common = {"vt": sb(st, "vtc", [128, 4, 257], BF16), "vte": sb(st, "vtec", [128, 4, 257], BF16),
                          "PM4": sb(st, "PM4c", [128, 4, 128], BF16), "sm": sb(st, "smc", [128, 5, 4], F32)}
                S16 = sb(st, "S16c", [128, 4, 2, 257], BF16)
                hm_t = sb(st, "hm_t", [128, D], F32)
                hh = sb(st, "hh", [128, D], BF16)
                st6 = sb(st, "st6", [128, 4, 6], F32)
                mv = sb(st, "mv", [128, 4, 2], F32)
                lnr = sb(st, "lnr", [128, 4], F32)
                t1 = cap(hcT, 0, [[1, D]])
                gv = cap(hcT, D, [[1, D]])
                ssq = sb(st, "ssq", [128, 2], F32)
                st6b = sb(st, "st6b", [128, 2, 6], F32)
                mvb = sb(st, "mvb", [128, 4], F32)
                vn = sb(st, "vn", [128, D], BF16)
                yaT = sb(st, "yaT", [128, 8, 512], BF16)
                ybT = sb(st, "ybT", [128, 8, 512], BF16)
                mixT = sb(st, "mixT", [128, 8, 512], BF16)
                tA = [sb(st, "tA0", [128, 512], F32)] * 2
                tB = [sb(st, "tB0", [128, 512], F32)] * 2
                sga = [sb(st, f"sga{i}", [128, 512], BF16) for i in range(2)]
                sgb = [sb(st, f"sgb{i}", [128, 512], BF16) for i in range(2)]
                wbr = [sb(st, f"wbr{i}", [128, 8, 512], BF16) for i in range(2)]

                def extra_for(c):
                    return [("og", og_d.rearrange("f p t -> p f t")[:, :, c * 128:(c + 1) * 128])]

                for h in range(4):
                    S.op('act', lambda h=h: ACT.activation(out=S16[:, h, :, :], in_=Sin[:, h, :].rearrange("p (k v) -> p k v", k=2), func=AF.Copy, scale=0.0625),
                         w=[("S16", h)])
                pending = []

                def flush_items():
                    while pending:
                        pending.pop(0)[1]()

                def c_hook(stage):
                    n_ab = sum(1 for k, _ in pending if k == 'ab')
                    if n_ab > 0:
                        n = n_ab if stage == 'u' else min(3, n_ab)
                    else:
                        n = min(1, len(pending))
                    for _ in range(n):
                        pending.pop(0)[1]()

                issue_loads(lds, 0, 0, extra_for(0))
                S.dma('sp', hb1[:], hb_d[0], w=["ld_hb"])
                for c in range(16):
                    slot = c % 2
                    cc = c % 4
                    tile = c // 4
                    if c < 15:
                        issue_loads(lds, 1 - slot, c + 1, extra_for(c + 1))
                    b = mk_bufs(lds, slot, common)
                    hbl, ogl = b["hb"], b["og"]

                    def emit_all(sm, row, hbl=hbl):
                        for h in range(4):
                            S.op('dve', lambda h=h: DVE.scalar_tensor_tensor(out=hm_t[:, h * 256:(h + 1) * 256], in0=ps[1 + h][:, 0:256], scalar=sm[:, row, h:h + 1],
                                                                             in1=hbl[:, h * 256:(h + 1) * 256], op0=ALU.mult, op1=ALU.add),
                                 r=[P(1 + h), "sm", "ld_hb"], w=[("hm", h)], small=True)
                        for h in range(4):
                            S.op('dve', lambda h=h: DVE.bn_stats(out=st6[:, h, :], in_=hm_t[:, h * 256:(h + 1) * 256]), r=[("hm", h)], w=[("st6", h)], small=True)
                        for h in range(4):
                            S.op('dve', lambda h=h: DVE.bn_aggr(out=mv[:, h, :], in_=st6[:, h, :]), r=[("st6", h)], w=[("mv", h)], small=True)
                    mlstm_chunk(c, 0, b, S16, emit_all, hook=c_hook)
                    if c < 15:
                        S.dma('sp', hb1[:], hb_d[c + 1], w=["ld_hb"])
                    S.op('act', lambda: ACT.activation(out=lnr[:], in_=mv[:, :, 1], func=AF.Ln, bias=eps_t[:, 0:1], scale=1.0), r=[("mv", h) for h in range(4)], w=["lnr"], small=True)
                    S.op('act', lambda: ACT.activation(out=lnr[:], in_=lnr[:], func=AF.Exp, scale=-0.5), w=["lnr"], small=True)
                    for h in range(4):
                        S.op('dve', lambda h=h: DVE.tensor_scalar(out=hh[:, h * 256:(h + 1) * 256], in0=hm_t[:, h * 256:(h + 1) * 256], scalar1=mv[:, h, 0:1],
                                                                  scalar2=lnr[:, h:h + 1], op0=ALU.subtract, op1=ALU.mult), r=["lnr", ("mv", h)], w=["hh"], strict=True, small=True)
                    psb = ps[0][:, :].bitcast(BF16)
                    for fc in range(8):
                        S.op('pe', lambda fc=fc: PE.transpose(out=psb[:, fc * 128:(fc + 1) * 128], in_=hh[:, fc * 128:(fc + 1) * 128], identity=ident_b[:]),
                             r=["hh"], w=[P(0)], sig=(fc == 7))
                    S.op('dve', lambda cc=cc, ogl=ogl: DVE.tensor_tensor(out=yaT[:, :, cc * 128:(cc + 1) * 128], in0=psb[:, :].rearrange("p (f t) -> p f t", f=8),
                                                                          in1=ogl[:], op=ALU.mult), r=[P(0), f"ld_og{slot}"], w=[("yaT", cc)])
                    if cc == 0:
                        S.dma('sp', ybT[:], gu_d.rearrange("f p t -> p f t")[:, :, 512 * tile:512 * (tile + 1)], w=["ybT"])
                    if c in (0, 4):
                        dbg_dump(f"hm{c}", hm_t[:], [("hm", h) for h in range(4)])
                        dbg_dump(f"hh{c}", hh[:], ["hh"])
                    if cc < 3:
                        continue
                    if tile in (0, 1):
                        dbg_dump(f"ya{tile}", yaT[:], [("yaT", i) for i in range(4)])
                        dbg_dump(f"yb{tile}", ybT[:], ["ybT"])
                    flush_items()
                    t0 = 512 * tile
                    YA = [("yaT", i) for i in range(4)]
                    YB = ["ybT"]
                    MX = [("mixT", i) for i in range(8)]

                    def ab_item(dc, t0=t0, YA=YA, YB=YB):
                        blk, j = dc // 4, dc % 4
                        if j == 0:
                            wload(wbr[0][:], "wbr0", w_ba.rearrange("(kc p) n -> p kc n", p=128)[:, :, blk * 512:(blk + 1) * 512])
                            wload(wbr[1][:], "wbr1", w_bb.rearrange("(kc p) n -> p kc n", p=128)[:, :, blk * 512:(blk + 1) * 512])
                        s2 = dc % 2
                        S.dma('sp', sga[s2][:], ga_d[dc][:, t0:t0 + 512], w=[f"sga{s2}"])
                        S.dma('sp', sgb[s2][:], gb_d[dc][:, t0:t0 + 512], w=[f"sgb{s2}"])
                        pa, pb2 = 7, 0
                        for kc in range(8):
                            S.op('pe', lambda kc=kc: PE.matmul(ps[pa][:, :], lhsT=wbr[0][:, kc, j * 128:(j + 1) * 128], rhs=yaT[:, kc, :],
                                                               start=(kc == 0), stop=(kc == 7)), r=["wbr0"] + YA, w=[P(pa)], sig=(kc == 7))
                        for kc in range(8):
                            S.op('pe', lambda kc=kc: PE.matmul(ps[pb2][:, :], lhsT=wbr[1][:, kc, j * 128:(j + 1) * 128], rhs=ybT[:, kc, :],
                                                               start=(kc == 0), stop=(kc == 7)), r=["wbr1"] + YB, w=[P(pb2)], sig=(kc == 7))
                        S.op('dve', lambda: DVE.tensor_tensor(out=tA[s2][:], in0=ps[pa][:, :], in1=sga[s2][:], op=ALU.mult),
                             r=[P(pa), f"sga{s2}"], w=["tA0"])
                        S.op('dve', lambda: DVE.tensor_tensor(out=tB[s2][:], in0=ps[pb2][:, :], in1=sgb[s2][:], op=ALU.mult),
                             r=[P(pb2), f"sgb{s2}"], w=["tB0"])
                        S.op('pool', lambda: POOL.tensor_tensor(out=mixT[:, dc, :], in0=tA[s2][:], in1=tB[s2][:], op=ALU.add),
                             r=["tA0", "tB0"], w=[("mixT", dc)])

                    def out_item(dc, t0=t0, tile=tile, MX=MX):
                        blk, j = dc // 4, dc % 4
                        if j == 0:
                            wload(wbr[blk][:], f"wbr{blk}", w_o.rearrange("(kc p) n -> p kc n", p=128)[:, :, blk * 512:(blk + 1) * 512])
                        pb = 7 if dc % 2 == 0 else 0
                        for kc in range(8):
                            S.op('pe', lambda kc=kc: PE.matmul(ps[pb][:, :], lhsT=wbr[blk][:, kc, j * 128:(j + 1) * 128], rhs=mixT[:, kc, :],
                                                               start=(kc == 0), stop=(kc == 7)), r=[f"wbr{blk}"] + MX, w=[P(pb)], sig=(kc == 7))
                        S.op('dve', lambda: DVE.scalar_tensor_tensor(
                            out=hT[:, dc, 1 + t0:1 + t0 + 512], in0=ps[pb][:, :], scalar=prm[:, P_G5, dc:dc + 1], in1=hT[:, dc, 1 + t0:1 + t0 + 512],
                            op0=ALU.mult, op1=ALU.add), r=[P(pb)], w=[("hT2", tile, dc)])

                    for dc in range(8):
                        pending.append(('ab', lambda dc=dc, f=ab_item: f(dc)))
                    for dc in range(8):
                        pending.append(('out', lambda dc=dc, f=out_item: f(dc)))
                    if KITEMS == 0:
                        flush_items()
                flush_items()
                S.barrier()
        for nm, src in [("qT", qT_d), ("kT", kT_d), ("ktok", ktok_d), ("v", v_d), ("hb", hb_d), ("og", og_d), ("gu", gu_d), ("ga", ga_d), ("vs", vs_d)]:
            if nm in dbg:
                S.dma('sp', dbg[nm], src)
        if "h2" in dbg:
            S.dma('sp', dbg["h2"].rearrange("dc p t -> p dc t"), hT[:, :, :])
        S.barrier()
        mx.close()
        S.barrier()

        if KSTOP == "all":
            ffn(w_f2i, w_f2o, main_tiles, (P_GS2, P_SH2, P_GT2), (P_GS1C, P_SH1C, P_GT1C), "f2")

        if "h1" in dbg:
            S.dma('sp', dbg["h1"].rearrange("dc p t -> p dc t"), hT[:, :, :], r=[])
            S.dma('sp', dbg["hc1"].rearrange("dc p t -> p dc t"), hcT[:, :, :], r=[])
            S.barrier()

        def final_out():
            with ExitStack() as st:
                sq = sb(st, "fsq", [128, 8, 512], BF16)
                rstd = sb(st, "frstd", [128, 512], F32)
                yT = [sb(st, f"fy{i}", [128, 8, 512], F32) for i in range(2)]
                ot = [sb(st, f"fot{i}", [128, D], F32) for i in range(2)]
                for t in range(4):
                    o0 = 1 + 512 * t
                    y = yT[t % 2]
                    ytok = f"fy{t % 2}"
                    for dc in range(8):
                        S.op('act', lambda dc=dc: ACT.activation(out=sq[:, dc, :], in_=hT[:, dc, o0:o0 + 512], func=AF.Square), w=["fsq"])
                    for dc in range(8):
                        S.op('pe', lambda dc=dc: PE.matmul(ps[7][:, :], lhsT=ones_b[:], rhs=sq[:, dc, :], start=(dc == 0), stop=(dc == 7)),
                             r=["fsq"], w=[P(7)], sig=(dc == 7))
                    S.op('act', lambda: ACT.activation(out=rstd[:], in_=ps[7][:, :], func=AF.Ln, scale=1.0 / D, bias=eps_t[:, 0:1]), r=[P(7)], w=["frstd"])
                    S.op('act', lambda: ACT.activation(out=rstd[:], in_=rstd[:], func=AF.Exp, scale=-0.5), w=["frstd"])
                    for dc in range(8):
                        S.op('dve', lambda dc=dc, y=y: DVE.scalar_tensor_tensor(out=y[:, dc, :], in0=hT[:, dc, o0:o0 + 512], scalar=vecT[:, R_GFIN + dc:R_GFIN + dc + 1],
                                                                              in1=rstd[:], op0=ALU.mult, op1=ALU.mult),
                             r=["frstd"], w=[(ytok, dc)])
                    for cc in range(4):
                        c = 4 * t + cc
                        o = ot[c % 2]
                        for half in range(2):
                            pb = 2 + half + 2 * (c % 2)
                            for k4 in range(4):
                                dc = half * 4 + k4
                                S.op('pe', lambda dc=dc, k4=k4, pb=pb, y=y, cc=cc: PE.transpose(out=ps[pb][:, k4 * 128:(k4 + 1) * 128], in_=y[:, dc, cc * 128:(cc + 1) * 128],
                                                                                          identity=ident_f[:]),
                                     r=[(ytok, dc)], w=[P(pb)], sig=(k4 == 3))
                            if half == 0:
                                S.op('act', lambda pb=pb, o=o: ACT.copy(out=o[:, 0:512], in_=ps[pb][:, :]), r=[P(pb)], w=[f"fot{c % 2}a"])
                            else:
                                S.op('dve', lambda pb=pb, o=o: DVE.tensor_copy(out=o[:, 512:1024], in_=ps[pb][:, :]), r=[P(pb)], w=[f"fot{c % 2}b"])
                        S.dma('sp', out[128 * c:128 * (c + 1), :], o[:], r=[f"fot{c % 2}a", f"fot{c % 2}b"], w=[])
            S.barrier()

        final_out()
    return nc


def _host_inputs(inp):
    x = np.ascontiguousarray(inp["x"], dtype=np.float32)
    f32 = np.float32
    vec_common = [
        inp["b_ada"][0].reshape(72, 128), inp["g_ffn1"][0].reshape(8, 128), inp["g_mix"][0].reshape(8, 128),
        inp["conv_qk_w"][0].reshape(48, 128), inp["conv_qk_b"][0].reshape(16, 128), inp["g_head"][0].reshape(8, 128),
        inp["g_ffn2"][0].reshape(8, 128), inp["g_final"].reshape(8, 128)]
    quarter = D // 4
    fr = np.exp(-math.log(10000.0) * np.arange(quarter, dtype=f32) / quarter).astype(f32)
    freq = np.ascontiguousarray(fr.reshape(2, 128).T)
    consts = np.zeros((128, 3, 128), f32)
    consts[:, 0, :] = np.eye(128, dtype=f32)
    ii = np.arange(128)
    consts[:, 1, :] = (ii[:, None] <= ii[None, :]).astype(f32)
    consts[:, 2, :] = (ii[:, None] >= ii[None, :]).astype(f32)
    shared = dict(
        consts=consts, freq=freq, g_sgu=np.ascontiguousarray(inp["g_sgu"][0]), b_gates=np.ascontiguousarray(inp["b_gates"][0]),
        b_s=np.ascontiguousarray(inp["b_s"][0].reshape(512)), w_s=np.ascontiguousarray(inp["w_s"][0]),
        w_ffn1_in=np.ascontiguousarray(inp["w_ffn1_in"][0]),
        w_ffn1_out=np.ascontiguousarray(inp["w_ffn1_out"][0]), w_in=np.ascontiguousarray(inp["w_in"][0]),
        w_branch_a=np.ascontiguousarray(inp["w_branch_a"][0]), w_branch_b=np.ascontiguousarray(inp["w_branch_b"][0]),
        w_out=np.ascontiguousarray(inp["w_out"][0]), w_ffn2_in=np.ascontiguousarray(inp["w_ffn2_in"][0]),
        w_ffn2_out=np.ascontiguousarray(inp["w_ffn2_out"][0]))
    maps = []
    for core in range(8):
        b, j = core // 4, core % 4
        a = j * NT
        xs = np.zeros((NX, D), f32)
        xs[1:NT + 1] = x[b, a:a + NT]
        if j > 0:
            xs[0] = x[b, a - 1]
        if j < 3:
            xs[NX - 1] = x[b, a + NT]
        vecs = np.concatenate(vec_common + [inp["c"][b].reshape(8, 128), inp["c_ctx"].reshape(8, 128)], axis=0).astype(f32)
        meta = np.zeros((128, 8), f32)
        meta[:, 0] = j * 32 - 1
        meta[:, 1 + j] = 1.0
        meta[:, 5] = 1.0 if j > 0 else 0.0
        meta[:, 6] = 1.0 if j < 3 else 0.0
        m = dict(shared)
        m.update(w_ada=np.ascontiguousarray(inp["w_ada"][0][:, j * 2304:(j + 1) * 2304]), xs=xs, ctxb=np.ascontiguousarray(inp["ctx"][b], dtype=f32), vecs=np.ascontiguousarray(vecs), meta=meta)
        maps.append(m)
    return maps


_NC_CACHE = {}


def kernel(**inputs):
    inp = {k: np.asarray(v) for k, v in inputs.items()}
    maps = _host_inputs(inp)
    if "nc" not in _NC_CACHE:
        _NC_CACHE["nc"] = build()
    res = run_bass_kernel_spmd(_NC_CACHE["nc"], maps, core_ids=list(range(8)))
    outp = np.zeros((2, 4 * NT, D), np.float32)
    for core in range(8):
        b, j = core // 4, core % 4
        outp[b, j * NT:(j + 1) * NT] = res.results[core]["out"]
    kernel.last_results = res.results
    return outp
```

```python
import math
from contextlib import ExitStack

import numpy as np
import concourse.bass as bass
import concourse.mybir as mybir
from concourse.bass_utils import run_bass_kernel_spmd

F32 = mybir.dt.float32
BF16 = mybir.dt.bfloat16
I32 = mybir.dt.int32
AF = mybir.ActivationFunctionType
ALU = mybir.AluOpType

D = 1024
NT = 2048
NX = NT + 2
NCTX = 256
NCH = 16
DFF = 2816
NFF = 22
DPROJ = 8208
EPS = 1e-6
TWO_PI = 2.0 * math.pi
PI_SAFE = 3.1415925
XW = 8 * 514 + 8

DEBUG = {}
import os
KSTOP = os.environ.get("KSTOP", "all")
KITEMS = int(os.environ.get("KITEMS", "1"))


class Sch:
    NDS = 40

    def __init__(self, nc, es):
        self.nc = nc
        self.E = {'pe': nc.tensor, 'act': nc.scalar, 'dve': nc.vector, 'pool': nc.gpsimd, 'sp': nc.sync}
        self.semobj = {}
        for e in ['pe', 'act', 'dve', 'pool']:
            self.semobj[e] = es.enter_context(nc.semaphore(f"sem_{e}"))
        self.cnt = {e: 0 for e in ['pe', 'act', 'dve', 'pool']}
        self.seen = {e: {} for e in self.E}
        self.lw = {}
        self.rd = {}
        self.pend = {e: [] for e in self.cnt}
        self.dcnt = [0] * self.NDS
        for i in range(self.NDS):
            self.semobj[('d', i)] = es.enter_context(nc.semaphore(f"sem_d{i}"))
        self.dnext = 0
        self.nops = 0
        self.semobj['cc'] = es.enter_context(nc.semaphore("sem_cc"))
        self.cccnt = 0
        self.smallp = set()

    def _deps(self, eng, r, w):
        deps = set()
        for t in r:
            d = self.lw.get(t)
            if d is not None:
                deps.add((d, True))
        for t in w:
            d = self.lw.get(t)
            if d is not None:
                deps.add((d, True))
            for d in self.rd.get(t, ()):
                deps.add((d, False))
        return deps

    def _wait(self, eng, deps, strict=False):
        for (d, is_w) in deps:
            if d[0] == 'PEND':
                assert d[1] == eng, f"dependency on unsignaled op of {d[1]} from {eng}"
                continue
            key, val, src = d
            if src == eng:
                if eng == 'pe' or not is_w:
                    continue
                if not (strict or (key, val) in self.smallp):
                    continue
            if self.seen[eng].get(key, 0) >= val:
                continue
            self.E[eng].wait_ge(self.semobj[key], val)
            self.seen[eng][key] = val

    def op(self, eng, fn, r=(), w=(), sig=True, strict=False, small=False):
        strict = strict or small
        self._wait(eng, self._deps(eng, r, w), strict)
        ins = fn()
        self.nops += 1
        if sig:
            self.cnt[eng] += 1
            ins.then_inc(self.semobj[eng], 1)
            me = (eng, self.cnt[eng], eng)
            if small:
                self.smallp.add((eng, self.cnt[eng]))
            for (pr, pw) in self.pend[eng] + [(r, w)]:
                for t in pw:
                    self.lw[t] = me
                    self.rd[t] = []
            pm = ('PEND', eng)
            for (pr, pw) in self.pend[eng] + [(r, w)]:
                for t in pr:
                    lst = self.rd.setdefault(t, [])
                    if pm in lst:
                        lst[:] = [d for d in lst if d != pm]
                    if me not in lst:
                        lst.append(me)
            self.pend[eng] = []
        else:
            self.pend[eng].append((tuple(r), tuple(w)))
            for t in w:
                self.lw[t] = ('PEND', eng)
                self.rd[t] = []
            for t in r:
                self.rd.setdefault(t, []).append(('PEND', eng))
        return ins

    def dma(self, q, out, in_, r=(), w=(), **kw):
        deps = self._deps(q, r, w)
        idx = self.dnext
        self.dnext = (self.dnext + 1) % self.NDS
        if self.dcnt[idx] > 0:
            deps.add(((('d', idx), self.dcnt[idx], 'dma'), True))
        self._wait(q, deps)
        self.dcnt[idx] += 16
        self.E[q].dma_start(out=out, in_=in_, **kw).then_inc(self.semobj[('d', idx)], 16)
        me = (('d', idx), self.dcnt[idx], 'dma')
        for t in w:
            self.lw[t] = me
            self.rd[t] = []
        for t in r:
            self.rd.setdefault(t, []).append(me)

    def custom(self, eng, fn, inc, r=(), w=()):
        deps = self._deps(eng, r, w)
        self._wait(eng, deps)
        self.cccnt += inc
        fn(self.semobj['cc'])
        me = ('cc', self.cccnt, 'dma')
        for t in w:
            self.lw[t] = me
            self.rd[t] = []
        for t in r:
            self.rd.setdefault(t, []).append(me)

    def barrier(self):
        for e in self.cnt:
            assert not self.pend[e], f"pending unsignaled ops on {e} at barrier"
        deps = set()
        for e in self.cnt:
            if self.cnt[e] > 0:
                deps.add(((e, self.cnt[e], e), False))
        for i in range(self.NDS):
            if self.dcnt[i] > 0:
                deps.add(((('d', i), self.dcnt[i], 'dma'), True))
        if self.cccnt > 0:
            deps.add((('cc', self.cccnt, 'dma'), True))
        for e in self.E:
            self._wait(e, deps)
        self.lw = {}
        self.rd = {}


def build():
    nc = bass.Bass("TRN2", target_bir_lowering=False)

    def din(name, shape, dt=F32):
        return nc.dram_tensor(name, list(shape), dt, kind="ExternalInput").ap()

    xs = din("xs", [NX, D])
    ctxb = din("ctxb", [NCTX, D])
    vecs = din("vecs", [192, 128])
    meta = din("meta", [128, 8])
    freq = din("freq", [128, 2])
    consts = din("consts", [128, 3, 128])
    g_sgu = din("g_sgu", [D])
    b_gates = din("b_gates", [16])
    b_s = din("b_s", [512])
    w_s = din("w_s", [4, 128, 128])
    w_ada = din("w_ada", [D, 9 * D // 4])
    w_f1i = din("w_ffn1_in", [D, 2 * DFF])
    w_f1o = din("w_ffn1_out", [DFF, D])
    w_in = din("w_in", [D, DPROJ])
    w_ba = din("w_branch_a", [D, D])
    w_bb = din("w_branch_b", [D, D])
    w_o = din("w_out", [D, D])
    w_f2i = din("w_ffn2_in", [D, 2 * DFF])
    w_f2o = din("w_ffn2_out", [DFF, D])
    out = nc.dram_tensor("out", [NT, D], F32, kind="ExternalOutput").ap()

    def dscr(name, shape, dt):
        return nc.dram_tensor(name, list(shape), dt).ap()

    qT_d = dscr("qT_d", [8, 128, NT], BF16)
    kT_d = dscr("kT_d", [8, 128, NT + NCTX], BF16)
    ktok_d = dscr("ktok_d", [18, 128, D], BF16)
    v_d = dscr("v_d", [18, 128, D], BF16)
    vs_d = dscr("vs_d", [16, 128, D], F32)
    og_d = dscr("og_d", [8, 128, NT], BF16)
    gu_d = dscr("gu_d", [8, 128, NT], BF16)
    ga_d = dscr("ga_d", [8, 128, NT], BF16)
    gb_d = dscr("gb_d", [8, 128, NT], BF16)
    hb_d = dscr("hb_d", [16, 128, D], F32)
    mp_in = dscr("mp_in", [128, 36], F32)
    mp_out = dscr("mp_out", [4 * 128, 36], F32)
    xin_l = [dscr(f"xin_d{i}", [128, 1030], F32) for i in range(4)]
    xout_l = [dscr(f"xout_d{i}", [4 * 128, 1030], F32) for i in range(4)]

    dbg = {}
    for name, (shape, dt) in DEBUG.items():
        dbg[name] = nc.dram_tensor("dbg_" + name, list(shape), dt, kind="ExternalOutput").ap()

    with ExitStack() as es:
        S = Sch(nc, es)
        ACT, DVE, PE, POOL = nc.scalar, nc.vector, nc.tensor, nc.gpsimd

        def sb(stack, name, shape, dt):
            return stack.enter_context(nc.sbuf_tensor(name, list(shape), dt))

        def pstep(t):
            return t[:].ap[0][0]

        def cap(t, off, dims):
            return bass.AP(t, off, [[pstep(t), 128]] + [list(d) for d in dims])

        psall = es.enter_context(nc.psum_tensor("psall", [128, 8, 512], F32))
        ps = [psall[:, i, :] for i in range(8)]

        def P(i):
            return ("ps", i)

        ident_f = sb(es, "ident_f", [128, 128], F32)
        triL = sb(es, "triL", [128, 128], F32)
        triU = sb(es, "triU", [128, 128], F32)
        ident_b = sb(es, "ident_b", [128, 128], BF16)
        mL16 = sb(es, "mL16", [128, 128], F32)
        mU16 = sb(es, "mU16", [128, 128], F32)
        ones_f = sb(es, "ones_f", [128, 128], F32)
        ones_b = sb(es, "ones_b", [128, 128], BF16)
        vecT = sb(es, "vecT", [128, 192], F32)
        metat = sb(es, "metat", [128, 8], F32)
        modB = sb(es, "modB", [128, 72], F32)
        modC = sb(es, "modC", [128, 72], F32)
        prm = sb(es, "prm", [128, 16, 8], F32)
        hT = sb(es, "hT", [128, 8, NX], F32)
        hcT = sb(es, "hcT", [128, 8, NCTX], F32)
        eps_t = sb(es, "eps_t", [128, 1], F32)

        R_BADA, R_GF1, R_GMIX, R_CW, R_CB, R_GH, R_GF2, R_GFIN, R_C, R_CC = 0, 72, 80, 88, 136, 152, 160, 168, 176, 184
        (P_GS1, P_SH1, P_GT1, P_GS1C, P_SH1C, P_GT1C, P_GSM, P_SHM, P_GSMC, P_SHMC, P_G5, P_GS2, P_SH2, P_GT2) = range(14)

        S.dma('sp', ident_f[:], consts[:, 0, :], w=["ident_f"])
        S.dma('sp', triL[:], consts[:, 1, :], w=["triL"])
        S.dma('sp', triU[:], consts[:, 2, :], w=["triU"])
        S.dma('sp', metat[:], meta, w=["meta"])
        S.op('dve', lambda: DVE.tensor_copy(out=ident_b[:], in_=ident_f[:]), r=["ident_f"], w=["ident_b"])
        S.op('dve', lambda: DVE.tensor_scalar(out=mL16[:], in0=triL[:], scalar1=0.0625, scalar2=None, op0=ALU.mult), r=["triL"], w=["mL16"])
        S.op('dve', lambda: DVE.tensor_scalar(out=mU16[:], in0=triU[:], scalar1=0.0625, scalar2=None, op0=ALU.mult), r=["triU"], w=["mU16"])
        S.op('dve', lambda: DVE.memset(ones_f[:], 1.0), w=["ones_f"])
        S.op('dve', lambda: DVE.memset(ones_b[:], 1.0), w=["ones_b"])

        def wload(dst_tile, dst_tok, src_ap):
            S.dma('pool', dst_tile, src_ap, w=[dst_tok])

        with ExitStack() as p1:
            vst = sb(p1, "vst", [96, 2, 128], F32)
            S.dma('sp', vst[:, 0, :], vecs[0:96, :], w=["vst0"])
            S.dma('sp', vst[:, 1, :], vecs[96:192, :], w=["vst1"])
            for i in range(2):
                S.op('pe', lambda i=i: PE.transpose(out=ps[0][:, i * 96:(i + 1) * 96], in_=vst[:, i, :], identity=ident_f[0:96, 0:96]),
                     r=[f"vst{i}", "ident_f"], w=[P(0)], sig=(i == 1))
            S.op('dve', lambda: DVE.tensor_copy(out=vecT[:], in_=ps[0][:, 0:192]), r=[P(0)], w=["vecT"])

            scT = sb(p1, "scT", [128, 8, 2], F32)
            S.op('act', lambda: ACT.activation(out=scT[:, :, 0], in_=vecT[:, R_C:R_C + 8], func=AF.Silu), r=["vecT"], w=["scT0"])
            S.op('act', lambda: ACT.activation(out=scT[:, :, 1], in_=vecT[:, R_CC:R_CC + 8], func=AF.Silu), r=["vecT"], w=["scT1"])

            wad = [sb(p1, f"wad{i}", [128, 8, 256], F32) for i in range(3)]
            w_ada_v = w_ada.rearrange("(kc p) n -> p kc n", p=128)
            for blk in range(9):
                slot = blk % 3
                S.dma('sp', wad[slot][:], w_ada_v[:, :, blk * 256:(blk + 1) * 256], w=[f"wad{slot}"])
                for j in range(2):
                    b128 = blk * 2 + j
                    for kc in range(8):
                        S.op('pe', lambda slot=slot, j=j, kc=kc, b128=b128: PE.matmul(
                            ps[1][:, b128 * 2:b128 * 2 + 2], lhsT=wad[slot][:, kc, j * 128:(j + 1) * 128], rhs=scT[:, kc, :],
                            start=(kc == 0), stop=(kc == 7)),
                            r=[f"wad{slot}", "scT0", "scT1"], w=[P(1)], sig=(kc == 7 and j == 1))
            mpart = sb(p1, "mpart", [128, 36], F32)
            mg = sb(p1, "mg", [128, 4, 36], F32)
            S.op('dve', lambda: DVE.tensor_copy(out=mpart[:], in_=ps[1][:, 0:36]), r=[P(1)], w=["mpart"])
            S.dma('sp', mp_in, mpart[:], r=["mpart"], w=["mp_in"])
            S.custom('pool', lambda sem: POOL.collective_compute("AllGather", ALU.bypass, replica_groups=[[0, 1, 2, 3], [4, 5, 6, 7]],
                                                                 ins=[mp_in.opt()], outs=[mp_out.opt()]).then_inc(sem, 1),
                     1, r=["mp_in"], w=["mp_out"])
            S.dma('sp', mg[:], mp_out.rearrange("(r p) w -> p r w", p=128), r=["mp_out"], w=["mg"])
            psm = mg[:, :, :].rearrange("p r (b t) -> p (r b) t", t=2)
            S.op('dve', lambda: DVE.tensor_tensor(out=modB[:], in0=psm[:, :, 0], in1=vecT[:, R_BADA:R_BADA + 72], op=ALU.add), r=["mg", "vecT"], w=["modB"], small=True)
            S.op('dve', lambda: DVE.tensor_tensor(out=modC[:], in0=psm[:, :, 1], in1=vecT[:, R_BADA:R_BADA + 72], op=ALU.add), r=["mg", "vecT"], w=["modC"], small=True)

            def mk_gs(slot, gain_row, mod, scale_idx):
                S.op('dve', lambda: DVE.scalar_tensor_tensor(out=prm[:, slot, :], in0=mod[:, scale_idx * 8:scale_idx * 8 + 8], scalar=1.0,
                                                             in1=vecT[:, gain_row:gain_row + 8], op0=ALU.add, op1=ALU.mult),
                     r=["modB", "modC", "vecT"], w=[("prm", slot)], small=True)

            def mk_cp(slot, mod, idx, mul=1.0):
                S.op('dve', lambda: DVE.tensor_scalar(out=prm[:, slot, :], in0=mod[:, idx * 8:idx * 8 + 8], scalar1=mul, scalar2=None, op0=ALU.mult),
                     r=["modB", "modC"], w=[("prm", slot)], small=True)

            mk_gs(P_GS1, R_GF1, modB, 1); mk_cp(P_SH1, modB, 0); mk_cp(P_GT1, modB, 2, 0.5)
            mk_gs(P_GS1C, R_GF1, modC, 1); mk_cp(P_SH1C, modC, 0); mk_cp(P_GT1C, modC, 2, 0.5)
            mk_gs(P_GSM, R_GMIX, modB, 4); mk_cp(P_SHM, modB, 3)
            mk_gs(P_GSMC, R_GMIX, modC, 4); mk_cp(P_SHMC, modC, 3)
            mk_cp(P_G5, modB, 5)
            mk_gs(P_GS2, R_GF2, modB, 7); mk_cp(P_SH2, modB, 6); mk_cp(P_GT2, modB, 8, 0.5)

            fq = sb(p1, "fq", [128, 2], F32)
            S.dma('sp', fq[:], freq, w=["fq"])
            tab_r = sb(p1, "tab_r", [128, 4, 34], F32)
            tab_c = sb(p1, "tab_c", [128, 4, 64], F32)
            io_i = sb(p1, "io_i", [128, 64], I32)
            io_f = sb(p1, "io_f", [128, 64], F32)
            rv = sb(p1, "rv", [128, 34], F32)
            S.op('pool', lambda: POOL.iota(io_i[:], pattern=[[1, 64]], base=0, channel_multiplier=0), w=["io_i"])
            S.op('dve', lambda: DVE.tensor_copy(out=io_f[:], in_=io_i[:]), r=["io_i"], w=["io_f"], small=True)
            S.op('dve', lambda: DVE.tensor_scalar(out=rv[:], in0=io_f[:, 0:34], scalar1=metat[:, 0:1], scalar2=None, op0=ALU.add), r=["io_f", "meta"], w=["rv"], small=True)

            def mk_tab(tab, vals, n):
                arg = sb(p1, f"arg_{n}", [128, 4, n], F32)
                ki = sb(p1, f"ki_{n}", [128, 4, n], I32)
                kf = sb(p1, f"kf_{n}", [128, 4, n], F32)
                for cj in range(2):
                    for sc in range(2):
                        idx = sc * 2 + cj
                        S.op('dve', lambda idx=idx, cj=cj, sc=sc: DVE.tensor_scalar(
                            out=arg[:, idx, :], in0=vals, scalar1=fq[:, cj:cj + 1], scalar2=(0.5 * math.pi if sc else 0.0),
                            op0=ALU.mult, op1=ALU.add), r=["rv", "io_f", "fq"], w=[f"arg{n}"], small=True)
                S.op('dve', lambda: DVE.tensor_scalar(out=kf[:], in0=arg[:], scalar1=1.0 / TWO_PI, scalar2=None, op0=ALU.mult), r=[f"arg{n}"], w=[f"kf{n}"], small=True)
                S.op('dve', lambda: DVE.tensor_copy(out=ki[:], in_=kf[:]), r=[f"kf{n}"], w=[f"ki{n}"], small=True)
                S.op('dve', lambda: DVE.tensor_copy(out=kf[:], in_=ki[:]), r=[f"ki{n}"], w=[f"kf{n}"], small=True)
                S.op('dve', lambda: DVE.scalar_tensor_tensor(out=arg[:], in0=kf[:], scalar=-TWO_PI, in1=arg[:], op0=ALU.mult, op1=ALU.add),
                     r=[f"kf{n}"], w=[f"arg{n}"], small=True)
                S.op('dve', lambda: DVE.tensor_scalar(out=arg[:], in0=arg[:], scalar1=-PI_SAFE, scalar2=PI_SAFE, op0=ALU.max, op1=ALU.min), w=[f"arg{n}"], small=True)
                S.op('act', lambda: ACT.activation(out=tab[:], in_=arg[:], func=AF.Sin), r=[f"arg{n}"], w=[f"tab{n}"], small=True)

            mk_tab(tab_r, rv[:], 34)
            mk_tab(tab_c, io_f[:], 64)

            xt = [sb(p1, f"xt{i}", [128, D], F32) for i in range(2)]
            xh = sb(p1, "xh", [2, D], F32)
            for c in range(NCH):
                slot = c % 2
                S.dma('sp', xt[slot][:], xs[1 + 128 * c:1 + 128 * (c + 1), :], w=[f"xt{slot}"])
                for half in range(2):
                    pb = 2 + half
                    for k4 in range(4):
                        dc = half * 4 + k4
                        S.op('pe', lambda slot=slot, dc=dc, pb=pb, k4=k4: PE.transpose(
                            out=ps[pb][:, k4 * 128:(k4 + 1) * 128], in_=xt[slot][:, dc * 128:(dc + 1) * 128], identity=ident_f[:]),
                            r=[f"xt{slot}", "ident_f"], w=[P(pb)], sig=(k4 == 3))
                    o_ap = cap(hT, half * 4 * NX + 1 + 128 * c, [[NX, 4], [64, 2], [1, 64]])
                    i_ap = ps[pb][:, :].rearrange("p (a b c) -> p a b c", a=4, b=2, c=64)
                    if half == 0:
                        t_ap = cap(tab_r, 1 + 2 * c, [[34, 4], [1, 2], [0, 64]])
                        S.op('dve', lambda o_ap=o_ap, i_ap=i_ap, t_ap=t_ap: DVE.tensor_tensor(out=o_ap, in0=i_ap, in1=t_ap, op=ALU.add),
                             r=[P(pb), "tab34"], w=[("hT", c)])
                    else:
                        t_ap = cap(tab_c, 0, [[64, 4], [0, 2], [1, 64]])
                        S.op('dve', lambda o_ap=o_ap, i_ap=i_ap, t_ap=t_ap: DVE.tensor_tensor(out=o_ap, in0=i_ap, in1=t_ap, op=ALU.add),
                             r=[P(pb), "tab64"], w=[("hT", c)])
            S.dma('sp', xh[0:1, :], xs[0:1, :], w=["xh0"])
            S.dma('sp', xh[1:2, :], xs[NX - 1:NX, :], w=["xh1"])
            for dc in range(8):
                S.op('pe', lambda dc=dc: PE.transpose(out=ps[2][:, dc * 2:dc * 2 + 2], in_=xh[:, dc * 128:(dc + 1) * 128], identity=ident_f[0:2, 0:2]),
                     r=["xh0", "xh1", "ident_f"], w=[P(2)], sig=(dc == 7))
            pv = ps[2][:, 0:16].rearrange("p (d t) -> p d t", t=2)
            S.op('dve', lambda: DVE.tensor_tensor(out=cap(hT, 0, [[NX, 4]]), in0=pv[:, 0:4, 0], in1=tab_r[:, :, 0], op=ALU.add), r=[P(2), "tab34"], w=[("hT", "h0a")])
            S.op('dve', lambda: DVE.tensor_tensor(out=cap(hT, 4 * NX, [[NX, 4]]), in0=pv[:, 4:8, 0], in1=tab_c[:, :, 63], op=ALU.add), r=[P(2), "tab64"], w=[("hT", "h0b")])
            S.op('dve', lambda: DVE.tensor_tensor(out=cap(hT, NX - 1, [[NX, 4]]), in0=pv[:, 0:4, 1], in1=tab_r[:, :, 33], op=ALU.add), r=[P(2), "tab34"], w=[("hT", "h1a")])
            S.op('dve', lambda: DVE.tensor_tensor(out=cap(hT, 4 * NX + NX - 1, [[NX, 4]]), in0=pv[:, 4:8, 1], in1=tab_c[:, :, 0], op=ALU.add), r=[P(2), "tab64"], w=[("hT", "h1b")])
            for c in range(2):
                slot = c % 2
                S.dma('sp', xt[slot][:], ctxb[128 * c:128 * (c + 1), :], w=[f"xt{slot}"])
                for half in range(2):
                    pb = 2 + half
                    for k4 in range(4):
                        dc = half * 4 + k4
                        S.op('pe', lambda slot=slot, dc=dc, pb=pb, k4=k4: PE.transpose(
                            out=ps[pb][:, k4 * 128:(k4 + 1) * 128], in_=xt[slot][:, dc * 128:(dc + 1) * 128], identity=ident_f[:]),
                            r=[f"xt{slot}", "ident_f"], w=[P(pb)], sig=(k4 == 3))
                    S.op('dve', lambda pb=pb, half=half, c=c: DVE.tensor_copy(
                        out=hcT[:, half * 4:half * 4 + 4, c * 128:(c + 1) * 128], in_=ps[pb][:, :].rearrange("p (a b) -> p a b", a=4)),
                        r=[P(pb)], w=[("hcT", c)])
            S.barrier()

        def norm_mod(stk, src, src_off, n, gs_slot, sh_slot, dst, dst_off, uid, stride=1):
            sq, rstd, tmp = stk["sq"], stk["rstd"], stk["tmp"]
            ssrc = src.shape[2]
            sdst = dst.shape[2]

            def s_ap(dc):
                return cap(src, dc * ssrc + src_off, [[stride, n]])

            for dc in range(8):
                S.op('act', lambda dc=dc: ACT.activation(out=sq[:, dc, 0:n], in_=s_ap(dc), func=AF.Square), r=[("src", uid)], w=["sq"])
            for dc in range(8):
                S.op('pe', lambda dc=dc: PE.matmul(ps[7][:, 0:n], lhsT=ones_b[:], rhs=sq[:, dc, 0:n], start=(dc == 0), stop=(dc == 7)),
                     r=["sq", "ones_b"], w=[P(7)], sig=(dc == 7))
            S.op('act', lambda: ACT.activation(out=rstd[:, 0:n], in_=ps[7][:, 0:n], func=AF.Ln, scale=1.0 / D, bias=eps_t[:, 0:1]), r=[P(7)], w=["rstd"], small=(n < 256))
            S.op('act', lambda: ACT.activation(out=rstd[:, 0:n], in_=rstd[:, 0:n], func=AF.Exp, scale=-0.5), w=["rstd"], small=(n < 256))
            for dc in range(8):
                tt = tmp[dc % 2]
                S.op('dve', lambda dc=dc, tt=tt: DVE.scalar_tensor_tensor(out=tt[:, 0:n], in0=s_ap(dc), scalar=prm[:, gs_slot, dc:dc + 1], in1=rstd[:, 0:n],
                                                                         op0=ALU.mult, op1=ALU.mult),
                     r=[("src", uid), "rstd", ("prm", gs_slot)], w=[f"nm_tmp{dc % 2}"])
                S.op('act', lambda dc=dc, tt=tt: ACT.activation(out=dst[:, dc, dst_off:dst_off + n], in_=tt[:, 0:n], func=AF.Identity,
                                                                bias=prm[:, sh_slot, dc:dc + 1], scale=1.0),
                     r=[f"nm_tmp{dc % 2}", ("prm", sh_slot)], w=[("hn", uid)])

        S.op('dve', lambda: DVE.memset(eps_t[:], EPS), w=["eps_t"])
        S.barrier()

        def ffn(w_i, w_o2, tiles, prm_main, prm_ctx, tag):
            with ExitStack() as st:
                hnT = sb(st, "hnT" + tag, [128, 8, NX], BF16)
                hncT = sb(st, "hncT" + tag, [128, 8, NCTX], BF16)
                stk = {"sq": sb(st, "sq" + tag, [128, 8, 512], BF16), "rstd": sb(st, "rstd" + tag, [128, 512], F32),
                       "tmp": [sb(st, f"nmt{i}" + tag, [128, 512], F32) for i in range(2)]}
                ntile = len(tiles)
                for ti, (kind, off, n) in enumerate(tiles):
                    if kind == 'm':
                        norm_mod(stk, hT, off, n, prm_main[0], prm_main[1], hnT, off, (tag, ti))
                    else:
                        norm_mod(stk, hcT, off, n, prm_ctx[0], prm_ctx[1], hncT, off, (tag, ti))
                GRP = 6
                groups = [(0, 6), (6, 6), (12, 6), (18, 4)]
                zT = sb(st, "zT" + tag, [128, GRP, NX + NCTX], BF16)
                wa = [sb(st, f"wa{i}" + tag, [128, 8, 256], BF16) for i in range(2)]
                wb = [sb(st, f"wb{i}" + tag, [128, 8, 256], BF16) for i in range(2)]
                wo = [sb(st, f"wo{i}" + tag, [128, GRP, D], BF16) for i in range(2)]
                sl = [sb(st, f"sl{i}" + tag, [128, 512], F32) for i in range(2)]
                w_i_v = w_i.rearrange("(kc p) n -> p kc n", p=128)
                w_o_v = w_o2.rearrange("(fc p) n -> p fc n", p=128)
                blk_ctr = 0
                for gi, (g0, gn) in enumerate(groups):
                    gslot = gi % 2
                    wload(wo[gslot][:, 0:gn, :], f"wo{gslot}" + tag, w_o_v[:, g0:g0 + gn, :])
                    for b2 in range(gn // 2):
                        f0 = g0 + 2 * b2
                        slot = blk_ctr % 2
                        blk_ctr += 1
                        wload(wa[slot][:], f"wa{slot}" + tag, w_i_v[:, :, f0 * 128:(f0 + 2) * 128])
                        wload(wb[slot][:], f"wb{slot}" + tag, w_i_v[:, :, DFF + f0 * 128:DFF + (f0 + 2) * 128])
                        for ti, (kind, off, n) in enumerate(tiles):
                            src = hnT if kind == 'm' else hncT
                            zoff = off if kind == 'm' else NX + off
                            for j in range(2):
                                fz = 2 * b2 + j
                                pa, pb = (0, 1) if (j == 0) else (2, 3)
                                for kc in range(8):
                                    S.op('pe', lambda kc=kc, j=j, pa=pa, src=src, off=off, n=n, slot=slot: PE.matmul(
                                        ps[pa][:, 0:n], lhsT=wa[slot][:, kc, j * 128:(j + 1) * 128], rhs=src[:, kc, off:off + n],
                                        start=(kc == 0), stop=(kc == 7)),
                                        r=[f"wa{slot}" + tag, ("hn", (tag, ti))], w=[P(pa)], sig=(kc == 7))
                                for kc in range(8):
                                    S.op('pe', lambda kc=kc, j=j, pb=pb, src=src, off=off, n=n, slot=slot: PE.matmul(
                                        ps[pb][:, 0:n], lhsT=wb[slot][:, kc, j * 128:(j + 1) * 128], rhs=src[:, kc, off:off + n],
                                        start=(kc == 0), stop=(kc == 7)),
                                        r=[f"wb{slot}" + tag, ("hn", (tag, ti))], w=[P(pb)], sig=(kc == 7))
                                S.op('act', lambda pa=pa, j=j, n=n: ACT.activation(out=sl[j][:, 0:n], in_=ps[pa][:, 0:n], func=AF.Silu),
                                     r=[P(pa)], w=[f"sl{j}" + tag])
                                S.op('dve', lambda pb=pb, j=j, n=n, fz=fz, zoff=zoff: DVE.tensor_tensor(
                                    out=zT[:, fz, zoff:zoff + n], in0=ps[pb][:, 0:n], in1=sl[j][:, 0:n], op=ALU.mult),
                                    r=[P(pb), f"sl{j}" + tag], w=[("z", tag, ti, fz)])
                    for ti, (kind, off, n) in enumerate(tiles):
                        dstT = hT if kind == 'm' else hcT
                        zoff = off if kind == 'm' else NX + off
                        gt = (prm_main if kind == 'm' else prm_ctx)[2]
                        for dc in range(8):
                            pb = 4 + (dc % 3)
                            for fz in range(gn):
                                S.op('pe', lambda dc=dc, pb=pb, fz=fz, zoff=zoff, n=n, gslot=gslot: PE.matmul(
                                    ps[pb][:, 0:n], lhsT=wo[gslot][:, fz, dc * 128:(dc + 1) * 128], rhs=zT[:, fz, zoff:zoff + n],
                                    start=(fz == 0), stop=(fz == gn - 1)),
                                    r=[f"wo{gslot}" + tag, ("z", tag, ti, fz)], w=[P(pb)], sig=(fz == gn - 1))
                            S.op('dve', lambda dc=dc, pb=pb, dstT=dstT, off=off, n=n, gt=gt: DVE.scalar_tensor_tensor(
                                out=dstT[:, dc, off:off + n], in0=ps[pb][:, 0:n], scalar=prm[:, gt, dc:dc + 1], in1=dstT[:, dc, off:off + n],
                                op0=ALU.mult, op1=ALU.add),
                                r=[P(pb), ("prm", gt)], w=[("res", tag, ti, dc)])
            S.barrier()

        main_tiles = [('m', 1 + 512 * i, 512) for i in range(4)]
        halo_tiles = [('m', 0, 1), ('m', NX - 1, 1)]
        ctx_tile = [('c', 0, NCTX)]

        ffn(w_f1i, w_f1o, main_tiles + halo_tiles + ctx_tile, (P_GS1, P_SH1, P_GT1), (P_GS1C, P_SH1C, P_GT1C), "f1")

        mx = ExitStack()
        gates_all = sb(mx, "gates_all", [128, 18, 16], F32)
        sc_all = sb(mx, "sc_all", [128, 18, 32], F32)
        cs_b = sb(mx, "cs_b", [128, 2, 18, 4], F32)
        cs_g = sb(mx, "cs_g", [128, 18, 8], F32)
        Gseg = sb(mx, "Gseg", [128, 8], F32)
        LL = sb(mx, "LL", [128, 18, 8], F32)
        psc = sb(mx, "psc", [128, 18, 8], F32)
        wsT = sb(mx, "wsT", [128, 4, 128], BF16)
        bs_row = sb(mx, "bs_row", [1, 512], BF16)
        bg_bc = sb(mx, "bg_bc", [128, 16], F32)
        one_t = sb(mx, "one_t", [128, 1], F32)

        def dbg_dump(name, src_ap, rtoks):
            if name in dbg:
                S.dma('sp', dbg[name], src_ap, r=rtoks)

        with ExitStack() as st:
            wst = sb(st, "wst", [128, 4, 128], F32)
            bsf = sb(st, "bsf", [1, 512], F32)
            S.dma('sp', wst[:], w_s.rearrange("g t s -> t g s"), w=["wst"])
            S.dma('sp', bsf[:], b_s.rearrange("(o n) -> o n", o=1), w=["bsf"])
            S.dma('sp', bg_bc[:], bass.AP(b_gates.tensor, 0, [[0, 128], [1, 16]]), w=["bg_bc"])
            S.op('dve', lambda: DVE.memset(one_t[:], 1.0), w=["one_t"])
            for g in range(4):
                S.op('pe', lambda g=g: PE.transpose(out=ps[0][:, g * 128:(g + 1) * 128], in_=wst[:, g, :], identity=ident_f[:]),
                     r=["wst"], w=[P(0)], sig=(g == 3))
            S.op('dve', lambda: DVE.tensor_copy(out=wsT[:], in_=ps[0][:, :].rearrange("p (g t) -> p g t", g=4)), r=[P(0)], w=["wsT"])
            S.op('dve', lambda: DVE.tensor_copy(out=bs_row[:], in_=bsf[:]), r=["bsf"], w=["bs_row"])
            S.barrier()

        GC = 1.5957691216057308

        class GeluPipe:
            def __init__(self):
                self.prev = None

            def push(self, x_ap, t1, out_ap, xtoks, t1tok, outtok, after=None):
                S.op('act', lambda: ACT.activation(out=t1, in_=x_ap, func=AF.Square), r=xtoks, w=[t1tok])
                S.op('dve', lambda: DVE.tensor_scalar(out=t1, in0=t1, scalar1=0.044715, scalar2=1.0, op0=ALU.mult, op1=ALU.add), w=[t1tok])
                S.op('dve', lambda: DVE.tensor_tensor(out=t1, in0=x_ap, in1=t1, op=ALU.mult), r=xtoks, w=[t1tok])
                self.flush()
                self.prev = (x_ap, t1, out_ap, xtoks, t1tok, outtok, after)

            def flush(self):
                if self.prev is None:
                    return
                x_ap, t1, out_ap, xtoks, t1tok, outtok, after = self.prev
                self.prev = None
                S.op('act', lambda: ACT.activation(out=t1, in_=t1, func=AF.Sigmoid, scale=GC), r=[t1tok], w=[t1tok])
                S.op('dve', lambda: DVE.tensor_tensor(out=out_ap, in0=x_ap, in1=t1, op=ALU.mult), r=xtoks + [t1tok], w=[outtok])
                if after:
                    after()

        gpipe = GeluPipe()

        with ExitStack() as st:
            hnT = sb(st, "hnT_m", [128, 8, NX], BF16)
            hncT = sb(st, "hncT_m", [128, 8, NCTX], BF16)
            with ExitStack() as nst:
                stk = {"sq": sb(nst, "sq_m", [128, 8, 512], BF16), "rstd": sb(nst, "rstd_m", [128, 512], F32),
                       "tmp": [sb(nst, f"nmt{i}_m", [128, 512], F32) for i in range(2)]}
                tl = main_tiles + halo_tiles
                for ti, (kind, off, n) in enumerate(tl):
                    norm_mod(stk, hT, off, n, P_GSM, P_SHM, hnT, off, ("mx", ti))
                norm_mod(stk, hcT, 0, NCTX, P_GSMC, P_SHMC, hncT, 0, ("mx", 6))
                S.barrier()
            HN_MAIN = [("hn", ("mx", i)) for i in range(4)]
            HN_HALO = [("hn", ("mx", 4)), ("hn", ("mx", 5))]
            HN_CTX = [("hn", ("mx", 6))]

            wblk = [sb(st, f"wblk{i}", [128, 8, 512], BF16) for i in range(2)]
            w_in_v = w_in.rearrange("(kc p) n -> p kc n", p=128)
            bctr = [0]
            BLKS = ([(i * 512, 512) for i in range(4)] + [(2048, 512), (2560, 512), (3072, 16), (5136, 512), (5648, 512)]
                    + [(c0 + b * 512, 512) for c0 in (3088, 4112, 6160, 7184) for b in range(2)])
            issued = [0]

            def _issue(i):
                c0, ncols = BLKS[i]
                wload(wblk[i % 2][:, :, 0:ncols], f"wblk{i % 2}", w_in_v[:, :, c0:c0 + ncols])

            def load_blk(c0, ncols):
                i = bctr[0]
                assert BLKS[i] == (c0, ncols), (i, BLKS[i], c0, ncols)
                bctr[0] += 1
                while issued[0] <= min(i + 1, len(BLKS) - 1):
                    _issue(issued[0])
                    issued[0] += 1
                return i % 2

            pre = [sb(st, f"pre{i}", [128, NX], F32) for i in range(2)]
            prec = sb(st, "prec", [128, NCTX + 2], F32)
            accs = [sb(st, f"acc{i}", [128, NT], F32) for i in range(2)]
            qks = [sb(st, f"qks{i}", [128, NT + NCTX], BF16) for i in range(2)]
            ktk = sb(st, "ktk", [128, 18, 128], BF16)
            stg = [sb(st, f"stg{i}", [128, NT], BF16) for i in range(2)]
            ut = [sb(st, f"ut{i}", [128, 512], F32) for i in range(3)]
            vstg = [sb(st, f"vstg{i}", [128, 512], BF16) for i in range(3)]
            vsstg = [sb(st, f"vsstg{i}", [128, 512], F32) for i in range(3)]
            S.op('dve', lambda: DVE.memset(prec[:], 0.0), w=["prec"])

            for blk in range(4):
                slot = load_blk(blk * 512, 512)
                for j in range(4):
                    fc = blk * 4 + j
                    is_k = fc >= 8
                    pr = pre[fc % 2]
                    ptok = f"pre{fc % 2}"
                    acc = accs[fc % 2]
                    atok = f"acc{fc % 2}"
                    for ti in range(4):
                        pb = (0, 1, 6, 7)[ti]
                        for kc in range(8):
                            S.op('pe', lambda kc=kc, j=j, pb=pb, ti=ti, slot=slot: PE.matmul(
                                ps[pb][:, :], lhsT=wblk[slot][:, kc, j * 128:(j + 1) * 128], rhs=hnT[:, kc, 1 + 512 * ti:1 + 512 * (ti + 1)],
                                start=(kc == 0), stop=(kc == 7)), r=[f"wblk{slot}", HN_MAIN[ti]], w=[P(pb)], sig=(kc == 7))
                        S.op('act', lambda pb=pb, ti=ti, pr=pr: ACT.copy(out=pr[:, 1 + 512 * ti:1 + 512 * (ti + 1)], in_=ps[pb][:, :]),
                             r=[P(pb)], w=[(ptok, ti)])
                    for kc in range(8):
                        S.op('pe', lambda kc=kc, j=j, slot=slot: PE.matmul(
                            ps[2][:, 0:2], lhsT=wblk[slot][:, kc, j * 128:(j + 1) * 128], rhs=cap(hnT, kc * NX, [[NX - 1, 2]]),
                            start=(kc == 0), stop=(kc == 7)), r=[f"wblk{slot}"] + HN_HALO, w=[P(2)], sig=(kc == 7))
                    S.op('dve', lambda pr=pr: DVE.tensor_tensor(out=cap(pr, 0, [[NX - 1, 2]]), in0=ps[2][:, 0:2], in1=metat[:, 5:7], op=ALU.mult),
                         r=[P(2), "meta"], w=[(ptok, 4)])
                    if is_k:
                        for kc in range(8):
                            S.op('pe', lambda kc=kc, j=j, slot=slot: PE.matmul(
                                ps[3][:, 0:NCTX], lhsT=wblk[slot][:, kc, j * 128:(j + 1) * 128], rhs=hncT[:, kc, :],
                                start=(kc == 0), stop=(kc == 7)), r=[f"wblk{slot}"] + HN_CTX, w=[P(3)], sig=(kc == 7))
                        S.op('act', lambda: ACT.copy(out=prec[:, 1:1 + NCTX], in_=ps[3][:, 0:NCTX]), r=[P(3)], w=["prec"])
                    w0 = vecT[:, R_CW + 0 * 16 + fc:R_CW + 0 * 16 + fc + 1]
                    w1 = vecT[:, R_CW + 1 * 16 + fc:R_CW + 1 * 16 + fc + 1]
                    w2 = vecT[:, R_CW + 2 * 16 + fc:R_CW + 2 * 16 + fc + 1]
                    cb = vecT[:, R_CB + fc:R_CB + fc + 1]
                    qs = qks[fc % 2]
                    qtok = f"qks{fc % 2}"
                    allpre = [(ptok, i) for i in range(5)]
                    S.op('pool', lambda pr=pr, w0=w0: POOL.tensor_scalar(out=acc[:], in0=pr[:, 0:NT], scalar1=w0, scalar2=0.0, op0=ALU.mult, op1=ALU.add),
                         r=allpre, w=[atok])
                    S.op('dve', lambda pr=pr, w1=w1: DVE.scalar_tensor_tensor(out=acc[:], in0=pr[:, 1:NT + 1], scalar=w1, in1=acc[:], op0=ALU.mult, op1=ALU.add),
                         r=allpre + [atok], w=[atok])
                    S.op('dve', lambda pr=pr, w2=w2: DVE.scalar_tensor_tensor(out=acc[:], in0=pr[:, 2:NT + 2], scalar=w2, in1=acc[:], op0=ALU.mult, op1=ALU.add),
                         r=allpre, w=[atok])
                    S.op('act', lambda qs=qs, cb=cb: ACT.activation(out=qs[:, 0:NT], in_=acc[:], func=AF.Silu, bias=cb, scale=1.0), r=[atok], w=[qtok])
                    if not is_k:
                        S.dma('sp', qT_d[fc], qs[:, 0:NT], r=[qtok], w=[("qT_d", fc)])
                    else:
                        S.op('dve', lambda w0=w0: DVE.tensor_scalar(out=acc[:, 0:NCTX], in0=prec[:, 0:NCTX], scalar1=w0, scalar2=None, op0=ALU.mult),
                             r=["prec", atok], w=[atok])
                        S.op('dve', lambda w1=w1: DVE.scalar_tensor_tensor(out=acc[:, 0:NCTX], in0=prec[:, 1:NCTX + 1], scalar=w1, in1=acc[:, 0:NCTX], op0=ALU.mult, op1=ALU.add),
                             r=["prec"], w=[atok])
                        S.op('dve', lambda w2=w2: DVE.scalar_tensor_tensor(out=acc[:, 0:NCTX], in0=prec[:, 2:NCTX + 2], scalar=w2, in1=acc[:, 0:NCTX], op0=ALU.mult, op1=ALU.add),
                             r=["prec"], w=[atok])
                        S.op('act', lambda qs=qs, cb=cb: ACT.activation(out=qs[:, NT:NT + NCTX], in_=acc[:, 0:NCTX], func=AF.Silu, bias=cb, scale=1.0),
                             r=[atok], w=[qtok])
                        S.dma('sp', kT_d[fc - 8], qs[:, :], r=[qtok], w=[("kT_d", fc - 8)])
                        for grp in range(3):
                            c0 = grp * 8
                            ncg = min(8, 18 - c0)
                            pbank = 4 + (grp % 2)
                            psb = ps[pbank][:, :].bitcast(BF16)
                            for ci in range(ncg):
                                S.op('pe', lambda ci=ci, c0=c0, psb=psb, qs=qs: PE.transpose(
                                    out=psb[:, ci * 128:(ci + 1) * 128], in_=qs[:, (c0 + ci) * 128:(c0 + ci + 1) * 128], identity=ident_b[:]),
                                    r=[qtok, "ident_b"], w=[P(pbank)], sig=(ci == ncg - 1))
                            S.op('dve', lambda c0=c0, ncg=ncg, psb=psb: DVE.tensor_copy(
                                out=ktk[:, c0:c0 + ncg, :], in_=psb[:, 0:ncg * 128].rearrange("p (c f) -> p c f", f=128)),
                                r=[P(pbank)], w=["ktk"])
                        S.dma('sp', ktok_d.rearrange("c p f -> p c f")[:, :, (fc - 8) * 128:(fc - 7) * 128], ktk[:], r=["ktk"], w=[("ktok_d", fc - 8)])

            def hn_chunk(c, kc):
                if c < 16:
                    return hnT[:, kc, 1 + 128 * c:1 + 128 * (c + 1)]
                return hncT[:, kc, (c - 16) * 128:(c - 15) * 128]

            def hn_tok(c):
                return HN_MAIN[c // 4] if c < 16 else HN_CTX[0]

            def bform(c0, ncols, nch, epi):
                slot = load_blk(c0, ncols)
                for c in range(nch):
                    pb = c % 6
                    for kc in range(8):
                        S.op('pe', lambda kc=kc, c=c, pb=pb, slot=slot: PE.matmul(
                            ps[pb][:, 0:ncols], lhsT=hn_chunk(c, kc), rhs=wblk[slot][:, kc, 0:ncols], start=(kc == 0), stop=(kc == 7)),
                            r=[f"wblk{slot}", hn_tok(c)], w=[P(pb)], sig=(kc == 7))
                    epi(c, pb)

            vctr = [0]
            for half in range(2):
                def epi_v(c, pb, half=half):
                    s3 = vctr[0] % 3
                    vctr[0] += 1
                    S.op('act', lambda: ACT.copy(out=vstg[s3][:], in_=ps[pb][:, :]), r=[P(pb)], w=[f"vstg{s3}"])
                    S.dma('sp', v_d[c][:, half * 512:(half + 1) * 512], vstg[s3][:], r=[f"vstg{s3}"], w=[("v_d", c, half)])
                bform(2048 + half * 512, 512, 18, epi_v)

            def epi_g(c, pb):
                S.op('dve', lambda: DVE.tensor_tensor(out=gates_all[:, c, :], in0=ps[pb][:, 0:16], in1=bg_bc[:], op=ALU.add),
                     r=[P(pb), "bg_bc"], w=[("gates", c)])
            bform(3072, 16, 18, epi_g)

            vsctr = [0]
            for half in range(2):
                def epi_vs(c, pb, half=half):
                    s2 = vsctr[0] % 3
                    vsctr[0] += 1
                    gpipe.push(ps[pb][:, :], ut[s2][:], vsstg[s2][:], [P(pb)], f"ut{s2}", f"vsstg{s2}",
                               after=lambda c=c, half=half, s2=s2: S.dma('sp', vs_d[c][:, half * 512:(half + 1) * 512], vsstg[s2][:], r=[f"vsstg{s2}"], w=[("vs_d", c, half)]))
                bform(5136 + half * 512, 512, 16, epi_vs)
            gpipe.flush()

            def aform(col0, kind, dst_d):
                for blk in range(2):
                    slot = load_blk(col0 + blk * 512, 512)
                    for j in range(4):
                        fc = blk * 4 + j
                        sg = stg[fc % 2]
                        stok = f"stg{fc % 2}"
                        for ti in range(4):
                            pb = (fc * 4 + ti) % 8
                            for kc in range(8):
                                S.op('pe', lambda kc=kc, j=j, pb=pb, ti=ti, slot=slot: PE.matmul(
                                    ps[pb][:, :], lhsT=wblk[slot][:, kc, j * 128:(j + 1) * 128], rhs=hnT[:, kc, 1 + 512 * ti:1 + 512 * (ti + 1)],
                                    start=(kc == 0), stop=(kc == 7)), r=[f"wblk{slot}", HN_MAIN[ti]], w=[P(pb)], sig=(kc == 7))
                            dst = sg[:, 512 * ti:512 * (ti + 1)]
                            if kind == 'sig':
                                S.op('act', lambda pb=pb, dst=dst: ACT.activation(out=dst, in_=ps[pb][:, :], func=AF.Sigmoid), r=[P(pb)], w=[(stok, ti)])
                            elif kind == 'sigg':
                                t1 = ut[ti % 3]
                                S.op('act', lambda pb=pb, t1=t1: ACT.activation(out=t1[:], in_=ps[pb][:, :], func=AF.Sigmoid), r=[P(pb)], w=[f"ut{ti % 3}"])
                                S.op('pool', lambda t1=t1, dst=dst, fc=fc: POOL.tensor_scalar(out=dst, in0=t1[:], scalar1=vecT[:, R_GH + fc:R_GH + fc + 1], scalar2=0.0,
                                                                                               op0=ALU.mult, op1=ALU.add), r=[f"ut{ti % 3}"], w=[(stok, ti)])
                            else:
                                gpipe.push(ps[pb][:, :], ut[ti % 3][:], dst, [P(pb)], f"ut{ti % 3}", (stok, ti))
                        if kind == 'gelu':
                            gpipe.flush()
                        S.dma('sp', dst_d[fc], sg[:], r=[(stok, i) for i in range(4)], w=[(dst_d.tensor.name, fc)])

            aform(3088, 'sigg', og_d)
            aform(4112, 'gelu', gu_d)
            aform(6160, 'sig', ga_d)
            aform(7184, 'sig', gb_d)
            S.barrier()

        Sin = sb(mx, "Sin", [128, 8, 514], F32)
        with ExitStack() as st:
            lfw = sb(st, "lfw", [128, 18, 8], F32)
            dif = sb(st, "dif", [128, 18, 8], F32)
            S.op('act', lambda: ACT.activation(out=lfw[:, :, 0:4], in_=gates_all[:, :, 4:8], func=AF.Exp, scale=-1.0), w=["lfw"], small=True)
            S.op('act', lambda: ACT.activation(out=lfw[:, :, 4:8], in_=gates_all[:, :, 12:16], func=AF.Exp, scale=-1.0), w=["lfw"], small=True)
            S.op('act', lambda: ACT.activation(out=lfw[:], in_=lfw[:], func=AF.Ln, bias=one_t[:, 0:1], scale=1.0), w=["lfw"], small=True)
            S.op('dve', lambda: DVE.tensor_scalar(out=lfw[:], in0=lfw[:], scalar1=-1.0, scalar2=None, op0=ALU.mult), r=["lfw"], w=["lfw"], small=True)
            S.op('pe', lambda: PE.matmul(ps[0][:, 0:72], lhsT=triL[:], rhs=lfw[:, :, 0:4], start=True, stop=True), r=["lfw"], w=[P(0)], sig=False)
            S.op('pe', lambda: PE.matmul(ps[0][:, 72:144], lhsT=triU[:], rhs=lfw[:, :, 4:8], start=True, stop=True), r=["lfw"], w=[P(0)], sig=False)
            S.op('pe', lambda: PE.matmul(ps[1][:, 0:144], lhsT=ones_f[:], rhs=lfw[:], start=True, stop=True), r=["lfw"], w=[P(1)], sig=True)
            S.op('dve', lambda: DVE.tensor_copy(out=cs_b[:], in_=ps[0][:, 0:144].rearrange("p (d c h) -> p d c h", d=2, c=18)), r=[P(0)], w=["cs_b"], small=True)
            S.op('dve', lambda: DVE.tensor_copy(out=cs_g[:], in_=ps[1][:, 0:144].rearrange("p (c h) -> p c h", c=18)), r=[P(1)], w=["cs_g"], small=True)
            S.op('act', lambda: ACT.activation(out=sc_all[:, :, 0:4], in_=cs_b[:, 0, :, :], func=AF.Exp), r=["cs_b"], w=["sc_all"], small=True)
            S.op('act', lambda: ACT.activation(out=sc_all[:, :, 4:8], in_=cs_b[:, 1, :, :], func=AF.Exp), r=["cs_b"], w=["sc_all"], small=True)
            S.op('dve', lambda: DVE.tensor_tensor(out=dif[:, :, 0:4], in0=gates_all[:, :, 0:4], in1=cs_b[:, 0, :, :], op=ALU.subtract), r=["cs_b"], w=["dif"], small=True)
            S.op('dve', lambda: DVE.tensor_tensor(out=dif[:, :, 4:8], in0=gates_all[:, :, 8:12], in1=cs_b[:, 1, :, :], op=ALU.subtract), r=["cs_b"], w=["dif"], small=True)
            S.op('act', lambda: ACT.activation(out=sc_all[:, :, 8:16], in_=dif[:], func=AF.Exp), r=["dif"], w=["sc_all"], small=True)
            S.op('act', lambda: ACT.activation(out=sc_all[:, :, 16:24], in_=cs_g[:], func=AF.Exp), r=["cs_g"], w=["sc_all"], small=True)
            S.op('dve', lambda: DVE.tensor_tensor(out=sc_all[:, :, 24:32], in0=sc_all[:, :, 8:16], in1=sc_all[:, :, 16:24], op=ALU.mult), r=["sc_all"], w=["sc_all"], small=True)
            S.op('dve', lambda: DVE.tensor_reduce(out=Gseg[:], in_=cap(cs_g, 0, [[1, 8], [8, 16]]), axis=mybir.AxisListType.X, op=ALU.add), r=["cs_g"], w=["Gseg"], small=True)
            S.op('dve', lambda: DVE.memset(LL[:], 0.0), w=["LL"], small=True)
            for c in range(14, -1, -1):
                S.op('dve', lambda c=c: DVE.tensor_tensor(out=LL[:, c, 0:4], in0=LL[:, c + 1, 0:4], in1=cs_g[:, c + 1, 0:4], op=ALU.add), r=["cs_g"], w=["LL"], small=True)
            for c in range(1, 16):
                S.op('dve', lambda c=c: DVE.tensor_tensor(out=LL[:, c, 4:8], in0=LL[:, c - 1, 4:8], in1=cs_g[:, c - 1, 4:8], op=ALU.add), r=["cs_g"], w=["LL"], small=True)
            S.op('dve', lambda: DVE.tensor_copy(out=LL[:, 16, 0:4], in_=cs_g[:, 17, 0:4]), w=["LL"], small=True)
            S.op('dve', lambda: DVE.tensor_copy(out=LL[:, 17, 4:8], in_=cs_g[:, 16, 4:8]), w=["LL"], small=True)
            S.op('act', lambda: ACT.activation(out=LL[:], in_=LL[:], func=AF.Exp), r=["LL"], w=["LL"], small=True)
            S.op('dve', lambda: DVE.tensor_tensor(out=psc[:], in0=LL[:], in1=sc_all[:, :, 24:32], op=ALU.mult), r=["LL", "sc_all"], w=["psc"], small=True)
            S.barrier()
        dbg_dump("gates", gates_all[:], [])
        dbg_dump("sc_all", sc_all[:], [])

        _slc = [0]

        def sweep_loads(pool, names):
            _slc[0] += 1
            return {n: [sb(pool, f"ld{_slc[0]}_{n}{i}", shp, dt) for i in range(2)] for n, (shp, dt) in names.items()}

        with ExitStack() as st:
            St = sb(st, "St", [128, 8, 514], F32)
            Sctx = sb(st, "Sctx", [128, 8, 514], F32)
            lds = sweep_loads(st, {"ktok": ([128, D], BF16), "v": ([128, D], BF16)})
            vtl = [sb(st, f"vtl{i}", [128, 4, 257], BF16) for i in range(2)]
            lctr = [0]

            def p1_pass(chunks, d, dst, dtok):
                n = len(chunks)
                for i, c in enumerate(chunks):
                    slot = lctr[0] % 2
                    lctr[0] += 1
                    kt, vv, vt = lds["ktok"][slot], lds["v"][slot], vtl[slot]
                    S.dma('sp', kt[:], ktok_d[c], w=[f"ld_ktok{slot}"])
                    S.dma('sp', vv[:], v_d[c], w=[f"ld_v{slot}"])
                    for h in range(4):
                        S.op('dve', lambda h=h: DVE.tensor_scalar(out=vt[:, h, 0:256], in0=vv[:, h * 256:(h + 1) * 256], scalar1=psc[:, c, 4 * d + h:4 * d + h + 1],
                                                                  scalar2=None, op0=ALU.mult), r=[f"ld_v{slot}", "psc"], w=[f"vtl{slot}"])
                    S.op('dve', lambda: DVE.tensor_copy(out=vt[:, :, 256], in_=psc[:, c, 4 * d:4 * d + 4]), w=[f"vtl{slot}"], small=True)
                    for h in range(4):
                        for kc in range(2):
                            pb = h * 2 + kc
                            S.op('pe', lambda h=h, kc=kc, pb=pb: PE.matmul(ps[pb][:, 0:257], lhsT=kt[:, h * 256 + kc * 128:h * 256 + (kc + 1) * 128], rhs=vt[:, h, :],
                                                                          start=(i == 0), stop=(i == n - 1)), r=[f"ld_ktok{slot}", f"vtl{slot}"], w=[P(pb)],
                                 sig=(i == n - 1 or (h == 3 and kc == 1)))
                for h in range(4):
                    for kc in range(2):
                        pb = h * 2 + kc
                        eng = 'act' if (pb % 2 == 0) else 'dve'
                        if eng == 'act':
                            S.op('act', lambda h=h, kc=kc, pb=pb: ACT.copy(out=dst[:, d * 4 + h, kc * 257:(kc + 1) * 257], in_=ps[pb][:, 0:257]), r=[P(pb)], w=[(dtok, d * 4 + h, kc)])
                        else:
                            S.op('dve', lambda h=h, kc=kc, pb=pb: DVE.tensor_copy(out=dst[:, d * 4 + h, kc * 257:(kc + 1) * 257], in_=ps[pb][:, 0:257]), r=[P(pb)], w=[(dtok, d * 4 + h, kc)])

            p1_pass([16, 17], 0, Sctx, "Sctx")
            p1_pass([17, 16], 1, Sctx, "Sctx")
            p1_pass(list(range(16)), 0, St, "St")
            p1_pass(list(range(16)), 1, St, "St")
            S.barrier()
            dbg_dump("Sctx", Sctx[:], ["Sctx"])
            dbg_dump("Sloc", St[:], [("St", i) for i in range(8)])
            xout_v = [xo.rearrange("(r p) w -> p r w", p=128) for xo in xout_l]
            for i in range(4):
                S.dma('sp', xin_l[i][:, 0:1028], St[:, 2 * i:2 * i + 2, :].rearrange("p a b -> p (a b)"), r=[("St", 2 * i), ("St", 2 * i + 1)], w=[("xin", i)])
                S.dma('sp', xin_l[i][:, 1028:1030], Gseg[:, 2 * i:2 * i + 2], r=[], w=[("xin2", i)])
            for i in range(4):
                if KSTOP == 'p1':
                    S.dma('sp', xout_l[i][0:128, :], xin_l[i], r=[("xin", i), ("xin2", i)], w=[("xout", i)])
                else:
                    S.custom('pool', lambda sem, i=i: POOL.collective_compute("AllGather", ALU.bypass, replica_groups=[[0, 1, 2, 3], [4, 5, 6, 7]],
                                                                              ins=[xin_l[i].opt()], outs=[xout_l[i].opt()]).then_inc(sem, 1),
                             1, r=[("xin", i), ("xin2", i)], w=[("xout", i)])
            with ExitStack() as sg:
                gvt = sb(sg, "sgu_gv", [128, 4, D], F32)
                vnt = sb(sg, "sgu_vn", [128, 4, D], BF16)
                gut = sb(sg, "sgu_gu", [128, 8, 512], BF16)
                ybt = sb(sg, "sgu_yb", [128, 8, 512], BF16)
                gsgu_bc = sb(sg, "gsgu_bc", [128, D], F32)
                st6s = sb(sg, "sgu_st6", [128, 4, 2, 6], F32)
                mvs = sb(sg, "sgu_mv", [128, 4, 2], F32)
                msq = sb(sg, "sgu_msq", [128, 2, 4], F32)
                S.dma('sp', gsgu_bc[:], bass.AP(g_sgu.tensor, 0, [[0, 128], [1, D]]), w=["gsgu_bc"])
                for g4 in range(4):
                    S.dma('sp', gvt[:], vs_d.rearrange("c p f -> p c f")[:, 4 * g4:4 * g4 + 4, :], w=["sgu_gv"])
                    S.dma('sp', gut[:], gu_d.rearrange("f p t -> p f t")[:, :, 512 * g4:512 * (g4 + 1)], w=["sgu_gu"])
                    for cq in range(4):
                        for i2 in range(2):
                            S.op('dve', lambda cq=cq, i2=i2: DVE.bn_stats(out=st6s[:, cq, i2, :], in_=gvt[:, cq, i2 * 512:(i2 + 1) * 512]),
                                 r=["sgu_gv"], w=[("sgu_st6", cq)], small=True)
                    for cq in range(4):
                        S.op('dve', lambda cq=cq: DVE.bn_aggr(out=mvs[:, cq, :], in_=st6s[:, cq, :, :].rearrange("p a b -> p (a b)")),
                             r=[("sgu_st6", cq)], w=["sgu_mv"], small=True)
                    S.op('dve', lambda: DVE.tensor_tensor(out=msq[:, 0, :], in0=mvs[:, :, 0], in1=mvs[:, :, 0], op=ALU.mult), r=["sgu_mv"], w=["sgu_msq"], small=True)
                    S.op('dve', lambda: DVE.tensor_tensor(out=msq[:, 0, :], in0=msq[:, 0, :], in1=mvs[:, :, 1], op=ALU.add), w=["sgu_msq"], small=True)
                    S.op('act', lambda: ACT.activation(out=msq[:, 1, :], in_=msq[:, 0, :], func=AF.Ln, bias=eps_t[:, 0:1], scale=1.0), r=["sgu_msq"], w=["sgu_rs"], small=True)
                    S.op('act', lambda: ACT.activation(out=msq[:, 1, :], in_=msq[:, 1, :], func=AF.Exp, scale=-0.5), w=["sgu_rs"], small=True)
                    for cq in range(4):
                        S.op('dve', lambda cq=cq: DVE.scalar_tensor_tensor(out=vnt[:, cq, :], in0=gvt[:, cq, :], scalar=msq[:, 1, cq:cq + 1], in1=gsgu_bc[:],
                                                                           op0=ALU.mult, op1=ALU.mult), r=["sgu_rs", "sgu_gv", "gsgu_bc"], w=[("sgu_vn", cq)], small=True)
                    for ccf in range(8):
                        gq = ccf // 2
                        for cq in range(4):
                            S.op('pe', lambda ccf=ccf, cq=cq, gq=gq: PE.matmul(ps[ccf][:, cq * 128:(cq + 1) * 128], lhsT=vnt[:, cq, ccf * 128:(ccf + 1) * 128], rhs=wsT[:, gq, :],
                                                                               start=True, stop=False), r=[("sgu_vn", cq), "wsT"], w=[P(ccf)], sig=False)
                            S.op('pe', lambda ccf=ccf, cq=cq, gq=gq: PE.matmul(ps[ccf][:, cq * 128:(cq + 1) * 128], lhsT=ones_b[0:1, :], rhs=bs_row[0:1, gq * 128:(gq + 1) * 128],
                                                                               start=False, stop=True), w=[P(ccf)], sig=(cq == 3))
                        S.op('dve', lambda ccf=ccf: DVE.tensor_tensor(out=ybt[:, ccf, :], in0=ps[ccf][:, :], in1=gut[:, ccf, :], op=ALU.mult),
                             r=[P(ccf), "sgu_gu"], w=[("sgu_yb", ccf)])
                    S.dma('sp', gu_d.rearrange("f p t -> p f t")[:, :, 512 * g4:512 * (g4 + 1)], ybt[:], r=[("sgu_yb", i) for i in range(8)] + ["sgu_gu"], w=[("gu_d", g4)])
                S.barrier()
            Gall = sb(st, "Gall", [128, 4, 8], F32)
            Ug = [sb(st, f"Ug{i}", [128, 4, 514], F32) for i in range(2)]
            Tt = sb(st, "Tt", [128, 514], F32)
            for i in range(4):
                S.dma('sp', Gall[:, :, 2 * i:2 * i + 2], xout_v[i][:, :, 1028:1030], r=[("xout", i)], w=[("Gall", i)])
            S.op('act', lambda: ACT.activation(out=Gall[:], in_=Gall[:], func=AF.Exp), r=[("Gall", i) for i in range(4)], w=["Gall"], small=True)
            for hd in range(8):
                d = hd // 4
                U = Ug[hd % 2]
                utok = f"Ug{hd % 2}"
                S.dma('sp', U[:], xout_v[hd // 2][:, :, (hd % 2) * 514:(hd % 2 + 1) * 514], r=[("xout", hd // 2)], w=[utok])
                S.op('dve', lambda hd=hd: DVE.tensor_copy(out=Tt[:], in_=Sctx[:, hd, :]), r=["Sctx"], w=["Tt"])
                first = 0 if d == 0 else 3
                S.op('dve', lambda hd=hd, first=first: DVE.tensor_scalar(out=Sin[:, hd, :], in0=Tt[:], scalar1=metat[:, 1 + first:2 + first], scalar2=None, op0=ALU.mult),
                     r=["meta"], w=[("Sin", hd)])
                order = [0, 1, 2] if d == 0 else [3, 2, 1]
                for i in order:
                    tgt = i + 1 if d == 0 else i - 1
                    S.op('dve', lambda i=i, hd=hd, U=U: DVE.scalar_tensor_tensor(out=Tt[:], in0=Tt[:], scalar=Gall[:, i, hd:hd + 1], in1=U[:, i, :],
                                                                                  op0=ALU.mult, op1=ALU.add), r=[utok, "Gall"], w=["Tt"])
                    S.op('dve', lambda tgt=tgt, hd=hd: DVE.scalar_tensor_tensor(out=Sin[:, hd, :], in0=Tt[:], scalar=metat[:, 1 + tgt:2 + tgt], in1=Sin[:, hd, :],
                                                                                op0=ALU.mult, op1=ALU.add), w=[("Sin", hd)])
            dbg_dump("Sin", Sin[:], [("Sin", i) for i in range(8)])
            S.barrier()

        def mlstm_chunk(c, d, bufs, S16, emit_all, hook=None):
            q_t, kT_t, kt_t, v_t = bufs["q"], bufs["kT"], bufs["ktok"], bufs["v"]
            vt, vte, PM4, sm = bufs["vt"], bufs["vte"], bufs["PM4"], bufs["sm"]
            ltoks = bufs["ltoks"]
            mask = mL16 if d == 0 else mU16
            rs0, vs0, eg0, vse0 = 0 + 4 * d, 8 + 4 * d, 16 + 4 * d, 24 + 4 * d
            for h in range(4):
                S.op('dve', lambda h=h: DVE.tensor_scalar(out=vt[:, h, 0:256], in0=v_t[:, h * 256:(h + 1) * 256], scalar1=sc_all[:, c, vs0 + h:vs0 + h + 1],
                                                          scalar2=None, op0=ALU.mult), r=[ltoks["v"]], w=["vt"])
            S.op('dve', lambda: DVE.tensor_copy(out=vt[:, :, 256], in_=sc_all[:, c, vs0:vs0 + 4]), w=["vt"], small=True)
            for h in range(4):
                S.op('pool', lambda h=h: POOL.tensor_scalar(out=vte[:, h, 0:256], in0=v_t[:, h * 256:(h + 1) * 256], scalar1=sc_all[:, c, vse0 + h:vse0 + h + 1],
                                                            scalar2=0.0, op0=ALU.mult, op1=ALU.add), r=[ltoks["v"]], w=["vte"])
            S.op('pool', lambda: POOL.tensor_copy(out=vte[:, :, 256], in_=sc_all[:, c, vse0:vse0 + 4]), w=["vte"], small=True)
            for h in range(4):
                for kc in range(2):
                    S.op('pe', lambda kc=kc, h=h: PE.matmul(ps[0][:, h * 128:(h + 1) * 128], lhsT=kT_t[:, h * 2 + kc, :], rhs=q_t[:, h * 2 + kc, :],
                                                           start=(kc == 0), stop=(kc == 1)), r=[ltoks["kT"], ltoks["q"]], w=[P(0)], sig=(h == 3 and kc == 1))
            S.op('dve', lambda: DVE.tensor_tensor(out=PM4[:], in0=ps[0][:, :].rearrange("p (h t) -> p h t", h=4),
                                                  in1=cap(mask, 0, [[0, 4], [1, 128]]), op=ALU.mult), r=[P(0)], w=["PM4"])
            if hook:
                hook('s')
            for h in range(4):
                S.op('pe', lambda h=h: PE.matmul(ps[1 + h][:, 0:257], lhsT=PM4[:, h, :], rhs=vt[:, h, :], start=True, stop=False),
                     r=["PM4", "vt"], w=[P(1 + h)], sig=False)
                for kc in range(2):
                    S.op('pe', lambda kc=kc, h=h: PE.matmul(ps[1 + h][:, 0:257], lhsT=q_t[:, h * 2 + kc, :], rhs=S16[:, h, kc, :], start=False, stop=(kc == 1)),
                         r=[ltoks["q"], ("S16", h)], w=[P(1 + h)], sig=(kc == 1))
            if hook:
                hook('o')
            den4 = psall[:, 1:5, 256]
            rs4 = sc_all[:, c, rs0:rs0 + 4]
            PO = [P(1 + h) for h in range(4)]
            S.op('dve', lambda: DVE.tensor_tensor(out=sm[:, 0, :], in0=den4, in1=rs4, op=ALU.mult), r=PO, w=["sm"], small=True)
            S.op('dve', lambda: DVE.tensor_scalar(out=sm[:, 1, :], in0=sm[:, 0, :], scalar1=-1.0, scalar2=1.0, op0=ALU.mult, op1=ALU.max), w=["sm"], small=True)
            S.op('dve', lambda: DVE.tensor_scalar(out=sm[:, 2, :], in0=sm[:, 0, :], scalar1=1.0, scalar2=None, op0=ALU.max), w=["sm"], small=True)
            S.op('dve', lambda: DVE.tensor_tensor(out=sm[:, 2, :], in0=sm[:, 2, :], in1=sm[:, 1, :], op=ALU.max), w=["sm"], small=True)
            S.op('dve', lambda: DVE.reciprocal(out=sm[:, 3, :], in_=sm[:, 2, :]), w=["sm"], small=True)
            S.op('dve', lambda: DVE.tensor_tensor(out=sm[:, 4, :], in0=sm[:, 3, :], in1=rs4, op=ALU.mult), w=["sm"], small=True)
            emit_all(sm, 4)
            for h in range(4):
                hd = d * 4 + h
                for kc in range(2):
                    pU = 5 + kc
                    S.op('pe', lambda kc=kc, h=h, pU=pU: PE.matmul(ps[pU][:, 0:257], lhsT=kt_t[:, h * 256 + kc * 128:h * 256 + (kc + 1) * 128], rhs=vte[:, h, :],
                                                                  start=True, stop=True), r=[ltoks["ktok"], "vte"], w=[P(pU)])
                    S.op('dve', lambda kc=kc, hd=hd, pU=pU, h=h: DVE.scalar_tensor_tensor(
                        out=Sin[:, hd, kc * 257:(kc + 1) * 257], in0=Sin[:, hd, kc * 257:(kc + 1) * 257], scalar=sc_all[:, c, eg0 + h:eg0 + h + 1],
                        in1=ps[pU][:, 0:257], op0=ALU.mult, op1=ALU.add), r=[P(pU)], w=[("Sin", hd)])
                S.op('act', lambda h=h, hd=hd: ACT.activation(out=S16[:, h, :, :], in_=Sin[:, hd, :].rearrange("p (k v) -> p k v", k=2), func=AF.Copy, scale=0.0625),
                     r=[("Sin", hd)], w=[("S16", h)])
            if hook:
                hook('u')

        SINGLE = {"hb", "vs"}

        def ltok(name, slot):
            return f"ld_{name}" if name in SINGLE else f"ld_{name}{slot}"

        def issue_loads(lds, slot, c, extra=()):
            S.dma('sp', lds["q"][slot][:], qT_d.rearrange("f p t -> p f t")[:, :, c * 128:(c + 1) * 128], w=[f"ld_q{slot}"])
            S.dma('sp', lds["kT"][slot][:], kT_d.rearrange("f p t -> p f t")[:, :, c * 128:(c + 1) * 128], w=[f"ld_kT{slot}"])
            S.dma('sp', lds["ktok"][slot][:], ktok_d[c], w=[f"ld_ktok{slot}"])
            S.dma('sp', lds["v"][slot][:], v_d[c], w=[f"ld_v{slot}"])
            for (name, src) in extra:
                S.dma('sp', lds[name][slot][:], src, w=[ltok(name, slot)])

        def mk_bufs(lds, slot, common):
            b = dict(common)
            for n in lds:
                b[n] = lds[n][slot]
            b["ltoks"] = {n: ltok(n, slot) for n in lds}
            return b

        with ExitStack() as st:
          if KSTOP not in ('xchg', 'p1'):
                lds = sweep_loads(st, {"q": ([128, 8, 128], BF16), "kT": ([128, 8, 128], BF16), "ktok": ([128, D], BF16), "v": ([128, D], BF16)})
                common = {"vt": sb(st, "vt", [128, 4, 257], BF16), "vte": sb(st, "vte", [128, 4, 257], BF16),
                          "PM4": sb(st, "PM4", [128, 4, 128], BF16), "sm": sb(st, "sm", [128, 5, 4], F32)}
                S16 = sb(st, "S16", [128, 4, 2, 257], BF16)
                hbt = [sb(st, f"hbt{i}", [128, D], F32) for i in range(2)]
                for h in range(4):
                    S.op('act', lambda h=h: ACT.activation(out=S16[:, h, :, :], in_=Sin[:, 4 + h, :].rearrange("p (k v) -> p k v", k=2), func=AF.Copy, scale=0.0625),
                         w=[("S16", h)])
                issue_loads(lds, 0, 15)
                for it, c in enumerate(range(15, -1, -1)):
                    slot = it % 2
                    if c > 0:
                        issue_loads(lds, 1 - slot, c - 1)
                    hb = hbt[slot]

                    def emit_all(sm, row, hb=hb, slot=slot):
                        for h in range(4):
                            S.op('act', lambda h=h: ACT.activation(out=hb[:, h * 256:(h + 1) * 256], in_=ps[1 + h][:, 0:256], func=AF.Identity, scale=sm[:, row, h:h + 1]),
                                 r=[P(1 + h), "sm"], w=[(f"hbt{slot}", h)])
                    mlstm_chunk(c, 1, mk_bufs(lds, slot, common), S16, emit_all)
                    S.dma('sp', hb_d[c], hb[:], r=[(f"hbt{slot}", h) for h in range(4)], w=[("hb_d", c)])
                dbg_dump("Sfin", Sin[:], [("Sin", i) for i in range(8)])
                S.barrier()

        with ExitStack() as st:
          if KSTOP not in ('xchg', 'bwd', 'p1'):
                lds = sweep_loads(st, {"q": ([128, 8, 128], BF16), "kT": ([128, 8, 128], BF16), "ktok": ([128, D], BF16), "v": ([128, D], BF16),
                                       "og": ([128, 8, 128], BF16)})
                hb1 = sb(st, "hb1", [128, D], F32)
                lds["hb"] = [hb1, hb1]
                common = {"vt": sb(st, "vtc", [128, 4, 257], BF16), "vte": sb(st, "vtec", [128, 4, 257], BF16),
                          "PM4": sb(st, "PM4c", [128, 4, 128], BF16), "sm": sb(st, "smc", [128, 5, 4], F32)}
                S16 = sb(st, "S16c", [128, 4, 2, 257], BF16)
                hm_t = sb(st, "hm_t", [128, D], F32)
                hh = sb(st, "hh", [128, D], BF16)
                st6 = sb(st, "st6", [128, 4, 6], F32)
                mv = sb(st, "mv", [128, 4, 2], F32)
                lnr = sb(st, "lnr", [128, 4], F32)
                t1 = cap(hcT, 0, [[1, D]])
                gv = cap(hcT, D, [[1, D]])
                ssq = sb(st, "ssq", [128, 2], F32)
                st6b = sb(st, "st6b", [128, 2, 6], F32)
                mvb = sb(st, "mvb", [128, 4], F32)
                vn = sb(st, "vn", [128, D], BF16)
                yaT = sb(st, "yaT", [128, 8, 512], BF16)
                ybT = sb(st, "ybT", [128, 8, 512], BF16)
                mixT = sb(st, "mixT", [128, 8, 512], BF16)
                tA = [sb(st, "tA0", [128, 512], F32)] * 2
                tB = [sb(st, "tB0", [128, 512], F32)] * 2
                sga = [sb(st, f"sga{i}", [128, 512], BF16) for i in range(2)]
                sgb = [sb(st, f"sgb{i}", [128, 512], BF16) for i in range(2)]
                wbr = [sb(st, f"wbr{i}", [128, 8, 512], BF16) for i in range(2)]

                def extra_for(c):
                    return [("og", og_d.rearrange("f p t -> p f t")[:, :, c * 128:(c + 1) * 128])]

                for h in range(4):
                    S.op('act', lambda h=h: ACT.activation(out=S16[:, h, :, :], in_=Sin[:, h, :].rearrange("p (k v) -> p k v", k=2), func=AF.Copy, scale=0.0625),
                         w=[("S16", h)])
                pending = []

                def flush_items():
                    while pending:
                        pending.pop(0)[1]()

                def c_hook(stage):
                    n_ab = sum(1 for k, _ in pending if k == 'ab')
                    if n_ab > 0:
                        n = n_ab if stage == 'u' else min(3, n_ab)
                    else:
                        n = min(1, len(pending))
                    for _ in range(n):
                        pending.pop(0)[1]()

                issue_loads(lds, 0, 0, extra_for(0))
                S.dma('sp', hb1[:], hb_d[0], w=["ld_hb"])
                for c in range(16):
                    slot = c % 2
                    cc = c % 4
                    tile = c // 4
                    if c < 15:
                        issue_loads(lds, 1 - slot, c + 1, extra_for(c + 1))
                    b = mk_bufs(lds, slot, common)
                    hbl, ogl = b["hb"], b["og"]

                    def emit_all(sm, row, hbl=hbl):
                        for h in range(4):
                            S.op('dve', lambda h=h: DVE.scalar_tensor_tensor(out=hm_t[:, h * 256:(h + 1) * 256], in0=ps[1 + h][:, 0:256], scalar=sm[:, row, h:h + 1],
                                                                             in1=hbl[:, h * 256:(h + 1) * 256], op0=ALU.mult, op1=ALU.add),
                                 r=[P(1 + h), "sm", "ld_hb"], w=[("hm", h)], small=True)
                        for h in range(4):
                            S.op('dve', lambda h=h: DVE.bn_stats(out=st6[:, h, :], in_=hm_t[:, h * 256:(h + 1) * 256]), r=[("hm", h)], w=[("st6", h)], small=True)
                        for h in range(4):
                            S.op('dve', lambda h=h: DVE.bn_aggr(out=mv[:, h, :], in_=st6[:, h, :]), r=[("st6", h)], w=[("mv", h)], small=True)
                    mlstm_chunk(c, 0, b, S16, emit_all, hook=c_hook)
                    if c < 15:
                        S.dma('sp', hb1[:], hb_d[c + 1], w=["ld_hb"])
                    S.op('act', lambda: ACT.activation(out=lnr[:], in_=mv[:, :, 1], func=AF.Ln, bias=eps_t[:, 0:1], scale=1.0), r=[("mv", h) for h in range(4)], w=["lnr"], small=True)
                    S.op('act', lambda: ACT.activation(out=lnr[:], in_=lnr[:], func=AF.Exp, scale=-0.5), w=["lnr"], small=True)
                    for h in range(4):
                        S.op('dve', lambda h=h: DVE.tensor_scalar(out=hh[:, h * 256:(h + 1) * 256], in0=hm_t[:, h * 256:(h + 1) * 256], scalar1=mv[:, h, 0:1],
                                                                  scalar2=lnr[:, h:h + 1], op0=ALU.subtract, op1=ALU.mult), r=["lnr", ("mv", h)], w=["hh"], strict=True, small=True)
                    psb = ps[0][:, :].bitcast(BF16)
                    for fc in range(8):
                        S.op('pe', lambda fc=fc: PE.transpose(out=psb[:, fc * 128:(fc + 1) * 128], in_=hh[:, fc * 128:(fc + 1) * 128], identity=ident_b[:]),
                             r=["hh"], w=[P(0)], sig=(fc == 7))
                    S.op('dve', lambda cc=cc, ogl=ogl: DVE.tensor_tensor(out=yaT[:, :, cc * 128:(cc + 1) * 128], in0=psb[:, :].rearrange("p (f t) -> p f t", f=8),
                                                                          in1=ogl[:], op=ALU.mult), r=[P(0), f"ld_og{slot}"], w=[("yaT", cc)])
                    if cc == 0:
                        S.dma('sp', ybT[:], gu_d.rearrange("f p t -> p f t")[:, :, 512 * tile:512 * (tile + 1)], w=["ybT"])
                    if c in (0, 4):
                        dbg_dump(f"hm{c}", hm_t[:], [("hm", h) for h in range(4)])
                        dbg_dump(f"hh{c}", hh[:], ["hh"])
                    if cc < 3:
                        continue
                    if tile in (0, 1):
                        dbg_dump(f"ya{tile}", yaT[:], [("yaT", i) for i in range(4)])
                        dbg_dump(f"yb{tile}", ybT[:], ["ybT"])
                    flush_items()
                    t0 = 512 * tile
                    YA = [("yaT", i) for i in range(4)]
                    YB = ["ybT"]
                    MX = [("mixT", i) for i in range(8)]

                    def ab_item(dc, t0=t0, YA=YA, YB=YB):
                        blk, j = dc // 4, dc % 4
                        if j == 0:
                            wload(wbr[0][:], "wbr0", w_ba.rearrange("(kc p) n -> p kc n", p=128)[:, :, blk * 512:(blk + 1) * 512])
                            wload(wbr[1][:], "wbr1", w_bb.rearrange("(kc p) n -> p kc n", p=128)[:, :, blk * 512:(blk + 1) * 512])
                        s2 = dc % 2
                        S.dma('sp', sga[s2][:], ga_d[dc][:, t0:t0 + 512], w=[f"sga{s2}"])
                        S.dma('sp', sgb[s2][:], gb_d[dc][:, t0:t0 + 512], w=[f"sgb{s2}"])
                        pa, pb2 = 7, 0
                        for kc in range(8):
                            S.op('pe', lambda kc=kc: PE.matmul(ps[pa][:, :], lhsT=wbr[0][:, kc, j * 128:(j + 1) * 128], rhs=yaT[:, kc, :],
                                                               start=(kc == 0), stop=(kc == 7)), r=["wbr0"] + YA, w=[P(pa)], sig=(kc == 7))
                        for kc in range(8):
                            S.op('pe', lambda kc=kc: PE.matmul(ps[pb2][:, :], lhsT=wbr[1][:, kc, j * 128:(j + 1) * 128], rhs=ybT[:, kc, :],
                                                               start=(kc == 0), stop=(kc == 7)), r=["wbr1"] + YB, w=[P(pb2)], sig=(kc == 7))
                        S.op('dve', lambda: DVE.tensor_tensor(out=tA[s2][:], in0=ps[pa][:, :], in1=sga[s2][:], op=ALU.mult),
                             r=[P(pa), f"sga{s2}"], w=["tA0"])
                        S.op('dve', lambda: DVE.tensor_tensor(out=tB[s2][:], in0=ps[pb2][:, :], in1=sgb[s2][:], op=ALU.mult),
                             r=[P(pb2), f"sgb{s2}"], w=["tB0"])
                        S.op('pool', lambda: POOL.tensor_tensor(out=mixT[:, dc, :], in0=tA[s2][:], in1=tB[s2][:], op=ALU.add),
                             r=["tA0", "tB0"], w=[("mixT", dc)])

                    def out_item(dc, t0=t0, tile=tile, MX=MX):
                        blk, j = dc // 4, dc % 4
                        if j == 0:
                            wload(wbr[blk][:], f"wbr{blk}", w_o.rearrange("(kc p) n -> p kc n", p=128)[:, :, blk * 512:(blk + 1) * 512])
                        pb = 7 if dc % 2 == 0 else 0
                        for kc in range(8):
                            S.op('pe', lambda kc=kc: PE.matmul(ps[pb][:, :], lhsT=wbr[blk][:, kc, j * 128:(j + 1) * 128], rhs=mixT[:, kc, :],
                                                               start=(kc == 0), stop=(kc == 7)), r=[f"wbr{blk}"] + MX, w=[P(pb)], sig=(kc == 7))
                        S.op('dve', lambda: DVE.scalar_tensor_tensor(
                            out=hT[:, dc, 1 + t0:1 + t0 + 512], in0=ps[pb][:, :], scalar=prm[:, P_G5, dc:dc + 1], in1=hT[:, dc, 1 + t0:1 + t0 + 512],
                            op0=ALU.mult, op1=ALU.add), r=[P(pb)], w=[("hT2", tile, dc)])

                    for dc in range(8):
                        pending.append(('ab', lambda dc=dc, f=ab_item: f(dc)))
                    for dc in range(8):
                        pending.append(('out', lambda dc=dc, f=out_item: f(dc)))
                    if KITEMS == 0:
                        flush_items()
                flush_items()
                S.barrier()
        for nm, src in [("qT", qT_d), ("kT", kT_d), ("ktok", ktok_d), ("v", v_d), ("hb", hb_d), ("og", og_d), ("gu", gu_d), ("ga", ga_d), ("vs", vs_d)]:
            if nm in dbg:
                S.dma('sp', dbg[nm], src)
        if "h2" in dbg:
            S.dma('sp', dbg["h2"].rearrange("dc p t -> p dc t"), hT[:, :, :])
        S.barrier()
        mx.close()
        S.barrier()

        if KSTOP == "all":
            ffn(w_f2i, w_f2o, main_tiles, (P_GS2, P_SH2, P_GT2), (P_GS1C, P_SH1C, P_GT1C), "f2")

        if "h1" in dbg:
            S.dma('sp', dbg["h1"].rearrange("dc p t -> p dc t"), hT[:, :, :], r=[])
            S.dma('sp', dbg["hc1"].rearrange("dc p t -> p dc t"), hcT[:, :, :], r=[])
            S.barrier()

        def final_out():
            with ExitStack() as st:
                sq = sb(st, "fsq", [128, 8, 512], BF16)
                rstd = sb(st, "frstd", [128, 512], F32)
                yT = [sb(st, f"fy{i}", [128, 8, 512], F32) for i in range(2)]
                ot = [sb(st, f"fot{i}", [128, D], F32) for i in range(2)]
                for t in range(4):
                    o0 = 1 + 512 * t
                    y = yT[t % 2]
                    ytok = f"fy{t % 2}"
                    for dc in range(8):
                        S.op('act', lambda dc=dc: ACT.activation(out=sq[:, dc, :], in_=hT[:, dc, o0:o0 + 512], func=AF.Square), w=["fsq"])
                    for dc in range(8):
                        S.op('pe', lambda dc=dc: PE.matmul(ps[7][:, :], lhsT=ones_b[:], rhs=sq[:, dc, :], start=(dc == 0), stop=(dc == 7)),
                             r=["fsq"], w=[P(7)], sig=(dc == 7))
                    S.op('act', lambda: ACT.activation(out=rstd[:], in_=ps[7][:, :], func=AF.Ln, scale=1.0 / D, bias=eps_t[:, 0:1]), r=[P(7)], w=["frstd"])
                    S.op('act', lambda: ACT.activation(out=rstd[:], in_=rstd[:], func=AF.Exp, scale=-0.5), w=["frstd"])
                    for dc in range(8):
                        S.op('dve', lambda dc=dc, y=y: DVE.scalar_tensor_tensor(out=y[:, dc, :], in0=hT[:, dc, o0:o0 + 512], scalar=vecT[:, R_GFIN + dc:R_GFIN + dc + 1],
                                                                              in1=rstd[:], op0=ALU.mult, op1=ALU.mult),
                             r=["frstd"], w=[(ytok, dc)])
                    for cc in range(4):
                        c = 4 * t + cc
                        o = ot[c % 2]
                        for half in range(2):
                            pb = 2 + half + 2 * (c % 2)
                            for k4 in range(4):
                                dc = half * 4 + k4
                                S.op('pe', lambda dc=dc, k4=k4, pb=pb, y=y, cc=cc: PE.transpose(out=ps[pb][:, k4 * 128:(k4 + 1) * 128], in_=y[:, dc, cc * 128:(cc + 1) * 128],
                                                                                          identity=ident_f[:]),
                                     r=[(ytok, dc)], w=[P(pb)], sig=(k4 == 3))
                            if half == 0:
                                S.op('act', lambda pb=pb, o=o: ACT.copy(out=o[:, 0:512], in_=ps[pb][:, :]), r=[P(pb)], w=[f"fot{c % 2}a"])
                            else:
                                S.op('dve', lambda pb=pb, o=o: DVE.tensor_copy(out=o[:, 512:1024], in_=ps[pb][:, :]), r=[P(pb)], w=[f"fot{c % 2}b"])
                        S.dma('sp', out[128 * c:128 * (c + 1), :], o[:], r=[f"fot{c % 2}a", f"fot{c % 2}b"], w=[])
            S.barrier()

        final_out()
    return nc


def _host_inputs(inp):
    x = np.ascontiguousarray(inp["x"], dtype=np.float32)
    f32 = np.float32
    vec_common = [
        inp["b_ada"][0].reshape(72, 128), inp["g_ffn1"][0].reshape(8, 128), inp["g_mix"][0].reshape(8, 128),
        inp["conv_qk_w"][0].reshape(48, 128), inp["conv_qk_b"][0].reshape(16, 128), inp["g_head"][0].reshape(8, 128),
        inp["g_ffn2"][0].reshape(8, 128), inp["g_final"].reshape(8, 128)]
    quarter = D // 4
    fr = np.exp(-math.log(10000.0) * np.arange(quarter, dtype=f32) / quarter).astype(f32)
    freq = np.ascontiguousarray(fr.reshape(2, 128).T)
    consts = np.zeros((128, 3, 128), f32)
    consts[:, 0, :] = np.eye(128, dtype=f32)
    ii = np.arange(128)
    consts[:, 1, :] = (ii[:, None] <= ii[None, :]).astype(f32)
    consts[:, 2, :] = (ii[:, None] >= ii[None, :]).astype(f32)
    shared = dict(
        consts=consts, freq=freq, g_sgu=np.ascontiguousarray(inp["g_sgu"][0]), b_gates=np.ascontiguousarray(inp["b_gates"][0]),
        b_s=np.ascontiguousarray(inp["b_s"][0].reshape(512)), w_s=np.ascontiguousarray(inp["w_s"][0]),
        w_ffn1_in=np.ascontiguousarray(inp["w_ffn1_in"][0]),
        w_ffn1_out=np.ascontiguousarray(inp["w_ffn1_out"][0]), w_in=np.ascontiguousarray(inp["w_in"][0]),
        w_branch_a=np.ascontiguousarray(inp["w_branch_a"][0]), w_branch_b=np.ascontiguousarray(inp["w_branch_b"][0]),
        w_out=np.ascontiguousarray(inp["w_out"][0]), w_ffn2_in=np.ascontiguousarray(inp["w_ffn2_in"][0]),
        w_ffn2_out=np.ascontiguousarray(inp["w_ffn2_out"][0]))
    maps = []
    for core in range(8):
        b, j = core // 4, core % 4
        a = j * NT
        xs = np.zeros((NX, D), f32)
        xs[1:NT + 1] = x[b, a:a + NT]
        if j > 0:
            xs[0] = x[b, a - 1]
        if j < 3:
            xs[NX - 1] = x[b, a + NT]
        vecs = np.concatenate(vec_common + [inp["c"][b].reshape(8, 128), inp["c_ctx"].reshape(8, 128)], axis=0).astype(f32)
        meta = np.zeros((128, 8), f32)
        meta[:, 0] = j * 32 - 1
        meta[:, 1 + j] = 1.0
        meta[:, 5] = 1.0 if j > 0 else 0.0
        meta[:, 6] = 1.0 if j < 3 else 0.0
        m = dict(shared)
        m.update(w_ada=np.ascontiguousarray(inp["w_ada"][0][:, j * 2304:(j + 1) * 2304]), xs=xs, ctxb=np.ascontiguousarray(inp["ctx"][b], dtype=f32), vecs=np.ascontiguousarray(vecs), meta=meta)
        maps.append(m)
    return maps


_NC_CACHE = {}


def kernel(**inputs):
    inp = {k: np.asarray(v) for k, v in inputs.items()}
    maps = _host_inputs(inp)
    if "nc" not in _NC_CACHE:
        _NC_CACHE["nc"] = build()
    res = run_bass_kernel_spmd(_NC_CACHE["nc"], maps, core_ids=list(range(8)))
    outp = np.zeros((2, 4 * NT, D), np.float32)
    for core in range(8):
        b, j = core // 4, core % 4
        outp[b, j * NT:(j + 1) * NT] = res.results[core]["out"]
    kernel.last_results = res.results
    return outp
```

```python
import math
from contextlib import ExitStack

import numpy as np
import concourse.bass as bass
import concourse.mybir as mybir
from concourse.bass_utils import run_bass_kernel_spmd

F32 = mybir.dt.float32
BF16 = mybir.dt.bfloat16
I32 = mybir.dt.int32
AF = mybir.ActivationFunctionType
ALU = mybir.AluOpType

D = 1024
NT = 2048
NX = NT + 2
NCTX = 256
NCH = 16
DFF = 2816
NFF = 22
DPROJ = 8208
EPS = 1e-6
TWO_PI = 2.0 * math.pi
PI_SAFE = 3.1415925
XW = 8 * 514 + 8

DEBUG = {}
import os
KSTOP = os.environ.get("KSTOP", "all")
KITEMS = int(os.environ.get("KITEMS", "1"))


class Sch:
    NDS = 40

    def __init__(self, nc, es):
        self.nc = nc
        self.E = {'pe': nc.tensor, 'act': nc.scalar, 'dve': nc.vector, 'pool': nc.gpsimd, 'sp': nc.sync}
        self.semobj = {}
        for e in ['pe', 'act', 'dve', 'pool']:
            self.semobj[e] = es.enter_context(nc.semaphore(f"sem_{e}"))
        self.cnt = {e: 0 for e in ['pe', 'act', 'dve', 'pool']}
        self.seen = {e: {} for e in self.E}
        self.lw = {}
        self.rd = {}
        self.pend = {e: [] for e in self.cnt}
        self.dcnt = [0] * self.NDS
        for i in range(self.NDS):
            self.semobj[('d', i)] = es.enter_context(nc.semaphore(f"sem_d{i}"))
        self.dnext = 0
        self.nops = 0
        self.semobj['cc'] = es.enter_context(nc.semaphore("sem_cc"))
        self.cccnt = 0
        self.smallp = set()

    def _deps(self, eng, r, w):
        deps = set()
        for t in r:
            d = self.lw.get(t)
            if d is not None:
                deps.add((d, True))
        for t in w:
            d = self.lw.get(t)
            if d is not None:
                deps.add((d, True))
            for d in self.rd.get(t, ()):
                deps.add((d, False))
        return deps

    def _wait(self, eng, deps, strict=False):
        for (d, is_w) in deps:
            if d[0] == 'PEND':
                assert d[1] == eng, f"dependency on unsignaled op of {d[1]} from {eng}"
                continue
            key, val, src = d
            if src == eng:
                if eng == 'pe' or not is_w:
                    continue
                if not (strict or (key, val) in self.smallp):
                    continue
            if self.seen[eng].get(key, 0) >= val:
                continue
            self.E[eng].wait_ge(self.semobj[key], val)
            self.seen[eng][key] = val

    def op(self, eng, fn, r=(), w=(), sig=True, strict=False, small=False):
        strict = strict or small
        self._wait(eng, self._deps(eng, r, w), strict)
        ins = fn()
        self.nops += 1
        if sig:
            self.cnt[eng] += 1
            ins.then_inc(self.semobj[eng], 1)
            me = (eng, self.cnt[eng], eng)
            if small:
                self.smallp.add((eng, self.cnt[eng]))
            for (pr, pw) in self.pend[eng] + [(r, w)]:
                for t in pw:
                    self.lw[t] = me
                    self.rd[t] = []
            pm = ('PEND', eng)
            for (pr, pw) in self.pend[eng] + [(r, w)]:
                for t in pr:
                    lst = self.rd.setdefault(t, [])
                    if pm in lst:
                        lst[:] = [d for d in lst if d != pm]
                    if me not in lst:
                        lst.append(me)
            self.pend[eng] = []
        else:
            self.pend[eng].append((tuple(r), tuple(w)))
            for t in w:
                self.lw[t] = ('PEND', eng)
                self.rd[t] = []
            for t in r:
                self.rd.setdefault(t, []).append(('PEND', eng))
        return ins

    def dma(self, q, out, in_, r=(), w=(), **kw):
        deps = self._deps(q, r, w)
        idx = self.dnext
        self.dnext = (self.dnext + 1) % self.NDS
        if self.dcnt[idx] > 0:
            deps.add(((('d', idx), self.dcnt[idx], 'dma'), True))
        self._wait(q, deps)
        self.dcnt[idx] += 16
        self.E[q].dma_start(out=out, in_=in_, **kw).then_inc(self.semobj[('d', idx)], 16)
        me = (('d', idx), self.dcnt[idx], 'dma')
        for t in w:
            self.lw[t] = me
            self.rd[t] = []
        for t in r:
            self.rd.setdefault(t, []).append(me)

    def custom(self, eng, fn, inc, r=(), w=()):
        deps = self._deps(eng, r, w)
        self._wait(eng, deps)
        self.cccnt += inc
        fn(self.semobj['cc'])
        me = ('cc', self.cccnt, 'dma')
        for t in w:
            self.lw[t] = me
            self.rd[t] = []
        for t in r:
            self.rd.setdefault(t, []).append(me)

    def barrier(self):
        for e in self.cnt:
            assert not self.pend[e], f"pending unsignaled ops on {e} at barrier"
        deps = set()
        for e in self.cnt:
            if self.cnt[e] > 0:
                deps.add(((e, self.cnt[e], e), False))
        for i in range(self.NDS):
            if self.dcnt[i] > 0:
                deps.add(((('d', i), self.dcnt[i], 'dma'), True))
        if self.cccnt > 0:
            deps.add((('cc', self.cccnt, 'dma'), True))
        for e in self.E:
            self._wait(e, deps)
        self.lw = {}
        self.rd = {}


def build():
    nc = bass.Bass("TRN2", target_bir_lowering=False)

    def din(name, shape, dt=F32):
        return nc.dram_tensor(name, list(shape), dt, kind="ExternalInput").ap()

    xs = din("xs", [NX, D])
    ctxb = din("ctxb", [NCTX, D])
    vecs = din("vecs", [192, 128])
    meta = din("meta", [128, 8])
    freq = din("freq", [128, 2])
    consts = din("consts", [128, 3, 128])
    g_sgu = din("g_sgu", [D])
    b_gates = din("b_gates", [16])
    b_s = din("b_s", [512])
    w_s = din("w_s", [4, 128, 128])
    w_ada = din("w_ada", [D, 9 * D // 4])
    w_f1i = din("w_ffn1_in", [D, 2 * DFF])
    w_f1o = din("w_ffn1_out", [DFF, D])
    w_in = din("w_in", [D, DPROJ])
    w_ba = din("w_branch_a", [D, D])
    w_bb = din("w_branch_b", [D, D])
    w_o = din("w_out", [D, D])
    w_f2i = din("w_ffn2_in", [D, 2 * DFF])
    w_f2o = din("w_ffn2_out", [DFF, D])
    out = nc.dram_tensor("out", [NT, D], F32, kind="ExternalOutput").ap()

    def dscr(name, shape, dt):
        return nc.dram_tensor(name, list(shape), dt).ap()

    qT_d = dscr("qT_d", [8, 128, NT], BF16)
    kT_d = dscr("kT_d", [8, 128, NT + NCTX], BF16)
    ktok_d = dscr("ktok_d", [18, 128, D], BF16)
    v_d = dscr("v_d", [18, 128, D], BF16)
    vs_d = dscr("vs_d", [16, 128, D], F32)
    og_d = dscr("og_d", [8, 128, NT], BF16)
    gu_d = dscr("gu_d", [8, 128, NT], BF16)
    ga_d = dscr("ga_d", [8, 128, NT], BF16)
    gb_d = dscr("gb_d", [8, 128, NT], BF16)
    hb_d = dscr("hb_d", [16, 128, D], F32)
    mp_in = dscr("mp_in", [128, 36], F32)
    mp_out = dscr("mp_out", [4 * 128, 36], F32)
    xin_l = [dscr(f"xin_d{i}", [128, 1030], F32) for i in range(4)]
    xout_l = [dscr(f"xout_d{i}", [4 * 128, 1030], F32) for i in range(4)]

    dbg = {}
    for name, (shape, dt) in DEBUG.items():
        dbg[name] = nc.dram_tensor("dbg_" + name, list(shape), dt, kind="ExternalOutput").ap()

    with ExitStack() as es:
        S = Sch(nc, es)
        ACT, DVE, PE, POOL = nc.scalar, nc.vector, nc.tensor, nc.gpsimd

        def sb(stack, name, shape, dt):
            return stack.enter_context(nc.sbuf_tensor(name, list(shape), dt))

        def pstep(t):
            return t[:].ap[0][0]

        def cap(t, off, dims):
            return bass.AP(t, off, [[pstep(t), 128]] + [list(d) for d in dims])

        psall = es.enter_context(nc.psum_tensor("psall", [128, 8, 512], F32))
        ps = [psall[:, i, :] for i in range(8)]

        def P(i):
            return ("ps", i)

        ident_f = sb(es, "ident_f", [128, 128], F32)
        triL = sb(es, "triL", [128, 128], F32)
        triU = sb(es, "triU", [128, 128], F32)
        ident_b = sb(es, "ident_b", [128, 128], BF16)
        mL16 = sb(es, "mL16", [128, 128], F32)
        mU16 = sb(es, "mU16", [128, 128], F32)
        ones_f = sb(es, "ones_f", [128, 128], F32)
        ones_b = sb(es, "ones_b", [128, 128], BF16)
        vecT = sb(es, "vecT", [128, 192], F32)
        metat = sb(es, "metat", [128, 8], F32)
        modB = sb(es, "modB", [128, 72], F32)
        modC = sb(es, "modC", [128, 72], F32)
        prm = sb(es, "prm", [128, 16, 8], F32)
        hT = sb(es, "hT", [128, 8, NX], F32)
        hcT = sb(es, "hcT", [128, 8, NCTX], F32)
        eps_t = sb(es, "eps_t", [128, 1], F32)

        R_BADA, R_GF1, R_GMIX, R_CW, R_CB, R_GH, R_GF2, R_GFIN, R_C, R_CC = 0, 72, 80, 88, 136, 152, 160, 168, 176, 184
        (P_GS1, P_SH1, P_GT1, P_GS1C, P_SH1C, P_GT1C, P_GSM, P_SHM, P_GSMC, P_SHMC, P_G5, P_GS2, P_SH2, P_GT2) = range(14)

        S.dma('sp', ident_f[:], consts[:, 0, :], w=["ident_f"])
        S.dma('sp', triL[:], consts[:, 1, :], w=["triL"])
        S.dma('sp', triU[:], consts[:, 2, :], w=["triU"])
        S.dma('sp', metat[:], meta, w=["meta"])
        S.op('dve', lambda: DVE.tensor_copy(out=ident_b[:], in_=ident_f[:]), r=["ident_f"], w=["ident_b"])
        S.op('dve', lambda: DVE.tensor_scalar(out=mL16[:], in0=triL[:], scalar1=0.0625, scalar2=None, op0=ALU.mult), r=["triL"], w=["mL16"])
        S.op('dve', lambda: DVE.tensor_scalar(out=mU16[:], in0=triU[:], scalar1=0.0625, scalar2=None, op0=ALU.mult), r=["triU"], w=["mU16"])
        S.op('dve', lambda: DVE.memset(ones_f[:], 1.0), w=["ones_f"])
        S.op('dve', lambda: DVE.memset(ones_b[:], 1.0), w=["ones_b"])

        def wload(dst_tile, dst_tok, src_ap):
            S.dma('pool', dst_tile, src_ap, w=[dst_tok])

        with ExitStack() as p1:
            vst = sb(p1, "vst", [96, 2, 128], F32)
            S.dma('sp', vst[:, 0, :], vecs[0:96, :], w=["vst0"])
            S.dma('sp', vst[:, 1, :], vecs[96:192, :], w=["vst1"])
            for i in range(2):
                S.op('pe', lambda i=i: PE.transpose(out=ps[0][:, i * 96:(i + 1) * 96], in_=vst[:, i, :], identity=ident_f[0:96, 0:96]),
                     r=[f"vst{i}", "ident_f"], w=[P(0)], sig=(i == 1))
            S.op('dve', lambda: DVE.tensor_copy(out=vecT[:], in_=ps[0][:, 0:192]), r=[P(0)], w=["vecT"])

            scT = sb(p1, "scT", [128, 8, 2], F32)
            S.op('act', lambda: ACT.activation(out=scT[:, :, 0], in_=vecT[:, R_C:R_C + 8], func=AF.Silu), r=["vecT"], w=["scT0"])
            S.op('act', lambda: ACT.activation(out=scT[:, :, 1], in_=vecT[:, R_CC:R_CC + 8], func=AF.Silu), r=["vecT"], w=["scT1"])

            wad = [sb(p1, f"wad{i}", [128, 8, 256], F32) for i in range(3)]
            w_ada_v = w_ada.rearrange("(kc p) n -> p kc n", p=128)
            for blk in range(9):
                slot = blk % 3
                S.dma('sp', wad[slot][:], w_ada_v[:, :, blk * 256:(blk + 1) * 256], w=[f"wad{slot}"])
                for j in range(2):
                    b128 = blk * 2 + j
                    for kc in range(8):
                        S.op('pe', lambda slot=slot, j=j, kc=kc, b128=b128: PE.matmul(
                            ps[1][:, b128 * 2:b128 * 2 + 2], lhsT=wad[slot][:, kc, j * 128:(j + 1) * 128], rhs=scT[:, kc, :],
                            start=(kc == 0), stop=(kc == 7)),
                            r=[f"wad{slot}", "scT0", "scT1"], w=[P(1)], sig=(kc == 7 and j == 1))
            mpart = sb(p1, "mpart", [128, 36], F32)
            mg = sb(p1, "mg", [128, 4, 36], F32)
            S.op('dve', lambda: DVE.tensor_copy(out=mpart[:], in_=ps[1][:, 0:36]), r=[P(1)], w=["mpart"])
            S.dma('sp', mp_in, mpart[:], r=["mpart"], w=["mp_in"])
            S.custom('pool', lambda sem: POOL.collective_compute("AllGather", ALU.bypass, replica_groups=[[0, 1, 2, 3], [4, 5, 6, 7]],
                                                                 ins=[mp_in.opt()], outs=[mp_out.opt()]).then_inc(sem, 1),
                     1, r=["mp_in"], w=["mp_out"])
            S.dma('sp', mg[:], mp_out.rearrange("(r p) w -> p r w", p=128), r=["mp_out"], w=["mg"])
            psm = mg[:, :, :].rearrange("p r (b t) -> p (r b) t", t=2)
            S.op('dve', lambda: DVE.tensor_tensor(out=modB[:], in0=psm[:, :, 0], in1=vecT[:, R_BADA:R_BADA + 72], op=ALU.add), r=["mg", "vecT"], w=["modB"], small=True)
            S.op('dve', lambda: DVE.tensor_tensor(out=modC[:], in0=psm[:, :, 1], in1=vecT[:, R_BADA:R_BADA + 72], op=ALU.add), r=["mg", "vecT"], w=["modC"], small=True)

            def mk_gs(slot, gain_row, mod, scale_idx):
                S.op('dve', lambda: DVE.scalar_tensor_tensor(out=prm[:, slot, :], in0=mod[:, scale_idx * 8:scale_idx * 8 + 8], scalar=1.0,
                                                             in1=vecT[:, gain_row:gain_row + 8], op0=ALU.add, op1=ALU.mult),
                     r=["modB", "modC", "vecT"], w=[("prm", slot)], small=True)

            def mk_cp(slot, mod, idx, mul=1.0):
                S.op('dve', lambda: DVE.tensor_scalar(out=prm[:, slot, :], in0=mod[:, idx * 8:idx * 8 + 8], scalar1=mul, scalar2=None, op0=ALU.mult),
                     r=["modB", "modC"], w=[("prm", slot)], small=True)

            mk_gs(P_GS1, R_GF1, modB, 1); mk_cp(P_SH1, modB, 0); mk_cp(P_GT1, modB, 2, 0.5)
            mk_gs(P_GS1C, R_GF1, modC, 1); mk_cp(P_SH1C, modC, 0); mk_cp(P_GT1C, modC, 2, 0.5)
            mk_gs(P_GSM, R_GMIX, modB, 4); mk_cp(P_SHM, modB, 3)
            mk_gs(P_GSMC, R_GMIX, modC, 4); mk_cp(P_SHMC, modC, 3)
            mk_cp(P_G5, modB, 5)
            mk_gs(P_GS2, R_GF2, modB, 7); mk_cp(P_SH2, modB, 6); mk_cp(P_GT2, modB, 8, 0.5)

            fq = sb(p1, "fq", [128, 2], F32)
            S.dma('sp', fq[:], freq, w=["fq"])
            tab_r = sb(p1, "tab_r", [128, 4, 34], F32)
            tab_c = sb(p1, "tab_c", [128, 4, 64], F32)
            io_i = sb(p1, "io_i", [128, 64], I32)
            io_f = sb(p1, "io_f", [128, 64], F32)
            rv = sb(p1, "rv", [128, 34], F32)
            S.op('pool', lambda: POOL.iota(io_i[:], pattern=[[1, 64]], base=0, channel_multiplier=0), w=["io_i"])
            S.op('dve', lambda: DVE.tensor_copy(out=io_f[:], in_=io_i[:]), r=["io_i"], w=["io_f"], small=True)
            S.op('dve', lambda: DVE.tensor_scalar(out=rv[:], in0=io_f[:, 0:34], scalar1=metat[:, 0:1], scalar2=None, op0=ALU.add), r=["io_f", "meta"], w=["rv"], small=True)

            def mk_tab(tab, vals, n):
                arg = sb(p1, f"arg_{n}", [128, 4, n], F32)
                ki = sb(p1, f"ki_{n}", [128, 4, n], I32)
                kf = sb(p1, f"kf_{n}", [128, 4, n], F32)
                for cj in range(2):
                    for sc in range(2):
                        idx = sc * 2 + cj
                        S.op('dve', lambda idx=idx, cj=cj, sc=sc: DVE.tensor_scalar(
                            out=arg[:, idx, :], in0=vals, scalar1=fq[:, cj:cj + 1], scalar2=(0.5 * math.pi if sc else 0.0),
                            op0=ALU.mult, op1=ALU.add), r=["rv", "io_f", "fq"], w=[f"arg{n}"], small=True)
                S.op('dve', lambda: DVE.tensor_scalar(out=kf[:], in0=arg[:], scalar1=1.0 / TWO_PI, scalar2=None, op0=ALU.mult), r=[f"arg{n}"], w=[f"kf{n}"], small=True)
                S.op('dve', lambda: DVE.tensor_copy(out=ki[:], in_=kf[:]), r=[f"kf{n}"], w=[f"ki{n}"], small=True)
                S.op('dve', lambda: DVE.tensor_copy(out=kf[:], in_=ki[:]), r=[f"ki{n}"], w=[f"kf{n}"], small=True)
                S.op('dve', lambda: DVE.scalar_tensor_tensor(out=arg[:], in0=kf[:], scalar=-TWO_PI, in1=arg[:], op0=ALU.mult, op1=ALU.add),
                     r=[f"kf{n}"], w=[f"arg{n}"], small=True)
                S.op('dve', lambda: DVE.tensor_scalar(out=arg[:], in0=arg[:], scalar1=-PI_SAFE, scalar2=PI_SAFE, op0=ALU.max, op1=ALU.min), w=[f"arg{n}"], small=True)
                S.op('act', lambda: ACT.activation(out=tab[:], in_=arg[:], func=AF.Sin), r=[f"arg{n}"], w=[f"tab{n}"], small=True)

            mk_tab(tab_r, rv[:], 34)
            mk_tab(tab_c, io_f[:], 64)

            xt = [sb(p1, f"xt{i}", [128, D], F32) for i in range(2)]
            xh = sb(p1, "xh", [2, D], F32)
            for c in range(NCH):
                slot = c % 2
                S.dma('sp', xt[slot][:], xs[1 + 128 * c:1 + 128 * (c + 1), :], w=[f"xt{slot}"])
                for half in range(2):
                    pb = 2 + half
                    for k4 in range(4):
                        dc = half * 4 + k4
                        S.op('pe', lambda slot=slot, dc=dc, pb=pb, k4=k4: PE.transpose(
                            out=ps[pb][:, k4 * 128:(k4 + 1) * 128], in_=xt[slot][:, dc * 128:(dc + 1) * 128], identity=ident_f[:]),
                            r=[f"xt{slot}", "ident_f"], w=[P(pb)], sig=(k4 == 3))
                    o_ap = cap(hT, half * 4 * NX + 1 + 128 * c, [[NX, 4], [64, 2], [1, 64]])
                    i_ap = ps[pb][:, :].rearrange("p (a b c) -> p a b c", a=4, b=2, c=64)
                    if half == 0:
                        t_ap = cap(tab_r, 1 + 2 * c, [[34, 4], [1, 2], [0, 64]])
                        S.op('dve', lambda o_ap=o_ap, i_ap=i_ap, t_ap=t_ap: DVE.tensor_tensor(out=o_ap, in0=i_ap, in1=t_ap, op=ALU.add),
                             r=[P(pb), "tab34"], w=[("hT", c)])
                    else:
                        t_ap = cap(tab_c, 0, [[64, 4], [0, 2], [1, 64]])
                        S.op('dve', lambda o_ap=o_ap, i_ap=i_ap, t_ap=t_ap: DVE.tensor_tensor(out=o_ap, in0=i_ap, in1=t_ap, op=ALU.add),
                             r=[P(pb), "tab64"], w=[("hT", c)])
            S.dma('sp', xh[0:1, :], xs[0:1, :], w=["xh0"])
            S.dma('sp', xh[1:2, :], xs[NX - 1:NX, :], w=["xh1"])
            for dc in range(8):
                S.op('pe', lambda dc=dc: PE.transpose(out=ps[2][:, dc * 2:dc * 2 + 2], in_=xh[:, dc * 128:(dc + 1) * 128], identity=ident_f[0:2, 0:2]),
                     r=["xh0", "xh1", "ident_f"], w=[P(2)], sig=(dc == 7))
            pv = ps[2][:, 0:16].rearrange("p (d t) -> p d t", t=2)
            S.op('dve', lambda: DVE.tensor_tensor(out=cap(hT, 0, [[NX, 4]]), in0=pv[:, 0:4, 0], in1=tab_r[:, :, 0], op=ALU.add), r=[P(2), "tab34"], w=[("hT", "h0a")])
            S.op('dve', lambda: DVE.tensor_tensor(out=cap(hT, 4 * NX, [[NX, 4]]), in0=pv[:, 4:8, 0], in1=tab_c[:, :, 63], op=ALU.add), r=[P(2), "tab64"], w=[("hT", "h0b")])
            S.op('dve', lambda: DVE.tensor_tensor(out=cap(hT, NX - 1, [[NX, 4]]), in0=pv[:, 0:4, 1], in1=tab_r[:, :, 33], op=ALU.add), r=[P(2), "tab34"], w=[("hT", "h1a")])
            S.op('dve', lambda: DVE.tensor_tensor(out=cap(hT, 4 * NX + NX - 1, [[NX, 4]]), in0=pv[:, 4:8, 1], in1=tab_c[:, :, 0], op=ALU.add), r=[P(2), "tab64"], w=[("hT", "h1b")])
            for c in range(2):
                slot = c % 2
                S.dma('sp', xt[slot][:], ctxb[128 * c:128 * (c + 1), :], w=[f"xt{slot}"])
                for half in range(2):
                    pb = 2 + half
                    for k4 in range(4):
                        dc = half * 4 + k4
                        S.op('pe', lambda slot=slot, dc=dc, pb=pb, k4=k4: PE.transpose(
                            out=ps[pb][:, k4 * 128:(k4 + 1) * 128], in_=xt[slot][:, dc * 128:(dc + 1) * 128], identity=ident_f[:]),
                            r=[f"xt{slot}", "ident_f"], w=[P(pb)], sig=(k4 == 3))
                    S.op('dve', lambda pb=pb, half=half, c=c: DVE.tensor_copy(
                        out=hcT[:, half * 4:half * 4 + 4, c * 128:(c + 1) * 128], in_=ps[pb][:, :].rearrange("p (a b) -> p a b", a=4)),
                        r=[P(pb)], w=[("hcT", c)])
            S.barrier()

        def norm_mod(stk, src, src_off, n, gs_slot, sh_slot, dst, dst_off, uid, stride=1):
            sq, rstd, tmp = stk["sq"], stk["rstd"], stk["tmp"]
            ssrc = src.shape[2]
            sdst = dst.shape[2]

            def s_ap(dc):
                return cap(src, dc * ssrc + src_off, [[stride, n]])

            for dc in range(8):
                S.op('act', lambda dc=dc: ACT.activation(out=sq[:, dc, 0:n], in_=s_ap(dc), func=AF.Square), r=[("src", uid)], w=["sq"])
            for dc in range(8):
                S.op('pe', lambda dc=dc: PE.matmul(ps[7][:, 0:n], lhsT=ones_b[:], rhs=sq[:, dc, 0:n], start=(dc == 0), stop=(dc == 7)),
                     r=["sq", "ones_b"], w=[P(7)], sig=(dc == 7))
            S.op('act', lambda: ACT.activation(out=rstd[:, 0:n], in_=ps[7][:, 0:n], func=AF.Ln, scale=1.0 / D, bias=eps_t[:, 0:1]), r=[P(7)], w=["rstd"], small=(n < 256))
            S.op('act', lambda: ACT.activation(out=rstd[:, 0:n], in_=rstd[:, 0:n], func=AF.Exp, scale=-0.5), w=["rstd"], small=(n < 256))
            for dc in range(8):
                tt = tmp[dc % 2]
                S.op('dve', lambda dc=dc, tt=tt: DVE.scalar_tensor_tensor(out=tt[:, 0:n], in0=s_ap(dc), scalar=prm[:, gs_slot, dc:dc + 1], in1=rstd[:, 0:n],
                                                                         op0=ALU.mult, op1=ALU.mult),
                     r=[("src", uid), "rstd", ("prm", gs_slot)], w=[f"nm_tmp{dc % 2}"])
                S.op('act', lambda dc=dc, tt=tt: ACT.activation(out=dst[:, dc, dst_off:dst_off + n], in_=tt[:, 0:n], func=AF.Identity,
                                                                bias=prm[:, sh_slot, dc:dc + 1], scale=1.0),
                     r=[f"nm_tmp{dc % 2}", ("prm", sh_slot)], w=[("hn", uid)])

        S.op('dve', lambda: DVE.memset(eps_t[:], EPS), w=["eps_t"])
        S.barrier()

        def ffn(w_i, w_o2, tiles, prm_main, prm_ctx, tag):
            with ExitStack() as st:
                hnT = sb(st, "hnT" + tag, [128, 8, NX], BF16)
                hncT = sb(st, "hncT" + tag, [128, 8, NCTX], BF16)
                stk = {"sq": sb(st, "sq" + tag, [128, 8, 512], BF16), "rstd": sb(st, "rstd" + tag, [128, 512], F32),
                       "tmp": [sb(st, f"nmt{i}" + tag, [128, 512], F32) for i in range(2)]}
                ntile = len(tiles)
                for ti, (kind, off, n) in enumerate(tiles):
                    if kind == 'm':
                        norm_mod(stk, hT, off, n, prm_main[0], prm_main[1], hnT, off, (tag, ti))
                    else:
                        norm_mod(stk, hcT, off, n, prm_ctx[0], prm_ctx[1], hncT, off, (tag, ti))
                GRP = 6
                groups = [(0, 6), (6, 6), (12, 6), (18, 4)]
                zT = sb(st, "zT" + tag, [128, GRP, NX + NCTX], BF16)
                wa = [sb(st, f"wa{i}" + tag, [128, 8, 256], BF16) for i in range(2)]
                wb = [sb(st, f"wb{i}" + tag, [128, 8, 256], BF16) for i in range(2)]
                wo = [sb(st, f"wo{i}" + tag, [128, GRP, D], BF16) for i in range(2)]
                sl = [sb(st, f"sl{i}" + tag, [128, 512], F32) for i in range(2)]
                w_i_v = w_i.rearrange("(kc p) n -> p kc n", p=128)
                w_o_v = w_o2.rearrange("(fc p) n -> p fc n", p=128)
                blk_ctr = 0
                for gi, (g0, gn) in enumerate(groups):
                    gslot = gi % 2
                    wload(wo[gslot][:, 0:gn, :], f"wo{gslot}" + tag, w_o_v[:, g0:g0 + gn, :])
                    for b2 in range(gn // 2):
                        f0 = g0 + 2 * b2
                        slot = blk_ctr % 2
                        blk_ctr += 1
                        wload(wa[slot][:], f"wa{slot}" + tag, w_i_v[:, :, f0 * 128:(f0 + 2) * 128])
                        wload(wb[slot][:], f"wb{slot}" + tag, w_i_v[:, :, DFF + f0 * 128:DFF + (f0 + 2) * 128])
                        for ti, (kind, off, n) in enumerate(tiles):
                            src = hnT if kind == 'm' else hncT
                            zoff = off if kind == 'm' else NX + off
                            for j in range(2):
                                fz = 2 * b2 + j
                                pa, pb = (0, 1) if (j == 0) else (2, 3)
                                for kc in range(8):
                                    S.op('pe', lambda kc=kc, j=j, pa=pa, src=src, off=off, n=n, slot=slot: PE.matmul(
                                        ps[pa][:, 0:n], lhsT=wa[slot][:, kc, j * 128:(j + 1) * 128], rhs=src[:, kc, off:off + n],
                                        start=(kc == 0), stop=(kc == 7)),
                                        r=[f"wa{slot}" + tag, ("hn", (tag, ti))], w=[P(pa)], sig=(kc == 7))
                                for kc in range(8):
                                    S.op('pe', lambda kc=kc, j=j, pb=pb, src=src, off=off, n=n, slot=slot: PE.matmul(
                                        ps[pb][:, 0:n], lhsT=wb[slot][:, kc, j * 128:(j + 1) * 128], rhs=src[:, kc, off:off + n],
                                        start=(kc == 0), stop=(kc == 7)),
                                        r=[f"wb{slot}" + tag, ("hn", (tag, ti))], w=[P(pb)], sig=(kc == 7))
                                S.op('act', lambda pa=pa, j=j, n=n: ACT.activation(out=sl[j][:, 0:n], in_=ps[pa][:, 0:n], func=AF.Silu),
                                     r=[P(pa)], w=[f"sl{j}" + tag])
                                S.op('dve', lambda pb=pb, j=j, n=n, fz=fz, zoff=zoff: DVE.tensor_tensor(
                                    out=zT[:, fz, zoff:zoff + n], in0=ps[pb][:, 0:n], in1=sl[j][:, 0:n], op=ALU.mult),
                                    r=[P(pb), f"sl{j}" + tag], w=[("z", tag, ti, fz)])
                    for ti, (kind, off, n) in enumerate(tiles):
                        dstT = hT if kind == 'm' else hcT
                        zoff = off if kind == 'm' else NX + off
                        gt = (prm_main if kind == 'm' else prm_ctx)[2]
                        for dc in range(8):
                            pb = 4 + (dc % 3)
                            for fz in range(gn):
                                S.op('pe', lambda dc=dc, pb=pb, fz=fz, zoff=zoff, n=n, gslot=gslot: PE.matmul(
                                    ps[pb][:, 0:n], lhsT=wo[gslot][:, fz, dc * 128:(dc + 1) * 128], rhs=zT[:, fz, zoff:zoff + n],
                                    start=(fz == 0), stop=(fz == gn - 1)),
                                    r=[f"wo{gslot}" + tag, ("z", tag, ti, fz)], w=[P(pb)], sig=(fz == gn - 1))
                            S.op('dve', lambda dc=dc, pb=pb, dstT=dstT, off=off, n=n, gt=gt: DVE.scalar_tensor_tensor(
                                out=dstT[:, dc, off:off + n], in0=ps[pb][:, 0:n], scalar=prm[:, gt, dc:dc + 1], in1=dstT[:, dc, off:off + n],
                                op0=ALU.mult, op1=ALU.add),
                                r=[P(pb), ("prm", gt)], w=[("res", tag, ti, dc)])
            S.barrier()

        main_tiles = [('m', 1 + 512 * i, 512) for i in range(4)]
        halo_tiles = [('m', 0, 1), ('m', NX - 1, 1)]
        ctx_tile = [('c', 0, NCTX)]

        ffn(w_f1i, w_f1o, main_tiles + halo_tiles + ctx_tile, (P_GS1, P_SH1, P_GT1), (P_GS1C, P_SH1C, P_GT1C), "f1")

        mx = ExitStack()
        gates_all = sb(mx, "gates_all", [128, 18, 16], F32)
        sc_all = sb(mx, "sc_all", [128, 18, 32], F32)
        cs_b = sb(mx, "cs_b", [128, 2, 18, 4], F32)
        cs_g = sb(mx, "cs_g", [128, 18, 8], F32)
        Gseg = sb(mx, "Gseg", [128, 8], F32)
        LL = sb(mx, "LL", [128, 18, 8], F32)
        psc = sb(mx, "psc", [128, 18, 8], F32)
        wsT = sb(mx, "wsT", [128, 4, 128], BF16)
        bs_row = sb(mx, "bs_row", [1, 512], BF16)
        bg_bc = sb(mx, "bg_bc", [128, 16], F32)
        one_t = sb(mx, "one_t", [128, 1], F32)

        def dbg_dump(name, src_ap, rtoks):
            if name in dbg:
                S.dma('sp', dbg[name], src_ap, r=rtoks)

        with ExitStack() as st:
            wst = sb(st, "wst", [128, 4, 128], F32)
            bsf = sb(st, "bsf", [1, 512], F32)
            S.dma('sp', wst[:], w_s.rearrange("g t s -> t g s"), w=["wst"])
            S.dma('sp', bsf[:], b_s.rearrange("(o n) -> o n", o=1), w=["bsf"])
            S.dma('sp', bg_bc[:], bass.AP(b_gates.tensor, 0, [[0, 128], [1, 16]]), w=["bg_bc"])
            S.op('dve', lambda: DVE.memset(one_t[:], 1.0), w=["one_t"])
            for g in range(4):
                S.op('pe', lambda g=g: PE.transpose(out=ps[0][:, g * 128:(g + 1) * 128], in_=wst[:, g, :], identity=ident_f[:]),
                     r=["wst"], w=[P(0)], sig=(g == 3))
            S.op('dve', lambda: DVE.tensor_copy(out=wsT[:], in_=ps[0][:, :].rearrange("p (g t) -> p g t", g=4)), r=[P(0)], w=["wsT"])
            S.op('dve', lambda: DVE.tensor_copy(out=bs_row[:], in_=bsf[:]), r=["bsf"], w=["bs_row"])
            S.barrier()

        GC = 1.5957691216057308

        class GeluPipe:
            def __init__(self):
                self.prev = None

            def push(self, x_ap, t1, out_ap, xtoks, t1tok, outtok, after=None):
                S.op('act', lambda: ACT.activation(out=t1, in_=x_ap, func=AF.Square), r=xtoks, w=[t1tok])
                S.op('dve', lambda: DVE.tensor_scalar(out=t1, in0=t1, scalar1=0.044715, scalar2=1.0, op0=ALU.mult, op1=ALU.add), w=[t1tok])
                S.op('dve', lambda: DVE.tensor_tensor(out=t1, in0=x_ap, in1=t1, op=ALU.mult), r=xtoks, w=[t1tok])
                self.flush()
                self.prev = (x_ap, t1, out_ap, xtoks, t1tok, outtok, after)

            def flush(self):
                if self.prev is None:
                    return
                x_ap, t1, out_ap, xtoks, t1tok, outtok, after = self.prev
                self.prev = None
                S.op('act', lambda: ACT.activation(out=t1, in_=t1, func=AF.Sigmoid, scale=GC), r=[t1tok], w=[t1tok])
                S.op('dve', lambda: DVE.tensor_tensor(out=out_ap, in0=x_ap, in1=t1, op=ALU.mult), r=xtoks + [t1tok], w=[outtok])
                if after:
                    after()

        gpipe = GeluPipe()

        with ExitStack() as st:
            hnT = sb(st, "hnT_m", [128, 8, NX], BF16)
            hncT = sb(st, "hncT_m", [128, 8, NCTX], BF16)
            with ExitStack() as nst:
                stk = {"sq": sb(nst, "sq_m", [128, 8, 512], BF16), "rstd": sb(nst, "rstd_m", [128, 512], F32),
                       "tmp": [sb(nst, f"nmt{i}_m", [128, 512], F32) for i in range(2)]}
                tl = main_tiles + halo_tiles
                for ti, (kind, off, n) in enumerate(tl):
                    norm_mod(stk, hT, off, n, P_GSM, P_SHM, hnT, off, ("mx", ti))
                norm_mod(stk, hcT, 0, NCTX, P_GSMC, P_SHMC, hncT, 0, ("mx", 6))
                S.barrier()
            HN_MAIN = [("hn", ("mx", i)) for i in range(4)]
            HN_HALO = [("hn", ("mx", 4)), ("hn", ("mx", 5))]
            HN_CTX = [("hn", ("mx", 6))]

            wblk = [sb(st, f"wblk{i}", [128, 8, 512], BF16) for i in range(2)]
            w_in_v = w_in.rearrange("(kc p) n -> p kc n", p=128)
            bctr = [0]
            BLKS = ([(i * 512, 512) for i in range(4)] + [(2048, 512), (2560, 512), (3072, 16), (5136, 512), (5648, 512)]
                    + [(c0 + b * 512, 512) for c0 in (3088, 4112, 6160, 7184) for b in range(2)])
            issued = [0]

            def _issue(i):
                c0, ncols = BLKS[i]
                wload(wblk[i % 2][:, :, 0:ncols], f"wblk{i % 2}", w_in_v[:, :, c0:c0 + ncols])

            def load_blk(c0, ncols):
                i = bctr[0]
                assert BLKS[i] == (c0, ncols), (i, BLKS[i], c0, ncols)
                bctr[0] += 1
                while issued[0] <= min(i + 1, len(BLKS) - 1):
                    _issue(issued[0])
                    issued[0] += 1
                return i % 2

            pre = [sb(st, f"pre{i}", [128, NX], F32) for i in range(2)]
            prec = sb(st, "prec", [128, NCTX + 2], F32)
            accs = [sb(st, f"acc{i}", [128, NT], F32) for i in range(2)]
            qks = [sb(st, f"qks{i}", [128, NT + NCTX], BF16) for i in range(2)]
            ktk = sb(st, "ktk", [128, 18, 128], BF16)
            stg = [sb(st, f"stg{i}", [128, NT], BF16) for i in range(2)]
            ut = [sb(st, f"ut{i}", [128, 512], F32) for i in range(3)]
            vstg = [sb(st, f"vstg{i}", [128, 512], BF16) for i in range(3)]
            vsstg = [sb(st, f"vsstg{i}", [128, 512], F32) for i in range(3)]
            S.op('dve', lambda: DVE.memset(prec[:], 0.0), w=["prec"])

            for blk in range(4):
                slot = load_blk(blk * 512, 512)
                for j in range(4):
                    fc = blk * 4 + j
                    is_k = fc >= 8
                    pr = pre[fc % 2]
                    ptok = f"pre{fc % 2}"
                    acc = accs[fc % 2]
                    atok = f"acc{fc % 2}"
                    for ti in range(4):
                        pb = (0, 1, 6, 7)[ti]
                        for kc in range(8):
                            S.op('pe', lambda kc=kc, j=j, pb=pb, ti=ti, slot=slot: PE.matmul(
                                ps[pb][:, :], lhsT=wblk[slot][:, kc, j * 128:(j + 1) * 128], rhs=hnT[:, kc, 1 + 512 * ti:1 + 512 * (ti + 1)],
                                start=(kc == 0), stop=(kc == 7)), r=[f"wblk{slot}", HN_MAIN[ti]], w=[P(pb)], sig=(kc == 7))
                        S.op('act', lambda pb=pb, ti=ti, pr=pr: ACT.copy(out=pr[:, 1 + 512 * ti:1 + 512 * (ti + 1)], in_=ps[pb][:, :]),
                             r=[P(pb)], w=[(ptok, ti)])
                    for kc in range(8):
                        S.op('pe', lambda kc=kc, j=j, slot=slot: PE.matmul(
                            ps[2][:, 0:2], lhsT=wblk[slot][:, kc, j * 128:(j + 1) * 128], rhs=cap(hnT, kc * NX, [[NX - 1, 2]]),
                            start=(kc == 0), stop=(kc == 7)), r=[f"wblk{slot}"] + HN_HALO, w=[P(2)], sig=(kc == 7))
                    S.op('dve', lambda pr=pr: DVE.tensor_tensor(out=cap(pr, 0, [[NX - 1, 2]]), in0=ps[2][:, 0:2], in1=metat[:, 5:7], op=ALU.mult),
                         r=[P(2), "meta"], w=[(ptok, 4)])
                    if is_k:
                        for kc in range(8):
                            S.op('pe', lambda kc=kc, j=j, slot=slot: PE.matmul(
                                ps[3][:, 0:NCTX], lhsT=wblk[slot][:, kc, j * 128:(j + 1) * 128], rhs=hncT[:, kc, :],
                                start=(kc == 0), stop=(kc == 7)), r=[f"wblk{slot}"] + HN_CTX, w=[P(3)], sig=(kc == 7))
                        S.op('act', lambda: ACT.copy(out=prec[:, 1:1 + NCTX], in_=ps[3][:, 0:NCTX]), r=[P(3)], w=["prec"])
                    w0 = vecT[:, R_CW + 0 * 16 + fc:R_CW + 0 * 16 + fc + 1]
                    w1 = vecT[:, R_CW + 1 * 16 + fc:R_CW + 1 * 16 + fc + 1]
                    w2 = vecT[:, R_CW + 2 * 16 + fc:R_CW + 2 * 16 + fc + 1]
                    cb = vecT[:, R_CB + fc:R_CB + fc + 1]
                    qs = qks[fc % 2]
                    qtok = f"qks{fc % 2}"
                    allpre = [(ptok, i) for i in range(5)]
                    S.op('pool', lambda pr=pr, w0=w0: POOL.tensor_scalar(out=acc[:], in0=pr[:, 0:NT], scalar1=w0, scalar2=0.0, op0=ALU.mult, op1=ALU.add),
                         r=allpre, w=[atok])
                    S.op('dve', lambda pr=pr, w1=w1: DVE.scalar_tensor_tensor(out=acc[:], in0=pr[:, 1:NT + 1], scalar=w1, in1=acc[:], op0=ALU.mult, op1=ALU.add),
                         r=allpre + [atok], w=[atok])
                    S.op('dve', lambda pr=pr, w2=w2: DVE.scalar_tensor_tensor(out=acc[:], in0=pr[:, 2:NT + 2], scalar=w2, in1=acc[:], op0=ALU.mult, op1=ALU.add),
                         r=allpre, w=[atok])
                    S.op('act', lambda qs=qs, cb=cb: ACT.activation(out=qs[:, 0:NT], in_=acc[:], func=AF.Silu, bias=cb, scale=1.0), r=[atok], w=[qtok])
                    if not is_k:
                        S.dma('sp', qT_d[fc], qs[:, 0:NT], r=[qtok], w=[("qT_d", fc)])
                    else:
                        S.op('dve', lambda w0=w0: DVE.tensor_scalar(out=acc[:, 0:NCTX], in0=prec[:, 0:NCTX], scalar1=w0, scalar2=None, op0=ALU.mult),
                             r=["prec", atok], w=[atok])
                        S.op('dve', lambda w1=w1: DVE.scalar_tensor_tensor(out=acc[:, 0:NCTX], in0=prec[:, 1:NCTX + 1], scalar=w1, in1=acc[:, 0:NCTX], op0=ALU.mult, op1=ALU.add),
                             r=["prec"], w=[atok])
                        S.op('dve', lambda w2=w2: DVE.scalar_tensor_tensor(out=acc[:, 0:NCTX], in0=prec[:, 2:NCTX + 2], scalar=w2, in1=acc[:, 0:NCTX], op0=ALU.mult, op1=ALU.add),
                             r=["prec"], w=[atok])
                        S.op('act', lambda qs=qs, cb=cb: ACT.activation(out=qs[:, NT:NT + NCTX], in_=acc[:, 0:NCTX], func=AF.Silu, bias=cb, scale=1.0),
                             r=[atok], w=[qtok])
                        S.dma('sp', kT_d[fc - 8], qs[:, :], r=[qtok], w=[("kT_d", fc - 8)])
                        for grp in range(3):
                            c0 = grp * 8
                            ncg = min(8, 18 - c0)
                            pbank = 4 + (grp % 2)
                            psb = ps[pbank][:, :].bitcast(BF16)
                            for ci in range(ncg):
                                S.op('pe', lambda ci=ci, c0=c0, psb=psb, qs=qs: PE.transpose(
                                    out=psb[:, ci * 128:(ci + 1) * 128], in_=qs[:, (c0 + ci) * 128:(c0 + ci + 1) * 128], identity=ident_b[:]),
                                    r=[qtok, "ident_b"], w=[P(pbank)], sig=(ci == ncg - 1))
                            S.op('dve', lambda c0=c0, ncg=ncg, psb=psb: DVE.tensor_copy(
                                out=ktk[:, c0:c0 + ncg, :], in_=psb[:, 0:ncg * 128].rearrange("p (c f) -> p c f", f=128)),
                                r=[P(pbank)], w=["ktk"])
                        S.dma('sp', ktok_d.rearrange("c p f -> p c f")[:, :, (fc - 8) * 128:(fc - 7) * 128], ktk[:], r=["ktk"], w=[("ktok_d", fc - 8)])

            def hn_chunk(c, kc):
                if c < 16:
                    return hnT[:, kc, 1 + 128 * c:1 + 128 * (c + 1)]
                return hncT[:, kc, (c - 16) * 128:(c - 15) * 128]

            def hn_tok(c):
                return HN_MAIN[c // 4] if c < 16 else HN_CTX[0]

            def bform(c0, ncols, nch, epi):
                slot = load_blk(c0, ncols)
                for c in range(nch):
                    pb = c % 6
                    for kc in range(8):
                        S.op('pe', lambda kc=kc, c=c, pb=pb, slot=slot: PE.matmul(
                            ps[pb][:, 0:ncols], lhsT=hn_chunk(c, kc), rhs=wblk[slot][:, kc, 0:ncols], start=(kc == 0), stop=(kc == 7)),
                            r=[f"wblk{slot}", hn_tok(c)], w=[P(pb)], sig=(kc == 7))
                    epi(c, pb)

            vctr = [0]
            for half in range(2):
                def epi_v(c, pb, half=half):
                    s3 = vctr[0] % 3
                    vctr[0] += 1
                    S.op('act', lambda: ACT.copy(out=vstg[s3][:], in_=ps[pb][:, :]), r=[P(pb)], w=[f"vstg{s3}"])
                    S.dma('sp', v_d[c][:, half * 512:(half + 1) * 512], vstg[s3][:], r=[f"vstg{s3}"], w=[("v_d", c, half)])
                bform(2048 + half * 512, 512, 18, epi_v)

            def epi_g(c, pb):
                S.op('dve', lambda: DVE.tensor_tensor(out=gates_all[:, c, :], in0=ps[pb][:, 0:16], in1=bg_bc[:], op=ALU.add),
                     r=[P(pb), "bg_bc"], w=[("gates", c)])
            bform(3072, 16, 18, epi_g)

            vsctr = [0]
            for half in range(2):
                def epi_vs(c, pb, half=half):
                    s2 = vsctr[0] % 3
                    vsctr[0] += 1
                    gpipe.push(ps[pb][:, :], ut[s2][:], vsstg[s2][:], [P(pb)], f"ut{s2}", f"vsstg{s2}",
                               after=lambda c=c, half=half, s2=s2: S.dma('sp', vs_d[c][:, half * 512:(half + 1) * 512], vsstg[s2][:], r=[f"vsstg{s2}"], w=[("vs_d", c, half)]))
                bform(5136 + half * 512, 512, 16, epi_vs)
            gpipe.flush()

            def aform(col0, kind, dst_d):
                for blk in range(2):
                    slot = load_blk(col0 + blk * 512, 512)
                    for j in range(4):
                        fc = blk * 4 + j
                        sg = stg[fc % 2]
                        stok = f"stg{fc % 2}"
                        for ti in range(4):
                            pb = (fc * 4 + ti) % 8
                            for kc in range(8):
                                S.op('pe', lambda kc=kc, j=j, pb=pb, ti=ti, slot=slot: PE.matmul(
                                    ps[pb][:, :], lhsT=wblk[slot][:, kc, j * 128:(j + 1) * 128], rhs=hnT[:, kc, 1 + 512 * ti:1 + 512 * (ti + 1)],
                                    start=(kc == 0), stop=(kc == 7)), r=[f"wblk{slot}", HN_MAIN[ti]], w=[P(pb)], sig=(kc == 7))
                            dst = sg[:, 512 * ti:512 * (ti + 1)]
                            if kind == 'sig':
                                S.op('act', lambda pb=pb, dst=dst: ACT.activation(out=dst, in_=ps[pb][:, :], func=AF.Sigmoid), r=[P(pb)], w=[(stok, ti)])
                            elif kind == 'sigg':
                                t1 = ut[ti % 3]
                                S.op('act', lambda pb=pb, t1=t1: ACT.activation(out=t1[:], in_=ps[pb][:, :], func=AF.Sigmoid), r=[P(pb)], w=[f"ut{ti % 3}"])
                                S.op('pool', lambda t1=t1, dst=dst, fc=fc: POOL.tensor_scalar(out=dst, in0=t1[:], scalar1=vecT[:, R_GH + fc:R_GH + fc + 1], scalar2=0.0,
                                                                                               op0=ALU.mult, op1=ALU.add), r=[f"ut{ti % 3}"], w=[(stok, ti)])
                            else:
                                gpipe.push(ps[pb][:, :], ut[ti % 3][:], dst, [P(pb)], f"ut{ti % 3}", (stok, ti))
                        if kind == 'gelu':
                            gpipe.flush()
                        S.dma('sp', dst_d[fc], sg[:], r=[(stok, i) for i in range(4)], w=[(dst_d.tensor.name, fc)])

            aform(3088, 'sigg', og_d)
            aform(4112, 'gelu', gu_d)
            aform(6160, 'sig', ga_d)
            aform(7184, 'sig', gb_d)
            S.barrier()

        Sin = sb(mx, "Sin", [128, 8, 514], F32)
        with ExitStack() as st:
            lfw = sb(st, "lfw", [128, 18, 8], F32)
            dif = sb(st, "dif", [128, 18, 8], F32)
            S.op('act', lambda: ACT.activation(out=lfw[:, :, 0:4], in_=gates_all[:, :, 4:8], func=AF.Exp, scale=-1.0), w=["lfw"], small=True)
            S.op('act', lambda: ACT.activation(out=lfw[:, :, 4:8], in_=gates_all[:, :, 12:16], func=AF.Exp, scale=-1.0), w=["lfw"], small=True)
            S.op('act', lambda: ACT.activation(out=lfw[:], in_=lfw[:], func=AF.Ln, bias=one_t[:, 0:1], scale=1.0), w=["lfw"], small=True)
            S.op('dve', lambda: DVE.tensor_scalar(out=lfw[:], in0=lfw[:], scalar1=-1.0, scalar2=None, op0=ALU.mult), r=["lfw"], w=["lfw"], small=True)
            S.op('pe', lambda: PE.matmul(ps[0][:, 0:72], lhsT=triL[:], rhs=lfw[:, :, 0:4], start=True, stop=True), r=["lfw"], w=[P(0)], sig=False)
            S.op('pe', lambda: PE.matmul(ps[0][:, 72:144], lhsT=triU[:], rhs=lfw[:, :, 4:8], start=True, stop=True), r=["lfw"], w=[P(0)], sig=False)
            S.op('pe', lambda: PE.matmul(ps[1][:, 0:144], lhsT=ones_f[:], rhs=lfw[:], start=True, stop=True), r=["lfw"], w=[P(1)], sig=True)
            S.op('dve', lambda: DVE.tensor_copy(out=cs_b[:], in_=ps[0][:, 0:144].rearrange("p (d c h) -> p d c h", d=2, c=18)), r=[P(0)], w=["cs_b"], small=True)
            S.op('dve', lambda: DVE.tensor_copy(out=cs_g[:], in_=ps[1][:, 0:144].rearrange("p (c h) -> p c h", c=18)), r=[P(1)], w=["cs_g"], small=True)
            S.op('act', lambda: ACT.activation(out=sc_all[:, :, 0:4], in_=cs_b[:, 0, :, :], func=AF.Exp), r=["cs_b"], w=["sc_all"], small=True)
            S.op('act', lambda: ACT.activation(out=sc_all[:, :, 4:8], in_=cs_b[:, 1, :, :], func=AF.Exp), r=["cs_b"], w=["sc_all"], small=True)
            S.op('dve', lambda: DVE.tensor_tensor(out=dif[:, :, 0:4], in0=gates_all[:, :, 0:4], in1=cs_b[:, 0, :, :], op=ALU.subtract), r=["cs_b"], w=["dif"], small=True)
            S.op('dve', lambda: DVE.tensor_tensor(out=dif[:, :, 4:8], in0=gates_all[:, :, 8:12], in1=cs_b[:, 1, :, :], op=ALU.subtract), r=["cs_b"], w=["dif"], small=True)
            S.op('act', lambda: ACT.activation(out=sc_all[:, :, 8:16], in_=dif[:], func=AF.Exp), r=["dif"], w=["sc_all"], small=True)
            S.op('act', lambda: ACT.activation(out=sc_all[:, :, 16:24], in_=cs_g[:], func=AF.Exp), r=["cs_g"], w=["sc_all"], small=True)
            S.op('dve', lambda: DVE.tensor_tensor(out=sc_all[:, :, 24:32], in0=sc_all[:, :, 8:16], in1=sc_all[:, :, 16:24], op=ALU.mult), r=["sc_all"], w=["sc_all"], small=True)
            S.op('dve', lambda: DVE.tensor_reduce(out=Gseg[:], in_=cap(cs_g, 0, [[1, 8], [8, 16]]), axis=mybir.AxisListType.X, op=ALU.add), r=["cs_g"], w=["Gseg"], small=True)
            S.op('dve', lambda: DVE.memset(LL[:], 0.0), w=["LL"], small=True)
            for c in range(14, -1, -1):
                S.op('dve', lambda c=c: DVE.tensor_tensor(out=LL[:, c, 0:4], in0=LL[:, c + 1, 0:4], in1=cs_g[:, c + 1, 0:4], op=ALU.add), r=["cs_g"], w=["LL"], small=True)
            for c in range(1, 16):
                S.op('dve', lambda c=c: DVE.tensor_tensor(out=LL[:, c, 4:8], in0=LL[:, c - 1, 4:8], in1=cs_g[:, c - 1, 4:8], op=ALU.add), r=["cs_g"], w=["LL"], small=True)
            S.op('dve', lambda: DVE.tensor_copy(out=LL[:, 16, 0:4], in_=cs_g[:, 17, 0:4]), w=["LL"], small=True)
            S.op('dve', lambda: DVE.tensor_copy(out=LL[:, 17, 4:8], in_=cs_g[:, 16, 4:8]), w=["LL"], small=True)
            S.op('act', lambda: ACT.activation(out=LL[:], in_=LL[:], func=AF.Exp), r=["LL"], w=["LL"], small=True)
            S.op('dve', lambda: DVE.tensor_tensor(out=psc[:], in0=LL[:], in1=sc_all[:, :, 24:32], op=ALU.mult), r=["LL", "sc_all"], w=["psc"], small=True)
            S.barrier()
        dbg_dump("gates", gates_all[:], [])
        dbg_dump("sc_all", sc_all[:], [])

        _slc = [0]

        def sweep_loads(pool, names):
            _slc[0] += 1
            return {n: [sb(pool, f"ld{_slc[0]}_{n}{i}", shp, dt) for i in range(2)] for n, (shp, dt) in names.items()}

        with ExitStack() as st:
            St = sb(st, "St", [128, 8, 514], F32)
            Sctx = sb(st, "Sctx", [128, 8, 514], F32)
            lds = sweep_loads(st, {"ktok": ([128, D], BF16), "v": ([128, D], BF16)})
            vtl = [sb(st, f"vtl{i}", [128, 4, 257], BF16) for i in range(2)]
            lctr = [0]

            def p1_pass(chunks, d, dst, dtok):
                n = len(chunks)
                for i, c in enumerate(chunks):
                    slot = lctr[0] % 2
                    lctr[0] += 1
                    kt, vv, vt = lds["ktok"][slot], lds["v"][slot], vtl[slot]
                    S.dma('sp', kt[:], ktok_d[c], w=[f"ld_ktok{slot}"])
                    S.dma('sp', vv[:], v_d[c], w=[f"ld_v{slot}"])
                    for h in range(4):
                        S.op('dve', lambda h=h: DVE.tensor_scalar(out=vt[:, h, 0:256], in0=vv[:, h * 256:(h + 1) * 256], scalar1=psc[:, c, 4 * d + h:4 * d + h + 1],
                                                                  scalar2=None, op0=ALU.mult), r=[f"ld_v{slot}", "psc"], w=[f"vtl{slot}"])
                    S.op('dve', lambda: DVE.tensor_copy(out=vt[:, :, 256], in_=psc[:, c, 4 * d:4 * d + 4]), w=[f"vtl{slot}"], small=True)
                    for h in range(4):
                        for kc in range(2):
                            pb = h * 2 + kc
                            S.op('pe', lambda h=h, kc=kc, pb=pb: PE.matmul(ps[pb][:, 0:257], lhsT=kt[:, h * 256 + kc * 128:h * 256 + (kc + 1) * 128], rhs=vt[:, h, :],
                                                                          start=(i == 0), stop=(i == n - 1)), r=[f"ld_ktok{slot}", f"vtl{slot}"], w=[P(pb)],
                                 sig=(i == n - 1 or (h == 3 and kc == 1)))
                for h in range(4):
                    for kc in range(2):
                        pb = h * 2 + kc
                        eng = 'act' if (pb % 2 == 0) else 'dve'
                        if eng == 'act':
                            S.op('act', lambda h=h, kc=kc, pb=pb: ACT.copy(out=dst[:, d * 4 + h, kc * 257:(kc + 1) * 257], in_=ps[pb][:, 0:257]), r=[P(pb)], w=[(dtok, d * 4 + h, kc)])
                        else:
                            S.op('dve', lambda h=h, kc=kc, pb=pb: DVE.tensor_copy(out=dst[:, d * 4 + h, kc * 257:(kc + 1) * 257], in_=ps[pb][:, 0:257]), r=[P(pb)], w=[(dtok, d * 4 + h, kc)])

            p1_pass([16, 17], 0, Sctx, "Sctx")
            p1_pass([17, 16], 1, Sctx, "Sctx")
            p1_pass(list(range(16)), 0, St, "St")
            p1_pass(list(range(16)), 1, St, "St")
            S.barrier()
            dbg_dump("Sctx", Sctx[:], ["Sctx"])
            dbg_dump("Sloc", St[:], [("St", i) for i in range(8)])
            xout_v = [xo.rearrange("(r p) w -> p r w", p=128) for xo in xout_l]
            for i in range(4):
                S.dma('sp', xin_l[i][:, 0:1028], St[:, 2 * i:2 * i + 2, :].rearrange("p a b -> p (a b)"), r=[("St", 2 * i), ("St", 2 * i + 1)], w=[("xin", i)])
                S.dma('sp', xin_l[i][:, 1028:1030], Gseg[:, 2 * i:2 * i + 2], r=[], w=[("xin2", i)])
            for i in range(4):
                if KSTOP == 'p1':
                    S.dma('sp', xout_l[i][0:128, :], xin_l[i], r=[("xin", i), ("xin2", i)], w=[("xout", i)])
                else:
                    S.custom('pool', lambda sem, i=i: POOL.collective_compute("AllGather", ALU.bypass, replica_groups=[[0, 1, 2, 3], [4, 5, 6, 7]],
                                                                              ins=[xin_l[i].opt()], outs=[xout_l[i].opt()]).then_inc(sem, 1),
                             1, r=[("xin", i), ("xin2", i)], w=[("xout", i)])
            with ExitStack() as sg:
                gvt = sb(sg, "sgu_gv", [128, 4, D], F32)
                vnt = sb(sg, "sgu_vn", [128, 4, D], BF16)
                gut = sb(sg, "sgu_gu", [128, 8, 512], BF16)
                ybt = sb(sg, "sgu_yb", [128, 8, 512], BF16)
                gsgu_bc = sb(sg, "gsgu_bc", [128, D], F32)
                st6s = sb(sg, "sgu_st6", [128, 4, 2, 6], F32)
                mvs = sb(sg, "sgu_mv", [128, 4, 2], F32)
                msq = sb(sg, "sgu_msq", [128, 2, 4], F32)
                S.dma('sp', gsgu_bc[:], bass.AP(g_sgu.tensor, 0, [[0, 128], [1, D]]), w=["gsgu_bc"])
                for g4 in range(4):
                    S.dma('sp', gvt[:], vs_d.rearrange("c p f -> p c f")[:, 4 * g4:4 * g4 + 4, :], w=["sgu_gv"])
                    S.dma('sp', gut[:], gu_d.rearrange("f p t -> p f t")[:, :, 512 * g4:512 * (g4 + 1)], w=["sgu_gu"])
                    for cq in range(4):
                        for i2 in range(2):
                            S.op('dve', lambda cq=cq, i2=i2: DVE.bn_stats(out=st6s[:, cq, i2, :], in_=gvt[:, cq, i2 * 512:(i2 + 1) * 512]),
                                 r=["sgu_gv"], w=[("sgu_st6", cq)], small=True)
                    for cq in range(4):
                        S.op('dve', lambda cq=cq: DVE.bn_aggr(out=mvs[:, cq, :], in_=st6s[:, cq, :, :].rearrange("p a b -> p (a b)")),
                             r=[("sgu_st6", cq)], w=["sgu_mv"], small=True)
                    S.op('dve', lambda: DVE.tensor_tensor(out=msq[:, 0, :], in0=mvs[:, :, 0], in1=mvs[:, :, 0], op=ALU.mult), r=["sgu_mv"], w=["sgu_msq"], small=True)
                    S.op('dve', lambda: DVE.tensor_tensor(out=msq[:, 0, :], in0=msq[:, 0, :], in1=mvs[:, :, 1], op=ALU.add), w=["sgu_msq"], small=True)
                    S.op('act', lambda: ACT.activation(out=msq[:, 1, :], in_=msq[:, 0, :], func=AF.Ln, bias=eps_t[:, 0:1], scale=1.0), r=["sgu_msq"], w=["sgu_rs"], small=True)
                    S.op('act', lambda: ACT.activation(out=msq[:, 1, :], in_=msq[:, 1, :], func=AF.Exp, scale=-0.5), w=["sgu_rs"], small=True)
                    for cq in range(4):
                        S.op('dve', lambda cq=cq: DVE.scalar_tensor_tensor(out=vnt[:, cq, :], in0=gvt[:, cq, :], scalar=msq[:, 1, cq:cq + 1], in1=gsgu_bc[:],
                                                                           op0=ALU.mult, op1=ALU.mult), r=["sgu_rs", "sgu_gv", "gsgu_bc"], w=[("sgu_vn", cq)], small=True)
                    for ccf in range(8):
                        gq = ccf // 2
                        for cq in range(4):
                            S.op('pe', lambda ccf=ccf, cq=cq, gq=gq: PE.matmul(ps[ccf][:, cq * 128:(cq + 1) * 128], lhsT=vnt[:, cq, ccf * 128:(ccf + 1) * 128], rhs=wsT[:, gq, :],
                                                                               start=True, stop=False), r=[("sgu_vn", cq), "wsT"], w=[P(ccf)], sig=False)
                            S.op('pe', lambda ccf=ccf, cq=cq, gq=gq: PE.matmul(ps[ccf][:, cq * 128:(cq + 1) * 128], lhsT=ones_b[0:1, :], rhs=bs_row[0:1, gq * 128:(gq + 1) * 128],
                                                                               start=False, stop=True), w=[P(ccf)], sig=(cq == 3))
                        S.op('dve', lambda ccf=ccf: DVE.tensor_tensor(out=ybt[:, ccf, :], in0=ps[ccf][:, :], in1=gut[:, ccf, :], op=ALU.mult),
                             r=[P(ccf), "sgu_gu"], w=[("sgu_yb", ccf)])
                    S.dma('sp', gu_d.rearrange("f p t -> p f t")[:, :, 512 * g4:512 * (g4 + 1)], ybt[:], r=[("sgu_yb", i) for i in range(8)] + ["sgu_gu"], w=[("gu_d", g4)])
                S.barrier()
            Gall = sb(st, "Gall", [128, 4, 8], F32)
            Ug = [sb(st, f"Ug{i}", [128, 4, 514], F32) for i in range(2)]
            Tt = sb(st, "Tt", [128, 514], F32)
            for i in range(4):
                S.dma('sp', Gall[:, :, 2 * i:2 * i + 2], xout_v[i][:, :, 1028:1030], r=[("xout", i)], w=[("Gall", i)])
            S.op('act', lambda: ACT.activation(out=Gall[:], in_=Gall[:], func=AF.Exp), r=[("Gall", i) for i in range(4)], w=["Gall"], small=True)
            for hd in range(8):
                d = hd // 4
                U = Ug[hd % 2]
                utok = f"Ug{hd % 2}"
                S.dma('sp', U[:], xout_v[hd // 2][:, :, (hd % 2) * 514:(hd % 2 + 1) * 514], r=[("xout", hd // 2)], w=[utok])
                S.op('dve', lambda hd=hd: DVE.tensor_copy(out=Tt[:], in_=Sctx[:, hd, :]), r=["Sctx"], w=["Tt"])
                first = 0 if d == 0 else 3
                S.op('dve', lambda hd=hd, first=first: DVE.tensor_scalar(out=Sin[:, hd, :], in0=Tt[:], scalar1=metat[:, 1 + first:2 + first], scalar2=None, op0=ALU.mult),
                     r=["meta"], w=[("Sin", hd)])
                order = [0, 1, 2] if d == 0 else [3, 2, 1]
                for i in order:
                    tgt = i + 1 if d == 0 else i - 1
                    S.op('dve', lambda i=i, hd=hd, U=U: DVE.scalar_tensor_tensor(out=Tt[:], in0=Tt[:], scalar=Gall[:, i, hd:hd + 1], in1=U[:, i, :],
                                                                                  op0=ALU.mult, op1=ALU.add), r=[utok, "Gall"], w=["Tt"])
                    S.op('dve', lambda tgt=tgt, hd=hd: DVE.scalar_tensor_tensor(out=Sin[:, hd, :], in0=Tt[:], scalar=metat[:, 1 + tgt:2 + tgt], in1=Sin[:, hd, :],
                                                                                op0=ALU.mult, op1=ALU.add), w=[("Sin", hd)])
            dbg_dump("Sin", Sin[:], [("Sin", i) for i in range(8)])
            S.barrier()

        def mlstm_chunk(c, d, bufs, S16, emit_all, hook=None):
            q_t, kT_t, kt_t, v_t = bufs["q"], bufs["kT"], bufs["ktok"], bufs["v"]
            vt, vte, PM4, sm = bufs["vt"], bufs["vte"], bufs["PM4"], bufs["sm"]
            ltoks = bufs["ltoks"]
            mask = mL16 if d == 0 else mU16
            rs0, vs0, eg0, vse0 = 0 + 4 * d, 8 + 4 * d, 16 + 4 * d, 24 + 4 * d
            for h in range(4):
                S.op('dve', lambda h=h: DVE.tensor_scalar(out=vt[:, h, 0:256], in0=v_t[:, h * 256:(h + 1) * 256], scalar1=sc_all[:, c, vs0 + h:vs0 + h + 1],
                                                          scalar2=None, op0=ALU.mult), r=[ltoks["v"]], w=["vt"])
            S.op('dve', lambda: DVE.tensor_copy(out=vt[:, :, 256], in_=sc_all[:, c, vs0:vs0 + 4]), w=["vt"], small=True)
            S.op('pool', lambda: POOL.tensor_tensor(out=vte[:, :, 0:256], in0=v_t[:, :].rearrange("p (h v) -> p h v", h=4),
                                                    in1=cap(sc_all, c * 32 + vse0, [[1, 4], [0, 256]]), op=ALU.mult), r=[ltoks["v"]], w=["vte"])
            S.op('pool', lambda: POOL.tensor_copy(out=vte[:, :, 256], in_=sc_all[:, c, vse0:vse0 + 4]), w=["vte"], small=True)
            for h in range(4):
                for kc in range(2):
                    S.op('pe', lambda kc=kc, h=h: PE.matmul(ps[0][:, h * 128:(h + 1) * 128], lhsT=kT_t[:, h * 2 + kc, :], rhs=q_t[:, h * 2 + kc, :],
                                                           start=(kc == 0), stop=(kc == 1)), r=[ltoks["kT"], ltoks["q"]], w=[P(0)], sig=(h == 3 and kc == 1))
            S.op('dve', lambda: DVE.tensor_tensor(out=PM4[:], in0=ps[0][:, :].rearrange("p (h t) -> p h t", h=4),
                                                  in1=cap(mask, 0, [[0, 4], [1, 128]]), op=ALU.mult), r=[P(0)], w=["PM4"])
            if hook:
                hook('s')
            for h in range(4):
                S.op('pe', lambda h=h: PE.matmul(ps[1 + h][:, 0:257], lhsT=PM4[:, h, :], rhs=vt[:, h, :], start=True, stop=False),
                     r=["PM4", "vt"], w=[P(1 + h)], sig=False)
                for kc in range(2):
                    S.op('pe', lambda kc=kc, h=h: PE.matmul(ps[1 + h][:, 0:257], lhsT=q_t[:, h * 2 + kc, :], rhs=S16[:, h, kc, :], start=False, stop=(kc == 1)),
                         r=[ltoks["q"], ("S16", h)], w=[P(1 + h)], sig=(kc == 1))
            if hook:
                hook('o')
            den4 = psall[:, 1:5, 256]
            rs4 = sc_all[:, c, rs0:rs0 + 4]
            PO = [P(1 + h) for h in range(4)]
            S.op('dve', lambda: DVE.tensor_tensor(out=sm[:, 0, :], in0=den4, in1=rs4, op=ALU.mult), r=PO, w=["sm"], small=True)
            S.op('dve', lambda: DVE.tensor_scalar(out=sm[:, 1, :], in0=sm[:, 0, :], scalar1=-1.0, scalar2=1.0, op0=ALU.mult, op1=ALU.max), w=["sm"], small=True)
            S.op('dve', lambda: DVE.tensor_scalar(out=sm[:, 2, :], in0=sm[:, 0, :], scalar1=1.0, scalar2=None, op0=ALU.max), w=["sm"], small=True)
            S.op('dve', lambda: DVE.tensor_tensor(out=sm[:, 2, :], in0=sm[:, 2, :], in1=sm[:, 1, :], op=ALU.max), w=["sm"], small=True)
            S.op('dve', lambda: DVE.reciprocal(out=sm[:, 3, :], in_=sm[:, 2, :]), w=["sm"], small=True)
            S.op('dve', lambda: DVE.tensor_tensor(out=sm[:, 4, :], in0=sm[:, 3, :], in1=rs4, op=ALU.mult), w=["sm"], small=True)
            emit_all(sm, 4)
            for h in range(4):
                hd = d * 4 + h
                for kc in range(2):
                    pU = 5 + kc
                    S.op('pe', lambda kc=kc, h=h, pU=pU: PE.matmul(ps[pU][:, 0:257], lhsT=kt_t[:, h * 256 + kc * 128:h * 256 + (kc + 1) * 128], rhs=vte[:, h, :],
                                                                  start=True, stop=True), r=[ltoks["ktok"], "vte"], w=[P(pU)])
                    S.op('dve', lambda kc=kc, hd=hd, pU=pU, h=h: DVE.scalar_tensor_tensor(
                        out=Sin[:, hd, kc * 257:(kc + 1) * 257], in0=Sin[:, hd, kc * 257:(kc + 1) * 257], scalar=sc_all[:, c, eg0 + h:eg0 + h + 1],
                        in1=ps[pU][:, 0:257], op0=ALU.mult, op1=ALU.add), r=[P(pU)], w=[("Sin", hd)])
                S.op('act', lambda h=h, hd=hd: ACT.activation(out=S16[:, h, :, :], in_=Sin[:, hd, :].rearrange("p (k v) -> p k v", k=2), func=AF.Copy, scale=0.0625),
                     r=[("Sin", hd)], w=[("S16", h)])
            if hook:
                hook('u')

        SINGLE = {"hb", "vs"}

        def ltok(name, slot):
            return f"ld_{name}" if name in SINGLE else f"ld_{name}{slot}"

        def issue_loads(lds, slot, c, extra=()):
            S.dma('sp', lds["q"][slot][:], qT_d.rearrange("f p t -> p f t")[:, :, c * 128:(c + 1) * 128], w=[f"ld_q{slot}"])
            S.dma('sp', lds["kT"][slot][:], kT_d.rearrange("f p t -> p f t")[:, :, c * 128:(c + 1) * 128], w=[f"ld_kT{slot}"])
            S.dma('sp', lds["ktok"][slot][:], ktok_d[c], w=[f"ld_ktok{slot}"])
            S.dma('sp', lds["v"][slot][:], v_d[c], w=[f"ld_v{slot}"])
            for (name, src) in extra:
                S.dma('sp', lds[name][slot][:], src, w=[ltok(name, slot)])

        def mk_bufs(lds, slot, common):
            b = dict(common)
            for n in lds:
                b[n] = lds[n][slot]
            b["ltoks"] = {n: ltok(n, slot) for n in lds}
            return b

        with ExitStack() as st:
          if KSTOP not in ('xchg', 'p1'):
                lds = sweep_loads(st, {"q": ([128, 8, 128], BF16), "kT": ([128, 8, 128], BF16), "ktok": ([128, D], BF16), "v": ([128, D], BF16)})
                common = {"vt": sb(st, "vt", [128, 4, 257], BF16), "vte": sb(st, "vte", [128, 4, 257], BF16),
                          "PM4": sb(st, "PM4", [128, 4, 128], BF16), "sm": sb(st, "sm", [128, 5, 4], F32)}
                S16 = sb(st, "S16", [128, 4, 2, 257], BF16)
                hbt = [sb(st, f"hbt{i}", [128, D], F32) for i in range(2)]
                for h in range(4):
                    S.op('act', lambda h=h: ACT.activation(out=S16[:, h, :, :], in_=Sin[:, 4 + h, :].rearrange("p (k v) -> p k v", k=2), func=AF.Copy, scale=0.0625),
                         w=[("S16", h)])
                issue_loads(lds, 0, 15)
                for it, c in enumerate(range(15, -1, -1)):
                    slot = it % 2
                    if c > 0:
                        issue_loads(lds, 1 - slot, c - 1)
                    hb = hbt[slot]

                    def emit_all(sm, row, hb=hb, slot=slot):
                        for h in range(4):
                            S.op('act', lambda h=h: ACT.activation(out=hb[:, h * 256:(h + 1) * 256], in_=ps[1 + h][:, 0:256], func=AF.Identity, scale=sm[:, row, h:h + 1]),
                                 r=[P(1 + h), "sm"], w=[(f"hbt{slot}", h)])
                    mlstm_chunk(c, 1, mk_bufs(lds, slot, common), S16, emit_all)
                    S.dma('sp', hb_d[c], hb[:], r=[(f"hbt{slot}", h) for h in range(4)], w=[("hb_d", c)])
                dbg_dump("Sfin", Sin[:], [("Sin", i) for i in range(8)])
                S.barrier()

        with ExitStack() as st:
          if KSTOP not in ('xchg', 'bwd', 'p1'):
                lds = sweep_loads(st, {"q": ([128, 8, 128], BF16), "kT": ([128, 8, 128], BF16), "ktok": ([128, D], BF16), "v": ([128, D], BF16),
                                       "og": ([128, 8, 128], BF16)})
                hb1 = sb(st, "hb1", [128, D], F32)
                lds["hb"] = [hb1, hb1]
                common = {"vt": sb(st, "vtc", [128, 4, 257], BF16), "vte": sb(st, "vtec", [128, 4, 257], BF16),
                          "PM4": sb(st, "PM4c", [128, 4, 128], BF16), "sm": sb(st, "smc", [128, 5, 4], F32)}
                S16 = sb(st, "S16c", [128, 4, 2, 257], BF16)
                hm_t = sb(st, "hm_t", [128, D], F32)
                hh = sb(st, "hh", [128, D], BF16)
                st6 = sb(st, "st6", [128, 4, 6], F32)
                mv = sb(st, "mv", [128, 4, 2], F32)
                lnr = sb(st, "lnr", [128, 4], F32)
                t1 = cap(hcT, 0, [[1, D]])
                gv = cap(hcT, D, [[1, D]])
                ssq = sb(st, "ssq", [128, 2], F32)
                st6b = sb(st, "st6b", [128, 2, 6], F32)
                mvb = sb(st, "mvb", [128, 4], F32)
                vn = sb(st, "vn", [128, D], BF16)
                yaT = sb(st, "yaT", [128, 8, 512], BF16)
                ybT = sb(st, "ybT", [128, 8, 512], BF16)
                mixT = sb(st, "mixT", [128, 8, 512], BF16)
                tA = [sb(st, "tA0", [128, 512], F32)] * 2
                tB = [sb(st, "tB0", [128, 512], F32)] * 2
                sga = [sb(st, f"sga{i}", [128, 512], BF16) for i in range(2)]
                sgb = [sb(st, f"sgb{i}", [128, 512], BF16) for i in range(2)]
                wbr = [sb(st, f"wbr{i}", [128, 8, 512], BF16) for i in range(2)]

                def extra_for(c):
                    return [("og", og_d.rearrange("f p t -> p f t")[:, :, c * 128:(c + 1) * 128])]

                for h in range(4):
                    S.op('act', lambda h=h: ACT.activation(out=S16[:, h, :, :], in_=Sin[:, h, :].rearrange("p (k v) -> p k v", k=2), func=AF.Copy, scale=0.0625),
                         w=[("S16", h)])
                pending = []

                def flush_items():
                    while pending:
                        pending.pop(0)[1]()

                def c_hook(stage):
                    n_ab = sum(1 for k, _ in pending if k == 'ab')
                    if n_ab > 0:
                        n = n_ab if stage == 'u' else min(3, n_ab)
                    else:
                        n = min(1, len(pending))
                    for _ in range(n):
                        pending.pop(0)[1]()

                issue_loads(lds, 0, 0, extra_for(0))
                S.dma('sp', hb1[:], hb_d[0], w=["ld_hb"])
                for c in range(16):
                    slot = c % 2
                    cc = c % 4
                    tile = c // 4
                    if c < 15:
                        issue_loads(lds, 1 - slot, c + 1, extra_for(c + 1))
                    b = mk_bufs(lds, slot, common)
                    hbl, ogl = b["hb"], b["og"]

                    def emit_all(sm, row, hbl=hbl):
                        for h in range(4):
                            S.op('dve', lambda h=h: DVE.scalar_tensor_tensor(out=hm_t[:, h * 256:(h + 1) * 256], in0=ps[1 + h][:, 0:256], scalar=sm[:, row, h:h + 1],
                                                                             in1=hbl[:, h * 256:(h + 1) * 256], op0=ALU.mult, op1=ALU.add),
                                 r=[P(1 + h), "sm", "ld_hb"], w=[("hm", h)], small=True)
                        for h in range(4):
                            S.op('dve', lambda h=h: DVE.bn_stats(out=st6[:, h, :], in_=hm_t[:, h * 256:(h + 1) * 256]), r=[("hm", h)], w=[("st6", h)], small=True)
                        for h in range(4):
                            S.op('dve', lambda h=h: DVE.bn_aggr(out=mv[:, h, :], in_=st6[:, h, :]), r=[("st6", h)], w=[("mv", h)], small=True)
                    mlstm_chunk(c, 0, b, S16, emit_all, hook=c_hook)
                    if c < 15:
                        S.dma('sp', hb1[:], hb_d[c + 1], w=["ld_hb"])
                    S.op('act', lambda: ACT.activation(out=lnr[:], in_=mv[:, :, 1], func=AF.Ln, bias=eps_t[:, 0:1], scale=1.0), r=[("mv", h) for h in range(4)], w=["lnr"], small=True)
                    S.op('act', lambda: ACT.activation(out=lnr[:], in_=lnr[:], func=AF.Exp, scale=-0.5), w=["lnr"], small=True)
                    for h in range(4):
                        S.op('dve', lambda h=h: DVE.tensor_scalar(out=hh[:, h * 256:(h + 1) * 256], in0=hm_t[:, h * 256:(h + 1) * 256], scalar1=mv[:, h, 0:1],
                                                                  scalar2=lnr[:, h:h + 1], op0=ALU.subtract, op1=ALU.mult), r=["lnr", ("mv", h)], w=["hh"], strict=True, small=True)
                    psb = ps[0][:, :].bitcast(BF16)
                    for fc in range(8):
                        S.op('pe', lambda fc=fc: PE.transpose(out=psb[:, fc * 128:(fc + 1) * 128], in_=hh[:, fc * 128:(fc + 1) * 128], identity=ident_b[:]),
                             r=["hh"], w=[P(0)], sig=(fc == 7))
                    S.op('dve', lambda cc=cc, ogl=ogl: DVE.tensor_tensor(out=yaT[:, :, cc * 128:(cc + 1) * 128], in0=psb[:, :].rearrange("p (f t) -> p f t", f=8),
                                                                          in1=ogl[:], op=ALU.mult), r=[P(0), f"ld_og{slot}"], w=[("yaT", cc)])
                    if cc == 0:
                        S.dma('sp', ybT[:], gu_d.rearrange("f p t -> p f t")[:, :, 512 * tile:512 * (tile + 1)], w=["ybT"])
                    if c in (0, 4):
                        dbg_dump(f"hm{c}", hm_t[:], [("hm", h) for h in range(4)])
                        dbg_dump(f"hh{c}", hh[:], ["hh"])
                    if cc < 3:
                        continue
                    if tile in (0, 1):
                        dbg_dump(f"ya{tile}", yaT[:], [("yaT", i) for i in range(4)])
                        dbg_dump(f"yb{tile}", ybT[:], ["ybT"])
                    flush_items()
                    t0 = 512 * tile
                    YA = [("yaT", i) for i in range(4)]
                    YB = ["ybT"]
                    MX = [("mixT", i) for i in range(8)]

                    def ab_item(dc, t0=t0, YA=YA, YB=YB):
                        blk, j = dc // 4, dc % 4
                        if j == 0:
                            wload(wbr[0][:], "wbr0", w_ba.rearrange("(kc p) n -> p kc n", p=128)[:, :, blk * 512:(blk + 1) * 512])
                            wload(wbr[1][:], "wbr1", w_bb.rearrange("(kc p) n -> p kc n", p=128)[:, :, blk * 512:(blk + 1) * 512])
                        s2 = dc % 2
                        S.dma('sp', sga[s2][:], ga_d[dc][:, t0:t0 + 512], w=[f"sga{s2}"])
                        S.dma('sp', sgb[s2][:], gb_d[dc][:, t0:t0 + 512], w=[f"sgb{s2}"])
                        pa, pb2 = 7, 0
                        for kc in range(8):
                            S.op('pe', lambda kc=kc: PE.matmul(ps[pa][:, :], lhsT=wbr[0][:, kc, j * 128:(j + 1) * 128], rhs=yaT[:, kc, :],
                                                               start=(kc == 0), stop=(kc == 7)), r=["wbr0"] + YA, w=[P(pa)], sig=(kc == 7))
                        for kc in range(8):
                            S.op('pe', lambda kc=kc: PE.matmul(ps[pb2][:, :], lhsT=wbr[1][:, kc, j * 128:(j + 1) * 128], rhs=ybT[:, kc, :],
                                                               start=(kc == 0), stop=(kc == 7)), r=["wbr1"] + YB, w=[P(pb2)], sig=(kc == 7))
                        S.op('dve', lambda: DVE.tensor_tensor(out=tA[s2][:], in0=ps[pa][:, :], in1=sga[s2][:], op=ALU.mult),
                             r=[P(pa), f"sga{s2}"], w=["tA0"])
                        S.op('dve', lambda: DVE.tensor_tensor(out=tB[s2][:], in0=ps[pb2][:, :], in1=sgb[s2][:], op=ALU.mult),
                             r=[P(pb2), f"sgb{s2}"], w=["tB0"])
                        S.op('pool', lambda: POOL.tensor_tensor(out=mixT[:, dc, :], in0=tA[s2][:], in1=tB[s2][:], op=ALU.add),
                             r=["tA0", "tB0"], w=[("mixT", dc)])

                    def out_item(dc, t0=t0, tile=tile, MX=MX):
                        blk, j = dc // 4, dc % 4
                        if j == 0:
                            wload(wbr[blk][:], f"wbr{blk}", w_o.rearrange("(kc p) n -> p kc n", p=128)[:, :, blk * 512:(blk + 1) * 512])
                        pb = 7 if dc % 2 == 0 else 0
                        for kc in range(8):
                            S.op('pe', lambda kc=kc: PE.matmul(ps[pb][:, :], lhsT=wbr[blk][:, kc, j * 128:(j + 1) * 128], rhs=mixT[:, kc, :],
                                                               start=(kc == 0), stop=(kc == 7)), r=[f"wbr{blk}"] + MX, w=[P(pb)], sig=(kc == 7))
                        S.op('dve', lambda: DVE.scalar_tensor_tensor(
                            out=hT[:, dc, 1 + t0:1 + t0 + 512], in0=ps[pb][:, :], scalar=prm[:, P_G5, dc:dc + 1], in1=hT[:, dc, 1 + t0:1 + t0 + 512],
                            op0=ALU.mult, op1=ALU.add), r=[P(pb)], w=[("hT2", tile, dc)])

                    for dc in range(8):
                        pending.append(('ab', lambda dc=dc, f=ab_item: f(dc)))
                    for dc in range(8):
                        pending.append(('out', lambda dc=dc, f=out_item: f(dc)))
                    if KITEMS == 0:
                        flush_items()
                flush_items()
                S.barrier()
        for nm, src in [("qT", qT_d), ("kT", kT_d), ("ktok", ktok_d), ("v", v_d), ("hb", hb_d), ("og", og_d), ("gu", gu_d), ("ga", ga_d), ("vs", vs_d)]:
            if nm in dbg:
                S.dma('sp', dbg[nm], src)
        if "h2" in dbg:
            S.dma('sp', dbg["h2"].rearrange("dc p t -> p dc t"), hT[:, :, :])
        S.barrier()
        mx.close()
        S.barrier()

        if KSTOP == "all":
            ffn(w_f2i, w_f2o, main_tiles, (P_GS2, P_SH2, P_GT2), (P_GS1C, P_SH1C, P_GT1C), "f2")

        if "h1" in dbg:
            S.dma('sp', dbg["h1"].rearrange("dc p t -> p dc t"), hT[:, :, :], r=[])
            S.dma('sp', dbg["hc1"].rearrange("dc p t -> p dc t"), hcT[:, :, :], r=[])
            S.barrier()

        def final_out():
            with ExitStack() as st:
                sq = sb(st, "fsq", [128, 8, 512], BF16)
                rstd = sb(st, "frstd", [128, 512], F32)
                yT = [sb(st, f"fy{i}", [128, 8, 512], F32) for i in range(2)]
                ot = [sb(st, f"fot{i}", [128, D], F32) for i in range(2)]
                for t in range(4):
                    o0 = 1 + 512 * t
                    y = yT[t % 2]
                    ytok = f"fy{t % 2}"
                    for dc in range(8):
                        S.op('act', lambda dc=dc: ACT.activation(out=sq[:, dc, :], in_=hT[:, dc, o0:o0 + 512], func=AF.Square), w=["fsq"])
                    for dc in range(8):
                        S.op('pe', lambda dc=dc: PE.matmul(ps[7][:, :], lhsT=ones_b[:], rhs=sq[:, dc, :], start=(dc == 0), stop=(dc == 7)),
                             r=["fsq"], w=[P(7)], sig=(dc == 7))
                    S.op('act', lambda: ACT.activation(out=rstd[:], in_=ps[7][:, :], func=AF.Ln, scale=1.0 / D, bias=eps_t[:, 0:1]), r=[P(7)], w=["frstd"])
                    S.op('act', lambda: ACT.activation(out=rstd[:], in_=rstd[:], func=AF.Exp, scale=-0.5), w=["frstd"])
                    for dc in range(8):
                        S.op('dve', lambda dc=dc, y=y: DVE.scalar_tensor_tensor(out=y[:, dc, :], in0=hT[:, dc, o0:o0 + 512], scalar=vecT[:, R_GFIN + dc:R_GFIN + dc + 1],
                                                                              in1=rstd[:], op0=ALU.mult, op1=ALU.mult),
                             r=["frstd"], w=[(ytok, dc)])
                    for cc in range(4):
                        c = 4 * t + cc
                        o = ot[c % 2]
                        for half in range(2):
                            pb = 2 + half + 2 * (c % 2)
                            for k4 in range(4):
                                dc = half * 4 + k4
                                S.op('pe', lambda dc=dc, k4=k4, pb=pb, y=y, cc=cc: PE.transpose(out=ps[pb][:, k4 * 128:(k4 + 1) * 128], in_=y[:, dc, cc * 128:(cc + 1) * 128],
                                                                                          identity=ident_f[:]),
                                     r=[(ytok, dc)], w=[P(pb)], sig=(k4 == 3))
                            if half == 0:
                                S.op('act', lambda pb=pb, o=o: ACT.copy(out=o[:, 0:512], in_=ps[pb][:, :]), r=[P(pb)], w=[f"fot{c % 2}a"])
                            else:
                                S.op('dve', lambda pb=pb, o=o: DVE.tensor_copy(out=o[:, 512:1024], in_=ps[pb][:, :]), r=[P(pb)], w=[f"fot{c % 2}b"])
                        S.dma('sp', out[128 * c:128 * (c + 1), :], o[:], r=[f"fot{c % 2}a", f"fot{c % 2}b"], w=[])
            S.barrier()

        final_out()
    return nc


def _host_inputs(inp):
    x = np.ascontiguousarray(inp["x"], dtype=np.float32)
    f32 = np.float32
    vec_common = [
        inp["b_ada"][0].reshape(72, 128), inp["g_ffn1"][0].reshape(8, 128), inp["g_mix"][0].reshape(8, 128),
        inp["conv_qk_w"][0].reshape(48, 128), inp["conv_qk_b"][0].reshape(16, 128), inp["g_head"][0].reshape(8, 128),
        inp["g_ffn2"][0].reshape(8, 128), inp["g_final"].reshape(8, 128)]
    quarter = D // 4
    fr = np.exp(-math.log(10000.0) * np.arange(quarter, dtype=f32) / quarter).astype(f32)
    freq = np.ascontiguousarray(fr.reshape(2, 128).T)
    consts = np.zeros((128, 3, 128), f32)
    consts[:, 0, :] = np.eye(128, dtype=f32)
    ii = np.arange(128)
    consts[:, 1, :] = (ii[:, None] <= ii[None, :]).astype(f32)
    consts[:, 2, :] = (ii[:, None] >= ii[None, :]).astype(f32)
    shared = dict(
        consts=consts, freq=freq, g_sgu=np.ascontiguousarray(inp["g_sgu"][0]), b_gates=np.ascontiguousarray(inp["b_gates"][0]),
        b_s=np.ascontiguousarray(inp["b_s"][0].reshape(512)), w_s=np.ascontiguousarray(inp["w_s"][0]),
        w_ffn1_in=np.ascontiguousarray(inp["w_ffn1_in"][0]),
        w_ffn1_out=np.ascontiguousarray(inp["w_ffn1_out"][0]), w_in=np.ascontiguousarray(inp["w_in"][0]),
        w_branch_a=np.ascontiguousarray(inp["w_branch_a"][0]), w_branch_b=np.ascontiguousarray(inp["w_branch_b"][0]),
        w_out=np.ascontiguousarray(inp["w_out"][0]), w_ffn2_in=np.ascontiguousarray(inp["w_ffn2_in"][0]),
        w_ffn2_out=np.ascontiguousarray(inp["w_ffn2_out"][0]))
    maps = []
    for core in range(8):
        b, j = core // 4, core % 4
        a = j * NT
        xs = np.zeros((NX, D), f32)
        xs[1:NT + 1] = x[b, a:a + NT]
        if j > 0:
            xs[0] = x[b, a - 1]
        if j < 3:
            xs[NX - 1] = x[b, a + NT]
        vecs = np.concatenate(vec_common + [inp["c"][b].reshape(8, 128), inp["c_ctx"].reshape(8, 128)], axis=0).astype(f32)
        meta = np.zeros((128, 8), f32)
        meta[:, 0] = j * 32 - 1
        meta[:, 1 + j] = 1.0
        meta[:, 5] = 1.0 if j > 0 else 0.0
        meta[:, 6] = 1.0 if j < 3 else 0.0
        m = dict(shared)
        m.update(w_ada=np.ascontiguousarray(inp["w_ada"][0][:, j * 2304:(j + 1) * 2304]), xs=xs, ctxb=np.ascontiguousarray(inp["ctx"][b], dtype=f32), vecs=np.ascontiguousarray(vecs), meta=meta)
        maps.append(m)
    return maps


_NC_CACHE = {}


def kernel(**inputs):
    inp = {k: np.asarray(v) for k, v in inputs.items()}
    maps = _host_inputs(inp)
    if "nc" not in _NC_CACHE:
        _NC_CACHE["nc"] = build()
    res = run_bass_kernel_spmd(_NC_CACHE["nc"], maps, core_ids=list(range(8)))
    outp = np.zeros((2, 4 * NT, D), np.float32)
    for core in range(8):
        b, j = core // 4, core % 4
        outp[b, j * NT:(j + 1) * NT] = res.results[core]["out"]
    kernel.last_results = res.results
    return outp
```

```python
import math
from contextlib import ExitStack

import numpy as np
import concourse.bass as bass
import concourse.mybir as mybir
from concourse.bass_utils import run_bass_kernel_spmd

F32 = mybir.dt.float32
BF16 = mybir.dt.bfloat16
I32 = mybir.dt.int32
AF = mybir.ActivationFunctionType
ALU = mybir.AluOpType

D = 1024
NT = 2048
NX = NT + 2
NCTX = 256
NCH = 16
DFF = 2816
NFF = 22
DPROJ = 8208
EPS = 1e-6
TWO_PI = 2.0 * math.pi
PI_SAFE = 3.1415925
XW = 8 * 514 + 8

DEBUG = {}
import os
KSTOP = os.environ.get("KSTOP", "all")
KITEMS = int(os.environ.get("KITEMS", "1"))


class Sch:
    NDS = 40

    def __init__(self, nc, es):
        self.nc = nc
        self.E = {'pe': nc.tensor, 'act': nc.scalar, 'dve': nc.vector, 'pool': nc.gpsimd, 'sp': nc.sync}
        self.semobj = {}
        for e in ['pe', 'act', 'dve', 'pool']:
            self.semobj[e] = es.enter_context(nc.semaphore(f"sem_{e}"))
        self.cnt = {e: 0 for e in ['pe', 'act', 'dve', 'pool']}
        self.seen = {e: {} for e in self.E}
        self.lw = {}
        self.rd = {}
        self.pend = {e: [] for e in self.cnt}
        self.dcnt = [0] * self.NDS
        for i in range(self.NDS):
            self.semobj[('d', i)] = es.enter_context(nc.semaphore(f"sem_d{i}"))
        self.dnext = 0
        self.nops = 0
        self.semobj['cc'] = es.enter_context(nc.semaphore("sem_cc"))
        self.cccnt = 0
        self.smallp = set()

    def _deps(self, eng, r, w):
        deps = set()
        for t in r:
            d = self.lw.get(t)
            if d is not None:
                deps.add((d, True))
        for t in w:
            d = self.lw.get(t)
            if d is not None:
                deps.add((d, True))
            for d in self.rd.get(t, ()):
                deps.add((d, False))
        return deps

    def _wait(self, eng, deps, strict=False):
        for (d, is_w) in deps:
            if d[0] == 'PEND':
                assert d[1] == eng, f"dependency on unsignaled op of {d[1]} from {eng}"
                continue
            key, val, src = d
            if src == eng:
                if eng == 'pe' or not is_w:
                    continue
                if not (strict or (key, val) in self.smallp):
                    continue
            if self.seen[eng].get(key, 0) >= val:
                continue
            self.E[eng].wait_ge(self.semobj[key], val)
            self.seen[eng][key] = val

    def op(self, eng, fn, r=(), w=(), sig=True, strict=False, small=False):
        strict = strict or small
        self._wait(eng, self._deps(eng, r, w), strict)
        ins = fn()
        self.nops += 1
        if sig:
            self.cnt[eng] += 1
            ins.then_inc(self.semobj[eng], 1)
            me = (eng, self.cnt[eng], eng)
            if small:
                self.smallp.add((eng, self.cnt[eng]))
            for (pr, pw) in self.pend[eng] + [(r, w)]:
                for t in pw:
                    self.lw[t] = me
                    self.rd[t] = []
            pm = ('PEND', eng)
            for (pr, pw) in self.pend[eng] + [(r, w)]:
                for t in pr:
                    lst = self.rd.setdefault(t, [])
                    if pm in lst:
                        lst[:] = [d for d in lst if d != pm]
                    if me not in lst:
                        lst.append(me)
            self.pend[eng] = []
        else:
            self.pend[eng].append((tuple(r), tuple(w)))
            for t in w:
                self.lw[t] = ('PEND', eng)
                self.rd[t] = []
            for t in r:
                self.rd.setdefault(t, []).append(('PEND', eng))
        return ins

    def dma(self, q, out, in_, r=(), w=(), **kw):
        deps = self._deps(q, r, w)
        idx = self.dnext
        self.dnext = (self.dnext + 1) % self.NDS
        if self.dcnt[idx] > 0:
            deps.add(((('d', idx), self.dcnt[idx], 'dma'), True))
        self._wait(q, deps)
        self.dcnt[idx] += 16
        self.E[q].dma_start(out=out, in_=in_, **kw).then_inc(self.semobj[('d', idx)], 16)
        me = (('d', idx), self.dcnt[idx], 'dma')
        for t in w:
            self.lw[t] = me
            self.rd[t] = []
        for t in r:
            self.rd.setdefault(t, []).append(me)

    def custom(self, eng, fn, inc, r=(), w=()):
        deps = self._deps(eng, r, w)
        self._wait(eng, deps)
        self.cccnt += inc
        fn(self.semobj['cc'])
        me = ('cc', self.cccnt, 'dma')
        for t in w:
            self.lw[t] = me
            self.rd[t] = []
        for t in r:
            self.rd.setdefault(t, []).append(me)

    def barrier(self):
        for e in self.cnt:
            assert not self.pend[e], f"pending unsignaled ops on {e} at barrier"
        deps = set()
        for e in self.cnt:
            if self.cnt[e] > 0:
                deps.add(((e, self.cnt[e], e), False))
        for i in range(self.NDS):
            if self.dcnt[i] > 0:
                deps.add(((('d', i), self.dcnt[i], 'dma'), True))
        if self.cccnt > 0:
            deps.add((('cc', self.cccnt, 'dma'), True))
        for e in self.E:
            self._wait(e, deps)
        self.lw = {}
        self.rd = {}


def build():
    nc = bass.Bass("TRN2", target_bir_lowering=False)

    def din(name, shape, dt=F32):
        return nc.dram_tensor(name, list(shape), dt, kind="ExternalInput").ap()

    xs = din("xs", [NX, D])
    ctxb = din("ctxb", [NCTX, D])
    vecs = din("vecs", [192, 128])
    meta = din("meta", [128, 8])
    freq = din("freq", [128, 2])
    consts = din("consts", [128, 3, 128])
    g_sgu = din("g_sgu", [D])
    b_gates = din("b_gates", [16])
    b_s = din("b_s", [512])
    w_s = din("w_s", [4, 128, 128])
    w_ada = din("w_ada", [D, 9 * D // 4])
    w_f1i = din("w_ffn1_in", [D, 2 * DFF])
    w_f1o = din("w_ffn1_out", [DFF, D])
    w_in = din("w_in", [D, DPROJ])
    w_ba = din("w_branch_a", [D, D])
    w_bb = din("w_branch_b", [D, D])
    w_o = din("w_out", [D, D])
    w_f2i = din("w_ffn2_in", [D, 2 * DFF])
    w_f2o = din("w_ffn2_out", [DFF, D])
    out = nc.dram_tensor("out", [NT, D], F32, kind="ExternalOutput").ap()

    def dscr(name, shape, dt):
        return nc.dram_tensor(name, list(shape), dt).ap()

    qT_d = dscr("qT_d", [8, 128, NT], BF16)
    kT_d = dscr("kT_d", [8, 128, NT + NCTX], BF16)
    ktok_d = dscr("ktok_d", [18, 128, D], BF16)
    v_d = dscr("v_d", [18, 128, D], BF16)
    vs_d = dscr("vs_d", [16, 128, D], F32)
    og_d = dscr("og_d", [8, 128, NT], BF16)
    gu_d = dscr("gu_d", [8, 128, NT], BF16)
    ga_d = dscr("ga_d", [8, 128, NT], BF16)
    gb_d = dscr("gb_d", [8, 128, NT], BF16)
    hb_d = dscr("hb_d", [16, 128, D], F32)
    mp_in = dscr("mp_in", [128, 36], F32)
    mp_out = dscr("mp_out", [4 * 128, 36], F32)
    xin_l = [dscr(f"xin_d{i}", [128, 1030], F32) for i in range(4)]
    xout_l = [dscr(f"xout_d{i}", [4 * 128, 1030], F32) for i in range(4)]

    dbg = {}
    for name, (shape, dt) in DEBUG.items():
        dbg[name] = nc.dram_tensor("dbg_" + name, list(shape), dt, kind="ExternalOutput").ap()

    with ExitStack() as es:
        S = Sch(nc, es)
        ACT, DVE, PE, POOL = nc.scalar, nc.vector, nc.tensor, nc.gpsimd

        def sb(stack, name, shape, dt):
            return stack.enter_context(nc.sbuf_tensor(name, list(shape), dt))

        def pstep(t):
            return t[:].ap[0][0]

        def cap(t, off, dims):
            return bass.AP(t, off, [[pstep(t), 128]] + [list(d) for d in dims])

        psall = es.enter_context(nc.psum_tensor("psall", [128, 8, 512], F32))
        ps = [psall[:, i, :] for i in range(8)]

        def P(i):
            return ("ps", i)

        ident_f = sb(es, "ident_f", [128, 128], F32)
        triL = sb(es, "triL", [128, 128], F32)
        triU = sb(es, "triU", [128, 128], F32)
        ident_b = sb(es, "ident_b", [128, 128], BF16)
        mL16 = sb(es, "mL16", [128, 128], F32)
        mU16 = sb(es, "mU16", [128, 128], F32)
        ones_f = sb(es, "ones_f", [128, 128], F32)
        ones_b = sb(es, "ones_b", [128, 128], BF16)
        vecT = sb(es, "vecT", [128, 192], F32)
        metat = sb(es, "metat", [128, 8], F32)
        modB = sb(es, "modB", [128, 72], F32)
        modC = sb(es, "modC", [128, 72], F32)
        prm = sb(es, "prm", [128, 16, 8], F32)
        hT = sb(es, "hT", [128, 8, NX], F32)
        hcT = sb(es, "hcT", [128, 8, NCTX], F32)
        eps_t = sb(es, "eps_t", [128, 1], F32)

        R_BADA, R_GF1, R_GMIX, R_CW, R_CB, R_GH, R_GF2, R_GFIN, R_C, R_CC = 0, 72, 80, 88, 136, 152, 160, 168, 176, 184
        (P_GS1, P_SH1, P_GT1, P_GS1C, P_SH1C, P_GT1C, P_GSM, P_SHM, P_GSMC, P_SHMC, P_G5, P_GS2, P_SH2, P_GT2) = range(14)

        S.dma('sp', ident_f[:], consts[:, 0, :], w=["ident_f"])
        S.dma('sp', triL[:], consts[:, 1, :], w=["triL"])
        S.dma('sp', triU[:], consts[:, 2, :], w=["triU"])
        S.dma('sp', metat[:], meta, w=["meta"])
        S.op('dve', lambda: DVE.tensor_copy(out=ident_b[:], in_=ident_f[:]), r=["ident_f"], w=["ident_b"])
        S.op('dve', lambda: DVE.tensor_scalar(out=mL16[:], in0=triL[:], scalar1=0.0625, scalar2=None, op0=ALU.mult), r=["triL"], w=["mL16"])
        S.op('dve', lambda: DVE.tensor_scalar(out=mU16[:], in0=triU[:], scalar1=0.0625, scalar2=None, op0=ALU.mult), r=["triU"], w=["mU16"])
        S.op('dve', lambda: DVE.memset(ones_f[:], 1.0), w=["ones_f"])
        S.op('dve', lambda: DVE.memset(ones_b[:], 1.0), w=["ones_b"])

        def wload(dst_tile, dst_tok, src_ap):
            S.dma('pool', dst_tile, src_ap, w=[dst_tok])

        with ExitStack() as p1:
            vst = sb(p1, "vst", [96, 2, 128], F32)
            S.dma('sp', vst[:, 0, :], vecs[0:96, :], w=["vst0"])
            S.dma('sp', vst[:, 1, :], vecs[96:192, :], w=["vst1"])
            for i in range(2):
                S.op('pe', lambda i=i: PE.transpose(out=ps[0][:, i * 96:(i + 1) * 96], in_=vst[:, i, :], identity=ident_f[0:96, 0:96]),
                     r=[f"vst{i}", "ident_f"], w=[P(0)], sig=(i == 1))
            S.op('dve', lambda: DVE.tensor_copy(out=vecT[:], in_=ps[0][:, 0:192]), r=[P(0)], w=["vecT"])

            scT = sb(p1, "scT", [128, 8, 2], F32)
            S.op('act', lambda: ACT.activation(out=scT[:, :, 0], in_=vecT[:, R_C:R_C + 8], func=AF.Silu), r=["vecT"], w=["scT0"])
            S.op('act', lambda: ACT.activation(out=scT[:, :, 1], in_=vecT[:, R_CC:R_CC + 8], func=AF.Silu), r=["vecT"], w=["scT1"])

            wad = [sb(p1, f"wad{i}", [128, 8, 256], F32) for i in range(3)]
            w_ada_v = w_ada.rearrange("(kc p) n -> p kc n", p=128)
            for blk in range(9):
                slot = blk % 3
                S.dma('sp', wad[slot][:], w_ada_v[:, :, blk * 256:(blk + 1) * 256], w=[f"wad{slot}"])
                for j in range(2):
                    b128 = blk * 2 + j
                    for kc in range(8):
                        S.op('pe', lambda slot=slot, j=j, kc=kc, b128=b128: PE.matmul(
                            ps[1][:, b128 * 2:b128 * 2 + 2], lhsT=wad[slot][:, kc, j * 128:(j + 1) * 128], rhs=scT[:, kc, :],
                            start=(kc == 0), stop=(kc == 7)),
                            r=[f"wad{slot}", "scT0", "scT1"], w=[P(1)], sig=(kc == 7 and j == 1))
            mpart = sb(p1, "mpart", [128, 36], F32)
            mg = sb(p1, "mg", [128, 4, 36], F32)
            S.op('dve', lambda: DVE.tensor_copy(out=mpart[:], in_=ps[1][:, 0:36]), r=[P(1)], w=["mpart"])
            S.dma('sp', mp_in, mpart[:], r=["mpart"], w=["mp_in"])
            S.custom('pool', lambda sem: POOL.collective_compute("AllGather", ALU.bypass, replica_groups=[[0, 1, 2, 3], [4, 5, 6, 7]],
                                                                 ins=[mp_in.opt()], outs=[mp_out.opt()]).then_inc(sem, 1),
                     1, r=["mp_in"], w=["mp_out"])
            S.dma('sp', mg[:], mp_out.rearrange("(r p) w -> p r w", p=128), r=["mp_out"], w=["mg"])
            psm = mg[:, :, :].rearrange("p r (b t) -> p (r b) t", t=2)
            S.op('dve', lambda: DVE.tensor_tensor(out=modB[:], in0=psm[:, :, 0], in1=vecT[:, R_BADA:R_BADA + 72], op=ALU.add), r=["mg", "vecT"], w=["modB"], small=True)
            S.op('dve', lambda: DVE.tensor_tensor(out=modC[:], in0=psm[:, :, 1], in1=vecT[:, R_BADA:R_BADA + 72], op=ALU.add), r=["mg", "vecT"], w=["modC"], small=True)

            def mk_gs(slot, gain_row, mod, scale_idx):
                S.op('dve', lambda: DVE.scalar_tensor_tensor(out=prm[:, slot, :], in0=mod[:, scale_idx * 8:scale_idx * 8 + 8], scalar=1.0,
                                                             in1=vecT[:, gain_row:gain_row + 8], op0=ALU.add, op1=ALU.mult),
                     r=["modB", "modC", "vecT"], w=[("prm", slot)], small=True)

            def mk_cp(slot, mod, idx, mul=1.0):
                S.op('dve', lambda: DVE.tensor_scalar(out=prm[:, slot, :], in0=mod[:, idx * 8:idx * 8 + 8], scalar1=mul, scalar2=None, op0=ALU.mult),
                     r=["modB", "modC"], w=[("prm", slot)], small=True)

            mk_gs(P_GS1, R_GF1, modB, 1); mk_cp(P_SH1, modB, 0); mk_cp(P_GT1, modB, 2, 0.5)
            mk_gs(P_GS1C, R_GF1, modC, 1); mk_cp(P_SH1C, modC, 0); mk_cp(P_GT1C, modC, 2, 0.5)
            mk_gs(P_GSM, R_GMIX, modB, 4); mk_cp(P_SHM, modB, 3)
            mk_gs(P_GSMC, R_GMIX, modC, 4); mk_cp(P_SHMC, modC, 3)
            mk_cp(P_G5, modB, 5)
            mk_gs(P_GS2, R_GF2, modB, 7); mk_cp(P_SH2, modB, 6); mk_cp(P_GT2, modB, 8, 0.5)

            fq = sb(p1, "fq", [128, 2], F32)
            S.dma('sp', fq[:], freq, w=["fq"])
            tab_r = sb(p1, "tab_r", [128, 4, 34], F32)
            tab_c = sb(p1, "tab_c", [128, 4, 64], F32)
            io_i = sb(p1, "io_i", [128, 64], I32)
            io_f = sb(p1, "io_f", [128, 64], F32)
            rv = sb(p1, "rv", [128, 34], F32)
            S.op('pool', lambda: POOL.iota(io_i[:], pattern=[[1, 64]], base=0, channel_multiplier=0), w=["io_i"])
            S.op('dve', lambda: DVE.tensor_copy(out=io_f[:], in_=io_i[:]), r=["io_i"], w=["io_f"], small=True)
            S.op('dve', lambda: DVE.tensor_scalar(out=rv[:], in0=io_f[:, 0:34], scalar1=metat[:, 0:1], scalar2=None, op0=ALU.add), r=["io_f", "meta"], w=["rv"], small=True)

            def mk_tab(tab, vals, n):
                arg = sb(p1, f"arg_{n}", [128, 4, n], F32)
                ki = sb(p1, f"ki_{n}", [128, 4, n], I32)
                kf = sb(p1, f"kf_{n}", [128, 4, n], F32)
                for cj in range(2):
                    for sc in range(2):
                        idx = sc * 2 + cj
                        S.op('dve', lambda idx=idx, cj=cj, sc=sc: DVE.tensor_scalar(
                            out=arg[:, idx, :], in0=vals, scalar1=fq[:, cj:cj + 1], scalar2=(0.5 * math.pi if sc else 0.0),
                            op0=ALU.mult, op1=ALU.add), r=["rv", "io_f", "fq"], w=[f"arg{n}"], small=True)
                S.op('dve', lambda: DVE.tensor_scalar(out=kf[:], in0=arg[:], scalar1=1.0 / TWO_PI, scalar2=None, op0=ALU.mult), r=[f"arg{n}"], w=[f"kf{n}"], small=True)
                S.op('dve', lambda: DVE.tensor_copy(out=ki[:], in_=kf[:]), r=[f"kf{n}"], w=[f"ki{n}"], small=True)
                S.op('dve', lambda: DVE.tensor_copy(out=kf[:], in_=ki[:]), r=[f"ki{n}"], w=[f"kf{n}"], small=True)
                S.op('dve', lambda: DVE.scalar_tensor_tensor(out=arg[:], in0=kf[:], scalar=-TWO_PI, in1=arg[:], op0=ALU.mult, op1=ALU.add),
                     r=[f"kf{n}"], w=[f"arg{n}"], small=True)
                S.op('dve', lambda: DVE.tensor_scalar(out=arg[:], in0=arg[:], scalar1=-PI_SAFE, scalar2=PI_SAFE, op0=ALU.max, op1=ALU.min), w=[f"arg{n}"], small=True)
                S.op('act', lambda: ACT.activation(out=tab[:], in_=arg[:], func=AF.Sin), r=[f"arg{n}"], w=[f"tab{n}"], small=True)

            mk_tab(tab_r, rv[:], 34)
            mk_tab(tab_c, io_f[:], 64)

            xt = [sb(p1, f"xt{i}", [128, D], F32) for i in range(2)]
            xh = sb(p1, "xh", [2, D], F32)
            for c in range(NCH):
                slot = c % 2
                S.dma('sp', xt[slot][:], xs[1 + 128 * c:1 + 128 * (c + 1), :], w=[f"xt{slot}"])
                for half in range(2):
                    pb = 2 + half
                    for k4 in range(4):
                        dc = half * 4 + k4
                        S.op('pe', lambda slot=slot, dc=dc, pb=pb, k4=k4: PE.transpose(
                            out=ps[pb][:, k4 * 128:(k4 + 1) * 128], in_=xt[slot][:, dc * 128:(dc + 1) * 128], identity=ident_f[:]),
                            r=[f"xt{slot}", "ident_f"], w=[P(pb)], sig=(k4 == 3))
                    o_ap = cap(hT, half * 4 * NX + 1 + 128 * c, [[NX, 4], [64, 2], [1, 64]])
                    i_ap = ps[pb][:, :].rearrange("p (a b c) -> p a b c", a=4, b=2, c=64)
                    if half == 0:
                        t_ap = cap(tab_r, 1 + 2 * c, [[34, 4], [1, 2], [0, 64]])
                        S.op('dve', lambda o_ap=o_ap, i_ap=i_ap, t_ap=t_ap: DVE.tensor_tensor(out=o_ap, in0=i_ap, in1=t_ap, op=ALU.add),
                             r=[P(pb), "tab34"], w=[("hT", c)])
                    else:
                        t_ap = cap(tab_c, 0, [[64, 4], [0, 2], [1, 64]])
                        S.op('dve', lambda o_ap=o_ap, i_ap=i_ap, t_ap=t_ap: DVE.tensor_tensor(out=o_ap, in0=i_ap, in1=t_ap, op=ALU.add),
                             r=[P(pb), "tab64"], w=[("hT", c)])
            S.dma('sp', xh[0:1, :], xs[0:1, :], w=["xh0"])
            S.dma('sp', xh[1:2, :], xs[NX - 1:NX, :], w=["xh1"])
            for dc in range(8):
                S.op('pe', lambda dc=dc: PE.transpose(out=ps[2][:, dc * 2:dc * 2 + 2], in_=xh[:, dc * 128:(dc + 1) * 128], identity=ident_f[0:2, 0:2]),
                     r=["xh0", "xh1", "ident_f"], w=[P(2)], sig=(dc == 7))
            pv = ps[2][:, 0:16].rearrange("p (d t) -> p d t", t=2)
            S.op('dve', lambda: DVE.tensor_tensor(out=cap(hT, 0, [[NX, 4]]), in0=pv[:, 0:4, 0], in1=tab_r[:, :, 0], op=ALU.add), r=[P(2), "tab34"], w=[("hT", "h0a")])
            S.op('dve', lambda: DVE.tensor_tensor(out=cap(hT, 4 * NX, [[NX, 4]]), in0=pv[:, 4:8, 0], in1=tab_c[:, :, 63], op=ALU.add), r=[P(2), "tab64"], w=[("hT", "h0b")])
            S.op('dve', lambda: DVE.tensor_tensor(out=cap(hT, NX - 1, [[NX, 4]]), in0=pv[:, 0:4, 1], in1=tab_r[:, :, 33], op=ALU.add), r=[P(2), "tab34"], w=[("hT", "h1a")])
            S.op('dve', lambda: DVE.tensor_tensor(out=cap(hT, 4 * NX + NX - 1, [[NX, 4]]), in0=pv[:, 4:8, 1], in1=tab_c[:, :, 0], op=ALU.add), r=[P(2), "tab64"], w=[("hT", "h1b")])
            for c in range(2):
                slot = c % 2
                S.dma('sp', xt[slot][:], ctxb[128 * c:128 * (c + 1), :], w=[f"xt{slot}"])
                for half in range(2):
                    pb = 2 + half
                    for k4 in range(4):
                        dc = half * 4 + k4
                        S.op('pe', lambda slot=slot, dc=dc, pb=pb, k4=k4: PE.transpose(
                            out=ps[pb][:, k4 * 128:(k4 + 1) * 128], in_=xt[slot][:, dc * 128:(dc + 1) * 128], identity=ident_f[:]),
                            r=[f"xt{slot}", "ident_f"], w=[P(pb)], sig=(k4 == 3))
                    S.op('dve', lambda pb=pb, half=half, c=c: DVE.tensor_copy(
                        out=hcT[:, half * 4:half * 4 + 4, c * 128:(c + 1) * 128], in_=ps[pb][:, :].rearrange("p (a b) -> p a b", a=4)),
                        r=[P(pb)], w=[("hcT", c)])
            S.barrier()

        def norm_mod(stk, src, src_off, n, gs_slot, sh_slot, dst, dst_off, uid, stride=1):
            sq, rstd, tmp = stk["sq"], stk["rstd"], stk["tmp"]
            ssrc = src.shape[2]
            sdst = dst.shape[2]

            def s_ap(dc):
                return cap(src, dc * ssrc + src_off, [[stride, n]])

            for dc in range(8):
                S.op('act', lambda dc=dc: ACT.activation(out=sq[:, dc, 0:n], in_=s_ap(dc), func=AF.Square), r=[("src", uid)], w=["sq"])
            for dc in range(8):
                S.op('pe', lambda dc=dc: PE.matmul(ps[7][:, 0:n], lhsT=ones_b[:], rhs=sq[:, dc, 0:n], start=(dc == 0), stop=(dc == 7)),
                     r=["sq", "ones_b"], w=[P(7)], sig=(dc == 7))
            S.op('act', lambda: ACT.activation(out=rstd[:, 0:n], in_=ps[7][:, 0:n], func=AF.Ln, scale=1.0 / D, bias=eps_t[:, 0:1]), r=[P(7)], w=["rstd"], small=(n < 256))
            S.op('act', lambda: ACT.activation(out=rstd[:, 0:n], in_=rstd[:, 0:n], func=AF.Exp, scale=-0.5), w=["rstd"], small=(n < 256))
            for dc in range(8):
                tt = tmp[dc % 2]
                S.op('dve', lambda dc=dc, tt=tt: DVE.scalar_tensor_tensor(out=tt[:, 0:n], in0=s_ap(dc), scalar=prm[:, gs_slot, dc:dc + 1], in1=rstd[:, 0:n],
                                                                         op0=ALU.mult, op1=ALU.mult),
                     r=[("src", uid), "rstd", ("prm", gs_slot)], w=[f"nm_tmp{dc % 2}"])
                S.op('act', lambda dc=dc, tt=tt: ACT.activation(out=dst[:, dc, dst_off:dst_off + n], in_=tt[:, 0:n], func=AF.Identity,
                                                                bias=prm[:, sh_slot, dc:dc + 1], scale=1.0),
                     r=[f"nm_tmp{dc % 2}", ("prm", sh_slot)], w=[("hn", uid)])

        S.op('dve', lambda: DVE.memset(eps_t[:], EPS), w=["eps_t"])
        S.barrier()

        def ffn(w_i, w_o2, tiles, prm_main, prm_ctx, tag):
            with ExitStack() as st:
                hnT = sb(st, "hnT" + tag, [128, 8, NX], BF16)
                hncT = sb(st, "hncT" + tag, [128, 8, NCTX], BF16)
                stk = {"sq": sb(st, "sq" + tag, [128, 8, 512], BF16), "rstd": sb(st, "rstd" + tag, [128, 512], F32),
                       "tmp": [sb(st, f"nmt{i}" + tag, [128, 512], F32) for i in range(2)]}
                ntile = len(tiles)
                for ti, (kind, off, n) in enumerate(tiles):
                    if kind == 'm':
                        norm_mod(stk, hT, off, n, prm_main[0], prm_main[1], hnT, off, (tag, ti))
                    else:
                        norm_mod(stk, hcT, off, n, prm_ctx[0], prm_ctx[1], hncT, off, (tag, ti))
                GRP = 6
                groups = [(0, 6), (6, 6), (12, 6), (18, 4)]
                zT = sb(st, "zT" + tag, [128, GRP, NX + NCTX], BF16)
                wa = [sb(st, f"wa{i}" + tag, [128, 8, 256], BF16) for i in range(2)]
                wb = [sb(st, f"wb{i}" + tag, [128, 8, 256], BF16) for i in range(2)]
                wo = [sb(st, f"wo{i}" + tag, [128, GRP, D], BF16) for i in range(2)]
                sl = [sb(st, f"sl{i}" + tag, [128, 512], F32) for i in range(2)]
                w_i_v = w_i.rearrange("(kc p) n -> p kc n", p=128)
                w_o_v = w_o2.rearrange("(fc p) n -> p fc n", p=128)
                blk_ctr = 0
                for gi, (g0, gn) in enumerate(groups):
                    gslot = gi % 2
                    wload(wo[gslot][:, 0:gn, :], f"wo{gslot}" + tag, w_o_v[:, g0:g0 + gn, :])
                    for b2 in range(gn // 2):
                        f0 = g0 + 2 * b2
                        slot = blk_ctr % 2
                        blk_ctr += 1
                        wload(wa[slot][:], f"wa{slot}" + tag, w_i_v[:, :, f0 * 128:(f0 + 2) * 128])
                        wload(wb[slot][:], f"wb{slot}" + tag, w_i_v[:, :, DFF + f0 * 128:DFF + (f0 + 2) * 128])
                        for ti, (kind, off, n) in enumerate(tiles):
                            src = hnT if kind == 'm' else hncT
                            zoff = off if kind == 'm' else NX + off
                            for j in range(2):
                                fz = 2 * b2 + j
                                pa, pb = (0, 1) if (j == 0) else (2, 3)
                                for kc in range(8):
                                    S.op('pe', lambda kc=kc, j=j, pa=pa, src=src, off=off, n=n, slot=slot: PE.matmul(
                                        ps[pa][:, 0:n], lhsT=wa[slot][:, kc, j * 128:(j + 1) * 128], rhs=src[:, kc, off:off + n],
                                        start=(kc == 0), stop=(kc == 7)),
                                        r=[f"wa{slot}" + tag, ("hn", (tag, ti))], w=[P(pa)], sig=(kc == 7))
                                for kc in range(8):
                                    S.op('pe', lambda kc=kc, j=j, pb=pb, src=src, off=off, n=n, slot=slot: PE.matmul(
                                        ps[pb][:, 0:n], lhsT=wb[slot][:, kc, j * 128:(j + 1) * 128], rhs=src[:, kc, off:off + n],
                                        start=(kc == 0), stop=(kc == 7)),
                                        r=[f"wb{slot}" + tag, ("hn", (tag, ti))], w=[P(pb)], sig=(kc == 7))
                                S.op('act', lambda pa=pa, j=j, n=n: ACT.activation(out=sl[j][:, 0:n], in_=ps[pa][:, 0:n], func=AF.Silu),
                                     r=[P(pa)], w=[f"sl{j}" + tag])
                                S.op('dve', lambda pb=pb, j=j, n=n, fz=fz, zoff=zoff: DVE.tensor_tensor(
                                    out=zT[:, fz, zoff:zoff + n], in0=ps[pb][:, 0:n], in1=sl[j][:, 0:n], op=ALU.mult),
                                    r=[P(pb), f"sl{j}" + tag], w=[("z", tag, ti, fz)])
                    for ti, (kind, off, n) in enumerate(tiles):
                        dstT = hT if kind == 'm' else hcT
                        zoff = off if kind == 'm' else NX + off
                        gt = (prm_main if kind == 'm' else prm_ctx)[2]
                        for dc in range(8):
                            pb = 4 + (dc % 3)
                            for fz in range(gn):
                                S.op('pe', lambda dc=dc, pb=pb, fz=fz, zoff=zoff, n=n, gslot=gslot: PE.matmul(
                                    ps[pb][:, 0:n], lhsT=wo[gslot][:, fz, dc * 128:(dc + 1) * 128], rhs=zT[:, fz, zoff:zoff + n],
                                    start=(fz == 0), stop=(fz == gn - 1)),
                                    r=[f"wo{gslot}" + tag, ("z", tag, ti, fz)], w=[P(pb)], sig=(fz == gn - 1))
                            S.op('dve', lambda dc=dc, pb=pb, dstT=dstT, off=off, n=n, gt=gt: DVE.scalar_tensor_tensor(
                                out=dstT[:, dc, off:off + n], in0=ps[pb][:, 0:n], scalar=prm[:, gt, dc:dc + 1], in1=dstT[:, dc, off:off + n],
                                op0=ALU.mult, op1=ALU.add),
                                r=[P(pb), ("prm", gt)], w=[("res", tag, ti, dc)])
            S.barrier()

        main_tiles = [('m', 1 + 512 * i, 512) for i in range(4)]
        halo_tiles = [('m', 0, 1), ('m', NX - 1, 1)]
        ctx_tile = [('c', 0, NCTX)]

        ffn(w_f1i, w_f1o, main_tiles + halo_tiles + ctx_tile, (P_GS1, P_SH1, P_GT1), (P_GS1C, P_SH1C, P_GT1C), "f1")

        mx = ExitStack()
        gates_all = sb(mx, "gates_all", [128, 18, 16], F32)
        sc_all = sb(mx, "sc_all", [128, 18, 32], F32)
        cs_b = sb(mx, "cs_b", [128, 2, 18, 4], F32)
        cs_g = sb(mx, "cs_g", [128, 18, 8], F32)
        Gseg = sb(mx, "Gseg", [128, 8], F32)
        LL = sb(mx, "LL", [128, 18, 8], F32)
        psc = sb(mx, "psc", [128, 18, 8], F32)
        wsT = sb(mx, "wsT", [128, 4, 128], BF16)
        bs_row = sb(mx, "bs_row", [1, 512], BF16)
        bg_bc = sb(mx, "bg_bc", [128, 16], F32)
        one_t = sb(mx, "one_t", [128, 1], F32)

        def dbg_dump(name, src_ap, rtoks):
            if name in dbg:
                S.dma('sp', dbg[name], src_ap, r=rtoks)

        with ExitStack() as st:
            wst = sb(st, "wst", [128, 4, 128], F32)
            bsf = sb(st, "bsf", [1, 512], F32)
            S.dma('sp', wst[:], w_s.rearrange("g t s -> t g s"), w=["wst"])
            S.dma('sp', bsf[:], b_s.rearrange("(o n) -> o n", o=1), w=["bsf"])
            S.dma('sp', bg_bc[:], bass.AP(b_gates.tensor, 0, [[0, 128], [1, 16]]), w=["bg_bc"])
            S.op('dve', lambda: DVE.memset(one_t[:], 1.0), w=["one_t"])
            for g in range(4):
                S.op('pe', lambda g=g: PE.transpose(out=ps[0][:, g * 128:(g + 1) * 128], in_=wst[:, g, :], identity=ident_f[:]),
                     r=["wst"], w=[P(0)], sig=(g == 3))
            S.op('dve', lambda: DVE.tensor_copy(out=wsT[:], in_=ps[0][:, :].rearrange("p (g t) -> p g t", g=4)), r=[P(0)], w=["wsT"])
            S.op('dve', lambda: DVE.tensor_copy(out=bs_row[:], in_=bsf[:]), r=["bsf"], w=["bs_row"])
            S.barrier()

        GC = 1.5957691216057308

        class GeluPipe:
            def __init__(self):
                self.prev = None

            def push(self, x_ap, t1, out_ap, xtoks, t1tok, outtok, after=None):
                S.op('act', lambda: ACT.activation(out=t1, in_=x_ap, func=AF.Square), r=xtoks, w=[t1tok])
                S.op('dve', lambda: DVE.tensor_scalar(out=t1, in0=t1, scalar1=0.044715, scalar2=1.0, op0=ALU.mult, op1=ALU.add), w=[t1tok])
                S.op('dve', lambda: DVE.tensor_tensor(out=t1, in0=x_ap, in1=t1, op=ALU.mult), r=xtoks, w=[t1tok])
                self.flush()
                self.prev = (x_ap, t1, out_ap, xtoks, t1tok, outtok, after)

            def flush(self):
                if self.prev is None:
                    return
                x_ap, t1, out_ap, xtoks, t1tok, outtok, after = self.prev
                self.prev = None
                S.op('act', lambda: ACT.activation(out=t1, in_=t1, func=AF.Sigmoid, scale=GC), r=[t1tok], w=[t1tok])
                S.op('dve', lambda: DVE.tensor_tensor(out=out_ap, in0=x_ap, in1=t1, op=ALU.mult), r=xtoks + [t1tok], w=[outtok])
                if after:
                    after()

        gpipe = GeluPipe()

        with ExitStack() as st:
            hnT = sb(st, "hnT_m", [128, 8, NX], BF16)
            hncT = sb(st, "hncT_m", [128, 8, NCTX], BF16)
            with ExitStack() as nst:
                stk = {"sq": sb(nst, "sq_m", [128, 8, 512], BF16), "rstd": sb(nst, "rstd_m", [128, 512], F32),
                       "tmp": [sb(nst, f"nmt{i}_m", [128, 512], F32) for i in range(2)]}
                tl = main_tiles + halo_tiles
                for ti, (kind, off, n) in enumerate(tl):
                    norm_mod(stk, hT, off, n, P_GSM, P_SHM, hnT, off, ("mx", ti))
                norm_mod(stk, hcT, 0, NCTX, P_GSMC, P_SHMC, hncT, 0, ("mx", 6))
                S.barrier()
            HN_MAIN = [("hn", ("mx", i)) for i in range(4)]
            HN_HALO = [("hn", ("mx", 4)), ("hn", ("mx", 5))]
            HN_CTX = [("hn", ("mx", 6))]

            wblk = [sb(st, f"wblk{i}", [128, 8, 512], BF16) for i in range(2)]
            w_in_v = w_in.rearrange("(kc p) n -> p kc n", p=128)
            bctr = [0]
            BLKS = ([(i * 512, 512) for i in range(4)] + [(2048, 512), (2560, 512), (3072, 16), (5136, 512), (5648, 512)]
                    + [(c0 + b * 512, 512) for c0 in (3088, 4112, 6160, 7184) for b in range(2)])
            issued = [0]

            def _issue(i):
                c0, ncols = BLKS[i]
                wload(wblk[i % 2][:, :, 0:ncols], f"wblk{i % 2}", w_in_v[:, :, c0:c0 + ncols])

            def load_blk(c0, ncols):
                i = bctr[0]
                assert BLKS[i] == (c0, ncols), (i, BLKS[i], c0, ncols)
                bctr[0] += 1
                while issued[0] <= min(i + 1, len(BLKS) - 1):
                    _issue(issued[0])
                    issued[0] += 1
                return i % 2

            pre = [sb(st, f"pre{i}", [128, NX], F32) for i in range(2)]
            prec = sb(st, "prec", [128, NCTX + 2], F32)
            accs = [sb(st, f"acc{i}", [128, NT], F32) for i in range(2)]
            qks = [sb(st, f"qks{i}", [128, NT + NCTX], BF16) for i in range(2)]
            ktk = sb(st, "ktk", [128, 18, 128], BF16)
            stg = [sb(st, f"stg{i}", [128, NT], BF16) for i in range(2)]
            ut = [sb(st, f"ut{i}", [128, 512], F32) for i in range(3)]
            vstg = [sb(st, f"vstg{i}", [128, 512], BF16) for i in range(3)]
            vsstg = [sb(st, f"vsstg{i}", [128, 512], F32) for i in range(3)]
            S.op('dve', lambda: DVE.memset(prec[:], 0.0), w=["prec"])

            for blk in range(4):
                slot = load_blk(blk * 512, 512)
                for j in range(4):
                    fc = blk * 4 + j
                    is_k = fc >= 8
                    pr = pre[fc % 2]
                    ptok = f"pre{fc % 2}"
                    acc = accs[fc % 2]
                    atok = f"acc{fc % 2}"
                    for ti in range(4):
                        pb = (0, 1, 6, 7)[ti]
                        for kc in range(8):
                            S.op('pe', lambda kc=kc, j=j, pb=pb, ti=ti, slot=slot: PE.matmul(
                                ps[pb][:, :], lhsT=wblk[slot][:, kc, j * 128:(j + 1) * 128], rhs=hnT[:, kc, 1 + 512 * ti:1 + 512 * (ti + 1)],
                                start=(kc == 0), stop=(kc == 7)), r=[f"wblk{slot}", HN_MAIN[ti]], w=[P(pb)], sig=(kc == 7))
                        S.op('act', lambda pb=pb, ti=ti, pr=pr: ACT.copy(out=pr[:, 1 + 512 * ti:1 + 512 * (ti + 1)], in_=ps[pb][:, :]),
                             r=[P(pb)], w=[(ptok, ti)])
                    for kc in range(8):
                        S.op('pe', lambda kc=kc, j=j, slot=slot: PE.matmul(
                            ps[2][:, 0:2], lhsT=wblk[slot][:, kc, j * 128:(j + 1) * 128], rhs=cap(hnT, kc * NX, [[NX - 1, 2]]),
                            start=(kc == 0), stop=(kc == 7)), r=[f"wblk{slot}"] + HN_HALO, w=[P(2)], sig=(kc == 7))
                    S.op('dve', lambda pr=pr: DVE.tensor_tensor(out=cap(pr, 0, [[NX - 1, 2]]), in0=ps[2][:, 0:2], in1=metat[:, 5:7], op=ALU.mult),
                         r=[P(2), "meta"], w=[(ptok, 4)])
                    if is_k:
                        for kc in range(8):
                            S.op('pe', lambda kc=kc, j=j, slot=slot: PE.matmul(
                                ps[3][:, 0:NCTX], lhsT=wblk[slot][:, kc, j * 128:(j + 1) * 128], rhs=hncT[:, kc, :],
                                start=(kc == 0), stop=(kc == 7)), r=[f"wblk{slot}"] + HN_CTX, w=[P(3)], sig=(kc == 7))
                        S.op('act', lambda: ACT.copy(out=prec[:, 1:1 + NCTX], in_=ps[3][:, 0:NCTX]), r=[P(3)], w=["prec"])
                    w0 = vecT[:, R_CW + 0 * 16 + fc:R_CW + 0 * 16 + fc + 1]
                    w1 = vecT[:, R_CW + 1 * 16 + fc:R_CW + 1 * 16 + fc + 1]
                    w2 = vecT[:, R_CW + 2 * 16 + fc:R_CW + 2 * 16 + fc + 1]
                    cb = vecT[:, R_CB + fc:R_CB + fc + 1]
                    qs = qks[fc % 2]
                    qtok = f"qks{fc % 2}"
                    allpre = [(ptok, i) for i in range(5)]
                    S.op('pool', lambda pr=pr, w0=w0: POOL.tensor_scalar(out=acc[:], in0=pr[:, 0:NT], scalar1=w0, scalar2=0.0, op0=ALU.mult, op1=ALU.add),
                         r=allpre, w=[atok])
                    S.op('dve', lambda pr=pr, w1=w1: DVE.scalar_tensor_tensor(out=acc[:], in0=pr[:, 1:NT + 1], scalar=w1, in1=acc[:], op0=ALU.mult, op1=ALU.add),
                         r=allpre + [atok], w=[atok])
                    S.op('dve', lambda pr=pr, w2=w2: DVE.scalar_tensor_tensor(out=acc[:], in0=pr[:, 2:NT + 2], scalar=w2, in1=acc[:], op0=ALU.mult, op1=ALU.add),
                         r=allpre, w=[atok])
                    S.op('act', lambda qs=qs, cb=cb: ACT.activation(out=qs[:, 0:NT], in_=acc[:], func=AF.Silu, bias=cb, scale=1.0), r=[atok], w=[qtok])
                    if not is_k:
                        S.dma('sp', qT_d[fc], qs[:, 0:NT], r=[qtok], w=[("qT_d", fc)])
                    else:
                        S.op('dve', lambda w0=w0: DVE.tensor_scalar(out=acc[:, 0:NCTX], in0=prec[:, 0:NCTX], scalar1=w0, scalar2=None, op0=ALU.mult),
                             r=["prec", atok], w=[atok])
                        S.op('dve', lambda w1=w1: DVE.scalar_tensor_tensor(out=acc[:, 0:NCTX], in0=prec[:, 1:NCTX + 1], scalar=w1, in1=acc[:, 0:NCTX], op0=ALU.mult, op1=ALU.add),
                             r=["prec"], w=[atok])
                        S.op('dve', lambda w2=w2: DVE.scalar_tensor_tensor(out=acc[:, 0:NCTX], in0=prec[:, 2:NCTX + 2], scalar=w2, in1=acc[:, 0:NCTX], op0=ALU.mult, op1=ALU.add),
                             r=["prec"], w=[atok])
                        S.op('act', lambda qs=qs, cb=cb: ACT.activation(out=qs[:, NT:NT + NCTX], in_=acc[:, 0:NCTX], func=AF.Silu, bias=cb, scale=1.0),
                             r=[atok], w=[qtok])
                        S.dma('sp', kT_d[fc - 8], qs[:, :], r=[qtok], w=[("kT_d", fc - 8)])
                        for grp in range(3):
                            c0 = grp * 8
                            ncg = min(8, 18 - c0)
                            pbank = 4 + (grp % 2)
                            psb = ps[pbank][:, :].bitcast(BF16)
                            for ci in range(ncg):
                                S.op('pe', lambda ci=ci, c0=c0, psb=psb, qs=qs: PE.transpose(
                                    out=psb[:, ci * 128:(ci + 1) * 128], in_=qs[:, (c0 + ci) * 128:(c0 + ci + 1) * 128], identity=ident_b[:]),
                                    r=[qtok, "ident_b"], w=[P(pbank)], sig=(ci == ncg - 1))
                            S.op('dve', lambda c0=c0, ncg=ncg, psb=psb: DVE.tensor_copy(
                                out=ktk[:, c0:c0 + ncg, :], in_=psb[:, 0:ncg * 128].rearrange("p (c f) -> p c f", f=128)),
                                r=[P(pbank)], w=["ktk"])
                        S.dma('sp', ktok_d.rearrange("c p f -> p c f")[:, :, (fc - 8) * 128:(fc - 7) * 128], ktk[:], r=["ktk"], w=[("ktok_d", fc - 8)])

            def hn_chunk(c, kc):
                if c < 16:
                    return hnT[:, kc, 1 + 128 * c:1 + 128 * (c + 1)]
                return hncT[:, kc, (c - 16) * 128:(c - 15) * 128]

            def hn_tok(c):
                return HN_MAIN[c // 4] if c < 16 else HN_CTX[0]

            def bform(c0, ncols, nch, epi):
                slot = load_blk(c0, ncols)
                for c in range(nch):
                    pb = c % 6
                    for kc in range(8):
                        S.op('pe', lambda kc=kc, c=c, pb=pb, slot=slot: PE.matmul(
                            ps[pb][:, 0:ncols], lhsT=hn_chunk(c, kc), rhs=wblk[slot][:, kc, 0:ncols], start=(kc == 0), stop=(kc == 7)),
                            r=[f"wblk{slot}", hn_tok(c)], w=[P(pb)], sig=(kc == 7))
                    epi(c, pb)

            vctr = [0]
            for half in range(2):
                def epi_v(c, pb, half=half):
                    s3 = vctr[0] % 3
                    vctr[0] += 1
                    S.op('act', lambda: ACT.copy(out=vstg[s3][:], in_=ps[pb][:, :]), r=[P(pb)], w=[f"vstg{s3}"])
                    S.dma('sp', v_d[c][:, half * 512:(half + 1) * 512], vstg[s3][:], r=[f"vstg{s3}"], w=[("v_d", c, half)])
                bform(2048 + half * 512, 512, 18, epi_v)

            def epi_g(c, pb):
                S.op('dve', lambda: DVE.tensor_tensor(out=gates_all[:, c, :], in0=ps[pb][:, 0:16], in1=bg_bc[:], op=ALU.add),
                     r=[P(pb), "bg_bc"], w=[("gates", c)])
            bform(3072, 16, 18, epi_g)

            vsctr = [0]
            for half in range(2):
                def epi_vs(c, pb, half=half):
                    s2 = vsctr[0] % 3
                    vsctr[0] += 1
                    gpipe.push(ps[pb][:, :], ut[s2][:], vsstg[s2][:], [P(pb)], f"ut{s2}", f"vsstg{s2}",
                               after=lambda c=c, half=half, s2=s2: S.dma('sp', vs_d[c][:, half * 512:(half + 1) * 512], vsstg[s2][:], r=[f"vsstg{s2}"], w=[("vs_d", c, half)]))
                bform(5136 + half * 512, 512, 16, epi_vs)
            gpipe.flush()

            def aform(col0, kind, dst_d):
                for blk in range(2):
                    slot = load_blk(col0 + blk * 512, 512)
                    for j in range(4):
                        fc = blk * 4 + j
                        sg = stg[fc % 2]
                        stok = f"stg{fc % 2}"
                        for ti in range(4):
                            pb = (fc * 4 + ti) % 8
                            for kc in range(8):
                                S.op('pe', lambda kc=kc, j=j, pb=pb, ti=ti, slot=slot: PE.matmul(
                                    ps[pb][:, :], lhsT=wblk[slot][:, kc, j * 128:(j + 1) * 128], rhs=hnT[:, kc, 1 + 512 * ti:1 + 512 * (ti + 1)],
                                    start=(kc == 0), stop=(kc == 7)), r=[f"wblk{slot}", HN_MAIN[ti]], w=[P(pb)], sig=(kc == 7))
                            dst = sg[:, 512 * ti:512 * (ti + 1)]
                            if kind == 'sig':
                                S.op('act', lambda pb=pb, dst=dst: ACT.activation(out=dst, in_=ps[pb][:, :], func=AF.Sigmoid), r=[P(pb)], w=[(stok, ti)])
                            elif kind == 'sigg':
                                t1 = ut[ti % 3]
                                S.op('act', lambda pb=pb, t1=t1: ACT.activation(out=t1[:], in_=ps[pb][:, :], func=AF.Sigmoid), r=[P(pb)], w=[f"ut{ti % 3}"])
                                S.op('pool', lambda t1=t1, dst=dst, fc=fc: POOL.tensor_scalar(out=dst, in0=t1[:], scalar1=vecT[:, R_GH + fc:R_GH + fc + 1], scalar2=0.0,
                                                                                               op0=ALU.mult, op1=ALU.add), r=[f"ut{ti % 3}"], w=[(stok, ti)])
                            else:
                                gpipe.push(ps[pb][:, :], ut[ti % 3][:], dst, [P(pb)], f"ut{ti % 3}", (stok, ti))
                        if kind == 'gelu':
                            gpipe.flush()
                        S.dma('sp', dst_d[fc], sg[:], r=[(stok, i) for i in range(4)], w=[(dst_d.tensor.name, fc)])

            aform(3088, 'sigg', og_d)
            aform(4112, 'gelu', gu_d)
            aform(6160, 'sig', ga_d)
            aform(7184, 'sig', gb_d)
            S.barrier()

        Sin = sb(mx, "Sin", [128, 8, 514], F32)
        with ExitStack() as st:
            lfw = sb(st, "lfw", [128, 18, 8], F32)
            dif = sb(st, "dif", [128, 18, 8], F32)
            S.op('act', lambda: ACT.activation(out=lfw[:, :, 0:4], in_=gates_all[:, :, 4:8], func=AF.Exp, scale=-1.0), w=["lfw"], small=True)
            S.op('act', lambda: ACT.activation(out=lfw[:, :, 4:8], in_=gates_all[:, :, 12:16], func=AF.Exp, scale=-1.0), w=["lfw"], small=True)
            S.op('act', lambda: ACT.activation(out=lfw[:], in_=lfw[:], func=AF.Ln, bias=one_t[:, 0:1], scale=1.0), w=["lfw"], small=True)
            S.op('dve', lambda: DVE.tensor_scalar(out=lfw[:], in0=lfw[:], scalar1=-1.0, scalar2=None, op0=ALU.mult), r=["lfw"], w=["lfw"], small=True)
            S.op('pe', lambda: PE.matmul(ps[0][:, 0:72], lhsT=triL[:], rhs=lfw[:, :, 0:4], start=True, stop=True), r=["lfw"], w=[P(0)], sig=False)
            S.op('pe', lambda: PE.matmul(ps[0][:, 72:144], lhsT=triU[:], rhs=lfw[:, :, 4:8], start=True, stop=True), r=["lfw"], w=[P(0)], sig=False)
            S.op('pe', lambda: PE.matmul(ps[1][:, 0:144], lhsT=ones_f[:], rhs=lfw[:], start=True, stop=True), r=["lfw"], w=[P(1)], sig=True)
            S.op('dve', lambda: DVE.tensor_copy(out=cs_b[:], in_=ps[0][:, 0:144].rearrange("p (d c h) -> p d c h", d=2, c=18)), r=[P(0)], w=["cs_b"], small=True)
            S.op('dve', lambda: DVE.tensor_copy(out=cs_g[:], in_=ps[1][:, 0:144].rearrange("p (c h) -> p c h", c=18)), r=[P(1)], w=["cs_g"], small=True)
            S.op('act', lambda: ACT.activation(out=sc_all[:, :, 0:4], in_=cs_b[:, 0, :, :], func=AF.Exp), r=["cs_b"], w=["sc_all"], small=True)
            S.op('act', lambda: ACT.activation(out=sc_all[:, :, 4:8], in_=cs_b[:, 1, :, :], func=AF.Exp), r=["cs_b"], w=["sc_all"], small=True)
            S.op('dve', lambda: DVE.tensor_tensor(out=dif[:, :, 0:4], in0=gates_all[:, :, 0:4], in1=cs_b[:, 0, :, :], op=ALU.subtract), r=["cs_b"], w=["dif"], small=True)
            S.op('dve', lambda: DVE.tensor_tensor(out=dif[:, :, 4:8], in0=gates_all[:, :, 8:12], in1=cs_b[:, 1, :, :], op=ALU.subtract), r=["cs_b"], w=["dif"], small=True)
            S.op('act', lambda: ACT.activation(out=sc_all[:, :, 8:16], in_=dif[:], func=AF.Exp), r=["dif"], w=["sc_all"], small=True)
            S.op('act', lambda: ACT.activation(out=sc_all[:, :, 16:24], in_=cs_g[:], func=AF.Exp), r=["cs_g"], w=["sc_all"], small=True)
            S.op('dve', lambda: DVE.tensor_tensor(out=sc_all[:, :, 24:32], in0=sc_all[:, :, 8:16], in1=sc_all[:, :, 16:24], op=ALU.mult), r=["sc_all"], w=["sc_all"], small=True)
            S.op('dve', lambda: DVE.tensor_reduce(out=Gseg[:], in_=cap(cs_g, 0, [[1, 8], [8, 16]]), axis=mybir.AxisListType.X, op=ALU.add), r=["cs_g"], w=["Gseg"], small=True)
            S.op('dve', lambda: DVE.memset(LL[:], 0.0), w=["LL"], small=True)
            for c in range(14, -1, -1):
                S.op('dve', lambda c=c: DVE.tensor_tensor(out=LL[:, c, 0:4], in0=LL[:, c + 1, 0:4], in1=cs_g[:, c + 1, 0:4], op=ALU.add), r=["cs_g"], w=["LL"], small=True)
            for c in range(1, 16):
                S.op('dve', lambda c=c: DVE.tensor_tensor(out=LL[:, c, 4:8], in0=LL[:, c - 1, 4:8], in1=cs_g[:, c - 1, 4:8], op=ALU.add), r=["cs_g"], w=["LL"], small=True)
            S.op('dve', lambda: DVE.tensor_copy(out=LL[:, 16, 0:4], in_=cs_g[:, 17, 0:4]), w=["LL"], small=True)
            S.op('dve', lambda: DVE.tensor_copy(out=LL[:, 17, 4:8], in_=cs_g[:, 16, 4:8]), w=["LL"], small=True)
            S.op('act', lambda: ACT.activation(out=LL[:], in_=LL[:], func=AF.Exp), r=["LL"], w=["LL"], small=True)
            S.op('dve', lambda: DVE.tensor_tensor(out=psc[:], in0=LL[:], in1=sc_all[:, :, 24:32], op=ALU.mult), r=["LL", "sc_all"], w=["psc"], small=True)
            S.barrier()
        dbg_dump("gates", gates_all[:], [])
        dbg_dump("sc_all", sc_all[:], [])

        _slc = [0]

        def sweep_loads(pool, names):
            _slc[0] += 1
            return {n: [sb(pool, f"ld{_slc[0]}_{n}{i}", shp, dt) for i in range(2)] for n, (shp, dt) in names.items()}

        with ExitStack() as st:
            St = sb(st, "St", [128, 8, 514], F32)
            Sctx = sb(st, "Sctx", [128, 8, 514], F32)
            lds = sweep_loads(st, {"ktok": ([128, D], BF16), "v": ([128, D], BF16)})
            vtl = [sb(st, f"vtl{i}", [128, 4, 257], BF16) for i in range(2)]
            lctr = [0]

            def p1_pass(chunks, d, dst, dtok):
                n = len(chunks)
                for i, c in enumerate(chunks):
                    slot = lctr[0] % 2
                    lctr[0] += 1
                    kt, vv, vt = lds["ktok"][slot], lds["v"][slot], vtl[slot]
                    S.dma('sp', kt[:], ktok_d[c], w=[f"ld_ktok{slot}"])
                    S.dma('sp', vv[:], v_d[c], w=[f"ld_v{slot}"])
                    S.op('dve', lambda: DVE.tensor_tensor(out=vt[:, :, 0:256], in0=vv[:, :].rearrange("p (h v) -> p h v", h=4),
                                                          in1=cap(psc, c * 8 + 4 * d, [[1, 4], [0, 256]]), op=ALU.mult),
                         r=[f"ld_v{slot}", "psc"], w=[f"vtl{slot}"])
                    S.op('dve', lambda: DVE.tensor_copy(out=vt[:, :, 256], in_=psc[:, c, 4 * d:4 * d + 4]), w=[f"vtl{slot}"], small=True)
                    for h in range(4):
                        for kc in range(2):
                            pb = h * 2 + kc
                            S.op('pe', lambda h=h, kc=kc, pb=pb: PE.matmul(ps[pb][:, 0:257], lhsT=kt[:, h * 256 + kc * 128:h * 256 + (kc + 1) * 128], rhs=vt[:, h, :],
                                                                          start=(i == 0), stop=(i == n - 1)), r=[f"ld_ktok{slot}", f"vtl{slot}"], w=[P(pb)],
                                 sig=(i == n - 1 or (h == 3 and kc == 1)))
                for h in range(4):
                    for kc in range(2):
                        pb = h * 2 + kc
                        eng = 'act' if (pb % 2 == 0) else 'dve'
                        if eng == 'act':
                            S.op('act', lambda h=h, kc=kc, pb=pb: ACT.copy(out=dst[:, d * 4 + h, kc * 257:(kc + 1) * 257], in_=ps[pb][:, 0:257]), r=[P(pb)], w=[(dtok, d * 4 + h, kc)])
                        else:
                            S.op('dve', lambda h=h, kc=kc, pb=pb: DVE.tensor_copy(out=dst[:, d * 4 + h, kc * 257:(kc + 1) * 257], in_=ps[pb][:, 0:257]), r=[P(pb)], w=[(dtok, d * 4 + h, kc)])

            p1_pass([16, 17], 0, Sctx, "Sctx")
            p1_pass([17, 16], 1, Sctx, "Sctx")
            p1_pass(list(range(16)), 0, St, "St")
            p1_pass(list(range(16)), 1, St, "St")
            S.barrier()
            dbg_dump("Sctx", Sctx[:], ["Sctx"])
            dbg_dump("Sloc", St[:], [("St", i) for i in range(8)])
            xout_v = [xo.rearrange("(r p) w -> p r w", p=128) for xo in xout_l]
            for i in range(4):
                S.dma('sp', xin_l[i][:, 0:1028], St[:, 2 * i:2 * i + 2, :].rearrange("p a b -> p (a b)"), r=[("St", 2 * i), ("St", 2 * i + 1)], w=[("xin", i)])
                S.dma('sp', xin_l[i][:, 1028:1030], Gseg[:, 2 * i:2 * i + 2], r=[], w=[("xin2", i)])
            for i in range(4):
                if KSTOP == 'p1':
                    S.dma('sp', xout_l[i][0:128, :], xin_l[i], r=[("xin", i), ("xin2", i)], w=[("xout", i)])
                else:
                    S.custom('pool', lambda sem, i=i: POOL.collective_compute("AllGather", ALU.bypass, replica_groups=[[0, 1, 2, 3], [4, 5, 6, 7]],
                                                                              ins=[xin_l[i].opt()], outs=[xout_l[i].opt()]).then_inc(sem, 1),
                             1, r=[("xin", i), ("xin2", i)], w=[("xout", i)])
            with ExitStack() as sg:
                gvt = sb(sg, "sgu_gv", [128, 4, D], F32)
                vnt = sb(sg, "sgu_vn", [128, 4, D], BF16)
                gut = sb(sg, "sgu_gu", [128, 8, 512], BF16)
                ybt = sb(sg, "sgu_yb", [128, 8, 512], BF16)
                gsgu_bc = sb(sg, "gsgu_bc", [128, D], F32)
                st6s = sb(sg, "sgu_st6", [128, 4, 2, 6], F32)
                mvs = sb(sg, "sgu_mv", [128, 4, 2], F32)
                msq = sb(sg, "sgu_msq", [128, 2, 4], F32)
                S.dma('sp', gsgu_bc[:], bass.AP(g_sgu.tensor, 0, [[0, 128], [1, D]]), w=["gsgu_bc"])
                for g4 in range(4):
                    S.dma('sp', gvt[:], vs_d.rearrange("c p f -> p c f")[:, 4 * g4:4 * g4 + 4, :], w=["sgu_gv"])
                    S.dma('sp', gut[:], gu_d.rearrange("f p t -> p f t")[:, :, 512 * g4:512 * (g4 + 1)], w=["sgu_gu"])
                    for cq in range(4):
                        for i2 in range(2):
                            S.op('dve', lambda cq=cq, i2=i2: DVE.bn_stats(out=st6s[:, cq, i2, :], in_=gvt[:, cq, i2 * 512:(i2 + 1) * 512]),
                                 r=["sgu_gv"], w=[("sgu_st6", cq)], small=True)
                    for cq in range(4):
                        S.op('dve', lambda cq=cq: DVE.bn_aggr(out=mvs[:, cq, :], in_=st6s[:, cq, :, :].rearrange("p a b -> p (a b)")),
                             r=[("sgu_st6", cq)], w=["sgu_mv"], small=True)
                    S.op('dve', lambda: DVE.tensor_tensor(out=msq[:, 0, :], in0=mvs[:, :, 0], in1=mvs[:, :, 0], op=ALU.mult), r=["sgu_mv"], w=["sgu_msq"], small=True)
                    S.op('dve', lambda: DVE.tensor_tensor(out=msq[:, 0, :], in0=msq[:, 0, :], in1=mvs[:, :, 1], op=ALU.add), w=["sgu_msq"], small=True)
                    S.op('act', lambda: ACT.activation(out=msq[:, 1, :], in_=msq[:, 0, :], func=AF.Ln, bias=eps_t[:, 0:1], scale=1.0), r=["sgu_msq"], w=["sgu_rs"], small=True)
                    S.op('act', lambda: ACT.activation(out=msq[:, 1, :], in_=msq[:, 1, :], func=AF.Exp, scale=-0.5), w=["sgu_rs"], small=True)
                    for cq in range(4):
                        S.op('dve', lambda cq=cq: DVE.scalar_tensor_tensor(out=vnt[:, cq, :], in0=gvt[:, cq, :], scalar=msq[:, 1, cq:cq + 1], in1=gsgu_bc[:],
                                                                           op0=ALU.mult, op1=ALU.mult), r=["sgu_rs", "sgu_gv", "gsgu_bc"], w=[("sgu_vn", cq)], small=True)
                    for ccf in range(8):
                        gq = ccf // 2
                        for cq in range(4):
                            S.op('pe', lambda ccf=ccf, cq=cq, gq=gq: PE.matmul(ps[ccf][:, cq * 128:(cq + 1) * 128], lhsT=vnt[:, cq, ccf * 128:(ccf + 1) * 128], rhs=wsT[:, gq, :],
                                                                               start=True, stop=False), r=[("sgu_vn", cq), "wsT"], w=[P(ccf)], sig=False)
                            S.op('pe', lambda ccf=ccf, cq=cq, gq=gq: PE.matmul(ps[ccf][:, cq * 128:(cq + 1) * 128], lhsT=ones_b[0:1, :], rhs=bs_row[0:1, gq * 128:(gq + 1) * 128],
                                                                               start=False, stop=True), w=[P(ccf)], sig=(cq == 3))
                        S.op('dve', lambda ccf=ccf: DVE.tensor_tensor(out=ybt[:, ccf, :], in0=ps[ccf][:, :], in1=gut[:, ccf, :], op=ALU.mult),
                             r=[P(ccf), "sgu_gu"], w=[("sgu_yb", ccf)])
                    S.dma('sp', gu_d.rearrange("f p t -> p f t")[:, :, 512 * g4:512 * (g4 + 1)], ybt[:], r=[("sgu_yb", i) for i in range(8)] + ["sgu_gu"], w=[("gu_d", g4)])
                S.barrier()
            Gall = sb(st, "Gall", [128, 4, 8], F32)
            Ug = [sb(st, f"Ug{i}", [128, 4, 514], F32) for i in range(2)]
            Tt = sb(st, "Tt", [128, 514], F32)
            for i in range(4):
                S.dma('sp', Gall[:, :, 2 * i:2 * i + 2], xout_v[i][:, :, 1028:1030], r=[("xout", i)], w=[("Gall", i)])
            S.op('act', lambda: ACT.activation(out=Gall[:], in_=Gall[:], func=AF.Exp), r=[("Gall", i) for i in range(4)], w=["Gall"], small=True)
            for hd in range(8):
                d = hd // 4
                U = Ug[hd % 2]
                utok = f"Ug{hd % 2}"
                S.dma('sp', U[:], xout_v[hd // 2][:, :, (hd % 2) * 514:(hd % 2 + 1) * 514], r=[("xout", hd // 2)], w=[utok])
                S.op('dve', lambda hd=hd: DVE.tensor_copy(out=Tt[:], in_=Sctx[:, hd, :]), r=["Sctx"], w=["Tt"])
                first = 0 if d == 0 else 3
                S.op('dve', lambda hd=hd, first=first: DVE.tensor_scalar(out=Sin[:, hd, :], in0=Tt[:], scalar1=metat[:, 1 + first:2 + first], scalar2=None, op0=ALU.mult),
                     r=["meta"], w=[("Sin", hd)])
                order = [0, 1, 2] if d == 0 else [3, 2, 1]
                for i in order:
                    tgt = i + 1 if d == 0 else i - 1
                    S.op('dve', lambda i=i, hd=hd, U=U: DVE.scalar_tensor_tensor(out=Tt[:], in0=Tt[:], scalar=Gall[:, i, hd:hd + 1], in1=U[:, i, :],
                                                                                  op0=ALU.mult, op1=ALU.add), r=[utok, "Gall"], w=["Tt"])
                    S.op('dve', lambda tgt=tgt, hd=hd: DVE.scalar_tensor_tensor(out=Sin[:, hd, :], in0=Tt[:], scalar=metat[:, 1 + tgt:2 + tgt], in1=Sin[:, hd, :],
                                                                                op0=ALU.mult, op1=ALU.add), w=[("Sin", hd)])
            dbg_dump("Sin", Sin[:], [("Sin", i) for i in range(8)])
            S.barrier()

        def mlstm_chunk(c, d, bufs, S16, emit_all, hook=None):
            q_t, kT_t, kt_t, v_t = bufs["q"], bufs["kT"], bufs["ktok"], bufs["v"]
            vt, vte, PM4, sm = bufs["vt"], bufs["vte"], bufs["PM4"], bufs["sm"]
            ltoks = bufs["ltoks"]
            mask = mL16 if d == 0 else mU16
            rs0, vs0, eg0, vse0 = 0 + 4 * d, 8 + 4 * d, 16 + 4 * d, 24 + 4 * d
            S.op('dve', lambda: DVE.tensor_tensor(out=vt[:, :, 0:256], in0=v_t[:, :].rearrange("p (h v) -> p h v", h=4),
                                                  in1=cap(sc_all, c * 32 + vs0, [[1, 4], [0, 256]]), op=ALU.mult), r=[ltoks["v"]], w=["vt"])
            S.op('dve', lambda: DVE.tensor_copy(out=vt[:, :, 256], in_=sc_all[:, c, vs0:vs0 + 4]), w=["vt"], small=True)
            S.op('pool', lambda: POOL.tensor_tensor(out=vte[:, :, 0:256], in0=v_t[:, :].rearrange("p (h v) -> p h v", h=4),
                                                    in1=cap(sc_all, c * 32 + vse0, [[1, 4], [0, 256]]), op=ALU.mult), r=[ltoks["v"]], w=["vte"])
            S.op('pool', lambda: POOL.tensor_copy(out=vte[:, :, 256], in_=sc_all[:, c, vse0:vse0 + 4]), w=["vte"], small=True)
            for h in range(4):
                for kc in range(2):
                    S.op('pe', lambda kc=kc, h=h: PE.matmul(ps[0][:, h * 128:(h + 1) * 128], lhsT=kT_t[:, h * 2 + kc, :], rhs=q_t[:, h * 2 + kc, :],
                                                           start=(kc == 0), stop=(kc == 1)), r=[ltoks["kT"], ltoks["q"]], w=[P(0)], sig=(h == 3 and kc == 1))
            S.op('dve', lambda: DVE.tensor_tensor(out=PM4[:], in0=ps[0][:, :].rearrange("p (h t) -> p h t", h=4),
                                                  in1=cap(mask, 0, [[0, 4], [1, 128]]), op=ALU.mult), r=[P(0)], w=["PM4"])
            if hook:
                hook('s')
            for h in range(4):
                S.op('pe', lambda h=h: PE.matmul(ps[1 + h][:, 0:257], lhsT=PM4[:, h, :], rhs=vt[:, h, :], start=True, stop=False),
                     r=["PM4", "vt"], w=[P(1 + h)], sig=False)
                for kc in range(2):
                    S.op('pe', lambda kc=kc, h=h: PE.matmul(ps[1 + h][:, 0:257], lhsT=q_t[:, h * 2 + kc, :], rhs=S16[:, h, kc, :], start=False, stop=(kc == 1)),
                         r=[ltoks["q"], ("S16", h)], w=[P(1 + h)], sig=(kc == 1))
            if hook:
                hook('o')
            den4 = psall[:, 1:5, 256]
            rs4 = sc_all[:, c, rs0:rs0 + 4]
            PO = [P(1 + h) for h in range(4)]
            S.op('dve', lambda: DVE.tensor_tensor(out=sm[:, 0, :], in0=den4, in1=rs4, op=ALU.mult), r=PO, w=["sm"], small=True)
            S.op('dve', lambda: DVE.tensor_scalar(out=sm[:, 1, :], in0=sm[:, 0, :], scalar1=-1.0, scalar2=1.0, op0=ALU.mult, op1=ALU.max), w=["sm"], small=True)
            S.op('dve', lambda: DVE.tensor_scalar(out=sm[:, 2, :], in0=sm[:, 0, :], scalar1=1.0, scalar2=None, op0=ALU.max), w=["sm"], small=True)
            S.op('dve', lambda: DVE.tensor_tensor(out=sm[:, 2, :], in0=sm[:, 2, :], in1=sm[:, 1, :], op=ALU.max), w=["sm"], small=True)
            S.op('dve', lambda: DVE.reciprocal(out=sm[:, 3, :], in_=sm[:, 2, :]), w=["sm"], small=True)
            S.op('dve', lambda: DVE.tensor_tensor(out=sm[:, 4, :], in0=sm[:, 3, :], in1=rs4, op=ALU.mult), w=["sm"], small=True)
            emit_all(sm, 4)
            for h in range(4):
                hd = d * 4 + h
                for kc in range(2):
                    pU = 5 + kc
                    S.op('pe', lambda kc=kc, h=h, pU=pU: PE.matmul(ps[pU][:, 0:257], lhsT=kt_t[:, h * 256 + kc * 128:h * 256 + (kc + 1) * 128], rhs=vte[:, h, :],
                                                                  start=True, stop=True), r=[ltoks["ktok"], "vte"], w=[P(pU)])
                    S.op('dve', lambda kc=kc, hd=hd, pU=pU, h=h: DVE.scalar_tensor_tensor(
                        out=Sin[:, hd, kc * 257:(kc + 1) * 257], in0=Sin[:, hd, kc * 257:(kc + 1) * 257], scalar=sc_all[:, c, eg0 + h:eg0 + h + 1],
                        in1=ps[pU][:, 0:257], op0=ALU.mult, op1=ALU.add), r=[P(pU)], w=[("Sin", hd)])
                S.op('act', lambda h=h, hd=hd: ACT.activation(out=S16[:, h, :, :], in_=Sin[:, hd, :].rearrange("p (k v) -> p k v", k=2), func=AF.Copy, scale=0.0625),
                     r=[("Sin", hd)], w=[("S16", h)])
            if hook:
                hook('u')

        SINGLE = {"hb", "vs"}

        def ltok(name, slot):
            return f"ld_{name}" if name in SINGLE else f"ld_{name}{slot}"

        def issue_loads(lds, slot, c, extra=()):
            S.dma('sp', lds["q"][slot][:], qT_d.rearrange("f p t -> p f t")[:, :, c * 128:(c + 1) * 128], w=[f"ld_q{slot}"])
            S.dma('sp', lds["kT"][slot][:], kT_d.rearrange("f p t -> p f t")[:, :, c * 128:(c + 1) * 128], w=[f"ld_kT{slot}"])
            S.dma('sp', lds["ktok"][slot][:], ktok_d[c], w=[f"ld_ktok{slot}"])
            S.dma('sp', lds["v"][slot][:], v_d[c], w=[f"ld_v{slot}"])
            for (name, src) in extra:
                S.dma('sp', lds[name][slot][:], src, w=[ltok(name, slot)])

        def mk_bufs(lds, slot, common):
            b = dict(common)
            for n in lds:
                b[n] = lds[n][slot]
            b["ltoks"] = {n: ltok(n, slot) for n in lds}
            return b

        with ExitStack() as st:
          if KSTOP not in ('xchg', 'p1'):
                lds = sweep_loads(st, {"q": ([128, 8, 128], BF16), "kT": ([128, 8, 128], BF16), "ktok": ([128, D], BF16), "v": ([128, D], BF16)})
                common = {"vt": sb(st, "vt", [128, 4, 257], BF16), "vte": sb(st, "vte", [128, 4, 257], BF16),
                          "PM4": sb(st, "PM4", [128, 4, 128], BF16), "sm": sb(st, "sm", [128, 5, 4], F32)}
                S16 = sb(st, "S16", [128, 4, 2, 257], BF16)
                hbt = [sb(st, f"hbt{i}", [128, D], F32) for i in range(2)]
                for h in range(4):
                    S.op('act', lambda h=h: ACT.activation(out=S16[:, h, :, :], in_=Sin[:, 4 + h, :].rearrange("p (k v) -> p k v", k=2), func=AF.Copy, scale=0.0625),
                         w=[("S16", h)])
                issue_loads(lds, 0, 15)
                for it, c in enumerate(range(15, -1, -1)):
                    slot = it % 2
                    if c > 0:
                        issue_loads(lds, 1 - slot, c - 1)
                    hb = hbt[slot]

                    def emit_all(sm, row, hb=hb, slot=slot):
                        for h in range(4):
                            S.op('act', lambda h=h: ACT.activation(out=hb[:, h * 256:(h + 1) * 256], in_=ps[1 + h][:, 0:256], func=AF.Identity, scale=sm[:, row, h:h + 1]),
                                 r=[P(1 + h), "sm"], w=[(f"hbt{slot}", h)])
                    mlstm_chunk(c, 1, mk_bufs(lds, slot, common), S16, emit_all)
                    S.dma('sp', hb_d[c], hb[:], r=[(f"hbt{slot}", h) for h in range(4)], w=[("hb_d", c)])
                dbg_dump("Sfin", Sin[:], [("Sin", i) for i in range(8)])
                S.barrier()

        with ExitStack() as st:
          if KSTOP not in ('xchg', 'bwd', 'p1'):
                lds = sweep_loads(st, {"q": ([128, 8, 128], BF16), "kT": ([128, 8, 128], BF16), "ktok": ([128, D], BF16), "v": ([128, D], BF16),
                                       "og": ([128, 8, 128], BF16)})
                hb1 = sb(st, "hb1", [128, D], F32)
                lds["hb"] = [hb1, hb1]
                common = {"vt": sb(st, "vtc", [128, 4, 257], BF16), "vte": sb(st, "vtec", [128, 4, 257], BF16),
                          "PM4": sb(st, "PM4c", [128, 4, 128], BF16), "sm": sb(st, "smc", [128, 5, 4], F32)}
                S16 = sb(st, "S16c", [128, 4, 2, 257], BF16)
                hm_t = sb(st, "hm_t", [128, D], F32)
                hh = sb(st, "hh", [128, D], BF16)
                st6 = sb(st, "st6", [128, 4, 6], F32)
                mv = sb(st, "mv", [128, 4, 2], F32)
                lnr = sb(st, "lnr", [128, 4], F32)
                t1 = cap(hcT, 0, [[1, D]])
                gv = cap(hcT, D, [[1, D]])
                ssq = sb(st, "ssq", [128, 2], F32)
                st6b = sb(st, "st6b", [128, 2, 6], F32)
                mvb = sb(st, "mvb", [128, 4], F32)
                vn = sb(st, "vn", [128, D], BF16)
                yaT = sb(st, "yaT", [128, 8, 512], BF16)
                ybT = sb(st, "ybT", [128, 8, 512], BF16)
                mixT = sb(st, "mixT", [128, 8, 512], BF16)
                tA = [sb(st, "tA0", [128, 512], F32)] * 2
                tB = [sb(st, "tB0", [128, 512], F32)] * 2
                sga = [sb(st, f"sga{i}", [128, 512], BF16) for i in range(2)]
                sgb = [sb(st, f"sgb{i}", [128, 512], BF16) for i in range(2)]
                wbr = [sb(st, f"wbr{i}", [128, 8, 512], BF16) for i in range(2)]

                def extra_for(c):
                    return [("og", og_d.rearrange("f p t -> p f t")[:, :, c * 128:(c + 1) * 128])]

                for h in range(4):
                    S.op('act', lambda h=h: ACT.activation(out=S16[:, h, :, :], in_=Sin[:, h, :].rearrange("p (k v) -> p k v", k=2), func=AF.Copy, scale=0.0625),
                         w=[("S16", h)])
                pending = []

                def flush_items():
                    while pending:
                        pending.pop(0)[1]()

                def c_hook(stage):
                    if stage == 's':
                        return
                    n_ab = sum(1 for k, _ in pending if k == 'ab')
                    if n_ab > 0:
                        n = n_ab if stage == 'u' else min(4, n_ab)
                    else:
                        n = min(1 if stage == 'o' else 2, len(pending))
                    for _ in range(n):
                        pending.pop(0)[1]()

                issue_loads(lds, 0, 0, extra_for(0))
                S.dma('sp', hb1[:], hb_d[0], w=["ld_hb"])
                for c in range(16):
                    slot = c % 2
                    cc = c % 4
                    tile = c // 4
                    if c < 15:
                        issue_loads(lds, 1 - slot, c + 1, extra_for(c + 1))
                    b = mk_bufs(lds, slot, common)
                    hbl, ogl = b["hb"], b["og"]

                    def emit_all(sm, row, hbl=hbl):
                        for h in range(4):
                            S.op('dve', lambda h=h: DVE.scalar_tensor_tensor(out=hm_t[:, h * 256:(h + 1) * 256], in0=ps[1 + h][:, 0:256], scalar=sm[:, row, h:h + 1],
                                                                             in1=hbl[:, h * 256:(h + 1) * 256], op0=ALU.mult, op1=ALU.add),
                                 r=[P(1 + h), "sm", "ld_hb"], w=[("hm", h)], small=True)
                        for h in range(4):
                            S.op('dve', lambda h=h: DVE.bn_stats(out=st6[:, h, :], in_=hm_t[:, h * 256:(h + 1) * 256]), r=[("hm", h)], w=[("st6", h)], small=True)
                        for h in range(4):
                            S.op('dve', lambda h=h: DVE.bn_aggr(out=mv[:, h, :], in_=st6[:, h, :]), r=[("st6", h)], w=[("mv", h)], small=True)
                    mlstm_chunk(c, 0, b, S16, emit_all, hook=c_hook)
                    if c < 15:
                        S.dma('sp', hb1[:], hb_d[c + 1], w=["ld_hb"])
                    S.op('act', lambda: ACT.activation(out=lnr[:], in_=mv[:, :, 1], func=AF.Ln, bias=eps_t[:, 0:1], scale=1.0), r=[("mv", h) for h in range(4)], w=["lnr"], small=True)
                    S.op('act', lambda: ACT.activation(out=lnr[:], in_=lnr[:], func=AF.Exp, scale=-0.5), w=["lnr"], small=True)
                    for h in range(4):
                        S.op('dve', lambda h=h: DVE.tensor_scalar(out=hh[:, h * 256:(h + 1) * 256], in0=hm_t[:, h * 256:(h + 1) * 256], scalar1=mv[:, h, 0:1],
                                                                  scalar2=lnr[:, h:h + 1], op0=ALU.subtract, op1=ALU.mult), r=["lnr", ("mv", h)], w=["hh"], strict=True, small=True)
                    psb = ps[0][:, :].bitcast(BF16)
                    for fc in range(8):
                        S.op('pe', lambda fc=fc: PE.transpose(out=psb[:, fc * 128:(fc + 1) * 128], in_=hh[:, fc * 128:(fc + 1) * 128], identity=ident_b[:]),
                             r=["hh"], w=[P(0)], sig=(fc == 7))
                    S.op('dve', lambda cc=cc, ogl=ogl: DVE.tensor_tensor(out=yaT[:, :, cc * 128:(cc + 1) * 128], in0=psb[:, :].rearrange("p (f t) -> p f t", f=8),
                                                                          in1=ogl[:], op=ALU.mult), r=[P(0), f"ld_og{slot}"], w=[("yaT", cc)])
                    if cc == 0:
                        S.dma('sp', ybT[:], gu_d.rearrange("f p t -> p f t")[:, :, 512 * tile:512 * (tile + 1)], w=["ybT"])
                    if c in (0, 4):
                        dbg_dump(f"hm{c}", hm_t[:], [("hm", h) for h in range(4)])
                        dbg_dump(f"hh{c}", hh[:], ["hh"])
                    if cc < 3:
                        continue
                    if tile in (0, 1):
                        dbg_dump(f"ya{tile}", yaT[:], [("yaT", i) for i in range(4)])
                        dbg_dump(f"yb{tile}", ybT[:], ["ybT"])
                    flush_items()
                    t0 = 512 * tile
                    YA = [("yaT", i) for i in range(4)]
                    YB = ["ybT"]
                    MX = [("mixT", i) for i in range(8)]

                    def ab_item(dc, t0=t0, YA=YA, YB=YB):
                        blk, j = dc // 4, dc % 4
                        if j == 0:
                            wload(wbr[0][:], "wbr0", w_ba.rearrange("(kc p) n -> p kc n", p=128)[:, :, blk * 512:(blk + 1) * 512])
                            wload(wbr[1][:], "wbr1", w_bb.rearrange("(kc p) n -> p kc n", p=128)[:, :, blk * 512:(blk + 1) * 512])
                        s2 = dc % 2
                        S.dma('sp', sga[s2][:], ga_d[dc][:, t0:t0 + 512], w=[f"sga{s2}"])
                        S.dma('sp', sgb[s2][:], gb_d[dc][:, t0:t0 + 512], w=[f"sgb{s2}"])
                        pa, pb2 = 7, 0
                        for kc in range(8):
                            S.op('pe', lambda kc=kc: PE.matmul(ps[pa][:, :], lhsT=wbr[0][:, kc, j * 128:(j + 1) * 128], rhs=yaT[:, kc, :],
                                                               start=(kc == 0), stop=(kc == 7)), r=["wbr0"] + YA, w=[P(pa)], sig=(kc == 7))
                        for kc in range(8):
                            S.op('pe', lambda kc=kc: PE.matmul(ps[pb2][:, :], lhsT=wbr[1][:, kc, j * 128:(j + 1) * 128], rhs=ybT[:, kc, :],
                                                               start=(kc == 0), stop=(kc == 7)), r=["wbr1"] + YB, w=[P(pb2)], sig=(kc == 7))
                        S.op('dve', lambda: DVE.tensor_tensor(out=tA[s2][:], in0=ps[pa][:, :], in1=sga[s2][:], op=ALU.mult),
                             r=[P(pa), f"sga{s2}"], w=["tA0"])
                        S.op('dve', lambda: DVE.tensor_tensor(out=tB[s2][:], in0=ps[pb2][:, :], in1=sgb[s2][:], op=ALU.mult),
                             r=[P(pb2), f"sgb{s2}"], w=["tB0"])
                        S.op('pool', lambda: POOL.tensor_tensor(out=mixT[:, dc, :], in0=tA[s2][:], in1=tB[s2][:], op=ALU.add),
                             r=["tA0", "tB0"], w=[("mixT", dc)])

                    def out_item(dc, t0=t0, tile=tile, MX=MX):
                        blk, j = dc // 4, dc % 4
                        if j == 0:
                            wload(wbr[blk][:], f"wbr{blk}", w_o.rearrange("(kc p) n -> p kc n", p=128)[:, :, blk * 512:(blk + 1) * 512])
                        pb = 7 if dc % 2 == 0 else 0
                        for kc in range(8):
                            S.op('pe', lambda kc=kc: PE.matmul(ps[pb][:, :], lhsT=wbr[blk][:, kc, j * 128:(j + 1) * 128], rhs=mixT[:, kc, :],
                                                               start=(kc == 0), stop=(kc == 7)), r=[f"wbr{blk}"] + MX, w=[P(pb)], sig=(kc == 7))
                        S.op('dve', lambda: DVE.scalar_tensor_tensor(
                            out=hT[:, dc, 1 + t0:1 + t0 + 512], in0=ps[pb][:, :], scalar=prm[:, P_G5, dc:dc + 1], in1=hT[:, dc, 1 + t0:1 + t0 + 512],
                            op0=ALU.mult, op1=ALU.add), r=[P(pb)], w=[("hT2", tile, dc)])

                    for dc in range(8):
                        pending.append(('ab', lambda dc=dc, f=ab_item: f(dc)))
                    for dc in range(8):
                        pending.append(('out', lambda dc=dc, f=out_item: f(dc)))
                    if KITEMS == 0:
                        flush_items()
                flush_items()
                S.barrier()
        for nm, src in [("qT", qT_d), ("kT", kT_d), ("ktok", ktok_d), ("v", v_d), ("hb", hb_d), ("og", og_d), ("gu", gu_d), ("ga", ga_d), ("vs", vs_d)]:
            if nm in dbg:
                S.dma('sp', dbg[nm], src)
        if "h2" in dbg:
            S.dma('sp', dbg["h2"].rearrange("dc p t -> p dc t"), hT[:, :, :])
        S.barrier()
        mx.close()
        S.barrier()

        if KSTOP == "all":
            ffn(w_f2i, w_f2o, main_tiles, (P_GS2, P_SH2, P_GT2), (P_GS1C, P_SH1C, P_GT1C), "f2")

        if "h1" in dbg:
            S.dma('sp', dbg["h1"].rearrange("dc p t -> p dc t"), hT[:, :, :], r=[])
            S.dma('sp', dbg["hc1"].rearrange("dc p t -> p dc t"), hcT[:, :, :], r=[])
            S.barrier()

        def final_out():
            with ExitStack() as st:
                sq = sb(st, "fsq", [128, 8, 512], BF16)
                rstd = sb(st, "frstd", [128, 512], F32)
                yT = [sb(st, f"fy{i}", [128, 8, 512], F32) for i in range(2)]
                ot = [sb(st, f"fot{i}", [128, D], F32) for i in range(2)]
                for t in range(4):
                    o0 = 1 + 512 * t
                    y = yT[t % 2]
                    ytok = f"fy{t % 2}"
                    for dc in range(8):
                        S.op('act', lambda dc=dc: ACT.activation(out=sq[:, dc, :], in_=hT[:, dc, o0:o0 + 512], func=AF.Square), w=["fsq"])
                    for dc in range(8):
                        S.op('pe', lambda dc=dc: PE.matmul(ps[7][:, :], lhsT=ones_b[:], rhs=sq[:, dc, :], start=(dc == 0), stop=(dc == 7)),
                             r=["fsq"], w=[P(7)], sig=(dc == 7))
                    S.op('act', lambda: ACT.activation(out=rstd[:], in_=ps[7][:, :], func=AF.Ln, scale=1.0 / D, bias=eps_t[:, 0:1]), r=[P(7)], w=["frstd"])
                    S.op('act', lambda: ACT.activation(out=rstd[:], in_=rstd[:], func=AF.Exp, scale=-0.5), w=["frstd"])
                    for dc in range(8):
                        S.op('dve', lambda dc=dc, y=y: DVE.scalar_tensor_tensor(out=y[:, dc, :], in0=hT[:, dc, o0:o0 + 512], scalar=vecT[:, R_GFIN + dc:R_GFIN + dc + 1],
                                                                              in1=rstd[:], op0=ALU.mult, op1=ALU.mult),
                             r=["frstd"], w=[(ytok, dc)])
                    for cc in range(4):
                        c = 4 * t + cc
                        o = ot[c % 2]
                        for half in range(2):
                            pb = 2 + half + 2 * (c % 2)
                            for k4 in range(4):
                                dc = half * 4 + k4
                                S.op('pe', lambda dc=dc, k4=k4, pb=pb, y=y, cc=cc: PE.transpose(out=ps[pb][:, k4 * 128:(k4 + 1) * 128], in_=y[:, dc, cc * 128:(cc + 1) * 128],
                                                                                          identity=ident_f[:]),
                                     r=[(ytok, dc)], w=[P(pb)], sig=(k4 == 3))
                            if half == 0:
                                S.op('act', lambda pb=pb, o=o: ACT.copy(out=o[:, 0:512], in_=ps[pb][:, :]), r=[P(pb)], w=[f"fot{c % 2}a"])
                            else:
                                S.op('dve', lambda pb=pb, o=o: DVE.tensor_copy(out=o[:, 512:1024], in_=ps[pb][:, :]), r=[P(pb)], w=[f"fot{c % 2}b"])
                        S.dma('sp', out[128 * c:128 * (c + 1), :], o[:], r=[f"fot{c % 2}a", f"fot{c % 2}b"], w=[])
            S.barrier()

        final_out()
    return nc


def _host_inputs(inp):
    x = np.ascontiguousarray(inp["x"], dtype=np.float32)
    f32 = np.float32
    vec_common = [
        inp["b_ada"][0].reshape(72, 128), inp["g_ffn1"][0].reshape(8, 128), inp["g_mix"][0].reshape(8, 128),
        inp["conv_qk_w"][0].reshape(48, 128), inp["conv_qk_b"][0].reshape(16, 128), inp["g_head"][0].reshape(8, 128),
        inp["g_ffn2"][0].reshape(8, 128), inp["g_final"].reshape(8, 128)]
    quarter = D // 4
    fr = np.exp(-math.log(10000.0) * np.arange(quarter, dtype=f32) / quarter).astype(f32)
    freq = np.ascontiguousarray(fr.reshape(2, 128).T)
    consts = np.zeros((128, 3, 128), f32)
    consts[:, 0, :] = np.eye(128, dtype=f32)
    ii = np.arange(128)
    consts[:, 1, :] = (ii[:, None] <= ii[None, :]).astype(f32)
    consts[:, 2, :] = (ii[:, None] >= ii[None, :]).astype(f32)
    shared = dict(
        consts=consts, freq=freq, g_sgu=np.ascontiguousarray(inp["g_sgu"][0]), b_gates=np.ascontiguousarray(inp["b_gates"][0]),
        b_s=np.ascontiguousarray(inp["b_s"][0].reshape(512)), w_s=np.ascontiguousarray(inp["w_s"][0]),
        w_ffn1_in=np.ascontiguousarray(inp["w_ffn1_in"][0]),
        w_ffn1_out=np.ascontiguousarray(inp["w_ffn1_out"][0]), w_in=np.ascontiguousarray(inp["w_in"][0]),
        w_branch_a=np.ascontiguousarray(inp["w_branch_a"][0]), w_branch_b=np.ascontiguousarray(inp["w_branch_b"][0]),
        w_out=np.ascontiguousarray(inp["w_out"][0]), w_ffn2_in=np.ascontiguousarray(inp["w_ffn2_in"][0]),
        w_ffn2_out=np.ascontiguousarray(inp["w_ffn2_out"][0]))
    maps = []
    for core in range(8):
        b, j = core // 4, core % 4
        a = j * NT
        xs = np.zeros((NX, D), f32)
        xs[1:NT + 1] = x[b, a:a + NT]
        if j > 0:
            xs[0] = x[b, a - 1]
        if j < 3:
            xs[NX - 1] = x[b, a + NT]
        vecs = np.concatenate(vec_common + [inp["c"][b].reshape(8, 128), inp["c_ctx"].reshape(8, 128)], axis=0).astype(f32)
        meta = np.zeros((128, 8), f32)
        meta[:, 0] = j * 32 - 1
        meta[:, 1 + j] = 1.0
        meta[:, 5] = 1.0 if j > 0 else 0.0
        meta[:, 6] = 1.0 if j < 3 else 0.0
        m = dict(shared)
        m.update(w_ada=np.ascontiguousarray(inp["w_ada"][0][:, j * 2304:(j + 1) * 2304]), xs=xs, ctxb=np.ascontiguousarray(inp["ctx"][b], dtype=f32), vecs=np.ascontiguousarray(vecs), meta=meta)
        maps.append(m)
    return maps


_NC_CACHE = {}


def kernel(**inputs):
    inp = {k: np.asarray(v) for k, v in inputs.items()}
    maps = _host_inputs(inp)
    if "nc" not in _NC_CACHE:
        _NC_CACHE["nc"] = build()
    res = run_bass_kernel_spmd(_NC_CACHE["nc"], maps, core_ids=list(range(8)))
    outp = np.zeros((2, 4 * NT, D), np.float32)
    for core in range(8):
        b, j = core // 4, core % 4
        outp[b, j * NT:(j + 1) * NT] = res.results[core]["out"]
    kernel.last_results = res.results
    return outp
```
